# Optimizing a Trainium2 kernel written in Bass

```python
import jax, jax.numpy as jnp
from jax import lax
import numpy as np

D_MODEL = 1024
BATCH = 4
SEQ = 4096
DEPTH = 1

CHUNK = 64
FOX_HEAD_DIM = 64
FOX_WIDTH = D_MODEL // 2
FOX_HEADS = FOX_WIDTH // FOX_HEAD_DIM
Q_BLOCK = 128
SGU_WIDTH = D_MODEL // 2
SGU_GROUP_DIM = 64
SGU_GROUPS = SGU_WIDTH // SGU_GROUP_DIM
SGU_WINDOW = 128
N_BRANCHES = 2
D_FF = -(-8 * D_MODEL // (3 * 256)) * 256
EPS = 1e-6
FORGET_BIAS = 2.0

Q_OFF = 0
K_OFF = Q_OFF + FOX_WIDTH
V_OFF = K_OFF + FOX_WIDTH
F_OFF = V_OFF + FOX_WIDTH
U_OFF = F_OFF + FOX_HEADS
G_OFF = U_OFF + 2 * SGU_WIDTH
IN_COLS = G_OFF + N_BRANCHES * D_MODEL

kernel_name = "fox_gmlp_gated_hybrid_block"


def rmsnorm(x, g):
    xf = x.astype(jnp.float32)
    y = xf * lax.rsqrt(jnp.mean(xf * xf, axis=-1, keepdims=True) + EPS)
    return (y * g.astype(jnp.float32)).astype(x.dtype)


def layernorm(x, g, b):
    xf = x.astype(jnp.float32)
    mu = jnp.mean(xf, axis=-1, keepdims=True)
    xc = xf - mu
    y = xc * lax.rsqrt(jnp.mean(xc * xc, axis=-1, keepdims=True) + EPS)
    return (y * g.astype(jnp.float32) + b.astype(jnp.float32)).astype(x.dtype)


def forgetting_attention(q, k, v, log_f):
    s_len = q.shape[2]
    d_cum = jnp.cumsum(log_f, axis=-1)
    scale = FOX_HEAD_DIM ** -0.5
    outs = []
    for i in range(s_len // Q_BLOCK):
        q0, q1 = i * Q_BLOCK, (i + 1) * Q_BLOCK
        qb = q[:, :, q0:q1]
        kb = k[:, :, :q1]
        vb = v[:, :, :q1]
        logits = jnp.einsum('bhqd,bhkd->bhqk', qb, kb).astype(jnp.float32) * scale
        logits = logits + d_cum[:, :, q0:q1, None] - d_cum[:, :, None, :q1]
        q_pos = jnp.arange(q0, q1)[:, None]
        k_pos = jnp.arange(q1)[None, :]
        logits = jnp.where(k_pos <= q_pos, logits, -jnp.inf)
        p = jax.nn.softmax(logits, axis=-1)
        outs.append(jnp.einsum('bhqk,bhkd->bhqd', p.astype(vb.dtype), vb))
    return jnp.concatenate(outs, axis=2)


def spatial_gating(uv, g_norm, b_norm, w_spatial, b_spatial):
    bsz, s_len, _ = uv.shape
    u, v = uv[..., :SGU_WIDTH], uv[..., SGU_WIDTH:]
    v = layernorm(v, g_norm, b_norm)
    v = v.reshape(bsz, s_len // SGU_WINDOW, SGU_WINDOW, SGU_GROUPS, SGU_GROUP_DIM)
    t_idx = jnp.arange(SGU_WINDOW)[:, None]
    s_idx = jnp.arange(SGU_WINDOW)[None, :]
    mask = (s_idx // CHUNK) <= (t_idx // CHUNK)
    ws = jnp.where(mask[None], w_spatial, jnp.zeros((), w_spatial.dtype))
    mixed = jnp.einsum('gts,bnsgc->bntgc', ws, v)
    mixed = mixed + jnp.transpose(b_spatial)[None, None, :, :, None]
    return u * mixed.reshape(bsz, s_len, SGU_WIDTH)


def setup_inputs(seed: int = 0) -> dict:
    key = jax.random.key(seed)
    ks = jax.random.split(key, 20)
    f32 = jnp.float32

    def nrm(k, shape, scale):
        return jax.random.normal(k, shape, f32) * scale

    def gain(k, shape):
        return 1.0 + 0.05 * jax.random.normal(k, shape, f32)

    L = DEPTH
    return {
        "x": jax.random.normal(ks[0], (BATCH, SEQ, D_MODEL), f32),
        "g_pre_mix": gain(ks[1], (L, D_MODEL)),
        "w_in": nrm(ks[2], (L, D_MODEL, IN_COLS), D_MODEL ** -0.5),
        "b_forget": FORGET_BIAS + 0.1 * jax.random.normal(ks[3], (L, FOX_HEADS), f32),
        "g_q": gain(ks[4], (L, FOX_HEAD_DIM)),
        "g_k": gain(ks[5], (L, FOX_HEAD_DIM)),
        "g_sgu": gain(ks[6], (L, SGU_WIDTH)),
        "b_sgu": nrm(ks[7], (L, SGU_WIDTH), 0.02),
        "w_spatial": nrm(ks[8], (L, SGU_GROUPS, SGU_WINDOW, SGU_WINDOW), SGU_WINDOW ** -0.5),
        "b_spatial": 1.0 + 0.05 * jax.random.normal(ks[9], (L, SGU_GROUPS, SGU_WINDOW), f32),
        "w_branch_a": nrm(ks[10], (L, FOX_WIDTH, D_MODEL), FOX_WIDTH ** -0.5),
        "w_branch_b": nrm(ks[11], (L, SGU_WIDTH, D_MODEL), SGU_WIDTH ** -0.5),
        "w_out": nrm(ks[12], (L, D_MODEL, D_MODEL), D_MODEL ** -0.5),
        "g_post_mix": gain(ks[13], (L, D_MODEL)),
        "g_pre_ffn": gain(ks[14], (L, D_MODEL)),
        "w_ffn_in": nrm(ks[15], (L, D_MODEL, 2 * D_FF), D_MODEL ** -0.5),
        "w_ffn_down": nrm(ks[16], (L, D_FF, D_MODEL), D_FF ** -0.5),
        "g_post_ffn": gain(ks[17], (L, D_MODEL)),
    }


def reference(x, g_pre_mix, w_in, b_forget, g_q, g_k, g_sgu, b_sgu, w_spatial, b_spatial,
              w_branch_a, w_branch_b, w_out, g_post_mix, g_pre_ffn, w_ffn_in, w_ffn_down,
              g_post_ffn):
    bsz, s_len, _ = x.shape
    for layer in range(DEPTH):
        h = rmsnorm(x, g_pre_mix[layer])
        proj = h @ w_in[layer]

        def heads(t):
            return t.reshape(bsz, s_len, FOX_HEADS, FOX_HEAD_DIM).transpose(0, 2, 1, 3)

        q = rmsnorm(heads(proj[..., Q_OFF:K_OFF]), g_q[layer])
        k = rmsnorm(heads(proj[..., K_OFF:V_OFF]), g_k[layer])
        v = heads(proj[..., V_OFF:F_OFF])
        f_logit = proj[..., F_OFF:U_OFF].astype(jnp.float32) + b_forget[layer].astype(jnp.float32)
        log_f = jnp.transpose(jax.nn.log_sigmoid(f_logit), (0, 2, 1))
        attn = forgetting_attention(q, k, v, log_f)
        attn = attn.transpose(0, 2, 1, 3).reshape(bsz, s_len, FOX_WIDTH)
        y_a = attn @ w_branch_a[layer]

        uv = jax.nn.gelu(proj[..., U_OFF:G_OFF])
        sgu = spatial_gating(uv, g_sgu[layer], b_sgu[layer], w_spatial[layer], b_spatial[layer])
        y_b = sgu @ w_branch_b[layer]

        gates = jax.nn.sigmoid(proj[..., G_OFF:])
        merged = gates[..., :D_MODEL] * y_a + gates[..., D_MODEL:] * y_b
        x = x + rmsnorm(merged @ w_out[layer], g_post_mix[layer])

        h2 = rmsnorm(x, g_pre_ffn[layer])
        gu = h2 @ w_ffn_in[layer]
        ff = (jax.nn.silu(gu[..., :D_FF]) * gu[..., D_FF:]) @ w_ffn_down[layer]
        x = x + rmsnorm(ff, g_post_ffn[layer])
    return x
```

```python
import numpy as np
from contextlib import ExitStack
import concourse.bass as bass
import concourse.mybir as mybir
from concourse.bass_utils import run_bass_kernel_spmd

F32 = mybir.dt.float32
BF16 = mybir.dt.bfloat16
AF = mybir.ActivationFunctionType
ALU = mybir.AluOpType
AX = mybir.AxisListType

D = 1024
SEQ = 4096
NB = 32
NS = 16
TOWN = 2048
HEADS = 8
DH = 64
Q_OFF, K_OFF, V_OFF, F_OFF, U_OFF, G_OFF = 0, 512, 1024, 1536, 1544, 2568
IN_COLS = 4616
DFF = 2816
NFC = 22
EPS = 1e-6
GELU_C = 0.7978845608028654


def J_of(slot):
    return 4 * (slot // 2) + (1 if slot % 2 == 0 else 3)


ENGS = ["pe", "act", "dve", "pool", "sp"]


class Prog:
    def __init__(self, nc):
        self.nc = nc
        self.recs = {e: [] for e in ENGS}
        self.lastw = {}
        self.readers = {}
        self.dma_count = {}

    def _deps(self, eng, reads, writes, is_dma):
        deps = set()
        for k in reads:
            t = self.lastw.get(k)
            if t is not None:
                deps.add(t)
            if isinstance(k, tuple) and k[0] == "ps":
                for t2 in self.readers.get(k, {}).values():
                    if not (t2[0] == "e" and t2[1] == eng):
                        deps.add(t2)
        strict = is_dma or eng != "pe"
        for k in writes:
            t = self.lastw.get(k)
            if t is not None and (strict or not (t[0] == "e" and t[1] == eng)):
                deps.add(t)
            for t2 in self.readers.get(k, {}).values():
                if strict or not (t2[0] == "e" and t2[1] == eng):
                    deps.add(t2)
        return deps

    def _register(self, tok, rk, reads, writes):
        for k in reads:
            self.readers.setdefault(k, {})[rk] = tok
        for k in writes:
            self.lastw[k] = tok
            self.readers[k] = {}

    @staticmethod
    def _expand(keys):
        out = []
        for k in keys:
            out.append(k)
            if isinstance(k, tuple) and len(k) == 2 and k[0] == "stg":
                out += [("sa", k[1]), ("sb", k[1]), ("sc", k[1])]
        return out

    def op(self, eng, fn, reads=(), writes=()):
        reads, writes = self._expand(reads), self._expand(writes)
        idx = len(self.recs[eng])
        deps = self._deps(eng, reads, writes, False)
        tok = ("e", eng, idx)
        self.recs[eng].append(dict(fn=fn, deps=deps, dma=None))
        self._register(tok, eng, reads, writes)

    def dma(self, eng, semkey, fn, reads=(), writes=()):
        reads, writes = self._expand(reads), self._expand(writes)
        n = self.dma_count.get(semkey, 0) + 1
        self.dma_count[semkey] = n
        tok = ("d", semkey, n)
        deps = self._deps(eng, reads, writes, True)
        self.recs[eng].append(dict(fn=fn, deps=deps, dma=semkey))
        self._register(tok, ("d", semkey), reads, writes)

    def barrier(self):
        toks = set()
        for e in ENGS:
            if self.recs[e]:
                for i in range(len(self.recs[e]) - 1, -1, -1):
                    r = self.recs[e][i]
                    if r["fn"] is not None and r["dma"] is None:
                        toks.add(("e", e, i))
                        break
        for k, n in self.dma_count.items():
            toks.add(("d", k, n))
        for e in ENGS:
            deps = set(t for t in toks if not (t[0] == "e" and t[1] == e))
            self.recs[e].append(dict(fn=None, deps=deps, dma=None))

    def final_wait(self, eng, semkeys):
        deps = set(("d", k, self.dma_count[k]) for k in semkeys if k in self.dma_count)
        self.recs[eng].append(dict(fn=None, deps=deps, dma=None))

    def emit(self):
        nc = self.nc
        sig = {e: [False] * len(self.recs[e]) for e in ENGS}
        for e in ENGS:
            for r in self.recs[e]:
                for t in r["deps"]:
                    if t[0] == "e":
                        sig[t[1]][t[2]] = True
        rank = {}
        for e in ENGS:
            c = 0
            rk = []
            for i in range(len(self.recs[e])):
                if sig[e][i]:
                    c += 1
                rk.append(c)
            rank[e] = rk
        with ExitStack() as st:
            esem = {e: st.enter_context(nc.semaphore("s_" + e)) for e in ENGS}
            dsem = {}
            for i, k in enumerate(sorted(self.dma_count.keys(), key=str)):
                dsem[k] = st.enter_context(nc.semaphore("d%d" % i))
            block = st.enter_context(nc.Block())
            bname = {"pe": "tensor", "act": "scalar", "dve": "vector", "pool": "gpsimd", "sp": "sync"}
            for e in ENGS:
                def body(engine, e=e):
                    waited = {}
                    for i, r in enumerate(self.recs[e]):
                        for t in sorted(r["deps"], key=str):
                            if t[0] == "e":
                                key = ("e", t[1]); val = rank[t[1]][t[2]]; sem = esem[t[1]]
                            else:
                                key = ("d", t[1]); val = 16 * t[2]; sem = dsem[t[1]]
                            if waited.get(key, 0) >= val:
                                continue
                            engine.wait_ge(sem, val)
                            waited[key] = val
                        if r["fn"] is None:
                            continue
                        ins = r["fn"](engine)
                        if r["dma"] is not None:
                            ins.then_inc(dsem[r["dma"]], 16)
                        elif sig[e][i]:
                            ins.then_inc(esem[e], 1)
                getattr(block, bname[e])(body)


class Arena:
    def __init__(self, sb, total):
        self.sb = sb
        self.total = total
        self.off = 0
        self.peak = 0

    def alloc(self, free_shape, dt):
        n = 1
        for s in free_shape:
            n *= s
        esz = 4 if dt == F32 else 2
        nbytes = (n * esz + 63) // 64 * 64
        o = self.off
        self.off += nbytes
        self.peak = max(self.peak, self.off)
        assert self.off <= self.total, ("SBUF arena overflow", self.off, self.total)
        v = self.sb[:, o // 2:o // 2 + n * esz // 2]
        if dt == F32:
            v = v.bitcast(F32)
        if len(free_shape) == 2:
            v = v.rearrange("p (a b) -> p a b", a=free_shape[0])
        elif len(free_shape) == 3:
            v = v.rearrange("p (a b c) -> p a b c", a=free_shape[0], b=free_shape[1])
        return v

    def mark(self):
        return self.off

    def release(self, m):
        self.off = m


def build_nc(debug=False, upto=3):
    nc = bass.Bass("TRN2", target_bir_lowering=False)

    def din(name, shape):
        return nc.dram_tensor(name, list(shape), F32, kind="ExternalInput").ap()

    xseq = din("xseq", [SEQ, D])
    xown = din("xown", [TOWN, D])
    w_in = din("w_in", [D, IN_COLS])
    w_a = din("w_a", [512, D])
    w_b = din("w_b", [512, D])
    w_out = din("w_out", [D, D])
    w_ffi = din("w_ffi", [D, 2 * DFF])
    w_ffd = din("w_ffd", [DFF, D])
    gpm_d = din("gpm", [128, D])
    gpo_d = din("gpo", [128, D])
    gpf_d = din("gpf", [128, D])
    gpff_d = din("gpff", [128, D])
    gsg_d = din("gsg", [128, 512])
    bsg_d = din("bsg", [128, 512])
    bfb_d = din("bfb", [128, 8])
    gqc_d = din("gqc", [128, 1])
    gkc_d = din("gkc", [128, 1])
    wsp_d = din("wsp", [8, 128, 128])
    bsp_d = din("bsp", [128, 8])
    msk_d = din("msk", [128, NS * 256])
    out_d = nc.dram_tensor("out", [TOWN, D], F32, kind="ExternalOutput").ap()
    dbg = {}
    if debug:
        for nm, shp in [("d_kt", [128, 4 * SEQ]), ("d_qt", [128, 4 * TOWN]), ("d_v", [128, NB * 8 * 65]),
                        ("d_c", [128, 256]), ("d_at", [128, 4 * TOWN]), ("d_x1", [128, 8 * 1024])]:
            dbg[nm] = nc.dram_tensor(nm, shp, F32, kind="ExternalOutput").ap()

    w_in_v = w_in.rearrange("(c p) n -> p c n", p=128)
    w_a_v = w_a.rearrange("(c p) n -> p c n", p=128)
    w_b_v = w_b.rearrange("(c p) n -> p c n", p=128)
    w_out_v = w_out.rearrange("(c p) n -> p c n", p=128)
    w_ffi_v = w_ffi.rearrange("(c p) n -> p c n", p=128)
    w_ffd_v = w_ffd.rearrange("(c p) n -> p c n", p=128)

    TOTAL = 212480
    with ExitStack() as stack:
        sb = stack.enter_context(nc.sbuf_tensor("sb", [128, TOTAL // 2], BF16))
        psF = [stack.enter_context(nc.psum_tensor("psf%d" % i, [128, 512], F32)) for i in range(6)]
        psBf = [stack.enter_context(nc.psum_tensor("psb%d" % i, [128, 512], F32)) for i in range(2)]
        psB = [t[:, :].bitcast(BF16) for t in psBf]
        A = Arena(sb, TOTAL)
        P = Prog(nc)

        def PSK(i):
            return ("ps", i)

        def MM(out, lhsT, rhs, start, stop, reads, writes):
            P.op("pe", lambda e: e.matmul(out, lhsT=lhsT, rhs=rhs, start=start, stop=stop), reads, writes)

        def TR(out, in_, reads, writes):
            P.op("pe", lambda e: e.transpose(out=out, in_=in_, identity=ident), list(reads) + ["ident"], writes)

        def ACT(out, in_, func, reads, writes, **kw):
            P.op("act", lambda e: e.activation(out=out, in_=in_, func=func, **kw), reads, writes)

        def TT(eng, out, in0, in1, op, reads, writes):
            P.op(eng, lambda e: e.tensor_tensor(out=out, in0=in0, in1=in1, op=op), reads, writes)

        def TS(eng, out, in0, s1, s2, op0, op1, reads, writes):
            if s2 is None:
                P.op(eng, lambda e: e.tensor_scalar(out=out, in0=in0, scalar1=s1, scalar2=None, op0=op0), reads, writes)
            else:
                P.op(eng, lambda e: e.tensor_scalar(out=out, in0=in0, scalar1=s1, scalar2=s2, op0=op0, op1=op1), reads, writes)

        def STT(out, in0, scalar, in1, op0, op1, reads, writes):
            P.op("dve", lambda e: e.scalar_tensor_tensor(out=out, in0=in0, scalar=scalar, in1=in1, op0=op0, op1=op1),
                 reads, writes)

        def CP(eng, out, in_, reads, writes):
            P.op(eng, lambda e: e.tensor_copy(out=out, in_=in_), reads, writes)

        def MS(eng, ap, val, reads, writes):
            P.op(eng, lambda e: e.memset(ap, val), reads, writes)

        def DMA(eng, semkey, out, in_, reads, writes):
            P.dma(eng, semkey, lambda e: e.dma_start(out=out, in_=in_), reads, writes)

        def load_w(dst_view, src_view, key):
            DMA("pool", ("w", key), dst_view, src_view, [], [key])

        ident = A.alloc([128], BF16)
        bones = A.alloc([128], BF16)
        tri = A.alloc([128], F32)
        aones = A.alloc([128], F32)
        gqc = A.alloc([1], F32)
        gkc = A.alloc([1], F32)
        bfb = A.alloc([8], F32)
        gpm = A.alloc([D], F32)
        attnT = A.alloc([4, TOWN], BF16)
        NSTG = 4
        stg = [A.alloc([8, 512], BF16) for _ in range(NSTG)]
        xt = [A.alloc([D], F32) for _ in range(2)]
        NXN = 4
        xn = [A.alloc([D], BF16) for _ in range(NXN)]
        stt_ = [A.alloc([8], F32) for _ in range(NXN)]

        MS("pool", ident, 0.0, [], ["ident"])
        P.op("pool", lambda e: e.affine_select(out=ident, in_=ident, pattern=[[-1, 128]], compare_op=ALU.not_equal,
                                               fill=1.0, base=0, channel_multiplier=1),
             reads=["ident"], writes=["ident"])
        MS("pool", bones, 0.0, [], ["bones"])
        MS("pool", bones[0:64, 0:64], 1.0, ["bones"], ["bones"])
        MS("pool", bones[64:128, 64:128], 1.0, ["bones"], ["bones"])
        MS("pool", aones, 1.0, [], ["aones"])
        MS("pool", tri, 1.0, [], ["tri"])
        P.op("pool", lambda e: e.affine_select(out=tri, in_=tri, pattern=[[1, 128]], compare_op=ALU.is_ge,
                                               fill=0.0, base=0, channel_multiplier=-1),
             reads=["tri"], writes=["tri"])
        DMA("sp", "c0", gqc, gqc_d, [], ["gqc"])
        DMA("sp", "c1", gkc, gkc_d, [], ["gkc"])
        DMA("sp", "c2", bfb, bfb_d, [], ["bfb"])
        DMA("sp", "c3", gpm, gpm_d, [], ["gpm"])

        gsg = A.alloc([512], F32)
        bsg = A.alloc([512], F32)
        wsT = A.alloc([8, 128], BF16)
        bsp = A.alloc([8], F32)
        wtmp8, wtmpb8 = [], []

        def prep_mix_consts_a():
            DMA("sp", "c7", gsg, gsg_d, [], ["gsg"])
            DMA("sp", "c8", bsg, bsg_d, [], ["bsg"])
            DMA("sp", "c9", bsp, bsp_d, [], ["bsp"])
            for g in range(8):
                DMA("pool", ("c10", g), wtmpb8[g], wsp_d[g], [], [("wtmpb", g)])
                MS("dve", wtmpb8[g][0:64, 64:128], 0.0, [("wtmpb", g)], [("wtmpb", g)])

        def prep_mix_consts_b():
            for g in range(8):
                w_ = g % 2
                TR(psB[w_][:, 0:128], wtmpb8[g], [("wtmpb", g)], [PSK(6 + w_)])
                CP("dve", wsT[:, g, :], psB[w_][:, 0:128], [PSK(6 + w_)], ["wsT"])

        xctr = [0]
        psb_ctr = [0]

        xnc = [0]

        def nt_a(src_rows, gbc, gkey, eps_scale=1.0, keep=None):
            if keep is None:
                b = xctr[0] % len(xt)
                xctr[0] += 1
                x_t, xk = xt[b], ("xt", b)
                DMA("sp", ("xt", b), x_t, src_rows, [], [xk])
            else:
                x_t, xk = keep
            xi = nt_a1(x_t, xk, eps_scale)
            nt_a2(xi, x_t, xk, gbc, gkey)
            return xi

        def nt_a1(x_t, xk, eps_scale=1.0):
            xi = xnc[0] % NXN
            xnc[0] += 1
            x_n, nk = xn[xi], ("xn", xi)
            s_t, sk = stt_[xi], ("st", xi)
            ACT(x_n, x_t, AF.Square, [xk], [nk, sk], accum_out=s_t[:, 0:1])
            ACT(s_t[:, 1:2], s_t[:, 0:1], AF.Ln, [sk], [sk], scale=1.0 / D, bias=EPS * eps_scale)
            ACT(s_t[:, 2:3], s_t[:, 1:2], AF.Exp, [sk], [sk], scale=-0.5)
            return xi

        def nt_a2(xi, x_t, xk, gbc, gkey):
            x_n, nk = xn[xi], ("xn", xi)
            s_t, sk = stt_[xi], ("st", xi)
            STT(x_n, x_t, s_t[:, 2:3], gbc, ALU.mult, ALU.mult, [xk, sk, gkey], [nk])

        def nt_b(xi, dst_view, dst_key, on_act=False):
            x_n, nk = xn[xi], ("xn", xi)
            pb = psb_ctr[0] % 2
            psb_ctr[0] += 1
            pst = psB[pb]
            for c in range(8):
                TR(pst[:, c * 128:(c + 1) * 128], x_n[:, c * 128:(c + 1) * 128], [nk], [PSK(6 + pb)])
            if on_act:
                ACT(dst_view, pst.rearrange("p (c t) -> p c t", c=8), AF.Copy, [PSK(6 + pb)], [dst_key])
            else:
                CP("dve", dst_view, pst.rearrange("p (c t) -> p c t", c=8), [PSK(6 + pb)], [dst_key])

        m_l1 = A.mark()
        KT = A.alloc([4, SEQ], BF16)
        Vaug = A.alloc([NB, 8, 65], BF16)
        QT = A.alloc([4, TOWN], BF16)
        zf = A.alloc([NB, 8], F32)
        Cc = A.alloc([NB, 8], F32)
        Eb = A.alloc([NB + 1, 8], F32)
        msk = A.alloc([NS, 256], BF16)
        m_l2 = A.mark()
        hTg = [A.alloc([8, 512], BF16) for _ in range(2)]
        sq = [A.alloc([512], BF16) for _ in range(3)]
        rs = [A.alloc([512], F32) for _ in range(3)]
        wv2 = A.alloc([8, 264], BF16)
        xt.append(A.alloc([D], F32))
        xt.append(A.alloc([D], F32))
        for _g in range(8):
            wtmpb8.append(A.alloc([128], BF16))

        MS("pool", Vaug.rearrange("p a b c -> p (a b) c")[:, :, 64:65], 1.0, [], ["vones"])

        load_w(stg[0], w_in_v[:, :, K_OFF:K_OFF + 512], ("stg", 0))
        load_w(stg[1][:, :, 0:256], w_in_v[:, :, V_OFF:V_OFF + 256], ("stg", 1))
        load_w(wv2, w_in_v[:, :, V_OFF + 256:V_OFF + 520], "wv2")
        def late_loads():
            load_w(stg[2], w_in_v[:, :, Q_OFF:Q_OFF + 512], ("stg", 2))
            load_w(msk.rearrange("p a b -> p (a b)"), msk_d, "msk")

        fm_ctr = [0]

        def proj_fm_1(wview, wkey, p, hT, hkey):
            i = fm_ctr[0] % 3
            fm_ctr[0] += 1
            ps = psF[i]
            for c in range(8):
                MM(ps[:, :], wview[:, c, p * 128:(p + 1) * 128], hT[:, c, :], c == 0, c == 7, [wkey, hkey], [PSK(i)])
            ACT(sq[i], ps[:, :], AF.Square, [PSK(i)], [("sq", i)])
            return i

        def proj_fm_2(i, dst, dkey, gcol, gckey):
            ps, ps2 = psF[i], psF[3]
            MM(ps2[:, :], bones, sq[i], True, True, ["bones", ("sq", i)], [PSK(3)])
            ACT(rs[i], ps2[:, :], AF.Ln, [PSK(3)], [("rs", i)], scale=1.0 / DH, bias=EPS)
            ACT(rs[i], rs[i], AF.Exp, [("rs", i)], [("rs", i)], scale=-0.5)
            STT(dst, ps[:, :], gcol[:, 0:1], rs[i], ALU.mult, ALU.mult, [PSK(i), ("rs", i), gckey], [dkey])

        NG = 12
        xis = {}

        def ab_s1a(g):
            src = xseq if g < 8 else xown
            g0 = g if g < 8 else g - 8
            xis[g] = [nt_a(src[(g0 * 4 + bl) * 128:(g0 * 4 + bl + 1) * 128, :], gpm, "gpm") for bl in range(4)]

        def ab_s1b(g):
            hb = g % 2
            for bl in range(4):
                nt_b(xis[g][bl], hTg[hb][:, :, bl * 128:(bl + 1) * 128], ("hTg", hb))

        def ab_s2(g):
            hb = g % 2
            if g < 8:
                grp = g
                for p in range(4):
                    bl = p
                    ii = proj_fm_1(stg[0], ("stg", 0), p, hTg[hb], ("hTg", hb))
                    if p >= 1:
                        proj_fm_2(prev[0], *prev[1])
                    prev = (ii, (KT[:, p, grp * 512:(grp + 1) * 512], ("KT", p, grp), gkc, "gkc"))
                    blk = grp * 4 + bl
                    for c in range(8):
                        MM(psF[4][:, 0:256], hTg[hb][:, c, bl * 128:(bl + 1) * 128], stg[1][:, c, 0:256], c == 0, c == 7,
                           [("hTg", hb), ("stg", 1)], [PSK(4)])
                    for c in range(8):
                        MM(psF[5][:, 0:264], hTg[hb][:, c, bl * 128:(bl + 1) * 128], wv2[:, c, :], c == 0, c == 7,
                           [("hTg", hb), "wv2"], [PSK(5)])
                    ACT(Vaug[:, blk, 0:4, 0:64], psF[4][:, 0:256].rearrange("p (h d) -> p h d", h=4), AF.Copy,
                        [PSK(4)], [("V", blk)])
                    ACT(Vaug[:, blk, 4:8, 0:64], psF[5][:, 0:256].rearrange("p (h d) -> p h d", h=4), AF.Copy,
                        [PSK(5), ("V", blk)], [("V", blk)])
                    TT("dve", zf[:, blk, :], psF[5][:, 256:264], bfb, ALU.add, [PSK(5), "bfb"], ["zf"])
                return prev
            else:
                grp = g - 8
                for p in range(4):
                    ii = proj_fm_1(stg[2], ("stg", 2), p, hTg[hb], ("hTg", hb))
                    if p >= 1:
                        proj_fm_2(prev[0], *prev[1])
                    prev = (ii, (QT[:, p, grp * 512:(grp + 1) * 512], ("QT", p, grp), gqc, "gqc"))
                return prev

        zf2 = zf.rearrange("p a b -> p (a b)")
        Cc2 = Cc.rearrange("p a b -> p (a b)")

        def phase_c1():
            ACT(zf2, zf2, AF.Exp, ["zf"], ["zf"], scale=-1.0)
            ACT(zf2, zf2, AF.Ln, ["zf"], ["zf"], bias=1.0)

        def phase_c2():
            MM(psF[0][:, 0:256], tri, zf2, True, True, ["tri", "zf"], [PSK(0)])
            MM(psF[1][:, 0:256], aones, zf2, True, True, ["aones", "zf"], [PSK(1)])
            MS("dve", Eb[:, 0, :], 0.0, [], ["Eb"])
            CP("dve", Cc2, psF[1][:, 0:256], [PSK(1)], ["Cc"])
            for j in range(1, NB + 1):
                TT("dve", Eb[:, j, :], Eb[:, j - 1, :], Cc[:, j - 1, :], ALU.add, ["Eb", "Cc"], ["Eb"])
            TT("dve", Cc2, psF[0][:, 0:256], Eb[:, 0:NB, :].rearrange("p a b -> p (a b)"), ALU.add, [PSK(0), "Eb"], ["Cc"])

        for t in range(NG + 1):
            if t == 2:
                late_loads()
            if t == 4:
                prep_mix_consts_a()
            if t == 5:
                prep_mix_consts_b()
            if t < NG:
                ab_s1a(t)
            last = ab_s2(t - 1) if t >= 1 else None
            if t < NG:
                ab_s1b(t)
            if last is not None:
                proj_fm_2(last[0], *last[1])
            if t == 8:
                phase_c1()
            if t == 9:
                phase_c2()

        def dump(items):
            P.barrier()
            mk_ = A.mark()
            dv = xn[0].bitcast(F32)
            for nm, src, n in items:
                for o in range(0, n, 256):
                    m = min(256, n - o)
                    CP("dve", dv[:, 0:m], src[:, o:o + m], ["dbgsrc"], ["dv"])
                    DMA("sp", "dbg", dbg[nm][:, o:o + m], dv[:, 0:m], ["dv"], ["dbgout"])
            P.barrier()
            A.release(mk_)

        def finish():
            P.final_wait("sp", [("out", 0), ("out", 1), "dbg"])
            P.emit()

        if debug:
            dump([("d_kt", KT.rearrange("p a b -> p (a b)"), 4 * SEQ), ("d_qt", QT.rearrange("p a b -> p (a b)"), 4 * TOWN),
                  ("d_v", Vaug.rearrange("p a b c -> p (a b c)"), NB * 8 * 65), ("d_c", Cc.rearrange("p a b -> p (a b)"), 256)])
            if upto == 1:
                finish()
                return nc

        hs = [stg[k // 2][:, :, (k % 2) * 256:(k % 2 + 1) * 256] for k in range(8)]

        def e2_b0(ch, wb):
            even = (ch % 2 == 0)
            if wb == 0:
                return 4 if even else 0
            return 0 if even else 4

        def e2_chunk_loads(ch, wb=0):
            b0 = e2_b0(ch, wb)
            c0 = ch * 256
            s0, s1_ = b0 // 2, b0 // 2 + 1
            load_w(hs[b0][:, 0:4, :], w_a_v[:, :, c0:c0 + 256], ("sa", s0))
            load_w(hs[b0][:, 4:8, :], w_b_v[:, :, c0:c0 + 256], ("sb", s0))
            load_w(hs[b0 + 1], w_in_v[:, :, G_OFF + c0:G_OFF + c0 + 256], ("sc", s0))
            load_w(hs[b0 + 2][:, 0:4, :], w_in_v[:, 0:4, G_OFF + 1024 + c0:G_OFF + 1024 + c0 + 256], ("sa", s1_))
            load_w(hs[b0 + 2][:, 4:8, :], w_in_v[:, 4:8, G_OFF + 1024 + c0:G_OFF + 1024 + c0 + 256], ("sb", s1_))

        pre_x = {}

        def e1_prefetch_x(half_):
            for sl_ in range(2):
                gs_ = half_ * 8 + sl_
                DMA("sp", ("xt", sl_), xt[sl_], xown[gs_ * 128:(gs_ + 1) * 128, :], [], [("xt", sl_)])
                pre_x[(half_, sl_)] = sl_

        def e1_loads(wb=0):
            load_w(stg[wb], w_in_v[:, :, U_OFF:U_OFF + 512], ("stg", wb))
            load_w(stg[wb + 1], w_in_v[:, :, U_OFF + 512:U_OFF + 1024], ("stg", wb + 1))

        def f_loads(fq, foff=0):
            nfc = 4 if fq < 5 else 2
            sg_, su_ = (2 * fq + foff) % NSTG, (2 * fq + 1 + foff) % NSTG
            load_w(stg[sg_][:, :, 0:nfc * 128], w_ffi_v[:, :, fq * 512:fq * 512 + nfc * 128], ("stg", sg_))
            load_w(stg[su_][:, :, 0:nfc * 128], w_ffi_v[:, :, DFF + fq * 512:DFF + fq * 512 + nfc * 128], ("stg", su_))

        P.barrier()
        A.release(m_l2)
        del xt[2:]
        pT = [A.alloc([NB * 128], BF16) for _ in range(2)]
        Vp = [A.alloc([NB, 65], BF16) for _ in range(4)]
        wS = [A.alloc([NB, 8], F32) for _ in range(2)]
        wB = [A.alloc([NB, 8], F32) for _ in range(2)]
        atok = [A.alloc([512], BF16) for _ in range(2)]
        rec = [A.alloc([8], F32) for _ in range(2)]

        units = [(s, p) for s in range(NS) for p in range(4)]
        items = []
        for ui, (s, p) in enumerate(units):
            nb_ = J_of(s) + 1
            for c0 in range(0, nb_, 4):
                items.append((ui, c0))
        cc = [0]

        NSET = 3
        LOOK = 6

        def acc_banks(s):
            return (psBf[0], PSK(6)), (psBf[1], PSK(7))

        def att_prep(ui):
            s, p = units[ui]
            J = J_of(s)
            nb = J + 1
            sb_ = s % 2
            if p == 0:
                TT("dve", wB[sb_][:, 0:nb, :], Cc[:, 0:nb, :], Eb[:, J:J + 1, :].to_broadcast([128, nb, 8]), ALU.subtract,
                   ["Cc", "Eb"], [("wB", sb_)])
                ACT(wS[sb_][:, 0:nb, :], wB[sb_][:, 0:nb, :], AF.Exp, [("wB", sb_)], [("wS", sb_)])
            for ab in range(2):
                h = 2 * p + ab
                vi = 2 * (ui % 2) + ab
                TT("dve", Vp[vi][:, 0:nb, :], Vaug[:, 0:nb, h, :], wS[sb_][:, 0:nb, h:h + 1].to_broadcast([128, nb, 65]), ALU.mult,
                   [("V", j) for j in range(nb)] + ["vones", ("wS", sb_)], [("Vp", vi)])

        def att_qk(ui, c0):
            s, p = units[ui]
            J = J_of(s)
            nb = J + 1
            n = min(4, nb - c0)
            set_ = cc[0] % NSET
            cc[0] += 1
            for jj in range(n):
                j = c0 + jj
                for ab in range(2):
                    r0 = ab * 64
                    bk = 2 * set_ + ab
                    MM(psF[bk][:, jj * 128:(jj + 1) * 128], KT[r0:r0 + 64, p, j * 128:(j + 1) * 128],
                       QT[r0:r0 + 64, p, s * 128:(s + 1) * 128], True, True,
                       [("KT", p, j // 4), ("QT", p, s // 4)], [PSK(bk)])
            for ab in range(2):
                bk = 2 * set_ + ab
                ACT(pT[ab][:, c0 * 128:(c0 + n) * 128], psF[bk][:, 0:n * 128], AF.Exp, [PSK(bk)], [("pT", ab, c0 // 4)], scale=0.125)
            if c0 + n == nb:
                for ab in range(2):
                    TT("pool", pT[ab][:, (J - 1) * 128:(J + 1) * 128], pT[ab][:, (J - 1) * 128:(J + 1) * 128], msk[:, s, :], ALU.mult,
                       [("pT", ab, c0 // 4), "msk"], [("pT", ab, c0 // 4)])

        def att_pv(ui, c0):
            s, p = units[ui]
            J = J_of(s)
            nb = J + 1
            n = min(4, nb - c0)
            banks = acc_banks(s)
            for ab in range(2):
                acc, ak = banks[ab]
                vi = 2 * (ui % 2) + ab
                for jj in range(n):
                    j = c0 + jj
                    MM(acc[:, p * 65:(p + 1) * 65], pT[ab][:, j * 128:(j + 1) * 128], Vp[vi][:, j, :], j == 0, j == J,
                       [("pT", ab, c0 // 4), ("Vp", vi)], [ak])
            if c0 + n == nb and p == 3:
                tb = s % 2
                for ab in range(2):
                    acc, ak = banks[ab]
                    a3 = acc[:, 0:260].rearrange("p (h d) -> p h d", h=4)
                    rk = ("rec", ab)
                    P.op("dve", lambda e, o_=rec[ab][:, 0:4], i_=a3[:, :, 64]: e.reciprocal(out=o_, in_=i_), [ak], [rk])
                    TT("dve", atok[tb].rearrange("p (q two d) -> p q two d", two=2, d=64)[:, :, ab, :], a3[:, :, 0:64],
                       rec[ab][:, 0:4].unsqueeze(2).to_broadcast([128, 4, 64]), ALU.mult, [ak, rk], [("atok", tb)])
                deferred.append((s, tb))

        deferred = []

        def att_fin2():
            while deferred:
                s, tb = deferred.pop(0)
                set_ = cc[0] % NSET
                cc[0] += 1
                bk = 2 * set_
                ptr = psF[bk][:, :].bitcast(BF16)
                for c in range(4):
                    TR(ptr[:, c * 128:(c + 1) * 128], atok[tb][:, c * 128:(c + 1) * 128], [("atok", tb)], [PSK(bk)])
                CP("dve", attnT[:, :, s * 128:(s + 1) * 128], ptr[:, 0:512].rearrange("p (c t) -> p c t", c=4),
                   [PSK(bk)], [("attnT", s)])

        pending = []
        last_unit = [-1]
        for k in range(len(items)):
            while pending and (len(pending) > LOOK or any(items[q][1] == items[k][1] for q in pending)):
                had = bool(deferred)
                att_pv(*items[pending.pop(0)])
                if had:
                    att_fin2()
            if items[k][0] != last_unit[0]:
                att_prep(items[k][0])
                last_unit[0] = items[k][0]
            att_qk(*items[k])
            pending.append(k)
            if k == 40:
                e1_loads()
                e2_chunk_loads(0)
            if k == len(items) - 12:
                e1_prefetch_x(0)
        while pending:
            att_pv(*items[pending.pop(0)])
        att_fin2()

        if debug:
            dump([("d_at", attnT.rearrange("p a b -> p (a b)"), 4 * TOWN)])
            if upto == 2:
                finish()
                return nc

        P.barrier()
        A.release(m_l1)
        gpo = A.alloc([D], F32)
        gpf = A.alloc([D], F32)
        gpff = A.alloc([D], F32)
        x1 = A.alloc([8, D], F32)
        h2T = A.alloc([8, 1024], BF16)
        DMA("sp", "c4", gpo, gpo_d, [], ["gpo"])
        DMA("sp", "c5", gpf, gpf_d, [], ["gpf"])
        DMA("sp", "c6", gpff, gpff_d, [], ["gpff"])
        m_l2b = A.mark()

        for half in range(2):
            def hk(nm, half=half):
                return (nm, half)
            wb = 0 if half == 0 else 2
            wo = 2 - wb

            A.release(m_l2b)
            hTo = A.alloc([8, 1024], BF16)
            sguT = A.alloc([4, 1024], BF16)
            mT = A.alloc([8, 1024], BF16)
            lst = [A.alloc([8], F32) for _ in range(3)]
            del xt[2:]
            xt.append(A.alloc([D], F32))
            wkb = [A.alloc([512], F32) for _ in range(12)]

            def WK(i):
                return hk(("wk", i))
            t1 = [[wkb[0], wkb[1]], [wkb[2], wkb[3]]]
            t1k = [[WK(0), WK(1)], [WK(2), WK(3)]]
            gl = [[wkb[4], wkb[5]], [wkb[6], wkb[7]]]
            glk = [[WK(4), WK(5)], [WK(6), WK(7)]]
            vn = [wkb[8].bitcast(BF16)[:, 0:512], wkb[9].bitcast(BF16)[:, 0:512]]
            vnk = [WK(8), WK(9)]
            sgu = [wkb[10].bitcast(BF16)[:, 0:512], wkb[11].bitcast(BF16)[:, 0:512]]
            sguk = [WK(10), WK(11)]
            tg2 = [[wkb[0], wkb[1]], [wkb[2], wkb[3]]]
            m122 = [[wkb[4], wkb[5]], [wkb[6], wkb[7]]]

            if half == 1:
                e2_chunk_loads(0, wb)
            e_x = {}
            SQC = 0.21145921592026346

            def e1_s0a_act(sl):
                gs = half * 8 + sl
                if (half, sl) in pre_x:
                    b = pre_x[(half, sl)]
                    xctr[0] = b + 1
                else:
                    b = xctr[0] % len(xt)
                    xctr[0] += 1
                    DMA("sp", ("xt", b), xt[b], xown[gs * 128:(gs + 1) * 128, :], [], [("xt", b)])
                e_x[sl] = (nt_a1(xt[b], ("xt", b)), b)

            def e1_s0a_dve(sl):
                xi, b = e_x[sl]
                nt_a2(xi, xt[b], ("xt", b), gpm, "gpm")

            def e1_s0b(sl):
                nt_b(e_x[sl][0], hTo[:, :, sl * 128:(sl + 1) * 128], hk(("hTo", sl)), on_act=True)

            def e1_s1_pe(sl):
                par = sl % 2
                for which in range(2):
                    ps, pk = psF[2 * par + which], PSK(2 * par + which)
                    for c in range(8):
                        MM(ps[:, :], hTo[:, c, sl * 128:(sl + 1) * 128], stg[wb + which][:, c, :], c == 0, c == 7,
                           [hk(("hTo", sl)), ("stg", wb + which)], [pk])
                for which in range(2):
                    ps, pk = psF[2 * par + which], PSK(2 * par + which)
                    tt_, tk = t1[par][which], t1k[par][which]
                    ACT(tt_, ps[:, :], AF.Square, [pk], [tk], scale=SQC)

            def e1_s1_inner(sl):
                par = sl % 2
                for which in range(2):
                    ps, pk = psF[2 * par + which], PSK(2 * par + which)
                    tt_, tk = t1[par][which], t1k[par][which]
                    STT(tt_, tt_, 1.0, ps[:, :], ALU.add, ALU.mult, [tk, pk], [tk])

            def e1_s1_exp(sl):
                par = sl % 2
                for which in range(2):
                    tt_, tk = t1[par][which], t1k[par][which]
                    ACT(tt_, tt_, AF.Exp, [tk], [tk], scale=-2.0 * GELU_C)
                for which in range(2):
                    tt_, tk = t1[par][which], t1k[par][which]
                    ACT(tt_, tt_, AF.Ln, [tk], [tk], bias=1.0)
                for which in range(2):
                    tt_, tk = t1[par][which], t1k[par][which]
                    ACT(tt_, tt_, AF.Exp, [tk], [tk], scale=-1.0)

            def e1_s1_gl(sl):
                par = sl % 2
                l_, lk = lst[par], hk(("lst", par))
                for which in range(2):
                    ps, pk = psF[2 * par + which], PSK(2 * par + which)
                    tt_, tk = t1[par][which], t1k[par][which]
                    if which == 0:
                        STT(gl[par][which], tt_, 2.0, ps[:, :], ALU.mult, ALU.mult, [tk, pk], [glk[par][which]])
                    else:
                        P.op("dve", lambda e, o_=gl[par][1], t_=tt_, p_=ps[:, :], a_=l_[:, 0:1]: e.scalar_tensor_tensor(
                            out=o_, in0=t_, scalar=2.0, in1=p_, op0=ALU.mult, op1=ALU.mult, accum_out=a_),
                             [tk, pk], [glk[par][1], lk])

            def e1_s2(sl):
                par = sl % 2
                l_, lk = lst[par], hk(("lst", par))
                v_, vk = gl[par][1], glk[par][1]
                j_, jk = t1[par][1], t1k[par][1]
                ACT(j_, v_, AF.Square, [vk], [jk, lk], accum_out=l_[:, 1:2])
                TS("dve", l_[:, 2:3], l_[:, 0:1], 1.0 / 512, None, ALU.mult, None, [lk], [lk])
                TT("dve", l_[:, 3:4], l_[:, 2:3], l_[:, 2:3], ALU.mult, [lk], [lk])
                STT(l_[:, 4:5], l_[:, 1:2], 1.0 / 512, l_[:, 3:4], ALU.mult, ALU.subtract, [lk], [lk])
                ACT(l_[:, 5:6], l_[:, 4:5], AF.Ln, [lk], [lk], bias=4.0 * EPS)
                ACT(l_[:, 6:7], l_[:, 5:6], AF.Exp, [lk], [lk], scale=-0.5)
                TS("dve", v_, v_, l_[:, 2:3], l_[:, 6:7], ALU.subtract, ALU.mult, [vk, lk], [vk])
                TT("dve", v_, v_, gsg, ALU.mult, [vk, "gsg"], [vk])
                TT("dve", vn[par], v_, bsg, ALU.add, [vk, "bsg"], [vnk[par]])

            def e1_s3(sl):
                par = sl % 2
                pm, pmk = psF[4 + par], PSK(4 + par)
                for g in range(8):
                    MM(pm[:, g * 64:(g + 1) * 64], wsT[:, g, :], vn[par][:, g * 64:(g + 1) * 64], True, True,
                       ["wsT", vnk[par]], [pmk])
                s1_, s1k = wkb[8 + par], vnk[par]
                TT("dve", s1_.rearrange("p (g c) -> p g c", g=8), pm.rearrange("p (g c) -> p g c", g=8),
                   bsp.unsqueeze(2).to_broadcast([128, 8, 64]), ALU.add, [pmk, "bsp"], [s1k])
                TT("dve", sgu[par], s1_, gl[par][0], ALU.mult, [s1k, glk[par][0]], [sguk[par]])

            def e1_s4(sl):
                par = sl % 2
                pb = psb_ctr[0] % 2
                psb_ctr[0] += 1
                for c in range(4):
                    TR(psB[pb][:, c * 128:(c + 1) * 128], sgu[par][:, c * 128:(c + 1) * 128], [sguk[par]], [PSK(6 + pb)])
                ACT(sguT[:, :, sl * 128:(sl + 1) * 128], psB[pb][:, 0:512].rearrange("p (c t) -> p c t", c=4), AF.Copy,
                    [PSK(6 + pb)], [hk(("sguT", sl // 4))])

            for t in range(8 + 4):
                s1ok = 0 <= t - 1 < 8
                if t < 8:
                    e1_s0a_act(t)
                if s1ok:
                    e1_s1_pe(t - 1)
                if t < 8:
                    e1_s0a_dve(t)
                if s1ok:
                    e1_s1_inner(t - 1)
                if t < 8:
                    e1_s0b(t)
                if s1ok:
                    e1_s1_exp(t - 1)
                if 0 <= t - 4 < 8:
                    e1_s4(t - 4)
                if 0 <= t - 3 < 8:
                    e1_s3(t - 3)
                if 0 <= t - 2 < 8:
                    e1_s2(t - 2)
                if s1ok:
                    e1_s1_gl(t - 1)

            for sl in range(8):
                gs = half * 8 + sl
                DMA("sp", ("x1ld", sl), x1[:, sl, :], xown[gs * 128:(gs + 1) * 128, :], [], [("x1", half, sl)])

            units2 = [(ch, tt, o) for ch in range(4) for tt in range(2) for o in range(2)]

            def e2_s1(ui):
                ch, tt, o = units2[ui]
                if tt == 0 and o == 0 and ch + 1 < 4:
                    e2_chunk_loads(ch + 1, wb)
                if ch == 3 and tt == 0 and o == 0:
                    load_w(stg[wo], w_out_v[:, :, 0:512], ("stg", wo))
                    load_w(stg[wo + 1], w_out_v[:, :, 512:1024], ("stg", wo + 1))
                b0 = e2_b0(ch, wb)
                s0, s1_ = b0 // 2, b0 // 2 + 1
                k0a, k0b, k1 = ("sa", s0), ("sb", s0), ("sc", s0)
                k2 = [("sa", s1_), ("sb", s1_)]
                pset = 4 * (ui % 2)
                tcols = slice(tt * 512, (tt + 1) * 512)
                slh = slice(half * 1024 + tt * 512, half * 1024 + (tt + 1) * 512)
                akeys = [("attnT", s_) for s_ in range(half * 8 + tt * 4, half * 8 + tt * 4 + 4)]
                hkeys = [hk(("hTo", s_)) for s_ in range(tt * 4, tt * 4 + 4)]
                ocols = slice(o * 128, (o + 1) * 128)
                bank = lambda i: (psF[pset + i] if pset + i < 6 else psBf[pset + i - 6])
                for kc in range(4):
                    MM(bank(0)[:, :], hs[b0][:, kc, ocols], attnT[:, kc, slh], kc == 0, kc == 3, [k0a] + akeys, [PSK(pset)])
                for c in range(8):
                    MM(bank(1)[:, :], hs[b0 + 1][:, c, ocols], hTo[:, c, tcols], c == 0, c == 7, [k1] + hkeys, [PSK(pset + 1)])
                for kc in range(4):
                    MM(bank(2)[:, :], hs[b0][:, 4 + kc, ocols], sguT[:, kc, tcols], kc == 0, kc == 3,
                       [k0b, hk(("sguT", tt))], [PSK(pset + 2)])
                for c in range(8):
                    MM(bank(3)[:, :], hs[b0 + 2][:, c, ocols], hTo[:, c, tcols], c == 0, c == 7, k2 + hkeys, [PSK(pset + 3)])

            def e2_s2(ui):
                ch, tt, o = units2[ui]
                par = ui % 2
                pset = 4 * par
                oc = ch * 2 + o
                tcols = slice(tt * 512, (tt + 1) * 512)
                bank = lambda i: (psF[pset + i] if pset + i < 6 else psBf[pset + i - 6])
                ta, tak = tg2[par][0], WK(2 * par)
                tb_, tbk = tg2[par][1], WK(2 * par + 1)
                m1, m1k = m122[par][0], WK(4 + 2 * par)
                m2, m2k = m122[par][1], WK(5 + 2 * par)
                ACT(ta, bank(1)[:, :], AF.Exp, [PSK(pset + 1)], [tak], scale=-1.0)
                ACT(tb_, bank(3)[:, :], AF.Exp, [PSK(pset + 3)], [tbk], scale=-1.0)
                ACT(ta, ta, AF.Ln, [tak], [tak], bias=1.0)
                ACT(tb_, tb_, AF.Ln, [tbk], [tbk], bias=1.0)
                ACT(ta, ta, AF.Exp, [tak], [tak], scale=-1.0)
                ACT(tb_, tb_, AF.Exp, [tbk], [tbk], scale=-1.0)
                TT("dve", m1, ta, bank(0)[:, :], ALU.mult, [tak, PSK(pset)], [m1k])
                TT("dve", m2, tb_, bank(2)[:, :], ALU.mult, [tbk, PSK(pset + 2)], [m2k])
                STT(mT[:, oc, tcols], m1, 2.0, m2, ALU.mult, ALU.add, [m1k, m2k], [hk(("mT", tt))])

            for t in range(len(units2) + 1):
                if t >= 1:
                    e2_s2(t - 1)
                if t < len(units2):
                    e2_s1(t)

            e3_xi = {}

            def zt_view(k):
                return (wkb[2 * k], wkb[2 * k + 1]), (WK(2 * k), WK(2 * k + 1))

            def e3_s1(sl):
                k = sl % 3
                tt = sl // 4
                (z0, z1), (zk0, zk1) = zt_view(k)
                l_, lk = lst[k], hk(("lst", k))
                for hf in range(2):
                    for c in range(8):
                        MM(psF[2 * k + hf][:, :], mT[:, c, sl * 128:(sl + 1) * 128], stg[wo + hf][:, c, :], c == 0, c == 7,
                           [hk(("mT", tt)), ("stg", wo + hf)], [PSK(2 * k + hf)])
                ACT(z0, psF[2 * k][:, :], AF.Square, [PSK(2 * k)], [zk0, lk], accum_out=l_[:, 0:1])
                ACT(z1, psF[2 * k + 1][:, :], AF.Square, [PSK(2 * k + 1)], [zk1, lk], accum_out=l_[:, 1:2])

            def e3_s2(sl):
                k = sl % 3
                l_, lk = lst[k], hk(("lst", k))
                TT("dve", l_[:, 2:3], l_[:, 0:1], l_[:, 1:2], ALU.add, [lk], [lk])
                ACT(l_[:, 3:4], l_[:, 2:3], AF.Ln, [lk], [lk], scale=1.0 / D, bias=4.0 * EPS)
                ACT(l_[:, 4:5], l_[:, 3:4], AF.Exp, [lk], [lk], scale=-0.5)

            def e3_s3(sl):
                k = sl % 3
                (z0, z1), (zk0, zk1) = zt_view(k)
                l_, lk = lst[k], hk(("lst", k))
                xk_ = ("x1", half, sl)
                STT(z0, psF[2 * k][:, :], l_[:, 4:5], gpo[:, 0:512], ALU.mult, ALU.mult, [PSK(2 * k), lk, "gpo"], [zk0])
                STT(z1, psF[2 * k + 1][:, :], l_[:, 4:5], gpo[:, 512:1024], ALU.mult, ALU.mult,
                    [PSK(2 * k + 1), lk, "gpo"], [zk1])
                TT("dve", x1[:, sl, 0:512], z0, x1[:, sl, 0:512], ALU.add, [zk0, xk_], [xk_])
                TT("dve", x1[:, sl, 512:1024], z1, x1[:, sl, 512:1024], ALU.add, [zk1, xk_], [xk_])

            def e3_s4(sl):
                e3_xi[sl] = nt_a1(x1[:, sl, :], ("x1", half, sl))

            def e3_s5(sl):
                nt_a2(e3_xi[sl], x1[:, sl, :], ("x1", half, sl), gpf, "gpf")

            def e3_s6(sl):
                nt_b(e3_xi[sl], h2T[:, :, sl * 128:(sl + 1) * 128], ("h2T", half, sl // 4), on_act=True)

            e3_stages = [e3_s1, e3_s2, e3_s3, e3_s4, e3_s5, e3_s6]
            for t in range(8 + 5):
                for si_ in range(5, -1, -1):
                    sl_ = t - si_
                    if 0 <= sl_ < 8:
                        e3_stages[si_](sl_)
                if t == 6:
                    f_loads(0, wb)

            if debug and upto == 4:
                dump([("d_x1", x1.rearrange("p a b -> p (a b)"), 8 * 1024)])
                finish()
                return nc
            P.barrier()
            A.release(m_l2b)
            del xt[2:]
            actT = A.alloc([NFC, 1024], BF16)
            ffA = A.alloc([8, 512], F32)
            fw = A.alloc([4, 512], F32)
            tgf = [fw[:, 0, :], fw[:, 2, :]]
            a1 = [fw[:, 1, :], fw[:, 3, :]]
            ot = [fw[:, 0:2, :].rearrange("p a b -> p (a b)"), fw[:, 2:4, :].rearrange("p a b -> p (a b)")]
            fst = A.alloc([8, 4], F32)
            fctr = 0
            for fq in range(6):
                nfc = 4 if fq < 5 else 2
                sg_, su_ = (2 * fq + wb) % NSTG, (2 * fq + 1 + wb) % NSTG
                if fq >= 1:
                    f_loads(fq, wb)
                for f in range(nfc):
                    fc = fq * 4 + f
                    fcols = slice(f * 128, (f + 1) * 128)
                    for tt in range(2):
                        tcols = slice(tt * 512, (tt + 1) * 512)
                        pi = fctr % 2
                        fctr += 1
                        pg, pu = psF[pi], psF[2 + pi]
                        for c in range(8):
                            MM(pg[:, :], stg[sg_][:, c, fcols], h2T[:, c, tcols], c == 0, c == 7,
                               [("stg", sg_), ("h2T", half, tt)], [PSK(pi)])
                        for c in range(8):
                            MM(pu[:, :], stg[su_][:, c, fcols], h2T[:, c, tcols], c == 0, c == 7,
                               [("stg", su_), ("h2T", half, tt)], [PSK(2 + pi)])
                        ACT(tgf[pi], pg[:, :], AF.Exp, [PSK(pi)], [hk(("fw", 2 * pi))], scale=-1.0)
                        ACT(tgf[pi], tgf[pi], AF.Ln, [hk(("fw", 2 * pi))], [hk(("fw", 2 * pi))], bias=1.0)
                        ACT(tgf[pi], tgf[pi], AF.Exp, [hk(("fw", 2 * pi))], [hk(("fw", 2 * pi))], scale=-1.0)
                        TT("dve", a1[pi], tgf[pi], pg[:, :], ALU.mult, [hk(("fw", 2 * pi)), PSK(pi)], [hk(("fw", 2 * pi + 1))])
                        TT("dve", actT[:, fc, tcols], a1[pi], pu[:, :], ALU.mult, [hk(("fw", 2 * pi + 1)), PSK(2 + pi)], [hk(("actT", tt))])

            if debug and upto == 5:
                finish()
                return nc
            gjunk = A.alloc([512], BF16)

            def bank_of(sl):
                return psF[sl] if sl < 6 else psBf[sl - 6]

            def g_keys(sl):
                ob = sl % 2
                return hk(("fw", 2 * ob)), hk(("fw", 2 * ob + 1)), hk(("fst", sl))

            def g_ev1(r, sl):
                pbank = bank_of(sl)
                ok, ok2, fk = g_keys(sl)
                ACT(gjunk, pbank[:, :], AF.Square, [PSK(sl)], [hk("gjunk"), fk], accum_out=fst[:, sl, r:r + 1])
                if r == 0:
                    CP("dve", ffA[:, sl, :], pbank[:, :], [PSK(sl)], [hk(("ffA", sl))])

            def g_ev2(sl):
                ok, ok2, fk = g_keys(sl)
                TT("dve", fst[:, sl, 2:3], fst[:, sl, 0:1], fst[:, sl, 1:2], ALU.add, [fk], [fk])
                ACT(fst[:, sl, 3:4], fst[:, sl, 2:3], AF.Ln, [fk], [fk], scale=1.0 / D, bias=EPS)
                ACT(fst[:, sl, 3:4], fst[:, sl, 3:4], AF.Exp, [fk], [fk], scale=-0.5)

            def g_ev3(sl):
                gs = half * 8 + sl
                pbank = bank_of(sl)
                ob = sl % 2
                ok, ok2, fk = g_keys(sl)
                STT(ot[ob][:, 0:512], ffA[:, sl, :], fst[:, sl, 3:4], gpff[:, 0:512], ALU.mult, ALU.mult,
                    [hk(("ffA", sl)), fk, "gpff"], [ok, ok2])
                STT(ot[ob][:, 512:1024], pbank[:, :], fst[:, sl, 3:4], gpff[:, 512:1024], ALU.mult, ALU.mult,
                    [PSK(sl), fk, "gpff"], [ok, ok2])
                TT("dve", ot[ob], ot[ob], x1[:, sl, :], ALU.add, [ok, ok2, ("x1", half, sl)], [ok, ok2])
                DMA("sp", ("out", ob), out_d[gs * 128:(gs + 1) * 128, :], ot[ob], [ok, ok2], [("outd", gs)])

            for r in range(2):
                sis = []
                for k3 in range(3):
                    n8 = 8 if k3 < 2 else 6
                    si = (k3 + r * 3 + wb) % NSTG
                    sis.append((si, n8))
                    load_w(stg[si][:, 0:n8, :], w_ffd_v[:, k3 * 8:k3 * 8 + n8, r * 512:(r + 1) * 512], ("stg", si))
                si, n8 = sis[0]
                for f in range(n8):
                    fc = f
                    for sl in range(8):
                        pbank = psF[sl] if sl < 6 else psBf[sl - 6]
                        MM(pbank[:, :], actT[:, fc, sl * 128:(sl + 1) * 128], stg[si][:, f, :], fc == 0, fc == NFC - 1,
                           [hk(("actT", sl // 4)), ("stg", si)], [PSK(sl)])
                if r == 1 and half == 0:
                    e1_loads(2)
                    e1_prefetch_x(1)
                for sl in range(8):
                    pbank = psF[sl] if sl < 6 else psBf[sl - 6]
                    for k3 in (1, 2):
                        si, n8 = sis[k3]
                        for f in range(n8):
                            fc = k3 * 8 + f
                            MM(pbank[:, :], actT[:, fc, sl * 128:(sl + 1) * 128], stg[si][:, f, :], fc == 0, fc == NFC - 1,
                               [hk(("actT", sl // 4)), ("stg", si)], [PSK(sl)])
                    g_ev1(r, sl)
                    if r == 1:
                        if sl >= 1:
                            g_ev2(sl - 1)
                        if sl >= 2:
                            g_ev3(sl - 2)
                if r == 1:
                    g_ev2(7)
                    g_ev3(6)
                    g_ev3(7)
            P.barrier()
            if debug and upto == 6:
                finish()
                return nc

        finish()
    return nc


_NC_CACHE = {}


def _layout_inputs(inp):
    f = lambda a: np.ascontiguousarray(np.asarray(a, dtype=np.float32))
    x = f(inp["x"])
    rep = lambda v, n=128: np.ascontiguousarray(np.broadcast_to(f(v).reshape(1, -1), (n, f(v).size)))
    common = {
        "w_in": f(inp["w_in"][0]), "w_a": f(inp["w_branch_a"][0]), "w_b": f(inp["w_branch_b"][0]),
        "w_out": f(inp["w_out"][0]), "w_ffi": f(inp["w_ffn_in"][0]), "w_ffd": f(inp["w_ffn_down"][0]),
        "gpm": rep(inp["g_pre_mix"][0]), "gpo": rep(inp["g_post_mix"][0]), "gpf": rep(inp["g_pre_ffn"][0]),
        "gpff": rep(inp["g_post_ffn"][0]), "gsg": rep(inp["g_sgu"][0]), "bsg": rep(inp["b_sgu"][0]),
        "bfb": rep(inp["b_forget"][0]),
        "gqc": np.ascontiguousarray(np.tile(f(inp["g_q"][0]).reshape(64, 1), (2, 1))),
        "gkc": np.ascontiguousarray(np.tile(f(inp["g_k"][0]).reshape(64, 1), (2, 1))),
        "wsp": f(inp["w_spatial"][0]),
        "bsp": np.ascontiguousarray(f(inp["b_spatial"][0]).T),
    }
    ones = np.ones((128, 128), np.float32)
    zeros = np.zeros((128, 128), np.float32)
    tri = np.triu(np.ones((128, 128), np.float32))
    in_maps = []
    for c in range(8):
        b, par = c // 2, c % 2
        blocks = []
        mk = np.zeros((128, NS, 2, 128), np.float32)
        for s in range(NS):
            m = s // 2
            if par == 0:
                i = 4 * m if s % 2 == 0 else 4 * m + 3
            else:
                i = 4 * m + 1 if s % 2 == 0 else 4 * m + 2
            blocks.append(i)
            J = J_of(s)
            for k in range(2):
                j = J - 1 + k
                mk[:, s, k, :] = ones if j < i else (tri if j == i else zeros)
        xo = np.concatenate([x[b, i * 128:(i + 1) * 128] for i in blocks], axis=0)
        d = dict(common)
        d["xseq"] = np.ascontiguousarray(x[b])
        d["xown"] = np.ascontiguousarray(xo)
        d["msk"] = np.ascontiguousarray(mk.reshape(128, NS * 256))
        in_maps.append((d, blocks))
    return in_maps


def kernel(**inputs):
    maps = _layout_inputs(inputs)
    if "nc" not in _NC_CACHE:
        _NC_CACHE["nc"] = build_nc()
    nc = _NC_CACHE["nc"]
    res = run_bass_kernel_spmd(nc, [m[0] for m in maps], core_ids=list(range(8)))
    out = np.empty((4, SEQ, D), np.float32)
    for c in range(8):
        b = c // 2
        o = res.results[c]["out"]
        for s, i in enumerate(maps[c][1]):
            out[b, i * 128:(i + 1) * 128] = o[s * 128:(s + 1) * 128]
    return out
```

```python
import numpy as np
from contextlib import ExitStack
import concourse.bass as bass
import concourse.mybir as mybir
from concourse.bass_utils import run_bass_kernel_spmd

F32 = mybir.dt.float32
BF16 = mybir.dt.bfloat16
AF = mybir.ActivationFunctionType
ALU = mybir.AluOpType
AX = mybir.AxisListType

D = 1024
SEQ = 4096
NB = 32
NS = 16
TOWN = 2048
HEADS = 8
DH = 64
Q_OFF, K_OFF, V_OFF, F_OFF, U_OFF, G_OFF = 0, 512, 1024, 1536, 1544, 2568
IN_COLS = 4616
DFF = 2816
NFC = 22
EPS = 1e-6
GELU_C = 0.7978845608028654


def J_of(slot):
    return 4 * (slot // 2) + (1 if slot % 2 == 0 else 3)


ENGS = ["pe", "act", "dve", "pool", "sp"]


class Prog:
    def __init__(self, nc):
        self.nc = nc
        self.recs = {e: [] for e in ENGS}
        self.lastw = {}
        self.readers = {}
        self.dma_count = {}

    def _deps(self, eng, reads, writes, is_dma):
        deps = set()
        for k in reads:
            t = self.lastw.get(k)
            if t is not None:
                deps.add(t)
            if isinstance(k, tuple) and k[0] == "ps":
                for t2 in self.readers.get(k, {}).values():
                    if not (t2[0] == "e" and t2[1] == eng):
                        deps.add(t2)
        strict = is_dma or eng != "pe"
        for k in writes:
            t = self.lastw.get(k)
            if t is not None and (strict or not (t[0] == "e" and t[1] == eng)):
                deps.add(t)
            for t2 in self.readers.get(k, {}).values():
                if strict or not (t2[0] == "e" and t2[1] == eng):
                    deps.add(t2)
        return deps

    def _register(self, tok, rk, reads, writes):
        for k in reads:
            self.readers.setdefault(k, {})[rk] = tok
        for k in writes:
            self.lastw[k] = tok
            self.readers[k] = {}

    @staticmethod
    def _expand(keys):
        out = []
        for k in keys:
            out.append(k)
            if isinstance(k, tuple) and len(k) == 2 and k[0] == "stg":
                out += [("sa", k[1]), ("sb", k[1]), ("sc", k[1])]
        return out

    def op(self, eng, fn, reads=(), writes=()):
        reads, writes = self._expand(reads), self._expand(writes)
        idx = len(self.recs[eng])
        deps = self._deps(eng, reads, writes, False)
        tok = ("e", eng, idx)
        self.recs[eng].append(dict(fn=fn, deps=deps, dma=None))
        self._register(tok, eng, reads, writes)

    def dma(self, eng, semkey, fn, reads=(), writes=()):
        reads, writes = self._expand(reads), self._expand(writes)
        n = self.dma_count.get(semkey, 0) + 1
        self.dma_count[semkey] = n
        tok = ("d", semkey, n)
        deps = self._deps(eng, reads, writes, True)
        self.recs[eng].append(dict(fn=fn, deps=deps, dma=semkey))
        self._register(tok, ("d", semkey), reads, writes)

    def barrier(self):
        toks = set()
        for e in ENGS:
            if self.recs[e]:
                for i in range(len(self.recs[e]) - 1, -1, -1):
                    r = self.recs[e][i]
                    if r["fn"] is not None and r["dma"] is None:
                        toks.add(("e", e, i))
                        break
        for k, n in self.dma_count.items():
            toks.add(("d", k, n))
        for e in ENGS:
            deps = set(t for t in toks if not (t[0] == "e" and t[1] == e))
            self.recs[e].append(dict(fn=None, deps=deps, dma=None))

    def final_wait(self, eng, semkeys):
        deps = set(("d", k, self.dma_count[k]) for k in semkeys if k in self.dma_count)
        self.recs[eng].append(dict(fn=None, deps=deps, dma=None))

    def emit(self):
        nc = self.nc
        sig = {e: [False] * len(self.recs[e]) for e in ENGS}
        for e in ENGS:
            for r in self.recs[e]:
                for t in r["deps"]:
                    if t[0] == "e":
                        sig[t[1]][t[2]] = True
        rank = {}
        for e in ENGS:
            c = 0
            rk = []
            for i in range(len(self.recs[e])):
                if sig[e][i]:
                    c += 1
                rk.append(c)
            rank[e] = rk
        with ExitStack() as st:
            esem = {e: st.enter_context(nc.semaphore("s_" + e)) for e in ENGS}
            dsem = {}
            for i, k in enumerate(sorted(self.dma_count.keys(), key=str)):
                dsem[k] = st.enter_context(nc.semaphore("d%d" % i))
            block = st.enter_context(nc.Block())
            bname = {"pe": "tensor", "act": "scalar", "dve": "vector", "pool": "gpsimd", "sp": "sync"}
            for e in ENGS:
                def body(engine, e=e):
                    waited = {}
                    for i, r in enumerate(self.recs[e]):
                        for t in sorted(r["deps"], key=str):
                            if t[0] == "e":
                                key = ("e", t[1]); val = rank[t[1]][t[2]]; sem = esem[t[1]]
                            else:
                                key = ("d", t[1]); val = 16 * t[2]; sem = dsem[t[1]]
                            if waited.get(key, 0) >= val:
                                continue
                            engine.wait_ge(sem, val)
                            waited[key] = val
                        if r["fn"] is None:
                            continue
                        ins = r["fn"](engine)
                        if r["dma"] is not None:
                            ins.then_inc(dsem[r["dma"]], 16)
                        elif sig[e][i]:
                            ins.then_inc(esem[e], 1)
                getattr(block, bname[e])(body)


class Arena:
    def __init__(self, sb, total):
        self.sb = sb
        self.total = total
        self.off = 0
        self.peak = 0

    def alloc(self, free_shape, dt):
        n = 1
        for s in free_shape:
            n *= s
        esz = 4 if dt == F32 else 2
        nbytes = (n * esz + 63) // 64 * 64
        o = self.off
        self.off += nbytes
        self.peak = max(self.peak, self.off)
        assert self.off <= self.total, ("SBUF arena overflow", self.off, self.total)
        v = self.sb[:, o // 2:o // 2 + n * esz // 2]
        if dt == F32:
            v = v.bitcast(F32)
        if len(free_shape) == 2:
            v = v.rearrange("p (a b) -> p a b", a=free_shape[0])
        elif len(free_shape) == 3:
            v = v.rearrange("p (a b c) -> p a b c", a=free_shape[0], b=free_shape[1])
        return v

    def mark(self):
        return self.off

    def release(self, m):
        self.off = m


def build_nc(debug=False, upto=3):
    nc = bass.Bass("TRN2", target_bir_lowering=False)

    def din(name, shape):
        return nc.dram_tensor(name, list(shape), F32, kind="ExternalInput").ap()

    xseq = din("xseq", [SEQ, D])
    xown = din("xown", [TOWN, D])
    w_in = din("w_in", [D, IN_COLS])
    w_a = din("w_a", [512, D])
    w_b = din("w_b", [512, D])
    w_out = din("w_out", [D, D])
    w_ffi = din("w_ffi", [D, 2 * DFF])
    w_ffd = din("w_ffd", [DFF, D])
    gpm_d = din("gpm", [128, D])
    gpo_d = din("gpo", [128, D])
    gpf_d = din("gpf", [128, D])
    gpff_d = din("gpff", [128, D])
    gsg_d = din("gsg", [128, 512])
    bsg_d = din("bsg", [128, 512])
    bfb_d = din("bfb", [128, 8])
    gqc_d = din("gqc", [128, 1])
    gkc_d = din("gkc", [128, 1])
    wsp_d = din("wsp", [8, 128, 128])
    bsp_d = din("bsp", [128, 8])
    msk_d = din("msk", [128, NS * 256])
    out_d = nc.dram_tensor("out", [TOWN, D], F32, kind="ExternalOutput").ap()
    dbg = {}
    if debug:
        for nm, shp in [("d_kt", [128, 4 * SEQ]), ("d_qt", [128, 4 * TOWN]), ("d_v", [128, NB * 8 * 65]),
                        ("d_c", [128, 256]), ("d_at", [128, 4 * TOWN]), ("d_x1", [128, 8 * 1024])]:
            dbg[nm] = nc.dram_tensor(nm, shp, F32, kind="ExternalOutput").ap()

    w_in_v = w_in.rearrange("(c p) n -> p c n", p=128)
    w_a_v = w_a.rearrange("(c p) n -> p c n", p=128)
    w_b_v = w_b.rearrange("(c p) n -> p c n", p=128)
    w_out_v = w_out.rearrange("(c p) n -> p c n", p=128)
    w_ffi_v = w_ffi.rearrange("(c p) n -> p c n", p=128)
    w_ffd_v = w_ffd.rearrange("(c p) n -> p c n", p=128)

    TOTAL = 212480
    with ExitStack() as stack:
        sb = stack.enter_context(nc.sbuf_tensor("sb", [128, TOTAL // 2], BF16))
        psF = [stack.enter_context(nc.psum_tensor("psf%d" % i, [128, 512], F32)) for i in range(6)]
        psBf = [stack.enter_context(nc.psum_tensor("psb%d" % i, [128, 512], F32)) for i in range(2)]
        psB = [t[:, :].bitcast(BF16) for t in psBf]
        A = Arena(sb, TOTAL)
        P = Prog(nc)

        def PSK(i):
            return ("ps", i)

        def MM(out, lhsT, rhs, start, stop, reads, writes):
            P.op("pe", lambda e: e.matmul(out, lhsT=lhsT, rhs=rhs, start=start, stop=stop), reads, writes)

        def TR(out, in_, reads, writes):
            P.op("pe", lambda e: e.transpose(out=out, in_=in_, identity=ident), list(reads) + ["ident"], writes)

        def ACT(out, in_, func, reads, writes, **kw):
            P.op("act", lambda e: e.activation(out=out, in_=in_, func=func, **kw), reads, writes)

        def TT(eng, out, in0, in1, op, reads, writes):
            P.op(eng, lambda e: e.tensor_tensor(out=out, in0=in0, in1=in1, op=op), reads, writes)

        def TS(eng, out, in0, s1, s2, op0, op1, reads, writes):
            if s2 is None:
                P.op(eng, lambda e: e.tensor_scalar(out=out, in0=in0, scalar1=s1, scalar2=None, op0=op0), reads, writes)
            else:
                P.op(eng, lambda e: e.tensor_scalar(out=out, in0=in0, scalar1=s1, scalar2=s2, op0=op0, op1=op1), reads, writes)

        def STT(out, in0, scalar, in1, op0, op1, reads, writes):
            P.op("dve", lambda e: e.scalar_tensor_tensor(out=out, in0=in0, scalar=scalar, in1=in1, op0=op0, op1=op1),
                 reads, writes)

        def CP(eng, out, in_, reads, writes):
            P.op(eng, lambda e: e.tensor_copy(out=out, in_=in_), reads, writes)

        def MS(eng, ap, val, reads, writes):
            P.op(eng, lambda e: e.memset(ap, val), reads, writes)

        def DMA(eng, semkey, out, in_, reads, writes):
            P.dma(eng, semkey, lambda e: e.dma_start(out=out, in_=in_), reads, writes)

        def load_w(dst_view, src_view, key):
            DMA("pool", ("w", key), dst_view, src_view, [], [key])

        ident = A.alloc([128], BF16)
        bones = A.alloc([128], BF16)
        tri = A.alloc([128], F32)
        aones = A.alloc([128], F32)
        gqc = A.alloc([1], F32)
        gkc = A.alloc([1], F32)
        bfb = A.alloc([8], F32)
        gpm = A.alloc([D], F32)
        attnT = A.alloc([4, TOWN], BF16)
        NSTG = 4
        stg = [A.alloc([8, 512], BF16) for _ in range(NSTG)]
        xt = [A.alloc([D], F32) for _ in range(2)]
        NXN = 4
        xn = [A.alloc([D], BF16) for _ in range(NXN)]
        stt_ = [A.alloc([8], F32) for _ in range(NXN)]

        MS("pool", ident, 0.0, [], ["ident"])
        P.op("pool", lambda e: e.affine_select(out=ident, in_=ident, pattern=[[-1, 128]], compare_op=ALU.not_equal,
                                               fill=1.0, base=0, channel_multiplier=1),
             reads=["ident"], writes=["ident"])
        MS("pool", bones, 0.0, [], ["bones"])
        MS("pool", bones[0:64, 0:64], 1.0, ["bones"], ["bones"])
        MS("pool", bones[64:128, 64:128], 1.0, ["bones"], ["bones"])
        MS("pool", aones, 1.0, [], ["aones"])
        MS("pool", tri, 1.0, [], ["tri"])
        P.op("pool", lambda e: e.affine_select(out=tri, in_=tri, pattern=[[1, 128]], compare_op=ALU.is_ge,
                                               fill=0.0, base=0, channel_multiplier=-1),
             reads=["tri"], writes=["tri"])
        DMA("sp", "c0", gqc, gqc_d, [], ["gqc"])
        DMA("sp", "c1", gkc, gkc_d, [], ["gkc"])
        DMA("sp", "c2", bfb, bfb_d, [], ["bfb"])
        DMA("sp", "c3", gpm, gpm_d, [], ["gpm"])

        gsg = A.alloc([512], F32)
        bsg = A.alloc([512], F32)
        wsT = A.alloc([8, 128], BF16)
        bsp = A.alloc([8], F32)
        wtmp8, wtmpb8 = [], []

        def prep_mix_consts_a():
            DMA("sp", "c7", gsg, gsg_d, [], ["gsg"])
            DMA("sp", "c8", bsg, bsg_d, [], ["bsg"])
            DMA("sp", "c9", bsp, bsp_d, [], ["bsp"])
            for g in range(8):
                DMA("pool", ("c10", g), wtmpb8[g], wsp_d[g], [], [("wtmpb", g)])
                MS("dve", wtmpb8[g][0:64, 64:128], 0.0, [("wtmpb", g)], [("wtmpb", g)])

        def prep_mix_consts_b():
            for g in range(8):
                w_ = g % 2
                TR(psB[w_][:, 0:128], wtmpb8[g], [("wtmpb", g)], [PSK(6 + w_)])
                CP("dve", wsT[:, g, :], psB[w_][:, 0:128], [PSK(6 + w_)], ["wsT"])

        xctr = [0]
        psb_ctr = [0]

        xnc = [0]

        def nt_a(src_rows, gbc, gkey, eps_scale=1.0, keep=None):
            if keep is None:
                b = xctr[0] % len(xt)
                xctr[0] += 1
                x_t, xk = xt[b], ("xt", b)
                DMA("sp", ("xt", b), x_t, src_rows, [], [xk])
            else:
                x_t, xk = keep
            xi = nt_a1(x_t, xk, eps_scale)
            nt_a2(xi, x_t, xk, gbc, gkey)
            return xi

        def nt_a1(x_t, xk, eps_scale=1.0):
            xi = xnc[0] % NXN
            xnc[0] += 1
            x_n, nk = xn[xi], ("xn", xi)
            s_t, sk = stt_[xi], ("st", xi)
            ACT(x_n, x_t, AF.Square, [xk], [nk, sk], accum_out=s_t[:, 0:1])
            ACT(s_t[:, 1:2], s_t[:, 0:1], AF.Ln, [sk], [sk], scale=1.0 / D, bias=EPS * eps_scale)
            ACT(s_t[:, 2:3], s_t[:, 1:2], AF.Exp, [sk], [sk], scale=-0.5)
            return xi

        def nt_a2(xi, x_t, xk, gbc, gkey):
            x_n, nk = xn[xi], ("xn", xi)
            s_t, sk = stt_[xi], ("st", xi)
            STT(x_n, x_t, s_t[:, 2:3], gbc, ALU.mult, ALU.mult, [xk, sk, gkey], [nk])

        def nt_b(xi, dst_view, dst_key, on_act=False):
            x_n, nk = xn[xi], ("xn", xi)
            pb = psb_ctr[0] % 2
            psb_ctr[0] += 1
            pst = psB[pb]
            for c in range(8):
                TR(pst[:, c * 128:(c + 1) * 128], x_n[:, c * 128:(c + 1) * 128], [nk], [PSK(6 + pb)])
            if on_act:
                ACT(dst_view, pst.rearrange("p (c t) -> p c t", c=8), AF.Copy, [PSK(6 + pb)], [dst_key])
            else:
                CP("dve", dst_view, pst.rearrange("p (c t) -> p c t", c=8), [PSK(6 + pb)], [dst_key])

        m_l1 = A.mark()
        KT = A.alloc([4, SEQ], BF16)
        Vaug = A.alloc([NB, 8, 65], BF16)
        QT = A.alloc([4, TOWN], BF16)
        zf = A.alloc([NB, 8], F32)
        Cc = A.alloc([NB, 8], F32)
        Eb = A.alloc([NB + 1, 8], F32)
        msk = A.alloc([NS, 256], BF16)
        m_l2 = A.mark()
        hTg = [A.alloc([8, 512], BF16) for _ in range(2)]
        sq = [A.alloc([512], BF16) for _ in range(3)]
        rs = [A.alloc([512], F32) for _ in range(3)]
        wv2 = A.alloc([8, 264], BF16)
        xt.append(A.alloc([D], F32))
        xt.append(A.alloc([D], F32))
        for _g in range(8):
            wtmpb8.append(A.alloc([128], BF16))

        MS("pool", Vaug.rearrange("p a b c -> p (a b) c")[:, :, 64:65], 1.0, [], ["vones"])

        load_w(stg[0], w_in_v[:, :, K_OFF:K_OFF + 512], ("stg", 0))
        load_w(stg[1][:, :, 0:256], w_in_v[:, :, V_OFF:V_OFF + 256], ("stg", 1))
        load_w(wv2, w_in_v[:, :, V_OFF + 256:V_OFF + 520], "wv2")
        def late_loads():
            load_w(stg[2], w_in_v[:, :, Q_OFF:Q_OFF + 512], ("stg", 2))
            load_w(msk.rearrange("p a b -> p (a b)"), msk_d, "msk")

        fm_ctr = [0]

        def proj_fm_1(wview, wkey, p, hT, hkey):
            i = fm_ctr[0] % 3
            fm_ctr[0] += 1
            ps = psF[i]
            for c in range(8):
                MM(ps[:, :], wview[:, c, p * 128:(p + 1) * 128], hT[:, c, :], c == 0, c == 7, [wkey, hkey], [PSK(i)])
            ACT(sq[i], ps[:, :], AF.Square, [PSK(i)], [("sq", i)])
            return i

        def proj_fm_2(i, dst, dkey, gcol, gckey):
            ps, ps2 = psF[i], psF[3]
            MM(ps2[:, :], bones, sq[i], True, True, ["bones", ("sq", i)], [PSK(3)])
            ACT(rs[i], ps2[:, :], AF.Ln, [PSK(3)], [("rs", i)], scale=1.0 / DH, bias=EPS)
            ACT(rs[i], rs[i], AF.Exp, [("rs", i)], [("rs", i)], scale=-0.5)
            STT(dst, ps[:, :], gcol[:, 0:1], rs[i], ALU.mult, ALU.mult, [PSK(i), ("rs", i), gckey], [dkey])

        NG = 12
        xis = {}

        def ab_s1a(g):
            src = xseq if g < 8 else xown
            g0 = g if g < 8 else g - 8
            xis[g] = [nt_a(src[(g0 * 4 + bl) * 128:(g0 * 4 + bl + 1) * 128, :], gpm, "gpm") for bl in range(4)]

        def ab_s1b(g):
            hb = g % 2
            for bl in range(4):
                nt_b(xis[g][bl], hTg[hb][:, :, bl * 128:(bl + 1) * 128], ("hTg", hb))

        def ab_s2(g):
            hb = g % 2
            if g < 8:
                grp = g
                for p in range(4):
                    bl = p
                    ii = proj_fm_1(stg[0], ("stg", 0), p, hTg[hb], ("hTg", hb))
                    if p >= 1:
                        proj_fm_2(prev[0], *prev[1])
                    prev = (ii, (KT[:, p, grp * 512:(grp + 1) * 512], ("KT", p, grp), gkc, "gkc"))
                    blk = grp * 4 + bl
                    for c in range(8):
                        MM(psF[4][:, 0:256], hTg[hb][:, c, bl * 128:(bl + 1) * 128], stg[1][:, c, 0:256], c == 0, c == 7,
                           [("hTg", hb), ("stg", 1)], [PSK(4)])
                    for c in range(8):
                        MM(psF[5][:, 0:264], hTg[hb][:, c, bl * 128:(bl + 1) * 128], wv2[:, c, :], c == 0, c == 7,
                           [("hTg", hb), "wv2"], [PSK(5)])
                    ACT(Vaug[:, blk, 0:4, 0:64], psF[4][:, 0:256].rearrange("p (h d) -> p h d", h=4), AF.Copy,
                        [PSK(4)], [("V", blk)])
                    ACT(Vaug[:, blk, 4:8, 0:64], psF[5][:, 0:256].rearrange("p (h d) -> p h d", h=4), AF.Copy,
                        [PSK(5), ("V", blk)], [("V", blk)])
                    TT("dve", zf[:, blk, :], psF[5][:, 256:264], bfb, ALU.add, [PSK(5), "bfb"], ["zf"])
                return prev
            else:
                grp = g - 8
                for p in range(4):
                    ii = proj_fm_1(stg[2], ("stg", 2), p, hTg[hb], ("hTg", hb))
                    if p >= 1:
                        proj_fm_2(prev[0], *prev[1])
                    prev = (ii, (QT[:, p, grp * 512:(grp + 1) * 512], ("QT", p, grp), gqc, "gqc"))
                return prev

        zf2 = zf.rearrange("p a b -> p (a b)")
        Cc2 = Cc.rearrange("p a b -> p (a b)")

        def phase_c1():
            ACT(zf2, zf2, AF.Exp, ["zf"], ["zf"], scale=-1.0)
            ACT(zf2, zf2, AF.Ln, ["zf"], ["zf"], bias=1.0)

        def phase_c2():
            MM(psF[0][:, 0:256], tri, zf2, True, True, ["tri", "zf"], [PSK(0)])
            MM(psF[1][:, 0:256], aones, zf2, True, True, ["aones", "zf"], [PSK(1)])
            MS("dve", Eb[:, 0, :], 0.0, [], ["Eb"])
            CP("dve", Cc2, psF[1][:, 0:256], [PSK(1)], ["Cc"])
            for j in range(1, NB + 1):
                TT("dve", Eb[:, j, :], Eb[:, j - 1, :], Cc[:, j - 1, :], ALU.add, ["Eb", "Cc"], ["Eb"])
            TT("dve", Cc2, psF[0][:, 0:256], Eb[:, 0:NB, :].rearrange("p a b -> p (a b)"), ALU.add, [PSK(0), "Eb"], ["Cc"])

        for t in range(NG + 1):
            if t == 2:
                late_loads()
            if t == 4:
                prep_mix_consts_a()
            if t == 5:
                prep_mix_consts_b()
            if t < NG:
                ab_s1a(t)
            last = ab_s2(t - 1) if t >= 1 else None
            if t < NG:
                ab_s1b(t)
            if last is not None:
                proj_fm_2(last[0], *last[1])
            if t == 8:
                phase_c1()
            if t == 9:
                phase_c2()

        def dump(items):
            P.barrier()
            mk_ = A.mark()
            dv = xn[0].bitcast(F32)
            for nm, src, n in items:
                for o in range(0, n, 256):
                    m = min(256, n - o)
                    CP("dve", dv[:, 0:m], src[:, o:o + m], ["dbgsrc"], ["dv"])
                    DMA("sp", "dbg", dbg[nm][:, o:o + m], dv[:, 0:m], ["dv"], ["dbgout"])
            P.barrier()
            A.release(mk_)

        def finish():
            P.final_wait("sp", [("out", 0), ("out", 1), "dbg"])
            P.emit()

        if debug:
            dump([("d_kt", KT.rearrange("p a b -> p (a b)"), 4 * SEQ), ("d_qt", QT.rearrange("p a b -> p (a b)"), 4 * TOWN),
                  ("d_v", Vaug.rearrange("p a b c -> p (a b c)"), NB * 8 * 65), ("d_c", Cc.rearrange("p a b -> p (a b)"), 256)])
            if upto == 1:
                finish()
                return nc

        hs = [stg[k // 2][:, :, (k % 2) * 256:(k % 2 + 1) * 256] for k in range(8)]

        def e2_b0(ch, wb):
            even = (ch % 2 == 0)
            if wb == 0:
                return 4 if even else 0
            return 0 if even else 4

        def e2_chunk_loads(ch, wb=0):
            b0 = e2_b0(ch, wb)
            c0 = ch * 256
            s0, s1_ = b0 // 2, b0 // 2 + 1
            load_w(hs[b0][:, 0:4, :], w_a_v[:, :, c0:c0 + 256], ("sa", s0))
            load_w(hs[b0][:, 4:8, :], w_b_v[:, :, c0:c0 + 256], ("sb", s0))
            load_w(hs[b0 + 1], w_in_v[:, :, G_OFF + c0:G_OFF + c0 + 256], ("sc", s0))
            load_w(hs[b0 + 2][:, 0:4, :], w_in_v[:, 0:4, G_OFF + 1024 + c0:G_OFF + 1024 + c0 + 256], ("sa", s1_))
            load_w(hs[b0 + 2][:, 4:8, :], w_in_v[:, 4:8, G_OFF + 1024 + c0:G_OFF + 1024 + c0 + 256], ("sb", s1_))

        pre_x = {}

        def e1_prefetch_x(half_):
            for sl_ in range(2):
                gs_ = half_ * 8 + sl_
                DMA("sp", ("xt", sl_), xt[sl_], xown[gs_ * 128:(gs_ + 1) * 128, :], [], [("xt", sl_)])
                pre_x[(half_, sl_)] = sl_

        def e1_loads(wb=0):
            load_w(stg[wb], w_in_v[:, :, U_OFF:U_OFF + 512], ("stg", wb))
            load_w(stg[wb + 1], w_in_v[:, :, U_OFF + 512:U_OFF + 1024], ("stg", wb + 1))

        def f_loads(fq, foff=0):
            nfc = 4 if fq < 5 else 2
            sg_, su_ = (2 * fq + foff) % NSTG, (2 * fq + 1 + foff) % NSTG
            load_w(stg[sg_][:, :, 0:nfc * 128], w_ffi_v[:, :, fq * 512:fq * 512 + nfc * 128], ("stg", sg_))
            load_w(stg[su_][:, :, 0:nfc * 128], w_ffi_v[:, :, DFF + fq * 512:DFF + fq * 512 + nfc * 128], ("stg", su_))

        P.barrier()
        A.release(m_l2)
        del xt[2:]
        pT = [A.alloc([NB * 128], BF16) for _ in range(2)]
        Vp = [A.alloc([NB, 65], BF16) for _ in range(4)]
        wS = [A.alloc([NB, 8], F32) for _ in range(2)]
        wB = [A.alloc([NB, 8], F32) for _ in range(2)]
        atok = [A.alloc([512], BF16) for _ in range(2)]
        rec = [A.alloc([8], F32) for _ in range(2)]

        units = [(s, p) for s in range(NS) for p in range(4)]
        items = []
        for ui, (s, p) in enumerate(units):
            nb_ = J_of(s) + 1
            for c0 in range(0, nb_, 4):
                items.append((ui, c0))
        cc = [0]

        NSET = 3
        LOOK = 6

        def acc_banks(s):
            return (psBf[0], PSK(6)), (psBf[1], PSK(7))

        def att_prep(ui):
            s, p = units[ui]
            J = J_of(s)
            nb = J + 1
            sb_ = s % 2
            if p == 0:
                TT("dve", wB[sb_][:, 0:nb, :], Cc[:, 0:nb, :], Eb[:, J:J + 1, :].to_broadcast([128, nb, 8]), ALU.subtract,
                   ["Cc", "Eb"], [("wB", sb_)])
                ACT(wS[sb_][:, 0:nb, :], wB[sb_][:, 0:nb, :], AF.Exp, [("wB", sb_)], [("wS", sb_)])
            for ab in range(2):
                h = 2 * p + ab
                vi = 2 * (ui % 2) + ab
                TT("dve", Vp[vi][:, 0:nb, :], Vaug[:, 0:nb, h, :], wS[sb_][:, 0:nb, h:h + 1].to_broadcast([128, nb, 65]), ALU.mult,
                   [("V", j) for j in range(nb)] + ["vones", ("wS", sb_)], [("Vp", vi)])

        def att_qk(ui, c0):
            s, p = units[ui]
            J = J_of(s)
            nb = J + 1
            n = min(4, nb - c0)
            set_ = cc[0] % NSET
            cc[0] += 1
            for jj in range(n):
                j = c0 + jj
                for ab in range(2):
                    r0 = ab * 64
                    bk = 2 * set_ + ab
                    MM(psF[bk][:, jj * 128:(jj + 1) * 128], KT[r0:r0 + 64, p, j * 128:(j + 1) * 128],
                       QT[r0:r0 + 64, p, s * 128:(s + 1) * 128], True, True,
                       [("KT", p, j // 4), ("QT", p, s // 4)], [PSK(bk)])
            for ab in range(2):
                bk = 2 * set_ + ab
                ACT(pT[ab][:, c0 * 128:(c0 + n) * 128], psF[bk][:, 0:n * 128], AF.Exp, [PSK(bk)], [("pT", ab, c0 // 4)], scale=0.125)
            if c0 + n == nb:
                for ab in range(2):
                    TT("pool", pT[ab][:, (J - 1) * 128:(J + 1) * 128], pT[ab][:, (J - 1) * 128:(J + 1) * 128], msk[:, s, :], ALU.mult,
                       [("pT", ab, c0 // 4), "msk"], [("pT", ab, c0 // 4)])

        def att_pv(ui, c0):
            s, p = units[ui]
            J = J_of(s)
            nb = J + 1
            n = min(4, nb - c0)
            banks = acc_banks(s)
            for ab in range(2):
                acc, ak = banks[ab]
                vi = 2 * (ui % 2) + ab
                for jj in range(n):
                    j = c0 + jj
                    MM(acc[:, p * 65:(p + 1) * 65], pT[ab][:, j * 128:(j + 1) * 128], Vp[vi][:, j, :], j == 0, j == J,
                       [("pT", ab, c0 // 4), ("Vp", vi)], [ak])
            if c0 + n == nb and p == 3:
                tb = s % 2
                for ab in range(2):
                    acc, ak = banks[ab]
                    a3 = acc[:, 0:260].rearrange("p (h d) -> p h d", h=4)
                    rk = ("rec", ab)
                    P.op("dve", lambda e, o_=rec[ab][:, 0:4], i_=a3[:, :, 64]: e.reciprocal(out=o_, in_=i_), [ak], [rk])
                    TT("dve", atok[tb].rearrange("p (q two d) -> p q two d", two=2, d=64)[:, :, ab, :], a3[:, :, 0:64],
                       rec[ab][:, 0:4].unsqueeze(2).to_broadcast([128, 4, 64]), ALU.mult, [ak, rk], [("atok", tb)])
                deferred.append((s, tb))

        deferred = []

        def att_fin2():
            while deferred:
                s, tb = deferred.pop(0)
                set_ = cc[0] % NSET
                cc[0] += 1
                bk = 2 * set_
                ptr = psF[bk][:, :].bitcast(BF16)
                for c in range(4):
                    TR(ptr[:, c * 128:(c + 1) * 128], atok[tb][:, c * 128:(c + 1) * 128], [("atok", tb)], [PSK(bk)])
                CP("dve", attnT[:, :, s * 128:(s + 1) * 128], ptr[:, 0:512].rearrange("p (c t) -> p c t", c=4),
                   [PSK(bk)], [("attnT", s)])

        pending = []
        last_unit = [-1]
        for k in range(len(items)):
            while pending and (len(pending) > LOOK or any(items[q][1] == items[k][1] for q in pending)):
                had = bool(deferred)
                att_pv(*items[pending.pop(0)])
                if had:
                    att_fin2()
            if items[k][0] != last_unit[0]:
                att_prep(items[k][0])
                last_unit[0] = items[k][0]
            att_qk(*items[k])
            pending.append(k)
            if k == 40:
                e1_loads()
                e2_chunk_loads(0)
            if k == len(items) - 12:
                e1_prefetch_x(0)
        while pending:
            att_pv(*items[pending.pop(0)])
        att_fin2()

        if debug:
            dump([("d_at", attnT.rearrange("p a b -> p (a b)"), 4 * TOWN)])
            if upto == 2:
                finish()
                return nc

        P.barrier()
        A.release(m_l1)
        gpo = A.alloc([D], F32)
        gpf = A.alloc([D], F32)
        gpff = A.alloc([D], F32)
        x1 = A.alloc([8, D], F32)
        h2T = A.alloc([8, 1024], BF16)
        DMA("sp", "c4", gpo, gpo_d, [], ["gpo"])
        DMA("sp", "c5", gpf, gpf_d, [], ["gpf"])
        DMA("sp", "c6", gpff, gpff_d, [], ["gpff"])
        m_l2b = A.mark()

        for half in range(2):
            def hk(nm, half=half):
                return (nm, half)
            wb = 0 if half == 0 else 2
            wo = 2 - wb

            A.release(m_l2b)
            hTo = A.alloc([8, 1024], BF16)
            sguT = A.alloc([4, 1024], BF16)
            mT = A.alloc([8, 1024], BF16)
            lst = [A.alloc([8], F32) for _ in range(3)]
            del xt[2:]
            xt.append(A.alloc([D], F32))
            wkb = [A.alloc([512], F32) for _ in range(12)]

            def WK(i):
                return hk(("wk", i))
            t1 = [[wkb[0], wkb[1]], [wkb[2], wkb[3]]]
            t1k = [[WK(0), WK(1)], [WK(2), WK(3)]]
            gl = [[wkb[4], wkb[5]], [wkb[6], wkb[7]]]
            glk = [[WK(4), WK(5)], [WK(6), WK(7)]]
            vn = [wkb[8].bitcast(BF16)[:, 0:512], wkb[9].bitcast(BF16)[:, 0:512]]
            vnk = [WK(8), WK(9)]
            sgu = [wkb[10].bitcast(BF16)[:, 0:512], wkb[11].bitcast(BF16)[:, 0:512]]
            sguk = [WK(10), WK(11)]
            tg2 = [[wkb[0], wkb[1]], [wkb[2], wkb[3]]]
            m122 = [[wkb[4], wkb[5]], [wkb[6], wkb[7]]]

            if half == 1:
                e2_chunk_loads(0, wb)
            e_x = {}
            SQC = 0.21145921592026346

            def e1_s0a_act(sl):
                gs = half * 8 + sl
                if (half, sl) in pre_x:
                    b = pre_x[(half, sl)]
                    xctr[0] = b + 1
                else:
                    b = xctr[0] % len(xt)
                    xctr[0] += 1
                    DMA("sp", ("xt", b), xt[b], xown[gs * 128:(gs + 1) * 128, :], [], [("xt", b)])
                e_x[sl] = (nt_a1(xt[b], ("xt", b)), b)

            def e1_s0a_dve(sl):
                xi, b = e_x[sl]
                nt_a2(xi, xt[b], ("xt", b), gpm, "gpm")

            def e1_s0b(sl):
                nt_b(e_x[sl][0], hTo[:, :, sl * 128:(sl + 1) * 128], hk(("hTo", sl)), on_act=True)

            def e1_s1_pe(sl):
                par = sl % 2
                for which in range(2):
                    ps, pk = psF[2 * par + which], PSK(2 * par + which)
                    for c in range(8):
                        MM(ps[:, :], hTo[:, c, sl * 128:(sl + 1) * 128], stg[wb + which][:, c, :], c == 0, c == 7,
                           [hk(("hTo", sl)), ("stg", wb + which)], [pk])
                for which in range(2):
                    ps, pk = psF[2 * par + which], PSK(2 * par + which)
                    tt_, tk = t1[par][which], t1k[par][which]
                    ACT(tt_, ps[:, :], AF.Square, [pk], [tk], scale=SQC)

            def e1_s1_inner(sl):
                par = sl % 2
                for which in range(2):
                    ps, pk = psF[2 * par + which], PSK(2 * par + which)
                    tt_, tk = t1[par][which], t1k[par][which]
                    STT(tt_, tt_, 1.0, ps[:, :], ALU.add, ALU.mult, [tk, pk], [tk])

            def e1_s1_exp(sl):
                par = sl % 2
                for which in range(2):
                    tt_, tk = t1[par][which], t1k[par][which]
                    ACT(tt_, tt_, AF.Exp, [tk], [tk], scale=-2.0 * GELU_C)
                for which in range(2):
                    tt_, tk = t1[par][which], t1k[par][which]
                    ACT(tt_, tt_, AF.Ln, [tk], [tk], bias=1.0)
                for which in range(2):
                    tt_, tk = t1[par][which], t1k[par][which]
                    ACT(tt_, tt_, AF.Exp, [tk], [tk], scale=-1.0)

            def e1_s1_gl(sl):
                par = sl % 2
                l_, lk = lst[par], hk(("lst", par))
                for which in range(2):
                    ps, pk = psF[2 * par + which], PSK(2 * par + which)
                    tt_, tk = t1[par][which], t1k[par][which]
                    if which == 0:
                        STT(gl[par][which], tt_, 2.0, ps[:, :], ALU.mult, ALU.mult, [tk, pk], [glk[par][which]])
                    else:
                        P.op("dve", lambda e, o_=gl[par][1], t_=tt_, p_=ps[:, :], a_=l_[:, 0:1]: e.scalar_tensor_tensor(
                            out=o_, in0=t_, scalar=2.0, in1=p_, op0=ALU.mult, op1=ALU.mult, accum_out=a_),
                             [tk, pk], [glk[par][1], lk])

            def e1_s2(sl):
                par = sl % 2
                l_, lk = lst[par], hk(("lst", par))
                v_, vk = gl[par][1], glk[par][1]
                j_, jk = t1[par][1], t1k[par][1]
                ACT(j_, v_, AF.Square, [vk], [jk, lk], accum_out=l_[:, 1:2])
                TS("dve", l_[:, 2:3], l_[:, 0:1], 1.0 / 512, None, ALU.mult, None, [lk], [lk])
                TT("dve", l_[:, 3:4], l_[:, 2:3], l_[:, 2:3], ALU.mult, [lk], [lk])
                STT(l_[:, 4:5], l_[:, 1:2], 1.0 / 512, l_[:, 3:4], ALU.mult, ALU.subtract, [lk], [lk])
                ACT(l_[:, 5:6], l_[:, 4:5], AF.Ln, [lk], [lk], bias=4.0 * EPS)
                ACT(l_[:, 6:7], l_[:, 5:6], AF.Exp, [lk], [lk], scale=-0.5)
                TS("dve", v_, v_, l_[:, 2:3], l_[:, 6:7], ALU.subtract, ALU.mult, [vk, lk], [vk])
                TT("dve", v_, v_, gsg, ALU.mult, [vk, "gsg"], [vk])
                TT("dve", vn[par], v_, bsg, ALU.add, [vk, "bsg"], [vnk[par]])

            def e1_s3(sl):
                par = sl % 2
                pm, pmk = psF[4 + par], PSK(4 + par)
                for g in range(8):
                    MM(pm[:, g * 64:(g + 1) * 64], wsT[:, g, :], vn[par][:, g * 64:(g + 1) * 64], True, True,
                       ["wsT", vnk[par]], [pmk])
                s1_, s1k = wkb[8 + par], vnk[par]
                TT("dve", s1_.rearrange("p (g c) -> p g c", g=8), pm.rearrange("p (g c) -> p g c", g=8),
                   bsp.unsqueeze(2).to_broadcast([128, 8, 64]), ALU.add, [pmk, "bsp"], [s1k])
                TT("dve", sgu[par], s1_, gl[par][0], ALU.mult, [s1k, glk[par][0]], [sguk[par]])

            def e1_s4(sl):
                par = sl % 2
                pb = psb_ctr[0] % 2
                psb_ctr[0] += 1
                for c in range(4):
                    TR(psB[pb][:, c * 128:(c + 1) * 128], sgu[par][:, c * 128:(c + 1) * 128], [sguk[par]], [PSK(6 + pb)])
                ACT(sguT[:, :, sl * 128:(sl + 1) * 128], psB[pb][:, 0:512].rearrange("p (c t) -> p c t", c=4), AF.Copy,
                    [PSK(6 + pb)], [hk(("sguT", sl // 4))])

            for t in range(8 + 4):
                s1ok = 0 <= t - 1 < 8
                if t < 8:
                    e1_s0a_act(t)
                if s1ok:
                    e1_s1_pe(t - 1)
                if t < 8:
                    e1_s0a_dve(t)
                if s1ok:
                    e1_s1_inner(t - 1)
                if t < 8:
                    e1_s0b(t)
                if s1ok:
                    e1_s1_exp(t - 1)
                if 0 <= t - 4 < 8:
                    e1_s4(t - 4)
                if 0 <= t - 3 < 8:
                    e1_s3(t - 3)
                if 0 <= t - 2 < 8:
                    e1_s2(t - 2)
                if s1ok:
                    e1_s1_gl(t - 1)

            for sl in range(8):
                gs = half * 8 + sl
                DMA("sp", ("x1ld", sl), x1[:, sl, :], xown[gs * 128:(gs + 1) * 128, :], [], [("x1", half, sl)])

            units2 = [(ch, tt, o) for ch in range(4) for tt in range(2) for o in range(2)]

            def e2_s1(ui):
                ch, tt, o = units2[ui]
                if tt == 0 and o == 0 and ch + 1 < 4:
                    e2_chunk_loads(ch + 1, wb)
                if ch == 3 and tt == 0 and o == 0:
                    load_w(stg[wo], w_out_v[:, :, 0:512], ("stg", wo))
                    load_w(stg[wo + 1], w_out_v[:, :, 512:1024], ("stg", wo + 1))
                b0 = e2_b0(ch, wb)
                s0, s1_ = b0 // 2, b0 // 2 + 1
                k0a, k0b, k1 = ("sa", s0), ("sb", s0), ("sc", s0)
                k2 = [("sa", s1_), ("sb", s1_)]
                pset = 4 * (ui % 2)
                tcols = slice(tt * 512, (tt + 1) * 512)
                slh = slice(half * 1024 + tt * 512, half * 1024 + (tt + 1) * 512)
                akeys = [("attnT", s_) for s_ in range(half * 8 + tt * 4, half * 8 + tt * 4 + 4)]
                hkeys = [hk(("hTo", s_)) for s_ in range(tt * 4, tt * 4 + 4)]
                ocols = slice(o * 128, (o + 1) * 128)
                bank = lambda i: (psF[pset + i] if pset + i < 6 else psBf[pset + i - 6])
                for kc in range(4):
                    MM(bank(0)[:, :], hs[b0][:, kc, ocols], attnT[:, kc, slh], kc == 0, kc == 3, [k0a] + akeys, [PSK(pset)])
                for c in range(8):
                    MM(bank(1)[:, :], hs[b0 + 1][:, c, ocols], hTo[:, c, tcols], c == 0, c == 7, [k1] + hkeys, [PSK(pset + 1)])
                for kc in range(4):
                    MM(bank(2)[:, :], hs[b0][:, 4 + kc, ocols], sguT[:, kc, tcols], kc == 0, kc == 3,
                       [k0b, hk(("sguT", tt))], [PSK(pset + 2)])
                for c in range(8):
                    MM(bank(3)[:, :], hs[b0 + 2][:, c, ocols], hTo[:, c, tcols], c == 0, c == 7, k2 + hkeys, [PSK(pset + 3)])

            def e2_s2(ui):
                ch, tt, o = units2[ui]
                par = ui % 2
                pset = 4 * par
                oc = ch * 2 + o
                tcols = slice(tt * 512, (tt + 1) * 512)
                bank = lambda i: (psF[pset + i] if pset + i < 6 else psBf[pset + i - 6])
                ta, tak = tg2[par][0], WK(2 * par)
                tb_, tbk = tg2[par][1], WK(2 * par + 1)
                m1, m1k = m122[par][0], WK(4 + 2 * par)
                m2, m2k = m122[par][1], WK(5 + 2 * par)
                ACT(ta, bank(1)[:, :], AF.Exp, [PSK(pset + 1)], [tak], scale=-1.0)
                ACT(tb_, bank(3)[:, :], AF.Exp, [PSK(pset + 3)], [tbk], scale=-1.0)
                ACT(ta, ta, AF.Ln, [tak], [tak], bias=1.0)
                ACT(tb_, tb_, AF.Ln, [tbk], [tbk], bias=1.0)
                ACT(ta, ta, AF.Exp, [tak], [tak], scale=-1.0)
                ACT(tb_, tb_, AF.Exp, [tbk], [tbk], scale=-1.0)
                TT("dve", m1, ta, bank(0)[:, :], ALU.mult, [tak, PSK(pset)], [m1k])
                TT("dve", m2, tb_, bank(2)[:, :], ALU.mult, [tbk, PSK(pset + 2)], [m2k])
                STT(mT[:, oc, tcols], m1, 2.0, m2, ALU.mult, ALU.add, [m1k, m2k], [hk(("mT", tt))])

            for t in range(len(units2) + 1):
                if t >= 1:
                    e2_s2(t - 1)
                if t < len(units2):
                    e2_s1(t)

            e3_xi = {}

            def zt_view(k):
                return (wkb[2 * k], wkb[2 * k + 1]), (WK(2 * k), WK(2 * k + 1))

            def e3_s1(sl):
                k = sl % 3
                tt = sl // 4
                (z0, z1), (zk0, zk1) = zt_view(k)
                l_, lk = lst[k], hk(("lst", k))
                for hf in range(2):
                    for c in range(8):
                        MM(psF[2 * k + hf][:, :], mT[:, c, sl * 128:(sl + 1) * 128], stg[wo + hf][:, c, :], c == 0, c == 7,
                           [hk(("mT", tt)), ("stg", wo + hf)], [PSK(2 * k + hf)])
                ACT(z0, psF[2 * k][:, :], AF.Square, [PSK(2 * k)], [zk0, lk], accum_out=l_[:, 0:1])
                ACT(z1, psF[2 * k + 1][:, :], AF.Square, [PSK(2 * k + 1)], [zk1, lk], accum_out=l_[:, 1:2])

            def e3_s2(sl):
                k = sl % 3
                l_, lk = lst[k], hk(("lst", k))
                TT("dve", l_[:, 2:3], l_[:, 0:1], l_[:, 1:2], ALU.add, [lk], [lk])
                ACT(l_[:, 3:4], l_[:, 2:3], AF.Ln, [lk], [lk], scale=1.0 / D, bias=4.0 * EPS)
                ACT(l_[:, 4:5], l_[:, 3:4], AF.Exp, [lk], [lk], scale=-0.5)

            def e3_s3(sl):
                k = sl % 3
                (z0, z1), (zk0, zk1) = zt_view(k)
                l_, lk = lst[k], hk(("lst", k))
                xk_ = ("x1", half, sl)
                STT(z0, psF[2 * k][:, :], l_[:, 4:5], gpo[:, 0:512], ALU.mult, ALU.mult, [PSK(2 * k), lk, "gpo"], [zk0])
                STT(z1, psF[2 * k + 1][:, :], l_[:, 4:5], gpo[:, 512:1024], ALU.mult, ALU.mult,
                    [PSK(2 * k + 1), lk, "gpo"], [zk1])
                TT("dve", x1[:, sl, 0:512], z0, x1[:, sl, 0:512], ALU.add, [zk0, xk_], [xk_])
                TT("dve", x1[:, sl, 512:1024], z1, x1[:, sl, 512:1024], ALU.add, [zk1, xk_], [xk_])

            def e3_s4(sl):
                e3_xi[sl] = nt_a1(x1[:, sl, :], ("x1", half, sl))

            def e3_s5(sl):
                nt_a2(e3_xi[sl], x1[:, sl, :], ("x1", half, sl), gpf, "gpf")

            def e3_s6(sl):
                nt_b(e3_xi[sl], h2T[:, :, sl * 128:(sl + 1) * 128], ("h2T", half, sl // 4), on_act=True)

            e3_stages = [e3_s1, e3_s2, e3_s3, e3_s4, e3_s5, e3_s6]
            for t in range(8 + 5):
                for si_ in range(5, -1, -1):
                    sl_ = t - si_
                    if 0 <= sl_ < 8:
                        e3_stages[si_](sl_)
                if t == 6:
                    f_loads(0, wb)

            if debug and upto == 4:
                dump([("d_x1", x1.rearrange("p a b -> p (a b)"), 8 * 1024)])
                finish()
                return nc
            P.barrier()
            A.release(m_l2b)
            del xt[2:]
            actT = A.alloc([NFC, 1024], BF16)
            ffA = A.alloc([8, 512], F32)
            fw = A.alloc([4, 512], F32)
            tgf = [fw[:, 0, :], fw[:, 2, :]]
            a1 = [fw[:, 1, :], fw[:, 3, :]]
            ot = [fw[:, 0:2, :].rearrange("p a b -> p (a b)"), fw[:, 2:4, :].rearrange("p a b -> p (a b)")]
            fst = A.alloc([8, 4], F32)
            fctr = 0
            for fq in range(6):
                nfc = 4 if fq < 5 else 2
                sg_, su_ = (2 * fq + wb) % NSTG, (2 * fq + 1 + wb) % NSTG
                if fq >= 1:
                    f_loads(fq, wb)
                for f in range(nfc):
                    fc = fq * 4 + f
                    fcols = slice(f * 128, (f + 1) * 128)
                    for tt in range(2):
                        tcols = slice(tt * 512, (tt + 1) * 512)
                        pi = fctr % 2
                        fctr += 1
                        pg, pu = psF[pi], psF[2 + pi]
                        for c in range(8):
                            MM(pg[:, :], stg[sg_][:, c, fcols], h2T[:, c, tcols], c == 0, c == 7,
                               [("stg", sg_), ("h2T", half, tt)], [PSK(pi)])
                        for c in range(8):
                            MM(pu[:, :], stg[su_][:, c, fcols], h2T[:, c, tcols], c == 0, c == 7,
                               [("stg", su_), ("h2T", half, tt)], [PSK(2 + pi)])
                        ACT(tgf[pi], pg[:, :], AF.Exp, [PSK(pi)], [hk(("fw", 2 * pi))], scale=-1.0)
                        ACT(tgf[pi], tgf[pi], AF.Ln, [hk(("fw", 2 * pi))], [hk(("fw", 2 * pi))], bias=1.0)
                        ACT(tgf[pi], tgf[pi], AF.Exp, [hk(("fw", 2 * pi))], [hk(("fw", 2 * pi))], scale=-1.0)
                        TT("dve", a1[pi], tgf[pi], pg[:, :], ALU.mult, [hk(("fw", 2 * pi)), PSK(pi)], [hk(("fw", 2 * pi + 1))])
                        TT("dve", actT[:, fc, tcols], a1[pi], pu[:, :], ALU.mult, [hk(("fw", 2 * pi + 1)), PSK(2 + pi)], [hk(("actT", tt))])

            if debug and upto == 5:
                finish()
                return nc
            gjunk = A.alloc([512], BF16)

            def bank_of(sl):
                return psF[sl] if sl < 6 else psBf[sl - 6]

            def g_keys(sl):
                ob = sl % 2
                return hk(("fw", 2 * ob)), hk(("fw", 2 * ob + 1)), hk(("fst", sl))

            def g_ev1(r, sl):
                pbank = bank_of(sl)
                ok, ok2, fk = g_keys(sl)
                ACT(gjunk, pbank[:, :], AF.Square, [PSK(sl)], [hk("gjunk"), fk], accum_out=fst[:, sl, r:r + 1])
                if r == 0:
                    TT("dve", ffA[:, sl, :], pbank[:, :], gpff[:, 0:512], ALU.mult, [PSK(sl), "gpff"], [hk(("ffA", sl))])

            def g_ev2(sl):
                ok, ok2, fk = g_keys(sl)
                TT("dve", fst[:, sl, 2:3], fst[:, sl, 0:1], fst[:, sl, 1:2], ALU.add, [fk], [fk])
                ACT(fst[:, sl, 3:4], fst[:, sl, 2:3], AF.Ln, [fk], [fk], scale=1.0 / D, bias=EPS)
                ACT(fst[:, sl, 3:4], fst[:, sl, 3:4], AF.Exp, [fk], [fk], scale=-0.5)

            def g_ev3(sl):
                gs = half * 8 + sl
                pbank = bank_of(sl)
                ob = sl % 2
                ok, ok2, fk = g_keys(sl)
                STT(ot[ob][:, 0:512], ffA[:, sl, :], fst[:, sl, 3:4], x1[:, sl, 0:512], ALU.mult, ALU.add,
                    [hk(("ffA", sl)), fk, ("x1", half, sl)], [ok, ok2])
                TT("dve", ot[ob][:, 512:1024], pbank[:, :], gpff[:, 512:1024], ALU.mult, [PSK(sl), "gpff"], [ok, ok2])
                STT(ot[ob][:, 512:1024], ot[ob][:, 512:1024], fst[:, sl, 3:4], x1[:, sl, 512:1024], ALU.mult, ALU.add,
                    [ok, ok2, fk, ("x1", half, sl)], [ok, ok2])
                DMA("sp", ("out", ob), out_d[gs * 128:(gs + 1) * 128, :], ot[ob], [ok, ok2], [("outd", gs)])

            for r in range(2):
                sis = []
                for k3 in range(3):
                    n8 = 8 if k3 < 2 else 6
                    si = (k3 + r * 3 + wb) % NSTG
                    sis.append((si, n8))
                    load_w(stg[si][:, 0:n8, :], w_ffd_v[:, k3 * 8:k3 * 8 + n8, r * 512:(r + 1) * 512], ("stg", si))
                si, n8 = sis[0]
                for f in range(n8):
                    fc = f
                    for sl in range(8):
                        pbank = psF[sl] if sl < 6 else psBf[sl - 6]
                        MM(pbank[:, :], actT[:, fc, sl * 128:(sl + 1) * 128], stg[si][:, f, :], fc == 0, fc == NFC - 1,
                           [hk(("actT", sl // 4)), ("stg", si)], [PSK(sl)])
                if r == 1 and half == 0:
                    e1_loads(2)
                    e1_prefetch_x(1)
                for sl in range(8):
                    pbank = psF[sl] if sl < 6 else psBf[sl - 6]
                    for k3 in (1, 2):
                        si, n8 = sis[k3]
                        for f in range(n8):
                            fc = k3 * 8 + f
                            MM(pbank[:, :], actT[:, fc, sl * 128:(sl + 1) * 128], stg[si][:, f, :], fc == 0, fc == NFC - 1,
                               [hk(("actT", sl // 4)), ("stg", si)], [PSK(sl)])
                    g_ev1(r, sl)
                    if r == 1:
                        if sl >= 1:
                            g_ev2(sl - 1)
                        if sl >= 2:
                            g_ev3(sl - 2)
                if r == 1:
                    g_ev2(7)
                    g_ev3(6)
                    g_ev3(7)
            P.barrier()
            if debug and upto == 6:
                finish()
                return nc

        finish()
    return nc


_NC_CACHE = {}


def _layout_inputs(inp):
    f = lambda a: np.ascontiguousarray(np.asarray(a, dtype=np.float32))
    x = f(inp["x"])
    rep = lambda v, n=128: np.ascontiguousarray(np.broadcast_to(f(v).reshape(1, -1), (n, f(v).size)))
    common = {
        "w_in": f(inp["w_in"][0]), "w_a": f(inp["w_branch_a"][0]), "w_b": f(inp["w_branch_b"][0]),
        "w_out": f(inp["w_out"][0]), "w_ffi": f(inp["w_ffn_in"][0]), "w_ffd": f(inp["w_ffn_down"][0]),
        "gpm": rep(inp["g_pre_mix"][0]), "gpo": rep(inp["g_post_mix"][0]), "gpf": rep(inp["g_pre_ffn"][0]),
        "gpff": rep(inp["g_post_ffn"][0]), "gsg": rep(inp["g_sgu"][0]), "bsg": rep(inp["b_sgu"][0]),
        "bfb": rep(inp["b_forget"][0]),
        "gqc": np.ascontiguousarray(np.tile(f(inp["g_q"][0]).reshape(64, 1), (2, 1))),
        "gkc": np.ascontiguousarray(np.tile(f(inp["g_k"][0]).reshape(64, 1), (2, 1))),
        "wsp": f(inp["w_spatial"][0]),
        "bsp": np.ascontiguousarray(f(inp["b_spatial"][0]).T),
    }
    ones = np.ones((128, 128), np.float32)
    zeros = np.zeros((128, 128), np.float32)
    tri = np.triu(np.ones((128, 128), np.float32))
    in_maps = []
    for c in range(8):
        b, par = c // 2, c % 2
        blocks = []
        mk = np.zeros((128, NS, 2, 128), np.float32)
        for s in range(NS):
            m = s // 2
            if par == 0:
                i = 4 * m if s % 2 == 0 else 4 * m + 3
            else:
                i = 4 * m + 1 if s % 2 == 0 else 4 * m + 2
            blocks.append(i)
            J = J_of(s)
            for k in range(2):
                j = J - 1 + k
                mk[:, s, k, :] = ones if j < i else (tri if j == i else zeros)
        xo = np.concatenate([x[b, i * 128:(i + 1) * 128] for i in blocks], axis=0)
        d = dict(common)
        d["xseq"] = np.ascontiguousarray(x[b])
        d["xown"] = np.ascontiguousarray(xo)
        d["msk"] = np.ascontiguousarray(mk.reshape(128, NS * 256))
        in_maps.append((d, blocks))
    return in_maps


def kernel(**inputs):
    maps = _layout_inputs(inputs)
    if "nc" not in _NC_CACHE:
        _NC_CACHE["nc"] = build_nc()
    nc = _NC_CACHE["nc"]
    res = run_bass_kernel_spmd(nc, [m[0] for m in maps], core_ids=list(range(8)))
    out = np.empty((4, SEQ, D), np.float32)
    for c in range(8):
        b = c // 2
        o = res.results[c]["out"]
        for s, i in enumerate(maps[c][1]):
            out[b, i * 128:(i + 1) * 128] = o[s * 128:(s + 1) * 128]
    return out
```

```python
import numpy as np
from contextlib import ExitStack
import concourse.bass as bass
import concourse.mybir as mybir
from concourse.bass_utils import run_bass_kernel_spmd

F32 = mybir.dt.float32
BF16 = mybir.dt.bfloat16
AF = mybir.ActivationFunctionType
ALU = mybir.AluOpType
AX = mybir.AxisListType

D = 1024
SEQ = 4096
NB = 32
NS = 16
TOWN = 2048
HEADS = 8
DH = 64
Q_OFF, K_OFF, V_OFF, F_OFF, U_OFF, G_OFF = 0, 512, 1024, 1536, 1544, 2568
IN_COLS = 4616
DFF = 2816
NFC = 22
EPS = 1e-6
GELU_C = 0.7978845608028654


def J_of(slot):
    return 4 * (slot // 2) + (1 if slot % 2 == 0 else 3)


ENGS = ["pe", "act", "dve", "pool", "sp"]


class Prog:
    def __init__(self, nc):
        self.nc = nc
        self.recs = {e: [] for e in ENGS}
        self.lastw = {}
        self.readers = {}
        self.dma_count = {}

    def _deps(self, eng, reads, writes, is_dma):
        deps = set()
        for k in reads:
            t = self.lastw.get(k)
            if t is not None:
                deps.add(t)
            if isinstance(k, tuple) and k[0] == "ps":
                for t2 in self.readers.get(k, {}).values():
                    if not (t2[0] == "e" and t2[1] == eng):
                        deps.add(t2)
        strict = is_dma or eng != "pe"
        for k in writes:
            t = self.lastw.get(k)
            if t is not None and (strict or not (t[0] == "e" and t[1] == eng)):
                deps.add(t)
            for t2 in self.readers.get(k, {}).values():
                if strict or not (t2[0] == "e" and t2[1] == eng):
                    deps.add(t2)
        return deps

    def _register(self, tok, rk, reads, writes):
        for k in reads:
            self.readers.setdefault(k, {})[rk] = tok
        for k in writes:
            self.lastw[k] = tok
            self.readers[k] = {}

    @staticmethod
    def _expand(keys):
        out = []
        for k in keys:
            out.append(k)
            if isinstance(k, tuple) and len(k) == 2 and k[0] == "stg":
                out += [("sa", k[1]), ("sb", k[1]), ("sc", k[1])]
        return out

    def op(self, eng, fn, reads=(), writes=()):
        reads, writes = self._expand(reads), self._expand(writes)
        idx = len(self.recs[eng])
        deps = self._deps(eng, reads, writes, False)
        tok = ("e", eng, idx)
        self.recs[eng].append(dict(fn=fn, deps=deps, dma=None))
        self._register(tok, eng, reads, writes)

    def dma(self, eng, semkey, fn, reads=(), writes=()):
        reads, writes = self._expand(reads), self._expand(writes)
        n = self.dma_count.get(semkey, 0) + 1
        self.dma_count[semkey] = n
        tok = ("d", semkey, n)
        deps = self._deps(eng, reads, writes, True)
        self.recs[eng].append(dict(fn=fn, deps=deps, dma=semkey))
        self._register(tok, ("d", semkey), reads, writes)

    def barrier(self):
        toks = set()
        for e in ENGS:
            if self.recs[e]:
                for i in range(len(self.recs[e]) - 1, -1, -1):
                    r = self.recs[e][i]
                    if r["fn"] is not None and r["dma"] is None:
                        toks.add(("e", e, i))
                        break
        for k, n in self.dma_count.items():
            toks.add(("d", k, n))
        for e in ENGS:
            deps = set(t for t in toks if not (t[0] == "e" and t[1] == e))
            self.recs[e].append(dict(fn=None, deps=deps, dma=None))

    def final_wait(self, eng, semkeys):
        deps = set(("d", k, self.dma_count[k]) for k in semkeys if k in self.dma_count)
        self.recs[eng].append(dict(fn=None, deps=deps, dma=None))

    def emit(self):
        nc = self.nc
        sig = {e: [False] * len(self.recs[e]) for e in ENGS}
        for e in ENGS:
            for r in self.recs[e]:
                for t in r["deps"]:
                    if t[0] == "e":
                        sig[t[1]][t[2]] = True
        rank = {}
        for e in ENGS:
            c = 0
            rk = []
            for i in range(len(self.recs[e])):
                if sig[e][i]:
                    c += 1
                rk.append(c)
            rank[e] = rk
        with ExitStack() as st:
            esem = {e: st.enter_context(nc.semaphore("s_" + e)) for e in ENGS}
            dsem = {}
            for i, k in enumerate(sorted(self.dma_count.keys(), key=str)):
                dsem[k] = st.enter_context(nc.semaphore("d%d" % i))
            block = st.enter_context(nc.Block())
            bname = {"pe": "tensor", "act": "scalar", "dve": "vector", "pool": "gpsimd", "sp": "sync"}
            for e in ENGS:
                def body(engine, e=e):
                    waited = {}
                    for i, r in enumerate(self.recs[e]):
                        for t in sorted(r["deps"], key=str):
                            if t[0] == "e":
                                key = ("e", t[1]); val = rank[t[1]][t[2]]; sem = esem[t[1]]
                            else:
                                key = ("d", t[1]); val = 16 * t[2]; sem = dsem[t[1]]
                            if waited.get(key, 0) >= val:
                                continue
                            engine.wait_ge(sem, val)
                            waited[key] = val
                        if r["fn"] is None:
                            continue
                        ins = r["fn"](engine)
                        if r["dma"] is not None:
                            ins.then_inc(dsem[r["dma"]], 16)
                        elif sig[e][i]:
                            ins.then_inc(esem[e], 1)
                getattr(block, bname[e])(body)


class Arena:
    def __init__(self, sb, total):
        self.sb = sb
        self.total = total
        self.off = 0
        self.peak = 0

    def alloc(self, free_shape, dt):
        n = 1
        for s in free_shape:
            n *= s
        esz = 4 if dt == F32 else 2
        nbytes = (n * esz + 63) // 64 * 64
        o = self.off
        self.off += nbytes
        self.peak = max(self.peak, self.off)
        assert self.off <= self.total, ("SBUF arena overflow", self.off, self.total)
        v = self.sb[:, o // 2:o // 2 + n * esz // 2]
        if dt == F32:
            v = v.bitcast(F32)
        if len(free_shape) == 2:
            v = v.rearrange("p (a b) -> p a b", a=free_shape[0])
        elif len(free_shape) == 3:
            v = v.rearrange("p (a b c) -> p a b c", a=free_shape[0], b=free_shape[1])
        return v

    def mark(self):
        return self.off

    def release(self, m):
        self.off = m


def build_nc(debug=False, upto=3):
    nc = bass.Bass("TRN2", target_bir_lowering=False)

    def din(name, shape):
        return nc.dram_tensor(name, list(shape), F32, kind="ExternalInput").ap()

    xseq = din("xseq", [SEQ, D])
    xown = din("xown", [TOWN, D])
    w_in = din("w_in", [D, IN_COLS])
    w_a = din("w_a", [512, D])
    w_b = din("w_b", [512, D])
    w_out = din("w_out", [D, D])
    w_ffi = din("w_ffi", [D, 2 * DFF])
    w_ffd = din("w_ffd", [DFF, D])
    gpm_d = din("gpm", [128, D])
    gpo_d = din("gpo", [128, D])
    gpf_d = din("gpf", [128, D])
    gpff_d = din("gpff", [128, D])
    gsg_d = din("gsg", [128, 512])
    bsg_d = din("bsg", [128, 512])
    bfb_d = din("bfb", [128, 8])
    gqc_d = din("gqc", [128, 1])
    gkc_d = din("gkc", [128, 1])
    wsp_d = din("wsp", [8, 128, 128])
    bsp_d = din("bsp", [128, 8])
    msk_d = din("msk", [128, NS * 256])
    out_d = nc.dram_tensor("out", [TOWN, D], F32, kind="ExternalOutput").ap()
    dbg = {}
    if debug:
        for nm, shp in [("d_kt", [128, 4 * SEQ]), ("d_qt", [128, 4 * TOWN]), ("d_v", [128, NB * 8 * 65]),
                        ("d_c", [128, 256]), ("d_at", [128, 4 * TOWN]), ("d_x1", [128, 8 * 1024])]:
            dbg[nm] = nc.dram_tensor(nm, shp, F32, kind="ExternalOutput").ap()

    w_in_v = w_in.rearrange("(c p) n -> p c n", p=128)
    w_a_v = w_a.rearrange("(c p) n -> p c n", p=128)
    w_b_v = w_b.rearrange("(c p) n -> p c n", p=128)
    w_out_v = w_out.rearrange("(c p) n -> p c n", p=128)
    w_ffi_v = w_ffi.rearrange("(c p) n -> p c n", p=128)
    w_ffd_v = w_ffd.rearrange("(c p) n -> p c n", p=128)

    TOTAL = 212480
    with ExitStack() as stack:
        sb = stack.enter_context(nc.sbuf_tensor("sb", [128, TOTAL // 2], BF16))
        psF = [stack.enter_context(nc.psum_tensor("psf%d" % i, [128, 512], F32)) for i in range(6)]
        psBf = [stack.enter_context(nc.psum_tensor("psb%d" % i, [128, 512], F32)) for i in range(2)]
        psB = [t[:, :].bitcast(BF16) for t in psBf]
        A = Arena(sb, TOTAL)
        P = Prog(nc)

        def PSK(i):
            return ("ps", i)

        def MM(out, lhsT, rhs, start, stop, reads, writes):
            P.op("pe", lambda e: e.matmul(out, lhsT=lhsT, rhs=rhs, start=start, stop=stop), reads, writes)

        def TR(out, in_, reads, writes):
            P.op("pe", lambda e: e.transpose(out=out, in_=in_, identity=ident), list(reads) + ["ident"], writes)

        def ACT(out, in_, func, reads, writes, **kw):
            P.op("act", lambda e: e.activation(out=out, in_=in_, func=func, **kw), reads, writes)

        def TT(eng, out, in0, in1, op, reads, writes):
            P.op(eng, lambda e: e.tensor_tensor(out=out, in0=in0, in1=in1, op=op), reads, writes)

        def TS(eng, out, in0, s1, s2, op0, op1, reads, writes):
            if s2 is None:
                P.op(eng, lambda e: e.tensor_scalar(out=out, in0=in0, scalar1=s1, scalar2=None, op0=op0), reads, writes)
            else:
                P.op(eng, lambda e: e.tensor_scalar(out=out, in0=in0, scalar1=s1, scalar2=s2, op0=op0, op1=op1), reads, writes)

        def STT(out, in0, scalar, in1, op0, op1, reads, writes):
            P.op("dve", lambda e: e.scalar_tensor_tensor(out=out, in0=in0, scalar=scalar, in1=in1, op0=op0, op1=op1),
                 reads, writes)

        def CP(eng, out, in_, reads, writes):
            P.op(eng, lambda e: e.tensor_copy(out=out, in_=in_), reads, writes)

        def MS(eng, ap, val, reads, writes):
            P.op(eng, lambda e: e.memset(ap, val), reads, writes)

        def DMA(eng, semkey, out, in_, reads, writes):
            P.dma(eng, semkey, lambda e: e.dma_start(out=out, in_=in_), reads, writes)

        def load_w(dst_view, src_view, key):
            DMA("pool", ("w", key), dst_view, src_view, [], [key])

        ident = A.alloc([128], BF16)
        bones = A.alloc([128], BF16)
        tri = A.alloc([128], F32)
        aones = A.alloc([128], F32)
        gqc = A.alloc([1], F32)
        gkc = A.alloc([1], F32)
        bfb = A.alloc([8], F32)
        gpm = A.alloc([D], F32)
        attnT = A.alloc([4, TOWN], BF16)
        NSTG = 4
        stg = [A.alloc([8, 512], BF16) for _ in range(NSTG)]
        xt = [A.alloc([D], F32) for _ in range(2)]
        NXN = 4
        xn = [A.alloc([D], BF16) for _ in range(NXN)]
        stt_ = [A.alloc([8], F32) for _ in range(NXN)]

        MS("pool", ident, 0.0, [], ["ident"])
        P.op("pool", lambda e: e.affine_select(out=ident, in_=ident, pattern=[[-1, 128]], compare_op=ALU.not_equal,
                                               fill=1.0, base=0, channel_multiplier=1),
             reads=["ident"], writes=["ident"])
        MS("pool", bones, 0.0, [], ["bones"])
        MS("pool", bones[0:64, 0:64], 1.0, ["bones"], ["bones"])
        MS("pool", bones[64:128, 64:128], 1.0, ["bones"], ["bones"])
        MS("pool", aones, 1.0, [], ["aones"])
        MS("pool", tri, 1.0, [], ["tri"])
        P.op("pool", lambda e: e.affine_select(out=tri, in_=tri, pattern=[[1, 128]], compare_op=ALU.is_ge,
                                               fill=0.0, base=0, channel_multiplier=-1),
             reads=["tri"], writes=["tri"])
        DMA("sp", "c0", gqc, gqc_d, [], ["gqc"])
        DMA("sp", "c1", gkc, gkc_d, [], ["gkc"])
        DMA("sp", "c2", bfb, bfb_d, [], ["bfb"])
        DMA("sp", "c3", gpm, gpm_d, [], ["gpm"])

        gsg = A.alloc([512], F32)
        bsg = A.alloc([512], F32)
        wsT = A.alloc([8, 128], BF16)
        bsp = A.alloc([8], F32)
        wtmp8, wtmpb8 = [], []

        def prep_mix_consts_a():
            DMA("sp", "c7", gsg, gsg_d, [], ["gsg"])
            DMA("sp", "c8", bsg, bsg_d, [], ["bsg"])
            DMA("sp", "c9", bsp, bsp_d, [], ["bsp"])
            for g in range(8):
                DMA("pool", ("c10", g), wtmpb8[g], wsp_d[g], [], [("wtmpb", g)])
                MS("dve", wtmpb8[g][0:64, 64:128], 0.0, [("wtmpb", g)], [("wtmpb", g)])

        def prep_mix_consts_b():
            for g in range(8):
                w_ = g % 2
                TR(psB[w_][:, 0:128], wtmpb8[g], [("wtmpb", g)], [PSK(6 + w_)])
                CP("dve", wsT[:, g, :], psB[w_][:, 0:128], [PSK(6 + w_)], ["wsT"])

        xctr = [0]
        psb_ctr = [0]

        xnc = [0]

        def nt_a(src_rows, gbc, gkey, eps_scale=1.0, keep=None):
            if keep is None:
                b = xctr[0] % len(xt)
                xctr[0] += 1
                x_t, xk = xt[b], ("xt", b)
                DMA("sp", ("xt", b), x_t, src_rows, [], [xk])
            else:
                x_t, xk = keep
            xi = nt_a1(x_t, xk, eps_scale)
            nt_a2(xi, x_t, xk, gbc, gkey)
            return xi

        def nt_a1(x_t, xk, eps_scale=1.0):
            xi = xnc[0] % NXN
            xnc[0] += 1
            x_n, nk = xn[xi], ("xn", xi)
            s_t, sk = stt_[xi], ("st", xi)
            ACT(x_n, x_t, AF.Square, [xk], [nk, sk], accum_out=s_t[:, 0:1])
            ACT(s_t[:, 1:2], s_t[:, 0:1], AF.Ln, [sk], [sk], scale=1.0 / D, bias=EPS * eps_scale)
            ACT(s_t[:, 2:3], s_t[:, 1:2], AF.Exp, [sk], [sk], scale=-0.5)
            return xi

        def nt_a2(xi, x_t, xk, gbc, gkey):
            x_n, nk = xn[xi], ("xn", xi)
            s_t, sk = stt_[xi], ("st", xi)
            STT(x_n, x_t, s_t[:, 2:3], gbc, ALU.mult, ALU.mult, [xk, sk, gkey], [nk])

        def nt_b(xi, dst_view, dst_key, on_act=False):
            x_n, nk = xn[xi], ("xn", xi)
            pb = psb_ctr[0] % 2
            psb_ctr[0] += 1
            pst = psB[pb]
            for c in range(8):
                TR(pst[:, c * 128:(c + 1) * 128], x_n[:, c * 128:(c + 1) * 128], [nk], [PSK(6 + pb)])
            if on_act:
                ACT(dst_view, pst.rearrange("p (c t) -> p c t", c=8), AF.Copy, [PSK(6 + pb)], [dst_key])
            else:
                CP("dve", dst_view, pst.rearrange("p (c t) -> p c t", c=8), [PSK(6 + pb)], [dst_key])

        m_l1 = A.mark()
        KT = A.alloc([4, SEQ], BF16)
        Vaug = A.alloc([NB, 8, 65], BF16)
        QT = A.alloc([4, TOWN], BF16)
        zf = A.alloc([NB, 8], F32)
        Cc = A.alloc([NB, 8], F32)
        Eb = A.alloc([NB + 1, 8], F32)
        msk = A.alloc([NS, 256], BF16)
        m_l2 = A.mark()
        hTg = [A.alloc([8, 512], BF16) for _ in range(2)]
        sq = [A.alloc([512], BF16) for _ in range(3)]
        rs = [A.alloc([512], F32) for _ in range(3)]
        wv2 = A.alloc([8, 264], BF16)
        xt.append(A.alloc([D], F32))
        xt.append(A.alloc([D], F32))
        for _g in range(8):
            wtmpb8.append(A.alloc([128], BF16))

        MS("pool", Vaug.rearrange("p a b c -> p (a b) c")[:, :, 64:65], 1.0, [], ["vones"])

        load_w(stg[0], w_in_v[:, :, K_OFF:K_OFF + 512], ("stg", 0))
        load_w(stg[1][:, :, 0:256], w_in_v[:, :, V_OFF:V_OFF + 256], ("stg", 1))
        load_w(wv2, w_in_v[:, :, V_OFF + 256:V_OFF + 520], "wv2")
        def late_loads():
            load_w(stg[2], w_in_v[:, :, Q_OFF:Q_OFF + 512], ("stg", 2))
            load_w(msk.rearrange("p a b -> p (a b)"), msk_d, "msk")

        fm_ctr = [0]

        def proj_fm_1(wview, wkey, p, hT, hkey):
            i = fm_ctr[0] % 3
            fm_ctr[0] += 1
            ps = psF[i]
            for c in range(8):
                MM(ps[:, :], wview[:, c, p * 128:(p + 1) * 128], hT[:, c, :], c == 0, c == 7, [wkey, hkey], [PSK(i)])
            ACT(sq[i], ps[:, :], AF.Square, [PSK(i)], [("sq", i)])
            return i

        def proj_fm_2(i, dst, dkey, gcol, gckey):
            ps, ps2 = psF[i], psF[3]
            MM(ps2[:, :], bones, sq[i], True, True, ["bones", ("sq", i)], [PSK(3)])
            ACT(rs[i], ps2[:, :], AF.Ln, [PSK(3)], [("rs", i)], scale=1.0 / DH, bias=EPS)
            ACT(rs[i], rs[i], AF.Exp, [("rs", i)], [("rs", i)], scale=-0.5)
            STT(dst, ps[:, :], gcol[:, 0:1], rs[i], ALU.mult, ALU.mult, [PSK(i), ("rs", i), gckey], [dkey])

        NG = 12
        xis = {}

        def ab_s1a(g):
            src = xseq if g < 8 else xown
            g0 = g if g < 8 else g - 8
            xis[g] = [nt_a(src[(g0 * 4 + bl) * 128:(g0 * 4 + bl + 1) * 128, :], gpm, "gpm") for bl in range(4)]

        def ab_s1b(g):
            hb = g % 2
            for bl in range(4):
                nt_b(xis[g][bl], hTg[hb][:, :, bl * 128:(bl + 1) * 128], ("hTg", hb))

        def ab_s2(g):
            hb = g % 2
            if g < 8:
                grp = g
                for p in range(4):
                    bl = p
                    ii = proj_fm_1(stg[0], ("stg", 0), p, hTg[hb], ("hTg", hb))
                    if p >= 1:
                        proj_fm_2(prev[0], *prev[1])
                    prev = (ii, (KT[:, p, grp * 512:(grp + 1) * 512], ("KT", p, grp), gkc, "gkc"))
                    blk = grp * 4 + bl
                    for c in range(8):
                        MM(psF[4][:, 0:256], hTg[hb][:, c, bl * 128:(bl + 1) * 128], stg[1][:, c, 0:256], c == 0, c == 7,
                           [("hTg", hb), ("stg", 1)], [PSK(4)])
                    for c in range(8):
                        MM(psF[5][:, 0:264], hTg[hb][:, c, bl * 128:(bl + 1) * 128], wv2[:, c, :], c == 0, c == 7,
                           [("hTg", hb), "wv2"], [PSK(5)])
                    ACT(Vaug[:, blk, 0:4, 0:64], psF[4][:, 0:256].rearrange("p (h d) -> p h d", h=4), AF.Copy,
                        [PSK(4)], [("V", blk)])
                    ACT(Vaug[:, blk, 4:8, 0:64], psF[5][:, 0:256].rearrange("p (h d) -> p h d", h=4), AF.Copy,
                        [PSK(5), ("V", blk)], [("V", blk)])
                    TT("dve", zf[:, blk, :], psF[5][:, 256:264], bfb, ALU.add, [PSK(5), "bfb"], ["zf"])
                return prev
            else:
                grp = g - 8
                for p in range(4):
                    ii = proj_fm_1(stg[2], ("stg", 2), p, hTg[hb], ("hTg", hb))
                    if p >= 1:
                        proj_fm_2(prev[0], *prev[1])
                    prev = (ii, (QT[:, p, grp * 512:(grp + 1) * 512], ("QT", p, grp), gqc, "gqc"))
                return prev

        zf2 = zf.rearrange("p a b -> p (a b)")
        Cc2 = Cc.rearrange("p a b -> p (a b)")

        def phase_c1():
            ACT(zf2, zf2, AF.Exp, ["zf"], ["zf"], scale=-1.0)
            ACT(zf2, zf2, AF.Ln, ["zf"], ["zf"], bias=1.0)

        def phase_c2():
            MM(psF[0][:, 0:256], tri, zf2, True, True, ["tri", "zf"], [PSK(0)])
            MM(psF[1][:, 0:256], aones, zf2, True, True, ["aones", "zf"], [PSK(1)])
            MS("dve", Eb[:, 0, :], 0.0, [], ["Eb"])
            CP("dve", Cc2, psF[1][:, 0:256], [PSK(1)], ["Cc"])
            for j in range(1, NB + 1):
                TT("dve", Eb[:, j, :], Eb[:, j - 1, :], Cc[:, j - 1, :], ALU.add, ["Eb", "Cc"], ["Eb"])
            TT("dve", Cc2, psF[0][:, 0:256], Eb[:, 0:NB, :].rearrange("p a b -> p (a b)"), ALU.add, [PSK(0), "Eb"], ["Cc"])

        for t in range(NG + 1):
            if t == 2:
                late_loads()
            if t == 4:
                prep_mix_consts_a()
            if t == 5:
                prep_mix_consts_b()
            if t < NG:
                ab_s1a(t)
            last = ab_s2(t - 1) if t >= 1 else None
            if t < NG:
                ab_s1b(t)
            if last is not None:
                proj_fm_2(last[0], *last[1])
            if t == 8:
                phase_c1()
            if t == 9:
                phase_c2()

        def dump(items):
            P.barrier()
            mk_ = A.mark()
            dv = xn[0].bitcast(F32)
            for nm, src, n in items:
                for o in range(0, n, 256):
                    m = min(256, n - o)
                    CP("dve", dv[:, 0:m], src[:, o:o + m], ["dbgsrc"], ["dv"])
                    DMA("sp", "dbg", dbg[nm][:, o:o + m], dv[:, 0:m], ["dv"], ["dbgout"])
            P.barrier()
            A.release(mk_)

        def finish():
            P.final_wait("sp", [("out", 0), ("out", 1), "dbg"])
            P.emit()

        if debug:
            dump([("d_kt", KT.rearrange("p a b -> p (a b)"), 4 * SEQ), ("d_qt", QT.rearrange("p a b -> p (a b)"), 4 * TOWN),
                  ("d_v", Vaug.rearrange("p a b c -> p (a b c)"), NB * 8 * 65), ("d_c", Cc.rearrange("p a b -> p (a b)"), 256)])
            if upto == 1:
                finish()
                return nc

        hs = [stg[k // 2][:, :, (k % 2) * 256:(k % 2 + 1) * 256] for k in range(8)]

        def e2_b0(ch, wb):
            even = (ch % 2 == 0)
            if wb == 0:
                return 4 if even else 0
            return 0 if even else 4

        def e2_chunk_loads(ch, wb=0):
            b0 = e2_b0(ch, wb)
            c0 = ch * 256
            s0, s1_ = b0 // 2, b0 // 2 + 1
            load_w(hs[b0][:, 0:4, :], w_a_v[:, :, c0:c0 + 256], ("sa", s0))
            load_w(hs[b0][:, 4:8, :], w_b_v[:, :, c0:c0 + 256], ("sb", s0))
            load_w(hs[b0 + 1], w_in_v[:, :, G_OFF + c0:G_OFF + c0 + 256], ("sc", s0))
            load_w(hs[b0 + 2][:, 0:4, :], w_in_v[:, 0:4, G_OFF + 1024 + c0:G_OFF + 1024 + c0 + 256], ("sa", s1_))
            load_w(hs[b0 + 2][:, 4:8, :], w_in_v[:, 4:8, G_OFF + 1024 + c0:G_OFF + 1024 + c0 + 256], ("sb", s1_))

        pre_x = {}

        def e1_prefetch_x(half_):
            for sl_ in range(2):
                gs_ = half_ * 8 + sl_
                DMA("sp", ("xt", sl_), xt[sl_], xown[gs_ * 128:(gs_ + 1) * 128, :], [], [("xt", sl_)])
                pre_x[(half_, sl_)] = sl_

        def e1_loads(wb=0):
            load_w(stg[wb], w_in_v[:, :, U_OFF:U_OFF + 512], ("stg", wb))
            load_w(stg[wb + 1], w_in_v[:, :, U_OFF + 512:U_OFF + 1024], ("stg", wb + 1))

        def f_loads(fq, foff=0):
            nfc = 4 if fq < 5 else 2
            sg_, su_ = (2 * fq + foff) % NSTG, (2 * fq + 1 + foff) % NSTG
            load_w(stg[sg_][:, :, 0:nfc * 128], w_ffi_v[:, :, fq * 512:fq * 512 + nfc * 128], ("stg", sg_))
            load_w(stg[su_][:, :, 0:nfc * 128], w_ffi_v[:, :, DFF + fq * 512:DFF + fq * 512 + nfc * 128], ("stg", su_))

        P.barrier()
        A.release(m_l2)
        del xt[2:]
        pT = [A.alloc([NB * 128], BF16) for _ in range(2)]
        Vp = [A.alloc([NB, 65], BF16) for _ in range(4)]
        wS = [A.alloc([NB, 8], F32) for _ in range(2)]
        wB = [A.alloc([NB, 8], F32) for _ in range(2)]
        atok = [A.alloc([512], BF16) for _ in range(2)]
        rec = [A.alloc([8], F32) for _ in range(2)]

        units = [(s, p) for s in range(NS) for p in range(4)]
        items = []
        for ui, (s, p) in enumerate(units):
            nb_ = J_of(s) + 1
            for c0 in range(0, nb_, 4):
                items.append((ui, c0))
        cc = [0]

        NSET = 3
        LOOK = 6

        def acc_banks(s):
            return (psBf[0], PSK(6)), (psBf[1], PSK(7))

        def att_prep(ui):
            s, p = units[ui]
            J = J_of(s)
            nb = J + 1
            sb_ = s % 2
            if p == 0:
                TT("dve", wB[sb_][:, 0:nb, :], Cc[:, 0:nb, :], Eb[:, J:J + 1, :].to_broadcast([128, nb, 8]), ALU.subtract,
                   ["Cc", "Eb"], [("wB", sb_)])
                ACT(wS[sb_][:, 0:nb, :], wB[sb_][:, 0:nb, :], AF.Exp, [("wB", sb_)], [("wS", sb_)])
            for ab in range(2):
                h = 2 * p + ab
                vi = 2 * (ui % 2) + ab
                TT("dve", Vp[vi][:, 0:nb, :], Vaug[:, 0:nb, h, :], wS[sb_][:, 0:nb, h:h + 1].to_broadcast([128, nb, 65]), ALU.mult,
                   [("V", j) for j in range(nb)] + ["vones", ("wS", sb_)], [("Vp", vi)])

        def att_qk(ui, c0):
            s, p = units[ui]
            J = J_of(s)
            nb = J + 1
            n = min(4, nb - c0)
            set_ = cc[0] % NSET
            cc[0] += 1
            for jj in range(n):
                j = c0 + jj
                for ab in range(2):
                    r0 = ab * 64
                    bk = 2 * set_ + ab
                    MM(psF[bk][:, jj * 128:(jj + 1) * 128], KT[r0:r0 + 64, p, j * 128:(j + 1) * 128],
                       QT[r0:r0 + 64, p, s * 128:(s + 1) * 128], True, True,
                       [("KT", p, j // 4), ("QT", p, s // 4)], [PSK(bk)])
            for ab in range(2):
                bk = 2 * set_ + ab
                ACT(pT[ab][:, c0 * 128:(c0 + n) * 128], psF[bk][:, 0:n * 128], AF.Exp, [PSK(bk)], [("pT", ab, c0 // 4)], scale=0.125)
            if c0 + n == nb:
                for ab in range(2):
                    TT("pool", pT[ab][:, (J - 1) * 128:(J + 1) * 128], pT[ab][:, (J - 1) * 128:(J + 1) * 128], msk[:, s, :], ALU.mult,
                       [("pT", ab, c0 // 4), "msk"], [("pT", ab, c0 // 4)])

        def att_pv(ui, c0):
            s, p = units[ui]
            J = J_of(s)
            nb = J + 1
            n = min(4, nb - c0)
            banks = acc_banks(s)
            for ab in range(2):
                acc, ak = banks[ab]
                vi = 2 * (ui % 2) + ab
                for jj in range(n):
                    j = c0 + jj
                    MM(acc[:, p * 65:(p + 1) * 65], pT[ab][:, j * 128:(j + 1) * 128], Vp[vi][:, j, :], j == 0, j == J,
                       [("pT", ab, c0 // 4), ("Vp", vi)], [ak])
            if c0 + n == nb and p == 3:
                tb = s % 2
                for ab in range(2):
                    acc, ak = banks[ab]
                    a3 = acc[:, 0:260].rearrange("p (h d) -> p h d", h=4)
                    rk = ("rec", ab)
                    P.op("dve", lambda e, o_=rec[ab][:, 0:4], i_=a3[:, :, 64]: e.reciprocal(out=o_, in_=i_), [ak], [rk])
                    TT("dve", atok[tb].rearrange("p (q two d) -> p q two d", two=2, d=64)[:, :, ab, :], a3[:, :, 0:64],
                       rec[ab][:, 0:4].unsqueeze(2).to_broadcast([128, 4, 64]), ALU.mult, [ak, rk], [("atok", tb)])
                deferred.append((s, tb))

        deferred = []

        def att_fin2():
            while deferred:
                s, tb = deferred.pop(0)
                set_ = cc[0] % NSET
                cc[0] += 1
                bk = 2 * set_
                ptr = psF[bk][:, :].bitcast(BF16)
                for c in range(4):
                    TR(ptr[:, c * 128:(c + 1) * 128], atok[tb][:, c * 128:(c + 1) * 128], [("atok", tb)], [PSK(bk)])
                CP("dve", attnT[:, :, s * 128:(s + 1) * 128], ptr[:, 0:512].rearrange("p (c t) -> p c t", c=4),
                   [PSK(bk)], [("attnT", s)])

        pending = []
        last_unit = [-1]
        for k in range(len(items)):
            while pending and (len(pending) > LOOK or any(items[q][1] == items[k][1] for q in pending)):
                had = bool(deferred)
                att_pv(*items[pending.pop(0)])
                if had:
                    att_fin2()
            if items[k][0] != last_unit[0]:
                att_prep(items[k][0])
                last_unit[0] = items[k][0]
            att_qk(*items[k])
            pending.append(k)
            if k == 40:
                e1_loads()
                e2_chunk_loads(0)
            if k == len(items) - 12:
                e1_prefetch_x(0)
        while pending:
            att_pv(*items[pending.pop(0)])
        att_fin2()

        if debug:
            dump([("d_at", attnT.rearrange("p a b -> p (a b)"), 4 * TOWN)])
            if upto == 2:
                finish()
                return nc

        P.barrier()
        A.release(m_l1)
        gpo = A.alloc([D], F32)
        gpf = A.alloc([D], F32)
        gpff = A.alloc([D], F32)
        x1 = A.alloc([8, D], F32)
        h2T = A.alloc([8, 1024], BF16)
        DMA("sp", "c4", gpo, gpo_d, [], ["gpo"])
        DMA("sp", "c5", gpf, gpf_d, [], ["gpf"])
        DMA("sp", "c6", gpff, gpff_d, [], ["gpff"])
        m_l2b = A.mark()

        for half in range(2):
            def hk(nm, half=half):
                return (nm, half)
            wb = 0 if half == 0 else 2
            wo = 2 - wb

            A.release(m_l2b)
            hTo = A.alloc([8, 1024], BF16)
            sguT = A.alloc([4, 1024], BF16)
            mT = A.alloc([8, 1024], BF16)
            lst = [A.alloc([8], F32) for _ in range(3)]
            ejunk = A.alloc([512], BF16)
            del xt[2:]
            xt.append(A.alloc([D], F32))
            wkb = [A.alloc([512], F32) for _ in range(12)]

            def WK(i):
                return hk(("wk", i))
            t1 = [[wkb[0], wkb[1]], [wkb[2], wkb[3]]]
            t1k = [[WK(0), WK(1)], [WK(2), WK(3)]]
            gl = [[wkb[4], wkb[5]], [wkb[6], wkb[7]]]
            glk = [[WK(4), WK(5)], [WK(6), WK(7)]]
            vn = [wkb[8].bitcast(BF16)[:, 0:512], wkb[9].bitcast(BF16)[:, 0:512]]
            vnk = [WK(8), WK(9)]
            sgu = [wkb[10].bitcast(BF16)[:, 0:512], wkb[11].bitcast(BF16)[:, 0:512]]
            sguk = [WK(10), WK(11)]
            tg2 = [[wkb[0], wkb[1]], [wkb[2], wkb[3]]]
            m122 = [[wkb[4], wkb[5]], [wkb[6], wkb[7]]]

            if half == 1:
                e2_chunk_loads(0, wb)
            e_x = {}
            SQC = 0.21145921592026346

            def e1_s0a_act(sl):
                gs = half * 8 + sl
                if (half, sl) in pre_x:
                    b = pre_x[(half, sl)]
                    xctr[0] = b + 1
                else:
                    b = xctr[0] % len(xt)
                    xctr[0] += 1
                    DMA("sp", ("xt", b), xt[b], xown[gs * 128:(gs + 1) * 128, :], [], [("xt", b)])
                e_x[sl] = (nt_a1(xt[b], ("xt", b)), b)

            def e1_s0a_dve(sl):
                xi, b = e_x[sl]
                nt_a2(xi, xt[b], ("xt", b), gpm, "gpm")

            def e1_s0b(sl):
                nt_b(e_x[sl][0], hTo[:, :, sl * 128:(sl + 1) * 128], hk(("hTo", sl)), on_act=True)

            def e1_s1_pe(sl):
                par = sl % 2
                for which in range(2):
                    ps, pk = psF[2 * par + which], PSK(2 * par + which)
                    for c in range(8):
                        MM(ps[:, :], hTo[:, c, sl * 128:(sl + 1) * 128], stg[wb + which][:, c, :], c == 0, c == 7,
                           [hk(("hTo", sl)), ("stg", wb + which)], [pk])
                for which in range(2):
                    ps, pk = psF[2 * par + which], PSK(2 * par + which)
                    tt_, tk = t1[par][which], t1k[par][which]
                    ACT(tt_, ps[:, :], AF.Square, [pk], [tk], scale=SQC)

            def e1_s1_inner(sl):
                par = sl % 2
                for which in range(2):
                    ps, pk = psF[2 * par + which], PSK(2 * par + which)
                    tt_, tk = t1[par][which], t1k[par][which]
                    STT(tt_, tt_, 1.0, ps[:, :], ALU.add, ALU.mult, [tk, pk], [tk])

            def e1_s1_exp(sl):
                par = sl % 2
                for which in range(2):
                    tt_, tk = t1[par][which], t1k[par][which]
                    ACT(tt_, tt_, AF.Exp, [tk], [tk], scale=-2.0 * GELU_C)
                for which in range(2):
                    tt_, tk = t1[par][which], t1k[par][which]
                    ACT(tt_, tt_, AF.Ln, [tk], [tk], bias=1.0)
                for which in range(2):
                    tt_, tk = t1[par][which], t1k[par][which]
                    ACT(tt_, tt_, AF.Exp, [tk], [tk], scale=-1.0)

            def e1_s1_gl(sl):
                par = sl % 2
                l_, lk = lst[par], hk(("lst", par))
                for which in range(2):
                    ps, pk = psF[2 * par + which], PSK(2 * par + which)
                    tt_, tk = t1[par][which], t1k[par][which]
                    if which == 0:
                        STT(gl[par][which], tt_, 2.0, ps[:, :], ALU.mult, ALU.mult, [tk, pk], [glk[par][which]])
                    else:
                        P.op("dve", lambda e, o_=gl[par][1], t_=tt_, p_=ps[:, :], a_=l_[:, 0:1]: e.scalar_tensor_tensor(
                            out=o_, in0=t_, scalar=2.0, in1=p_, op0=ALU.mult, op1=ALU.mult, accum_out=a_),
                             [tk, pk], [glk[par][1], lk])

            def e1_s2(sl):
                par = sl % 2
                l_, lk = lst[par], hk(("lst", par))
                v_, vk = gl[par][1], glk[par][1]
                j_, jk = t1[par][1], t1k[par][1]
                ACT(j_, v_, AF.Square, [vk], [jk, lk], accum_out=l_[:, 1:2])
                TS("dve", l_[:, 2:3], l_[:, 0:1], 1.0 / 512, None, ALU.mult, None, [lk], [lk])
                TT("dve", l_[:, 3:4], l_[:, 2:3], l_[:, 2:3], ALU.mult, [lk], [lk])
                STT(l_[:, 4:5], l_[:, 1:2], 1.0 / 512, l_[:, 3:4], ALU.mult, ALU.subtract, [lk], [lk])
                ACT(l_[:, 5:6], l_[:, 4:5], AF.Ln, [lk], [lk], bias=4.0 * EPS)
                ACT(l_[:, 6:7], l_[:, 5:6], AF.Exp, [lk], [lk], scale=-0.5)
                TS("dve", v_, v_, l_[:, 2:3], l_[:, 6:7], ALU.subtract, ALU.mult, [vk, lk], [vk])
                TT("dve", v_, v_, gsg, ALU.mult, [vk, "gsg"], [vk])
                TT("dve", vn[par], v_, bsg, ALU.add, [vk, "bsg"], [vnk[par]])

            def e1_s3(sl):
                par = sl % 2
                pm, pmk = psF[4 + par], PSK(4 + par)
                for g in range(8):
                    MM(pm[:, g * 64:(g + 1) * 64], wsT[:, g, :], vn[par][:, g * 64:(g + 1) * 64], True, True,
                       ["wsT", vnk[par]], [pmk])
                s1_, s1k = wkb[8 + par], vnk[par]
                TT("dve", s1_.rearrange("p (g c) -> p g c", g=8), pm.rearrange("p (g c) -> p g c", g=8),
                   bsp.unsqueeze(2).to_broadcast([128, 8, 64]), ALU.add, [pmk, "bsp"], [s1k])
                TT("dve", sgu[par], s1_, gl[par][0], ALU.mult, [s1k, glk[par][0]], [sguk[par]])

            def e1_s4(sl):
                par = sl % 2
                pb = psb_ctr[0] % 2
                psb_ctr[0] += 1
                for c in range(4):
                    TR(psB[pb][:, c * 128:(c + 1) * 128], sgu[par][:, c * 128:(c + 1) * 128], [sguk[par]], [PSK(6 + pb)])
                ACT(sguT[:, :, sl * 128:(sl + 1) * 128], psB[pb][:, 0:512].rearrange("p (c t) -> p c t", c=4), AF.Copy,
                    [PSK(6 + pb)], [hk(("sguT", sl // 4))])

            for t in range(8 + 4):
                s1ok = 0 <= t - 1 < 8
                if t < 8:
                    e1_s0a_act(t)
                if s1ok:
                    e1_s1_pe(t - 1)
                if t < 8:
                    e1_s0a_dve(t)
                if s1ok:
                    e1_s1_inner(t - 1)
                if t < 8:
                    e1_s0b(t)
                if s1ok:
                    e1_s1_exp(t - 1)
                if 0 <= t - 4 < 8:
                    e1_s4(t - 4)
                if 0 <= t - 3 < 8:
                    e1_s3(t - 3)
                if 0 <= t - 2 < 8:
                    e1_s2(t - 2)
                if s1ok:
                    e1_s1_gl(t - 1)

            for sl in range(8):
                gs = half * 8 + sl
                DMA("sp", ("x1ld", sl), x1[:, sl, :], xown[gs * 128:(gs + 1) * 128, :], [], [("x1", half, sl)])

            units2 = [(ch, tt, o) for ch in range(4) for tt in range(2) for o in range(2)]

            def e2_s1(ui):
                ch, tt, o = units2[ui]
                if tt == 0 and o == 0 and ch + 1 < 4:
                    e2_chunk_loads(ch + 1, wb)
                if ch == 3 and tt == 0 and o == 0:
                    load_w(stg[wo], w_out_v[:, :, 0:512], ("stg", wo))
                    load_w(stg[wo + 1], w_out_v[:, :, 512:1024], ("stg", wo + 1))
                b0 = e2_b0(ch, wb)
                s0, s1_ = b0 // 2, b0 // 2 + 1
                k0a, k0b, k1 = ("sa", s0), ("sb", s0), ("sc", s0)
                k2 = [("sa", s1_), ("sb", s1_)]
                pset = 4 * (ui % 2)
                tcols = slice(tt * 512, (tt + 1) * 512)
                slh = slice(half * 1024 + tt * 512, half * 1024 + (tt + 1) * 512)
                akeys = [("attnT", s_) for s_ in range(half * 8 + tt * 4, half * 8 + tt * 4 + 4)]
                hkeys = [hk(("hTo", s_)) for s_ in range(tt * 4, tt * 4 + 4)]
                ocols = slice(o * 128, (o + 1) * 128)
                bank = lambda i: (psF[pset + i] if pset + i < 6 else psBf[pset + i - 6])
                for kc in range(4):
                    MM(bank(0)[:, :], hs[b0][:, kc, ocols], attnT[:, kc, slh], kc == 0, kc == 3, [k0a] + akeys, [PSK(pset)])
                for c in range(8):
                    MM(bank(1)[:, :], hs[b0 + 1][:, c, ocols], hTo[:, c, tcols], c == 0, c == 7, [k1] + hkeys, [PSK(pset + 1)])
                for kc in range(4):
                    MM(bank(2)[:, :], hs[b0][:, 4 + kc, ocols], sguT[:, kc, tcols], kc == 0, kc == 3,
                       [k0b, hk(("sguT", tt))], [PSK(pset + 2)])
                for c in range(8):
                    MM(bank(3)[:, :], hs[b0 + 2][:, c, ocols], hTo[:, c, tcols], c == 0, c == 7, k2 + hkeys, [PSK(pset + 3)])

            def e2_s2(ui):
                ch, tt, o = units2[ui]
                par = ui % 2
                pset = 4 * par
                oc = ch * 2 + o
                tcols = slice(tt * 512, (tt + 1) * 512)
                bank = lambda i: (psF[pset + i] if pset + i < 6 else psBf[pset + i - 6])
                ta, tak = tg2[par][0], WK(2 * par)
                tb_, tbk = tg2[par][1], WK(2 * par + 1)
                m1, m1k = m122[par][0], WK(4 + 2 * par)
                m2, m2k = m122[par][1], WK(5 + 2 * par)
                ACT(ta, bank(1)[:, :], AF.Exp, [PSK(pset + 1)], [tak], scale=-1.0)
                ACT(tb_, bank(3)[:, :], AF.Exp, [PSK(pset + 3)], [tbk], scale=-1.0)
                ACT(ta, ta, AF.Ln, [tak], [tak], bias=1.0)
                ACT(tb_, tb_, AF.Ln, [tbk], [tbk], bias=1.0)
                ACT(ta, ta, AF.Exp, [tak], [tak], scale=-1.0)
                ACT(tb_, tb_, AF.Exp, [tbk], [tbk], scale=-1.0)
                TT("dve", m1, ta, bank(0)[:, :], ALU.mult, [tak, PSK(pset)], [m1k])
                TT("dve", m2, tb_, bank(2)[:, :], ALU.mult, [tbk, PSK(pset + 2)], [m2k])
                STT(mT[:, oc, tcols], m1, 2.0, m2, ALU.mult, ALU.add, [m1k, m2k], [hk(("mT", tt))])

            for t in range(len(units2) + 1):
                if t >= 1:
                    e2_s2(t - 1)
                if t < len(units2):
                    e2_s1(t)

            e3_xi = {}

            def zt_view(k):
                return (wkb[2 * k], wkb[2 * k + 1]), (WK(2 * k), WK(2 * k + 1))

            def e3_s1(sl):
                k = sl % 3
                tt = sl // 4
                (z0, z1), (zk0, zk1) = zt_view(k)
                l_, lk = lst[k], hk(("lst", k))
                for hf in range(2):
                    for c in range(8):
                        MM(psF[2 * k + hf][:, :], mT[:, c, sl * 128:(sl + 1) * 128], stg[wo + hf][:, c, :], c == 0, c == 7,
                           [hk(("mT", tt)), ("stg", wo + hf)], [PSK(2 * k + hf)])
                ACT(ejunk, psF[2 * k][:, :], AF.Square, [PSK(2 * k)], [hk("ejunk"), lk], accum_out=l_[:, 0:1])
                ACT(ejunk, psF[2 * k + 1][:, :], AF.Square, [PSK(2 * k + 1)], [hk("ejunk"), lk], accum_out=l_[:, 1:2])
                TT("dve", z0, psF[2 * k][:, :], gpo[:, 0:512], ALU.mult, [PSK(2 * k), "gpo"], [zk0])
                TT("dve", z1, psF[2 * k + 1][:, :], gpo[:, 512:1024], ALU.mult, [PSK(2 * k + 1), "gpo"], [zk1])

            def e3_s2(sl):
                k = sl % 3
                l_, lk = lst[k], hk(("lst", k))
                TT("dve", l_[:, 2:3], l_[:, 0:1], l_[:, 1:2], ALU.add, [lk], [lk])
                ACT(l_[:, 3:4], l_[:, 2:3], AF.Ln, [lk], [lk], scale=1.0 / D, bias=4.0 * EPS)
                ACT(l_[:, 4:5], l_[:, 3:4], AF.Exp, [lk], [lk], scale=-0.5)

            def e3_s3(sl):
                k = sl % 3
                (z0, z1), (zk0, zk1) = zt_view(k)
                l_, lk = lst[k], hk(("lst", k))
                xk_ = ("x1", half, sl)
                STT(x1[:, sl, 0:512], z0, l_[:, 4:5], x1[:, sl, 0:512], ALU.mult, ALU.add, [zk0, lk, xk_], [xk_])
                STT(x1[:, sl, 512:1024], z1, l_[:, 4:5], x1[:, sl, 512:1024], ALU.mult, ALU.add, [zk1, lk, xk_], [xk_])

            def e3_s4(sl):
                e3_xi[sl] = nt_a1(x1[:, sl, :], ("x1", half, sl))

            def e3_s5(sl):
                nt_a2(e3_xi[sl], x1[:, sl, :], ("x1", half, sl), gpf, "gpf")

            def e3_s6(sl):
                nt_b(e3_xi[sl], h2T[:, :, sl * 128:(sl + 1) * 128], ("h2T", half, sl // 4), on_act=True)

            e3_stages = [e3_s1, e3_s2, e3_s3, e3_s4, e3_s5, e3_s6]
            for t in range(8 + 5):
                for si_ in range(5, -1, -1):
                    sl_ = t - si_
                    if 0 <= sl_ < 8:
                        e3_stages[si_](sl_)
                if t == 6:
                    f_loads(0, wb)

            if debug and upto == 4:
                dump([("d_x1", x1.rearrange("p a b -> p (a b)"), 8 * 1024)])
                finish()
                return nc
            P.barrier()
            A.release(m_l2b)
            del xt[2:]
            actT = A.alloc([NFC, 1024], BF16)
            ffA = A.alloc([8, 512], F32)
            fw = A.alloc([4, 512], F32)
            tgf = [fw[:, 0, :], fw[:, 2, :]]
            a1 = [fw[:, 1, :], fw[:, 3, :]]
            ot = [fw[:, 0:2, :].rearrange("p a b -> p (a b)"), fw[:, 2:4, :].rearrange("p a b -> p (a b)")]
            fst = A.alloc([8, 4], F32)
            fctr = 0
            for fq in range(6):
                nfc = 4 if fq < 5 else 2
                sg_, su_ = (2 * fq + wb) % NSTG, (2 * fq + 1 + wb) % NSTG
                if fq >= 1:
                    f_loads(fq, wb)
                for f in range(nfc):
                    fc = fq * 4 + f
                    fcols = slice(f * 128, (f + 1) * 128)
                    for tt in range(2):
                        tcols = slice(tt * 512, (tt + 1) * 512)
                        pi = fctr % 2
                        fctr += 1
                        pg, pu = psF[pi], psF[2 + pi]
                        for c in range(8):
                            MM(pg[:, :], stg[sg_][:, c, fcols], h2T[:, c, tcols], c == 0, c == 7,
                               [("stg", sg_), ("h2T", half, tt)], [PSK(pi)])
                        for c in range(8):
                            MM(pu[:, :], stg[su_][:, c, fcols], h2T[:, c, tcols], c == 0, c == 7,
                               [("stg", su_), ("h2T", half, tt)], [PSK(2 + pi)])
                        ACT(tgf[pi], pg[:, :], AF.Exp, [PSK(pi)], [hk(("fw", 2 * pi))], scale=-1.0)
                        ACT(tgf[pi], tgf[pi], AF.Ln, [hk(("fw", 2 * pi))], [hk(("fw", 2 * pi))], bias=1.0)
                        ACT(tgf[pi], tgf[pi], AF.Exp, [hk(("fw", 2 * pi))], [hk(("fw", 2 * pi))], scale=-1.0)
                        TT("dve", a1[pi], tgf[pi], pg[:, :], ALU.mult, [hk(("fw", 2 * pi)), PSK(pi)], [hk(("fw", 2 * pi + 1))])
                        TT("dve", actT[:, fc, tcols], a1[pi], pu[:, :], ALU.mult, [hk(("fw", 2 * pi + 1)), PSK(2 + pi)], [hk(("actT", tt))])

            if debug and upto == 5:
                finish()
                return nc
            gjunk = A.alloc([512], BF16)

            def bank_of(sl):
                return psF[sl] if sl < 6 else psBf[sl - 6]

            def g_keys(sl):
                ob = sl % 2
                return hk(("fw", 2 * ob)), hk(("fw", 2 * ob + 1)), hk(("fst", sl))

            def g_ev1(r, sl):
                pbank = bank_of(sl)
                ok, ok2, fk = g_keys(sl)
                ACT(gjunk, pbank[:, :], AF.Square, [PSK(sl)], [hk("gjunk"), fk], accum_out=fst[:, sl, r:r + 1])
                if r == 0:
                    TT("dve", ffA[:, sl, :], pbank[:, :], gpff[:, 0:512], ALU.mult, [PSK(sl), "gpff"], [hk(("ffA", sl))])

            def g_ev2(sl):
                ok, ok2, fk = g_keys(sl)
                TT("dve", fst[:, sl, 2:3], fst[:, sl, 0:1], fst[:, sl, 1:2], ALU.add, [fk], [fk])
                ACT(fst[:, sl, 3:4], fst[:, sl, 2:3], AF.Ln, [fk], [fk], scale=1.0 / D, bias=EPS)
                ACT(fst[:, sl, 3:4], fst[:, sl, 3:4], AF.Exp, [fk], [fk], scale=-0.5)

            def g_ev3(sl):
                gs = half * 8 + sl
                pbank = bank_of(sl)
                ob = sl % 2
                ok, ok2, fk = g_keys(sl)
                STT(ot[ob][:, 0:512], ffA[:, sl, :], fst[:, sl, 3:4], x1[:, sl, 0:512], ALU.mult, ALU.add,
                    [hk(("ffA", sl)), fk, ("x1", half, sl)], [ok, ok2])
                TT("dve", ot[ob][:, 512:1024], pbank[:, :], gpff[:, 512:1024], ALU.mult, [PSK(sl), "gpff"], [ok, ok2])
                STT(ot[ob][:, 512:1024], ot[ob][:, 512:1024], fst[:, sl, 3:4], x1[:, sl, 512:1024], ALU.mult, ALU.add,
                    [ok, ok2, fk, ("x1", half, sl)], [ok, ok2])
                DMA("sp", ("out", ob), out_d[gs * 128:(gs + 1) * 128, :], ot[ob], [ok, ok2], [("outd", gs)])

            for r in range(2):
                sis = []
                for k3 in range(3):
                    n8 = 8 if k3 < 2 else 6
                    si = (k3 + r * 3 + wb) % NSTG
                    sis.append((si, n8))
                    load_w(stg[si][:, 0:n8, :], w_ffd_v[:, k3 * 8:k3 * 8 + n8, r * 512:(r + 1) * 512], ("stg", si))
                si, n8 = sis[0]
                for f in range(n8):
                    fc = f
                    for sl in range(8):
                        pbank = psF[sl] if sl < 6 else psBf[sl - 6]
                        MM(pbank[:, :], actT[:, fc, sl * 128:(sl + 1) * 128], stg[si][:, f, :], fc == 0, fc == NFC - 1,
                           [hk(("actT", sl // 4)), ("stg", si)], [PSK(sl)])
                if r == 1 and half == 0:
                    e1_loads(2)
                    e1_prefetch_x(1)
                for sl in range(8):
                    pbank = psF[sl] if sl < 6 else psBf[sl - 6]
                    for k3 in (1, 2):
                        si, n8 = sis[k3]
                        for f in range(n8):
                            fc = k3 * 8 + f
                            MM(pbank[:, :], actT[:, fc, sl * 128:(sl + 1) * 128], stg[si][:, f, :], fc == 0, fc == NFC - 1,
                               [hk(("actT", sl // 4)), ("stg", si)], [PSK(sl)])
                    g_ev1(r, sl)
                    if r == 1:
                        if sl >= 1:
                            g_ev2(sl - 1)
                        if sl >= 2:
                            g_ev3(sl - 2)
                if r == 1:
                    g_ev2(7)
                    g_ev3(6)
                    g_ev3(7)
            P.barrier()
            if debug and upto == 6:
                finish()
                return nc

        finish()
    return nc


_NC_CACHE = {}


def _layout_inputs(inp):
    f = lambda a: np.ascontiguousarray(np.asarray(a, dtype=np.float32))
    x = f(inp["x"])
    rep = lambda v, n=128: np.ascontiguousarray(np.broadcast_to(f(v).reshape(1, -1), (n, f(v).size)))
    common = {
        "w_in": f(inp["w_in"][0]), "w_a": f(inp["w_branch_a"][0]), "w_b": f(inp["w_branch_b"][0]),
        "w_out": f(inp["w_out"][0]), "w_ffi": f(inp["w_ffn_in"][0]), "w_ffd": f(inp["w_ffn_down"][0]),
        "gpm": rep(inp["g_pre_mix"][0]), "gpo": rep(inp["g_post_mix"][0]), "gpf": rep(inp["g_pre_ffn"][0]),
        "gpff": rep(inp["g_post_ffn"][0]), "gsg": rep(inp["g_sgu"][0]), "bsg": rep(inp["b_sgu"][0]),
        "bfb": rep(inp["b_forget"][0]),
        "gqc": np.ascontiguousarray(np.tile(f(inp["g_q"][0]).reshape(64, 1), (2, 1))),
        "gkc": np.ascontiguousarray(np.tile(f(inp["g_k"][0]).reshape(64, 1), (2, 1))),
        "wsp": f(inp["w_spatial"][0]),
        "bsp": np.ascontiguousarray(f(inp["b_spatial"][0]).T),
    }
    ones = np.ones((128, 128), np.float32)
    zeros = np.zeros((128, 128), np.float32)
    tri = np.triu(np.ones((128, 128), np.float32))
    in_maps = []
    for c in range(8):
        b, par = c // 2, c % 2
        blocks = []
        mk = np.zeros((128, NS, 2, 128), np.float32)
        for s in range(NS):
            m = s // 2
            if par == 0:
                i = 4 * m if s % 2 == 0 else 4 * m + 3
            else:
                i = 4 * m + 1 if s % 2 == 0 else 4 * m + 2
            blocks.append(i)
            J = J_of(s)
            for k in range(2):
                j = J - 1 + k
                mk[:, s, k, :] = ones if j < i else (tri if j == i else zeros)
        xo = np.concatenate([x[b, i * 128:(i + 1) * 128] for i in blocks], axis=0)
        d = dict(common)
        d["xseq"] = np.ascontiguousarray(x[b])
        d["xown"] = np.ascontiguousarray(xo)
        d["msk"] = np.ascontiguousarray(mk.reshape(128, NS * 256))
        in_maps.append((d, blocks))
    return in_maps


def kernel(**inputs):
    maps = _layout_inputs(inputs)
    if "nc" not in _NC_CACHE:
        _NC_CACHE["nc"] = build_nc()
    nc = _NC_CACHE["nc"]
    res = run_bass_kernel_spmd(nc, [m[0] for m in maps], core_ids=list(range(8)))
    out = np.empty((4, SEQ, D), np.float32)
    for c in range(8):
        b = c // 2
        o = res.results[c]["out"]
        for s, i in enumerate(maps[c][1]):
            out[b, i * 128:(i + 1) * 128] = o[s * 128:(s + 1) * 128]
    return out
```

```python
import numpy as np
from contextlib import ExitStack
import concourse.bass as bass
import concourse.mybir as mybir
from concourse.bass_utils import run_bass_kernel_spmd

F32 = mybir.dt.float32
BF16 = mybir.dt.bfloat16
AF = mybir.ActivationFunctionType
ALU = mybir.AluOpType
AX = mybir.AxisListType

D = 1024
SEQ = 4096
NB = 32
NS = 16
TOWN = 2048
HEADS = 8
DH = 64
Q_OFF, K_OFF, V_OFF, F_OFF, U_OFF, G_OFF = 0, 512, 1024, 1536, 1544, 2568
IN_COLS = 4616
DFF = 2816
NFC = 22
EPS = 1e-6
GELU_C = 0.7978845608028654


def J_of(slot):
    return 4 * (slot // 2) + (1 if slot % 2 == 0 else 3)


ENGS = ["pe", "act", "dve", "pool", "sp"]


class Prog:
    def __init__(self, nc):
        self.nc = nc
        self.recs = {e: [] for e in ENGS}
        self.lastw = {}
        self.readers = {}
        self.dma_count = {}

    def _deps(self, eng, reads, writes, is_dma):
        deps = set()
        for k in reads:
            t = self.lastw.get(k)
            if t is not None:
                deps.add(t)
            if isinstance(k, tuple) and k[0] == "ps":
                for t2 in self.readers.get(k, {}).values():
                    if not (t2[0] == "e" and t2[1] == eng):
                        deps.add(t2)
        strict = is_dma or eng != "pe"
        for k in writes:
            t = self.lastw.get(k)
            if t is not None and (strict or not (t[0] == "e" and t[1] == eng)):
                deps.add(t)
            for t2 in self.readers.get(k, {}).values():
                if strict or not (t2[0] == "e" and t2[1] == eng):
                    deps.add(t2)
        return deps

    def _register(self, tok, rk, reads, writes):
        for k in reads:
            self.readers.setdefault(k, {})[rk] = tok
        for k in writes:
            self.lastw[k] = tok
            self.readers[k] = {}

    @staticmethod
    def _expand(keys):
        out = []
        for k in keys:
            out.append(k)
            if isinstance(k, tuple) and len(k) == 2 and k[0] == "stg":
                out += [("sa", k[1]), ("sb", k[1]), ("sc", k[1])]
        return out

    def op(self, eng, fn, reads=(), writes=()):
        reads, writes = self._expand(reads), self._expand(writes)
        idx = len(self.recs[eng])
        deps = self._deps(eng, reads, writes, False)
        tok = ("e", eng, idx)
        self.recs[eng].append(dict(fn=fn, deps=deps, dma=None))
        self._register(tok, eng, reads, writes)

    def dma(self, eng, semkey, fn, reads=(), writes=()):
        reads, writes = self._expand(reads), self._expand(writes)
        n = self.dma_count.get(semkey, 0) + 1
        self.dma_count[semkey] = n
        tok = ("d", semkey, n)
        deps = self._deps(eng, reads, writes, True)
        self.recs[eng].append(dict(fn=fn, deps=deps, dma=semkey))
        self._register(tok, ("d", semkey), reads, writes)

    def barrier(self):
        toks = set()
        for e in ENGS:
            if self.recs[e]:
                for i in range(len(self.recs[e]) - 1, -1, -1):
                    r = self.recs[e][i]
                    if r["fn"] is not None and r["dma"] is None:
                        toks.add(("e", e, i))
                        break
        for k, n in self.dma_count.items():
            toks.add(("d", k, n))
        for e in ENGS:
            deps = set(t for t in toks if not (t[0] == "e" and t[1] == e))
            self.recs[e].append(dict(fn=None, deps=deps, dma=None))

    def final_wait(self, eng, semkeys):
        deps = set(("d", k, self.dma_count[k]) for k in semkeys if k in self.dma_count)
        self.recs[eng].append(dict(fn=None, deps=deps, dma=None))

    def emit(self):
        nc = self.nc
        sig = {e: [False] * len(self.recs[e]) for e in ENGS}
        for e in ENGS:
            for r in self.recs[e]:
                for t in r["deps"]:
                    if t[0] == "e":
                        sig[t[1]][t[2]] = True
        rank = {}
        for e in ENGS:
            c = 0
            rk = []
            for i in range(len(self.recs[e])):
                if sig[e][i]:
                    c += 1
                rk.append(c)
            rank[e] = rk
        with ExitStack() as st:
            esem = {e: st.enter_context(nc.semaphore("s_" + e)) for e in ENGS}
            dsem = {}
            for i, k in enumerate(sorted(self.dma_count.keys(), key=str)):
                dsem[k] = st.enter_context(nc.semaphore("d%d" % i))
            block = st.enter_context(nc.Block())
            bname = {"pe": "tensor", "act": "scalar", "dve": "vector", "pool": "gpsimd", "sp": "sync"}
            for e in ENGS:
                def body(engine, e=e):
                    waited = {}
                    for i, r in enumerate(self.recs[e]):
                        for t in sorted(r["deps"], key=str):
                            if t[0] == "e":
                                key = ("e", t[1]); val = rank[t[1]][t[2]]; sem = esem[t[1]]
                            else:
                                key = ("d", t[1]); val = 16 * t[2]; sem = dsem[t[1]]
                            if waited.get(key, 0) >= val:
                                continue
                            engine.wait_ge(sem, val)
                            waited[key] = val
                        if r["fn"] is None:
                            continue
                        ins = r["fn"](engine)
                        if r["dma"] is not None:
                            ins.then_inc(dsem[r["dma"]], 16)
                        elif sig[e][i]:
                            ins.then_inc(esem[e], 1)
                getattr(block, bname[e])(body)


class Arena:
    def __init__(self, sb, total):
        self.sb = sb
        self.total = total
        self.off = 0
        self.peak = 0

    def alloc(self, free_shape, dt):
        n = 1
        for s in free_shape:
            n *= s
        esz = 4 if dt == F32 else 2
        nbytes = (n * esz + 63) // 64 * 64
        o = self.off
        self.off += nbytes
        self.peak = max(self.peak, self.off)
        assert self.off <= self.total, ("SBUF arena overflow", self.off, self.total)
        v = self.sb[:, o // 2:o // 2 + n * esz // 2]
        if dt == F32:
            v = v.bitcast(F32)
        if len(free_shape) == 2:
            v = v.rearrange("p (a b) -> p a b", a=free_shape[0])
        elif len(free_shape) == 3:
            v = v.rearrange("p (a b c) -> p a b c", a=free_shape[0], b=free_shape[1])
        return v

    def mark(self):
        return self.off

    def release(self, m):
        self.off = m


def build_nc(debug=False, upto=3):
    nc = bass.Bass("TRN2", target_bir_lowering=False)

    def din(name, shape):
        return nc.dram_tensor(name, list(shape), F32, kind="ExternalInput").ap()

    xseq = din("xseq", [SEQ, D])
    xown = din("xown", [TOWN, D])
    w_in = din("w_in", [D, IN_COLS])
    w_a = din("w_a", [512, D])
    w_b = din("w_b", [512, D])
    w_out = din("w_out", [D, D])
    w_ffi = din("w_ffi", [D, 2 * DFF])
    w_ffd = din("w_ffd", [DFF, D])
    gpm_d = din("gpm", [128, D])
    gpo_d = din("gpo", [128, D])
    gpf_d = din("gpf", [128, D])
    gpff_d = din("gpff", [128, D])
    gsg_d = din("gsg", [128, 512])
    bsg_d = din("bsg", [128, 512])
    bfb_d = din("bfb", [128, 8])
    gqc_d = din("gqc", [128, 1])
    gkc_d = din("gkc", [128, 1])
    wsp_d = din("wsp", [8, 128, 128])
    bsp_d = din("bsp", [128, 8])
    msk_d = din("msk", [128, NS * 256])
    out_d = nc.dram_tensor("out", [TOWN, D], F32, kind="ExternalOutput").ap()
    dbg = {}
    if debug:
        for nm, shp in [("d_kt", [128, 4 * SEQ]), ("d_qt", [128, 4 * TOWN]), ("d_v", [128, NB * 8 * 65]),
                        ("d_c", [128, 256]), ("d_at", [128, 4 * TOWN]), ("d_x1", [128, 8 * 1024])]:
            dbg[nm] = nc.dram_tensor(nm, shp, F32, kind="ExternalOutput").ap()

    w_in_v = w_in.rearrange("(c p) n -> p c n", p=128)
    w_a_v = w_a.rearrange("(c p) n -> p c n", p=128)
    w_b_v = w_b.rearrange("(c p) n -> p c n", p=128)
    w_out_v = w_out.rearrange("(c p) n -> p c n", p=128)
    w_ffi_v = w_ffi.rearrange("(c p) n -> p c n", p=128)
    w_ffd_v = w_ffd.rearrange("(c p) n -> p c n", p=128)

    TOTAL = 212480
    with ExitStack() as stack:
        sb = stack.enter_context(nc.sbuf_tensor("sb", [128, TOTAL // 2], BF16))
        psF = [stack.enter_context(nc.psum_tensor("psf%d" % i, [128, 512], F32)) for i in range(6)]
        psBf = [stack.enter_context(nc.psum_tensor("psb%d" % i, [128, 512], F32)) for i in range(2)]
        psB = [t[:, :].bitcast(BF16) for t in psBf]
        A = Arena(sb, TOTAL)
        P = Prog(nc)

        def PSK(i):
            return ("ps", i)

        def MM(out, lhsT, rhs, start, stop, reads, writes):
            P.op("pe", lambda e: e.matmul(out, lhsT=lhsT, rhs=rhs, start=start, stop=stop), reads, writes)

        def TR(out, in_, reads, writes):
            P.op("pe", lambda e: e.transpose(out=out, in_=in_, identity=ident), list(reads) + ["ident"], writes)

        def ACT(out, in_, func, reads, writes, **kw):
            P.op("act", lambda e: e.activation(out=out, in_=in_, func=func, **kw), reads, writes)

        def TT(eng, out, in0, in1, op, reads, writes):
            P.op(eng, lambda e: e.tensor_tensor(out=out, in0=in0, in1=in1, op=op), reads, writes)

        def TS(eng, out, in0, s1, s2, op0, op1, reads, writes):
            if s2 is None:
                P.op(eng, lambda e: e.tensor_scalar(out=out, in0=in0, scalar1=s1, scalar2=None, op0=op0), reads, writes)
            else:
                P.op(eng, lambda e: e.tensor_scalar(out=out, in0=in0, scalar1=s1, scalar2=s2, op0=op0, op1=op1), reads, writes)

        def STT(out, in0, scalar, in1, op0, op1, reads, writes):
            P.op("dve", lambda e: e.scalar_tensor_tensor(out=out, in0=in0, scalar=scalar, in1=in1, op0=op0, op1=op1),
                 reads, writes)

        def CP(eng, out, in_, reads, writes):
            P.op(eng, lambda e: e.tensor_copy(out=out, in_=in_), reads, writes)

        def MS(eng, ap, val, reads, writes):
            P.op(eng, lambda e: e.memset(ap, val), reads, writes)

        def DMA(eng, semkey, out, in_, reads, writes):
            P.dma(eng, semkey, lambda e: e.dma_start(out=out, in_=in_), reads, writes)

        def load_w(dst_view, src_view, key):
            DMA("pool", ("w", key), dst_view, src_view, [], [key])

        ident = A.alloc([128], BF16)
        bones = A.alloc([128], BF16)
        tri = A.alloc([128], F32)
        aones = A.alloc([128], F32)
        gqc = A.alloc([1], F32)
        gkc = A.alloc([1], F32)
        bfb = A.alloc([8], F32)
        gpm = A.alloc([D], F32)
        attnT = A.alloc([4, TOWN], BF16)
        NSTG = 4
        stg = [A.alloc([8, 512], BF16) for _ in range(NSTG)]
        xt = [A.alloc([D], F32) for _ in range(2)]
        NXN = 4
        xn = [A.alloc([D], BF16) for _ in range(NXN)]
        stt_ = [A.alloc([8], F32) for _ in range(NXN)]

        MS("pool", ident, 0.0, [], ["ident"])
        P.op("pool", lambda e: e.affine_select(out=ident, in_=ident, pattern=[[-1, 128]], compare_op=ALU.not_equal,
                                               fill=1.0, base=0, channel_multiplier=1),
             reads=["ident"], writes=["ident"])
        MS("pool", bones, 0.0, [], ["bones"])
        MS("pool", bones[0:64, 0:64], 1.0, ["bones"], ["bones"])
        MS("pool", bones[64:128, 64:128], 1.0, ["bones"], ["bones"])
        MS("pool", aones, 1.0, [], ["aones"])
        MS("pool", tri, 1.0, [], ["tri"])
        P.op("pool", lambda e: e.affine_select(out=tri, in_=tri, pattern=[[1, 128]], compare_op=ALU.is_ge,
                                               fill=0.0, base=0, channel_multiplier=-1),
             reads=["tri"], writes=["tri"])
        DMA("sp", "c0", gqc, gqc_d, [], ["gqc"])
        DMA("sp", "c1", gkc, gkc_d, [], ["gkc"])
        DMA("sp", "c2", bfb, bfb_d, [], ["bfb"])
        DMA("sp", "c3", gpm, gpm_d, [], ["gpm"])

        gsg = A.alloc([512], F32)
        bsg = A.alloc([512], F32)
        wsT = A.alloc([8, 128], BF16)
        bsp = A.alloc([8], F32)
        wtmp8, wtmpb8 = [], []

        def prep_mix_consts_a():
            DMA("sp", "c7", gsg, gsg_d, [], ["gsg"])
            DMA("sp", "c8", bsg, bsg_d, [], ["bsg"])
            DMA("sp", "c9", bsp, bsp_d, [], ["bsp"])
            for g in range(8):
                DMA("pool", ("c10", g), wtmpb8[g], wsp_d[g], [], [("wtmpb", g)])
                MS("dve", wtmpb8[g][0:64, 64:128], 0.0, [("wtmpb", g)], [("wtmpb", g)])

        def prep_mix_consts_b():
            for g in range(8):
                w_ = g % 2
                TR(psB[w_][:, 0:128], wtmpb8[g], [("wtmpb", g)], [PSK(6 + w_)])
                CP("dve", wsT[:, g, :], psB[w_][:, 0:128], [PSK(6 + w_)], ["wsT"])

        xctr = [0]
        psb_ctr = [0]

        xnc = [0]

        def nt_a(src_rows, gbc, gkey, eps_scale=1.0, keep=None):
            if keep is None:
                b = xctr[0] % len(xt)
                xctr[0] += 1
                x_t, xk = xt[b], ("xt", b)
                DMA("sp", ("xt", b), x_t, src_rows, [], [xk])
            else:
                x_t, xk = keep
            xi = nt_a1(x_t, xk, eps_scale)
            nt_a2(xi, x_t, xk, gbc, gkey)
            return xi

        def nt_a1(x_t, xk, eps_scale=1.0):
            xi = xnc[0] % NXN
            xnc[0] += 1
            x_n, nk = xn[xi], ("xn", xi)
            s_t, sk = stt_[xi], ("st", xi)
            ACT(x_n, x_t, AF.Square, [xk], [nk, sk], accum_out=s_t[:, 0:1])
            ACT(s_t[:, 1:2], s_t[:, 0:1], AF.Ln, [sk], [sk], scale=1.0 / D, bias=EPS * eps_scale)
            ACT(s_t[:, 2:3], s_t[:, 1:2], AF.Exp, [sk], [sk], scale=-0.5)
            return xi

        def nt_a2(xi, x_t, xk, gbc, gkey):
            x_n, nk = xn[xi], ("xn", xi)
            s_t, sk = stt_[xi], ("st", xi)
            STT(x_n, x_t, s_t[:, 2:3], gbc, ALU.mult, ALU.mult, [xk, sk, gkey], [nk])

        def nt_b(xi, dst_view, dst_key, on_act=False):
            x_n, nk = xn[xi], ("xn", xi)
            pb = psb_ctr[0] % 2
            psb_ctr[0] += 1
            pst = psB[pb]
            for c in range(8):
                TR(pst[:, c * 128:(c + 1) * 128], x_n[:, c * 128:(c + 1) * 128], [nk], [PSK(6 + pb)])
            if on_act:
                ACT(dst_view, pst.rearrange("p (c t) -> p c t", c=8), AF.Copy, [PSK(6 + pb)], [dst_key])
            else:
                CP("dve", dst_view, pst.rearrange("p (c t) -> p c t", c=8), [PSK(6 + pb)], [dst_key])

        m_l1 = A.mark()
        KT = A.alloc([4, SEQ], BF16)
        Vaug = A.alloc([NB, 8, 65], BF16)
        QT = A.alloc([4, TOWN], BF16)
        zf = A.alloc([NB, 8], F32)
        Cc = A.alloc([NB, 8], F32)
        Eb = A.alloc([NB + 1, 8], F32)
        msk = A.alloc([NS, 256], BF16)
        m_l2 = A.mark()
        hTg = [A.alloc([8, 512], BF16) for _ in range(2)]
        sq = [A.alloc([512], BF16) for _ in range(3)]
        rs = [A.alloc([512], F32) for _ in range(3)]
        wv2 = A.alloc([8, 264], BF16)
        xt.append(A.alloc([D], F32))
        xt.append(A.alloc([D], F32))
        for _g in range(8):
            wtmpb8.append(A.alloc([128], BF16))

        MS("pool", Vaug.rearrange("p a b c -> p (a b) c")[:, :, 64:65], 1.0, [], ["vones"])

        load_w(stg[0], w_in_v[:, :, K_OFF:K_OFF + 512], ("stg", 0))
        load_w(stg[1][:, :, 0:256], w_in_v[:, :, V_OFF:V_OFF + 256], ("stg", 1))
        load_w(wv2, w_in_v[:, :, V_OFF + 256:V_OFF + 520], "wv2")
        def late_loads():
            load_w(stg[2], w_in_v[:, :, Q_OFF:Q_OFF + 512], ("stg", 2))
            load_w(msk.rearrange("p a b -> p (a b)"), msk_d, "msk")

        fm_ctr = [0]

        def proj_fm_1(wview, wkey, p, hT, hkey):
            i = fm_ctr[0] % 3
            fm_ctr[0] += 1
            ps = psF[i]
            for c in range(8):
                MM(ps[:, :], wview[:, c, p * 128:(p + 1) * 128], hT[:, c, :], c == 0, c == 7, [wkey, hkey], [PSK(i)])
            ACT(sq[i], ps[:, :], AF.Square, [PSK(i)], [("sq", i)])
            return i

        def proj_fm_2(i, dst, dkey, gcol, gckey):
            ps, ps2 = psF[i], psF[3]
            MM(ps2[:, :], bones, sq[i], True, True, ["bones", ("sq", i)], [PSK(3)])
            ACT(rs[i], ps2[:, :], AF.Ln, [PSK(3)], [("rs", i)], scale=1.0 / DH, bias=EPS)
            ACT(rs[i], rs[i], AF.Exp, [("rs", i)], [("rs", i)], scale=-0.5)
            STT(dst, ps[:, :], gcol[:, 0:1], rs[i], ALU.mult, ALU.mult, [PSK(i), ("rs", i), gckey], [dkey])

        NG = 12
        xis = {}

        def ab_s1a(g):
            src = xseq if g < 8 else xown
            g0 = g if g < 8 else g - 8
            xis[g] = [nt_a(src[(g0 * 4 + bl) * 128:(g0 * 4 + bl + 1) * 128, :], gpm, "gpm") for bl in range(4)]

        def ab_s1b(g):
            hb = g % 2
            for bl in range(4):
                nt_b(xis[g][bl], hTg[hb][:, :, bl * 128:(bl + 1) * 128], ("hTg", hb))

        def ab_s2(g):
            hb = g % 2
            if g < 8:
                grp = g
                for p in range(4):
                    bl = p
                    ii = proj_fm_1(stg[0], ("stg", 0), p, hTg[hb], ("hTg", hb))
                    if p >= 1:
                        proj_fm_2(prev[0], *prev[1])
                    prev = (ii, (KT[:, p, grp * 512:(grp + 1) * 512], ("KT", p, grp), gkc, "gkc"))
                    blk = grp * 4 + bl
                    for c in range(8):
                        MM(psF[4][:, 0:256], hTg[hb][:, c, bl * 128:(bl + 1) * 128], stg[1][:, c, 0:256], c == 0, c == 7,
                           [("hTg", hb), ("stg", 1)], [PSK(4)])
                    for c in range(8):
                        MM(psF[5][:, 0:264], hTg[hb][:, c, bl * 128:(bl + 1) * 128], wv2[:, c, :], c == 0, c == 7,
                           [("hTg", hb), "wv2"], [PSK(5)])
                    ACT(Vaug[:, blk, 0:4, 0:64], psF[4][:, 0:256].rearrange("p (h d) -> p h d", h=4), AF.Copy,
                        [PSK(4)], [("V", blk)])
                    ACT(Vaug[:, blk, 4:8, 0:64], psF[5][:, 0:256].rearrange("p (h d) -> p h d", h=4), AF.Copy,
                        [PSK(5), ("V", blk)], [("V", blk)])
                    TT("dve", zf[:, blk, :], psF[5][:, 256:264], bfb, ALU.add, [PSK(5), "bfb"], ["zf"])
                return prev
            else:
                grp = g - 8
                for p in range(4):
                    ii = proj_fm_1(stg[2], ("stg", 2), p, hTg[hb], ("hTg", hb))
                    if p >= 1:
                        proj_fm_2(prev[0], *prev[1])
                    prev = (ii, (QT[:, p, grp * 512:(grp + 1) * 512], ("QT", p, grp), gqc, "gqc"))
                return prev

        zf2 = zf.rearrange("p a b -> p (a b)")
        Cc2 = Cc.rearrange("p a b -> p (a b)")

        def phase_c1():
            ACT(zf2, zf2, AF.Exp, ["zf"], ["zf"], scale=-1.0)
            ACT(zf2, zf2, AF.Ln, ["zf"], ["zf"], bias=1.0)

        def phase_c2():
            MM(psF[0][:, 0:256], tri, zf2, True, True, ["tri", "zf"], [PSK(0)])
            MM(psF[1][:, 0:256], aones, zf2, True, True, ["aones", "zf"], [PSK(1)])
            MS("dve", Eb[:, 0, :], 0.0, [], ["Eb"])
            CP("dve", Cc2, psF[1][:, 0:256], [PSK(1)], ["Cc"])
            for j in range(1, NB + 1):
                TT("dve", Eb[:, j, :], Eb[:, j - 1, :], Cc[:, j - 1, :], ALU.add, ["Eb", "Cc"], ["Eb"])
            TT("dve", Cc2, psF[0][:, 0:256], Eb[:, 0:NB, :].rearrange("p a b -> p (a b)"), ALU.add, [PSK(0), "Eb"], ["Cc"])

        for t in range(NG + 1):
            if t == 2:
                late_loads()
            if t == 4:
                prep_mix_consts_a()
            if t == 5:
                prep_mix_consts_b()
            if t < NG:
                ab_s1a(t)
            last = ab_s2(t - 1) if t >= 1 else None
            if t < NG:
                ab_s1b(t)
            if last is not None:
                proj_fm_2(last[0], *last[1])
            if t == 8:
                phase_c1()
            if t == 9:
                phase_c2()

        def dump(items):
            P.barrier()
            mk_ = A.mark()
            dv = xn[0].bitcast(F32)
            for nm, src, n in items:
                for o in range(0, n, 256):
                    m = min(256, n - o)
                    CP("dve", dv[:, 0:m], src[:, o:o + m], ["dbgsrc"], ["dv"])
                    DMA("sp", "dbg", dbg[nm][:, o:o + m], dv[:, 0:m], ["dv"], ["dbgout"])
            P.barrier()
            A.release(mk_)

        def finish():
            P.final_wait("sp", [("out", 0), ("out", 1), "dbg"])
            P.emit()

        if debug:
            dump([("d_kt", KT.rearrange("p a b -> p (a b)"), 4 * SEQ), ("d_qt", QT.rearrange("p a b -> p (a b)"), 4 * TOWN),
                  ("d_v", Vaug.rearrange("p a b c -> p (a b c)"), NB * 8 * 65), ("d_c", Cc.rearrange("p a b -> p (a b)"), 256)])
            if upto == 1:
                finish()
                return nc

        hs = [stg[k // 2][:, :, (k % 2) * 256:(k % 2 + 1) * 256] for k in range(8)]

        def e2_b0(ch, wb):
            even = (ch % 2 == 0)
            if wb == 0:
                return 4 if even else 0
            return 0 if even else 4

        def e2_chunk_loads(ch, wb=0):
            b0 = e2_b0(ch, wb)
            c0 = ch * 256
            s0, s1_ = b0 // 2, b0 // 2 + 1
            load_w(hs[b0][:, 0:4, :], w_a_v[:, :, c0:c0 + 256], ("sa", s0))
            load_w(hs[b0][:, 4:8, :], w_b_v[:, :, c0:c0 + 256], ("sb", s0))
            load_w(hs[b0 + 1], w_in_v[:, :, G_OFF + c0:G_OFF + c0 + 256], ("sc", s0))
            load_w(hs[b0 + 2][:, 0:4, :], w_in_v[:, 0:4, G_OFF + 1024 + c0:G_OFF + 1024 + c0 + 256], ("sa", s1_))
            load_w(hs[b0 + 2][:, 4:8, :], w_in_v[:, 4:8, G_OFF + 1024 + c0:G_OFF + 1024 + c0 + 256], ("sb", s1_))

        pre_x = {}

        def e1_prefetch_x(half_):
            for sl_ in range(2):
                gs_ = half_ * 8 + sl_
                DMA("sp", ("xt", sl_), xt[sl_], xown[gs_ * 128:(gs_ + 1) * 128, :], [], [("xt", sl_)])
                pre_x[(half_, sl_)] = sl_

        def e1_loads(wb=0):
            load_w(stg[wb], w_in_v[:, :, U_OFF:U_OFF + 512], ("stg", wb))
            load_w(stg[wb + 1], w_in_v[:, :, U_OFF + 512:U_OFF + 1024], ("stg", wb + 1))

        def f_loads(fq, foff=0):
            nfc = 4 if fq < 5 else 2
            sg_, su_ = (2 * fq + foff) % NSTG, (2 * fq + 1 + foff) % NSTG
            load_w(stg[sg_][:, :, 0:nfc * 128], w_ffi_v[:, :, fq * 512:fq * 512 + nfc * 128], ("stg", sg_))
            load_w(stg[su_][:, :, 0:nfc * 128], w_ffi_v[:, :, DFF + fq * 512:DFF + fq * 512 + nfc * 128], ("stg", su_))

        P.barrier()
        A.release(m_l2)
        del xt[2:]
        pT = [A.alloc([NB * 128], BF16) for _ in range(2)]
        Vp = [A.alloc([NB, 65], BF16) for _ in range(4)]
        wS = [A.alloc([NB, 8], F32) for _ in range(2)]
        wB = [A.alloc([NB, 8], F32) for _ in range(2)]
        atok = [A.alloc([512], BF16) for _ in range(2)]
        rec = [A.alloc([8], F32) for _ in range(2)]

        units = [(s, p) for s in range(NS) for p in range(4)]
        items = []
        for ui, (s, p) in enumerate(units):
            nb_ = J_of(s) + 1
            for c0 in range(0, nb_, 4):
                items.append((ui, c0))
        cc = [0]

        NSET = 3
        LOOK = 6

        def acc_banks(s):
            return (psBf[0], PSK(6)), (psBf[1], PSK(7))

        def att_prep(ui):
            s, p = units[ui]
            J = J_of(s)
            nb = J + 1
            sb_ = s % 2
            if p == 0:
                TT("dve", wB[sb_][:, 0:nb, :], Cc[:, 0:nb, :], Eb[:, J:J + 1, :].to_broadcast([128, nb, 8]), ALU.subtract,
                   ["Cc", "Eb"], [("wB", sb_)])
                ACT(wS[sb_][:, 0:nb, :], wB[sb_][:, 0:nb, :], AF.Exp, [("wB", sb_)], [("wS", sb_)])
            for ab in range(2):
                h = 2 * p + ab
                vi = 2 * (ui % 2) + ab
                TT("dve", Vp[vi][:, 0:nb, :], Vaug[:, 0:nb, h, :], wS[sb_][:, 0:nb, h:h + 1].to_broadcast([128, nb, 65]), ALU.mult,
                   [("V", j) for j in range(nb)] + ["vones", ("wS", sb_)], [("Vp", vi)])

        def att_qk(ui, c0):
            s, p = units[ui]
            J = J_of(s)
            nb = J + 1
            n = min(4, nb - c0)
            set_ = cc[0] % NSET
            cc[0] += 1
            for jj in range(n):
                j = c0 + jj
                for ab in range(2):
                    r0 = ab * 64
                    bk = 2 * set_ + ab
                    MM(psF[bk][:, jj * 128:(jj + 1) * 128], KT[r0:r0 + 64, p, j * 128:(j + 1) * 128],
                       QT[r0:r0 + 64, p, s * 128:(s + 1) * 128], True, True,
                       [("KT", p, j // 4), ("QT", p, s // 4)], [PSK(bk)])
            for ab in range(2):
                bk = 2 * set_ + ab
                ACT(pT[ab][:, c0 * 128:(c0 + n) * 128], psF[bk][:, 0:n * 128], AF.Exp, [PSK(bk)], [("pT", ab, c0 // 4)], scale=0.125)
            if c0 + n == nb:
                for ab in range(2):
                    TT("pool", pT[ab][:, (J - 1) * 128:(J + 1) * 128], pT[ab][:, (J - 1) * 128:(J + 1) * 128], msk[:, s, :], ALU.mult,
                       [("pT", ab, c0 // 4), "msk"], [("pT", ab, c0 // 4)])

        def att_pv(ui, c0):
            s, p = units[ui]
            J = J_of(s)
            nb = J + 1
            n = min(4, nb - c0)
            banks = acc_banks(s)
            for ab in range(2):
                acc, ak = banks[ab]
                vi = 2 * (ui % 2) + ab
                for jj in range(n):
                    j = c0 + jj
                    MM(acc[:, p * 65:(p + 1) * 65], pT[ab][:, j * 128:(j + 1) * 128], Vp[vi][:, j, :], j == 0, j == J,
                       [("pT", ab, c0 // 4), ("Vp", vi)], [ak])
            if c0 + n == nb and p == 3:
                tb = s % 2
                for ab in range(2):
                    acc, ak = banks[ab]
                    a3 = acc[:, 0:260].rearrange("p (h d) -> p h d", h=4)
                    rk = ("rec", ab)
                    P.op("dve", lambda e, o_=rec[ab][:, 0:4], i_=a3[:, :, 64]: e.reciprocal(out=o_, in_=i_), [ak], [rk])
                    TT("dve", atok[tb].rearrange("p (q two d) -> p q two d", two=2, d=64)[:, :, ab, :], a3[:, :, 0:64],
                       rec[ab][:, 0:4].unsqueeze(2).to_broadcast([128, 4, 64]), ALU.mult, [ak, rk], [("atok", tb)])
                deferred.append((s, tb))

        deferred = []

        def att_fin2():
            while deferred:
                s, tb = deferred.pop(0)
                set_ = cc[0] % NSET
                cc[0] += 1
                bk = 2 * set_
                ptr = psF[bk][:, :].bitcast(BF16)
                for c in range(4):
                    TR(ptr[:, c * 128:(c + 1) * 128], atok[tb][:, c * 128:(c + 1) * 128], [("atok", tb)], [PSK(bk)])
                CP("dve", attnT[:, :, s * 128:(s + 1) * 128], ptr[:, 0:512].rearrange("p (c t) -> p c t", c=4),
                   [PSK(bk)], [("attnT", s)])

        pending = []
        last_unit = [-1]
        for k in range(len(items)):
            while pending and (len(pending) > LOOK or any(items[q][1] == items[k][1] for q in pending)):
                had = bool(deferred)
                att_pv(*items[pending.pop(0)])
                if had:
                    att_fin2()
            if items[k][0] != last_unit[0]:
                att_prep(items[k][0])
                last_unit[0] = items[k][0]
            att_qk(*items[k])
            pending.append(k)
            if k == 40:
                e1_loads()
                e2_chunk_loads(0)
            if k == len(items) - 12:
                e1_prefetch_x(0)
        while pending:
            att_pv(*items[pending.pop(0)])
        att_fin2()

        if debug:
            dump([("d_at", attnT.rearrange("p a b -> p (a b)"), 4 * TOWN)])
            if upto == 2:
                finish()
                return nc

        P.barrier()
        A.release(m_l1)
        gpo = A.alloc([D], F32)
        gpf = A.alloc([D], F32)
        gpff = A.alloc([D], F32)
        x1 = A.alloc([8, D], F32)
        h2T = A.alloc([8, 1024], BF16)
        DMA("sp", "c4", gpo, gpo_d, [], ["gpo"])
        DMA("sp", "c5", gpf, gpf_d, [], ["gpf"])
        DMA("sp", "c6", gpff, gpff_d, [], ["gpff"])
        m_l2b = A.mark()

        for half in range(2):
            def hk(nm, half=half):
                return (nm, half)
            wb = 0 if half == 0 else 2
            wo = 2 - wb

            A.release(m_l2b)
            hTo = A.alloc([8, 1024], BF16)
            sguT = A.alloc([4, 1024], BF16)
            mT = A.alloc([8, 1024], BF16)
            lst = [A.alloc([8], F32) for _ in range(3)]
            del xt[2:]
            xt.append(A.alloc([D], F32))
            wkb = [A.alloc([512], F32) for _ in range(12)]

            def WK(i):
                return hk(("wk", i))
            t1 = [[wkb[0], wkb[1]], [wkb[2], wkb[3]]]
            t1k = [[WK(0), WK(1)], [WK(2), WK(3)]]
            gl = [[wkb[4], wkb[5]], [wkb[6], wkb[7]]]
            glk = [[WK(4), WK(5)], [WK(6), WK(7)]]
            vn = [wkb[8].bitcast(BF16)[:, 0:512], wkb[9].bitcast(BF16)[:, 0:512]]
            vnk = [WK(8), WK(9)]
            sgu = [wkb[10].bitcast(BF16)[:, 0:512], wkb[11].bitcast(BF16)[:, 0:512]]
            sguk = [WK(10), WK(11)]
            tg2 = [[wkb[0], wkb[1]], [wkb[2], wkb[3]]]
            m122 = [[wkb[4], wkb[5]], [wkb[6], wkb[7]]]

            if half == 1:
                e2_chunk_loads(0, wb)
            e_x = {}
            SQC = 0.21145921592026346

            def e1_s0a_act(sl):
                gs = half * 8 + sl
                if (half, sl) in pre_x:
                    b = pre_x[(half, sl)]
                    xctr[0] = b + 1
                else:
                    b = xctr[0] % len(xt)
                    xctr[0] += 1
                    DMA("sp", ("xt", b), xt[b], xown[gs * 128:(gs + 1) * 128, :], [], [("xt", b)])
                e_x[sl] = (nt_a1(xt[b], ("xt", b)), b)

            def e1_s0a_dve(sl):
                xi, b = e_x[sl]
                nt_a2(xi, xt[b], ("xt", b), gpm, "gpm")

            def e1_s0b(sl):
                nt_b(e_x[sl][0], hTo[:, :, sl * 128:(sl + 1) * 128], hk(("hTo", sl)), on_act=True)

            def e1_s1_pe(sl):
                par = sl % 2
                for which in range(2):
                    ps, pk = psF[2 * par + which], PSK(2 * par + which)
                    for c in range(8):
                        MM(ps[:, :], hTo[:, c, sl * 128:(sl + 1) * 128], stg[wb + which][:, c, :], c == 0, c == 7,
                           [hk(("hTo", sl)), ("stg", wb + which)], [pk])
                for which in range(2):
                    ps, pk = psF[2 * par + which], PSK(2 * par + which)
                    tt_, tk = t1[par][which], t1k[par][which]
                    ACT(tt_, ps[:, :], AF.Square, [pk], [tk], scale=SQC)

            def e1_s1_inner(sl):
                par = sl % 2
                for which in range(2):
                    ps, pk = psF[2 * par + which], PSK(2 * par + which)
                    tt_, tk = t1[par][which], t1k[par][which]
                    STT(tt_, tt_, 1.0, ps[:, :], ALU.add, ALU.mult, [tk, pk], [tk])

            def e1_s1_exp(sl):
                par = sl % 2
                for which in range(2):
                    tt_, tk = t1[par][which], t1k[par][which]
                    ACT(tt_, tt_, AF.Exp, [tk], [tk], scale=-2.0 * GELU_C)
                for which in range(2):
                    tt_, tk = t1[par][which], t1k[par][which]
                    ACT(tt_, tt_, AF.Ln, [tk], [tk], bias=1.0)
                for which in range(2):
                    tt_, tk = t1[par][which], t1k[par][which]
                    ACT(tt_, tt_, AF.Exp, [tk], [tk], scale=-1.0)

            def e1_s1_gl(sl):
                par = sl % 2
                l_, lk = lst[par], hk(("lst", par))
                for which in range(2):
                    ps, pk = psF[2 * par + which], PSK(2 * par + which)
                    tt_, tk = t1[par][which], t1k[par][which]
                    if which == 0:
                        STT(gl[par][which], tt_, 2.0, ps[:, :], ALU.mult, ALU.mult, [tk, pk], [glk[par][which]])
                    else:
                        P.op("dve", lambda e, o_=gl[par][1], t_=tt_, p_=ps[:, :], a_=l_[:, 0:1]: e.scalar_tensor_tensor(
                            out=o_, in0=t_, scalar=2.0, in1=p_, op0=ALU.mult, op1=ALU.mult, accum_out=a_),
                             [tk, pk], [glk[par][1], lk])

            def e1_s2(sl):
                par = sl % 2
                l_, lk = lst[par], hk(("lst", par))
                v_, vk = gl[par][1], glk[par][1]
                j_, jk = t1[par][1], t1k[par][1]
                ACT(j_, v_, AF.Square, [vk], [jk, lk], accum_out=l_[:, 1:2])
                TS("dve", l_[:, 2:3], l_[:, 0:1], 1.0 / 512, None, ALU.mult, None, [lk], [lk])
                TT("dve", l_[:, 3:4], l_[:, 2:3], l_[:, 2:3], ALU.mult, [lk], [lk])
                STT(l_[:, 4:5], l_[:, 1:2], 1.0 / 512, l_[:, 3:4], ALU.mult, ALU.subtract, [lk], [lk])
                ACT(l_[:, 5:6], l_[:, 4:5], AF.Ln, [lk], [lk], bias=4.0 * EPS)
                ACT(l_[:, 6:7], l_[:, 5:6], AF.Exp, [lk], [lk], scale=-0.5)
                TS("dve", v_, v_, l_[:, 2:3], l_[:, 6:7], ALU.subtract, ALU.mult, [vk, lk], [vk])
                TT("dve", v_, v_, gsg, ALU.mult, [vk, "gsg"], [vk])
                TT("dve", vn[par], v_, bsg, ALU.add, [vk, "bsg"], [vnk[par]])

            def e1_s3(sl):
                par = sl % 2
                pm, pmk = psF[4 + par], PSK(4 + par)
                for g in range(8):
                    MM(pm[:, g * 64:(g + 1) * 64], wsT[:, g, :], vn[par][:, g * 64:(g + 1) * 64], True, True,
                       ["wsT", vnk[par]], [pmk])
                s1_, s1k = wkb[8 + par], vnk[par]
                TT("dve", s1_.rearrange("p (g c) -> p g c", g=8), pm.rearrange("p (g c) -> p g c", g=8),
                   bsp.unsqueeze(2).to_broadcast([128, 8, 64]), ALU.add, [pmk, "bsp"], [s1k])
                TT("dve", sgu[par], s1_, gl[par][0], ALU.mult, [s1k, glk[par][0]], [sguk[par]])

            def e1_s4(sl):
                par = sl % 2
                pb = psb_ctr[0] % 2
                psb_ctr[0] += 1
                for c in range(4):
                    TR(psB[pb][:, c * 128:(c + 1) * 128], sgu[par][:, c * 128:(c + 1) * 128], [sguk[par]], [PSK(6 + pb)])
                ACT(sguT[:, :, sl * 128:(sl + 1) * 128], psB[pb][:, 0:512].rearrange("p (c t) -> p c t", c=4), AF.Copy,
                    [PSK(6 + pb)], [hk(("sguT", sl // 4))])

            for t in range(8 + 4):
                s1ok = 0 <= t - 1 < 8
                if t < 8:
                    e1_s0a_act(t)
                if s1ok:
                    e1_s1_pe(t - 1)
                if t < 8:
                    e1_s0a_dve(t)
                if s1ok:
                    e1_s1_inner(t - 1)
                if t < 8:
                    e1_s0b(t)
                if s1ok:
                    e1_s1_exp(t - 1)
                if 0 <= t - 4 < 8:
                    e1_s4(t - 4)
                if 0 <= t - 3 < 8:
                    e1_s3(t - 3)
                if 0 <= t - 2 < 8:
                    e1_s2(t - 2)
                if s1ok:
                    e1_s1_gl(t - 1)

            for sl in range(8):
                gs = half * 8 + sl
                DMA("sp", ("x1ld", sl), x1[:, sl, :], xown[gs * 128:(gs + 1) * 128, :], [], [("x1", half, sl)])

            units2 = [(ch, tt, o) for ch in range(4) for tt in range(2) for o in range(2)]

            def e2_s1(ui):
                ch, tt, o = units2[ui]
                if tt == 0 and o == 0 and ch + 1 < 4:
                    e2_chunk_loads(ch + 1, wb)
                if ch == 3 and tt == 0 and o == 0:
                    load_w(stg[wo], w_out_v[:, :, 0:512], ("stg", wo))
                    load_w(stg[wo + 1], w_out_v[:, :, 512:1024], ("stg", wo + 1))
                b0 = e2_b0(ch, wb)
                s0, s1_ = b0 // 2, b0 // 2 + 1
                k0a, k0b, k1 = ("sa", s0), ("sb", s0), ("sc", s0)
                k2 = [("sa", s1_), ("sb", s1_)]
                pset = 4 * (ui % 2)
                tcols = slice(tt * 512, (tt + 1) * 512)
                slh = slice(half * 1024 + tt * 512, half * 1024 + (tt + 1) * 512)
                akeys = [("attnT", s_) for s_ in range(half * 8 + tt * 4, half * 8 + tt * 4 + 4)]
                hkeys = [hk(("hTo", s_)) for s_ in range(tt * 4, tt * 4 + 4)]
                ocols = slice(o * 128, (o + 1) * 128)
                bank = lambda i: (psF[pset + i] if pset + i < 6 else psBf[pset + i - 6])
                for kc in range(4):
                    MM(bank(0)[:, :], hs[b0][:, kc, ocols], attnT[:, kc, slh], kc == 0, kc == 3, [k0a] + akeys, [PSK(pset)])
                for c in range(8):
                    MM(bank(1)[:, :], hs[b0 + 1][:, c, ocols], hTo[:, c, tcols], c == 0, c == 7, [k1] + hkeys, [PSK(pset + 1)])
                for kc in range(4):
                    MM(bank(2)[:, :], hs[b0][:, 4 + kc, ocols], sguT[:, kc, tcols], kc == 0, kc == 3,
                       [k0b, hk(("sguT", tt))], [PSK(pset + 2)])
                for c in range(8):
                    MM(bank(3)[:, :], hs[b0 + 2][:, c, ocols], hTo[:, c, tcols], c == 0, c == 7, k2 + hkeys, [PSK(pset + 3)])

            def e2_s2(ui):
                ch, tt, o = units2[ui]
                par = ui % 2
                pset = 4 * par
                oc = ch * 2 + o
                tcols = slice(tt * 512, (tt + 1) * 512)
                bank = lambda i: (psF[pset + i] if pset + i < 6 else psBf[pset + i - 6])
                ta, tak = tg2[par][0], WK(2 * par)
                tb_, tbk = tg2[par][1], WK(2 * par + 1)
                m1, m1k = m122[par][0], WK(4 + 2 * par)
                m2, m2k = m122[par][1], WK(5 + 2 * par)
                ACT(ta, bank(1)[:, :], AF.Exp, [PSK(pset + 1)], [tak], scale=-1.0)
                ACT(tb_, bank(3)[:, :], AF.Exp, [PSK(pset + 3)], [tbk], scale=-1.0)
                ACT(ta, ta, AF.Ln, [tak], [tak], bias=1.0)
                ACT(tb_, tb_, AF.Ln, [tbk], [tbk], bias=1.0)
                ACT(ta, ta, AF.Exp, [tak], [tak], scale=-1.0)
                ACT(tb_, tb_, AF.Exp, [tbk], [tbk], scale=-1.0)
                TT("dve", m1, ta, bank(0)[:, :], ALU.mult, [tak, PSK(pset)], [m1k])
                TT("dve", m2, tb_, bank(2)[:, :], ALU.mult, [tbk, PSK(pset + 2)], [m2k])
                STT(mT[:, oc, tcols], m1, 2.0, m2, ALU.mult, ALU.add, [m1k, m2k], [hk(("mT", tt))])

            for t in range(len(units2) + 1):
                if t >= 1:
                    e2_s2(t - 1)
                if t < len(units2):
                    e2_s1(t)

            e3_xi = {}

            def zt_view(k):
                return (wkb[2 * k], wkb[2 * k + 1]), (WK(2 * k), WK(2 * k + 1))

            def e3_s1(sl):
                k = sl % 3
                tt = sl // 4
                (z0, z1), (zk0, zk1) = zt_view(k)
                l_, lk = lst[k], hk(("lst", k))
                for hf in range(2):
                    for c in range(8):
                        MM(psF[2 * k + hf][:, :], mT[:, c, sl * 128:(sl + 1) * 128], stg[wo + hf][:, c, :], c == 0, c == 7,
                           [hk(("mT", tt)), ("stg", wo + hf)], [PSK(2 * k + hf)])
                ACT(z0, psF[2 * k][:, :], AF.Square, [PSK(2 * k)], [zk0, lk], accum_out=l_[:, 0:1])
                ACT(z1, psF[2 * k + 1][:, :], AF.Square, [PSK(2 * k + 1)], [zk1, lk], accum_out=l_[:, 1:2])

            def e3_s2(sl):
                k = sl % 3
                l_, lk = lst[k], hk(("lst", k))
                TT("dve", l_[:, 2:3], l_[:, 0:1], l_[:, 1:2], ALU.add, [lk], [lk])
                ACT(l_[:, 3:4], l_[:, 2:3], AF.Ln, [lk], [lk], scale=1.0 / D, bias=4.0 * EPS)
                ACT(l_[:, 4:5], l_[:, 3:4], AF.Exp, [lk], [lk], scale=-0.5)

            def e3_s3(sl):
                k = sl % 3
                (z0, z1), (zk0, zk1) = zt_view(k)
                l_, lk = lst[k], hk(("lst", k))
                xk_ = ("x1", half, sl)
                STT(z0, psF[2 * k][:, :], l_[:, 4:5], gpo[:, 0:512], ALU.mult, ALU.mult, [PSK(2 * k), lk, "gpo"], [zk0])
                STT(z1, psF[2 * k + 1][:, :], l_[:, 4:5], gpo[:, 512:1024], ALU.mult, ALU.mult,
                    [PSK(2 * k + 1), lk, "gpo"], [zk1])
                TT("dve", x1[:, sl, 0:512], z0, x1[:, sl, 0:512], ALU.add, [zk0, xk_], [xk_])
                TT("dve", x1[:, sl, 512:1024], z1, x1[:, sl, 512:1024], ALU.add, [zk1, xk_], [xk_])

            def e3_s4(sl):
                e3_xi[sl] = nt_a1(x1[:, sl, :], ("x1", half, sl))

            def e3_s5(sl):
                nt_a2(e3_xi[sl], x1[:, sl, :], ("x1", half, sl), gpf, "gpf")

            def e3_s6(sl):
                nt_b(e3_xi[sl], h2T[:, :, sl * 128:(sl + 1) * 128], ("h2T", half, sl // 4), on_act=True)

            e3_stages = [e3_s1, e3_s2, e3_s3, e3_s4, e3_s5, e3_s6]
            for t in range(8 + 5):
                for si_ in range(5, -1, -1):
                    sl_ = t - si_
                    if 0 <= sl_ < 8:
                        e3_stages[si_](sl_)
                if t == 6:
                    f_loads(0, wb)

            if debug and upto == 4:
                dump([("d_x1", x1.rearrange("p a b -> p (a b)"), 8 * 1024)])
                finish()
                return nc
            P.barrier()
            A.release(m_l2b)
            del xt[2:]
            actT = A.alloc([NFC, 1024], BF16)
            ffA = A.alloc([8, 512], F32)
            fw = A.alloc([4, 512], F32)
            tgf = [fw[:, 0, :], fw[:, 2, :]]
            a1 = [fw[:, 1, :], fw[:, 3, :]]
            ot = [fw[:, 0:2, :].rearrange("p a b -> p (a b)"), fw[:, 2:4, :].rearrange("p a b -> p (a b)")]
            fst = A.alloc([8, 4], F32)
            fctr = 0
            for fq in range(6):
                nfc = 4 if fq < 5 else 2
                sg_, su_ = (2 * fq + wb) % NSTG, (2 * fq + 1 + wb) % NSTG
                if fq >= 1:
                    f_loads(fq, wb)
                for f in range(nfc):
                    fc = fq * 4 + f
                    fcols = slice(f * 128, (f + 1) * 128)
                    for tt in range(2):
                        tcols = slice(tt * 512, (tt + 1) * 512)
                        pi = fctr % 2
                        fctr += 1
                        pg, pu = psF[pi], psF[2 + pi]
                        for c in range(8):
                            MM(pg[:, :], stg[sg_][:, c, fcols], h2T[:, c, tcols], c == 0, c == 7,
                               [("stg", sg_), ("h2T", half, tt)], [PSK(pi)])
                        for c in range(8):
                            MM(pu[:, :], stg[su_][:, c, fcols], h2T[:, c, tcols], c == 0, c == 7,
                               [("stg", su_), ("h2T", half, tt)], [PSK(2 + pi)])
                        ACT(tgf[pi], pg[:, :], AF.Exp, [PSK(pi)], [hk(("fw", 2 * pi))], scale=-1.0)
                        ACT(tgf[pi], tgf[pi], AF.Ln, [hk(("fw", 2 * pi))], [hk(("fw", 2 * pi))], bias=1.0)
                        ACT(tgf[pi], tgf[pi], AF.Exp, [hk(("fw", 2 * pi))], [hk(("fw", 2 * pi))], scale=-1.0)
                        TT("dve", a1[pi], tgf[pi], pg[:, :], ALU.mult, [hk(("fw", 2 * pi)), PSK(pi)], [hk(("fw", 2 * pi + 1))])
                        TT("dve", actT[:, fc, tcols], a1[pi], pu[:, :], ALU.mult, [hk(("fw", 2 * pi + 1)), PSK(2 + pi)], [hk(("actT", tt))])

            if debug and upto == 5:
                finish()
                return nc
            gjunk = A.alloc([512], BF16)

            def bank_of(sl):
                return psF[sl] if sl < 6 else psBf[sl - 6]

            def g_keys(sl):
                ob = sl % 2
                return hk(("fw", 2 * ob)), hk(("fw", 2 * ob + 1)), hk(("fst", sl))

            def g_ev1(r, sl):
                pbank = bank_of(sl)
                ok, ok2, fk = g_keys(sl)
                ACT(gjunk, pbank[:, :], AF.Square, [PSK(sl)], [hk("gjunk"), fk], accum_out=fst[:, sl, r:r + 1])
                if r == 0:
                    TT("dve", ffA[:, sl, :], pbank[:, :], gpff[:, 0:512], ALU.mult, [PSK(sl), "gpff"], [hk(("ffA", sl))])

            def g_ev2(sl):
                ok, ok2, fk = g_keys(sl)
                TT("dve", fst[:, sl, 2:3], fst[:, sl, 0:1], fst[:, sl, 1:2], ALU.add, [fk], [fk])
                ACT(fst[:, sl, 3:4], fst[:, sl, 2:3], AF.Ln, [fk], [fk], scale=1.0 / D, bias=EPS)
                ACT(fst[:, sl, 3:4], fst[:, sl, 3:4], AF.Exp, [fk], [fk], scale=-0.5)

            def g_ev3(sl):
                gs = half * 8 + sl
                pbank = bank_of(sl)
                ob = sl % 2
                ok, ok2, fk = g_keys(sl)
                STT(ot[ob][:, 0:512], ffA[:, sl, :], fst[:, sl, 3:4], x1[:, sl, 0:512], ALU.mult, ALU.add,
                    [hk(("ffA", sl)), fk, ("x1", half, sl)], [ok, ok2])
                TT("dve", ot[ob][:, 512:1024], pbank[:, :], gpff[:, 512:1024], ALU.mult, [PSK(sl), "gpff"], [ok, ok2])
                STT(ot[ob][:, 512:1024], ot[ob][:, 512:1024], fst[:, sl, 3:4], x1[:, sl, 512:1024], ALU.mult, ALU.add,
                    [ok, ok2, fk, ("x1", half, sl)], [ok, ok2])
                DMA("sp", ("out", ob), out_d[gs * 128:(gs + 1) * 128, :], ot[ob], [ok, ok2], [("outd", gs)])

            for r in range(2):
                sis = []
                for k3 in range(3):
                    n8 = 8 if k3 < 2 else 6
                    si = (k3 + r * 3 + wb) % NSTG
                    sis.append((si, n8))
                    load_w(stg[si][:, 0:n8, :], w_ffd_v[:, k3 * 8:k3 * 8 + n8, r * 512:(r + 1) * 512], ("stg", si))
                si, n8 = sis[0]
                for f in range(n8):
                    fc = f
                    for sl in range(8):
                        pbank = psF[sl] if sl < 6 else psBf[sl - 6]
                        MM(pbank[:, :], actT[:, fc, sl * 128:(sl + 1) * 128], stg[si][:, f, :], fc == 0, fc == NFC - 1,
                           [hk(("actT", sl // 4)), ("stg", si)], [PSK(sl)])
                if r == 1 and half == 0:
                    e1_loads(2)
                    e1_prefetch_x(1)
                for sl in range(8):
                    pbank = psF[sl] if sl < 6 else psBf[sl - 6]
                    for k3 in (1, 2):
                        si, n8 = sis[k3]
                        for f in range(n8):
                            fc = k3 * 8 + f
                            MM(pbank[:, :], actT[:, fc, sl * 128:(sl + 1) * 128], stg[si][:, f, :], fc == 0, fc == NFC - 1,
                               [hk(("actT", sl // 4)), ("stg", si)], [PSK(sl)])
                    g_ev1(r, sl)
                    if r == 1:
                        if sl >= 1:
                            g_ev2(sl - 1)
                        if sl >= 2:
                            g_ev3(sl - 2)
                if r == 1:
                    g_ev2(7)
                    g_ev3(6)
                    g_ev3(7)
            P.barrier()
            if debug and upto == 6:
                finish()
                return nc

        finish()
    return nc


_NC_CACHE = {}


def _layout_inputs(inp):
    f = lambda a: np.ascontiguousarray(np.asarray(a, dtype=np.float32))
    x = f(inp["x"])
    rep = lambda v, n=128: np.ascontiguousarray(np.broadcast_to(f(v).reshape(1, -1), (n, f(v).size)))
    common = {
        "w_in": f(inp["w_in"][0]), "w_a": f(inp["w_branch_a"][0]), "w_b": f(inp["w_branch_b"][0]),
        "w_out": f(inp["w_out"][0]), "w_ffi": f(inp["w_ffn_in"][0]), "w_ffd": f(inp["w_ffn_down"][0]),
        "gpm": rep(inp["g_pre_mix"][0]), "gpo": rep(inp["g_post_mix"][0]), "gpf": rep(inp["g_pre_ffn"][0]),
        "gpff": rep(inp["g_post_ffn"][0]), "gsg": rep(inp["g_sgu"][0]), "bsg": rep(inp["b_sgu"][0]),
        "bfb": rep(inp["b_forget"][0]),
        "gqc": np.ascontiguousarray(np.tile(f(inp["g_q"][0]).reshape(64, 1), (2, 1))),
        "gkc": np.ascontiguousarray(np.tile(f(inp["g_k"][0]).reshape(64, 1), (2, 1))),
        "wsp": f(inp["w_spatial"][0]),
        "bsp": np.ascontiguousarray(f(inp["b_spatial"][0]).T),
    }
    ones = np.ones((128, 128), np.float32)
    zeros = np.zeros((128, 128), np.float32)
    tri = np.triu(np.ones((128, 128), np.float32))
    in_maps = []
    for c in range(8):
        b, par = c // 2, c % 2
        blocks = []
        mk = np.zeros((128, NS, 2, 128), np.float32)
        for s in range(NS):
            m = s // 2
            if par == 0:
                i = 4 * m if s % 2 == 0 else 4 * m + 3
            else:
                i = 4 * m + 1 if s % 2 == 0 else 4 * m + 2
            blocks.append(i)
            J = J_of(s)
            for k in range(2):
                j = J - 1 + k
                mk[:, s, k, :] = ones if j < i else (tri if j == i else zeros)
        xo = np.concatenate([x[b, i * 128:(i + 1) * 128] for i in blocks], axis=0)
        d = dict(common)
        d["xseq"] = np.ascontiguousarray(x[b])
        d["xown"] = np.ascontiguousarray(xo)
        d["msk"] = np.ascontiguousarray(mk.reshape(128, NS * 256))
        in_maps.append((d, blocks))
    return in_maps


def kernel(**inputs):
    maps = _layout_inputs(inputs)
    if "nc" not in _NC_CACHE:
        _NC_CACHE["nc"] = build_nc()
    nc = _NC_CACHE["nc"]
    res = run_bass_kernel_spmd(nc, [m[0] for m in maps], core_ids=list(range(8)))
    out = np.empty((4, SEQ, D), np.float32)
    for c in range(8):
        b = c // 2
        o = res.results[c]["out"]
        for s, i in enumerate(maps[c][1]):
            out[b, i * 128:(i + 1) * 128] = o[s * 128:(s + 1) * 128]
    return out
```

```python
import numpy as np
from contextlib import ExitStack
import concourse.bass as bass
import concourse.mybir as mybir
from concourse.bass_utils import run_bass_kernel_spmd

F32 = mybir.dt.float32
BF16 = mybir.dt.bfloat16
AF = mybir.ActivationFunctionType
ALU = mybir.AluOpType
AX = mybir.AxisListType

D = 1024
SEQ = 4096
NB = 32
NS = 16
TOWN = 2048
HEADS = 8
DH = 64
Q_OFF, K_OFF, V_OFF, F_OFF, U_OFF, G_OFF = 0, 512, 1024, 1536, 1544, 2568
IN_COLS = 4616
DFF = 2816
NFC = 22
EPS = 1e-6
GELU_C = 0.7978845608028654


def J_of(slot):
    return 4 * (slot // 2) + (1 if slot % 2 == 0 else 3)


ENGS = ["pe", "act", "dve", "pool", "sp"]


class Prog:
    def __init__(self, nc):
        self.nc = nc
        self.recs = {e: [] for e in ENGS}
        self.lastw = {}
        self.readers = {}
        self.dma_count = {}

    def _deps(self, eng, reads, writes, is_dma):
        deps = set()
        for k in reads:
            t = self.lastw.get(k)
            if t is not None:
                deps.add(t)
            if isinstance(k, tuple) and k[0] == "ps":
                for t2 in self.readers.get(k, {}).values():
                    if not (t2[0] == "e" and t2[1] == eng):
                        deps.add(t2)
        strict = is_dma or eng != "pe"
        for k in writes:
            t = self.lastw.get(k)
            if t is not None and (strict or not (t[0] == "e" and t[1] == eng)):
                deps.add(t)
            for t2 in self.readers.get(k, {}).values():
                if strict or not (t2[0] == "e" and t2[1] == eng):
                    deps.add(t2)
        return deps

    def _register(self, tok, rk, reads, writes):
        for k in reads:
            self.readers.setdefault(k, {})[rk] = tok
        for k in writes:
            self.lastw[k] = tok
            self.readers[k] = {}

    @staticmethod
    def _expand(keys):
        out = []
        for k in keys:
            out.append(k)
            if isinstance(k, tuple) and len(k) == 2 and k[0] == "stg":
                out += [("sa", k[1]), ("sb", k[1]), ("sc", k[1])]
        return out

    def op(self, eng, fn, reads=(), writes=()):
        reads, writes = self._expand(reads), self._expand(writes)
        idx = len(self.recs[eng])
        deps = self._deps(eng, reads, writes, False)
        tok = ("e", eng, idx)
        self.recs[eng].append(dict(fn=fn, deps=deps, dma=None))
        self._register(tok, eng, reads, writes)

    def dma(self, eng, semkey, fn, reads=(), writes=()):
        reads, writes = self._expand(reads), self._expand(writes)
        n = self.dma_count.get(semkey, 0) + 1
        self.dma_count[semkey] = n
        tok = ("d", semkey, n)
        deps = self._deps(eng, reads, writes, True)
        self.recs[eng].append(dict(fn=fn, deps=deps, dma=semkey))
        self._register(tok, ("d", semkey), reads, writes)

    def barrier(self):
        toks = set()
        for e in ENGS:
            if self.recs[e]:
                for i in range(len(self.recs[e]) - 1, -1, -1):
                    r = self.recs[e][i]
                    if r["fn"] is not None and r["dma"] is None:
                        toks.add(("e", e, i))
                        break
        for k, n in self.dma_count.items():
            toks.add(("d", k, n))
        for e in ENGS:
            deps = set(t for t in toks if not (t[0] == "e" and t[1] == e))
            self.recs[e].append(dict(fn=None, deps=deps, dma=None))

    def final_wait(self, eng, semkeys):
        deps = set(("d", k, self.dma_count[k]) for k in semkeys if k in self.dma_count)
        self.recs[eng].append(dict(fn=None, deps=deps, dma=None))

    def emit(self):
        nc = self.nc
        sig = {e: [False] * len(self.recs[e]) for e in ENGS}
        for e in ENGS:
            for r in self.recs[e]:
                for t in r["deps"]:
                    if t[0] == "e":
                        sig[t[1]][t[2]] = True
        rank = {}
        for e in ENGS:
            c = 0
            rk = []
            for i in range(len(self.recs[e])):
                if sig[e][i]:
                    c += 1
                rk.append(c)
            rank[e] = rk
        with ExitStack() as st:
            esem = {e: st.enter_context(nc.semaphore("s_" + e)) for e in ENGS}
            dsem = {}
            for i, k in enumerate(sorted(self.dma_count.keys(), key=str)):
                dsem[k] = st.enter_context(nc.semaphore("d%d" % i))
            block = st.enter_context(nc.Block())
            bname = {"pe": "tensor", "act": "scalar", "dve": "vector", "pool": "gpsimd", "sp": "sync"}
            for e in ENGS:
                def body(engine, e=e):
                    waited = {}
                    for i, r in enumerate(self.recs[e]):
                        for t in sorted(r["deps"], key=str):
                            if t[0] == "e":
                                key = ("e", t[1]); val = rank[t[1]][t[2]]; sem = esem[t[1]]
                            else:
                                key = ("d", t[1]); val = 16 * t[2]; sem = dsem[t[1]]
                            if waited.get(key, 0) >= val:
                                continue
                            engine.wait_ge(sem, val)
                            waited[key] = val
                        if r["fn"] is None:
                            continue
                        ins = r["fn"](engine)
                        if r["dma"] is not None:
                            ins.then_inc(dsem[r["dma"]], 16)
                        elif sig[e][i]:
                            ins.then_inc(esem[e], 1)
                getattr(block, bname[e])(body)


class Arena:
    def __init__(self, sb, total):
        self.sb = sb
        self.total = total
        self.off = 0
        self.peak = 0

    def alloc(self, free_shape, dt):
        n = 1
        for s in free_shape:
            n *= s
        esz = 4 if dt == F32 else 2
        nbytes = (n * esz + 63) // 64 * 64
        o = self.off
        self.off += nbytes
        self.peak = max(self.peak, self.off)
        assert self.off <= self.total, ("SBUF arena overflow", self.off, self.total)
        v = self.sb[:, o // 2:o // 2 + n * esz // 2]
        if dt == F32:
            v = v.bitcast(F32)
        if len(free_shape) == 2:
            v = v.rearrange("p (a b) -> p a b", a=free_shape[0])
        elif len(free_shape) == 3:
            v = v.rearrange("p (a b c) -> p a b c", a=free_shape[0], b=free_shape[1])
        return v

    def mark(self):
        return self.off

    def release(self, m):
        self.off = m


def build_nc(debug=False, upto=3):
    nc = bass.Bass("TRN2", target_bir_lowering=False)

    def din(name, shape):
        return nc.dram_tensor(name, list(shape), F32, kind="ExternalInput").ap()

    xseq = din("xseq", [SEQ, D])
    xown = din("xown", [TOWN, D])
    w_in = din("w_in", [D, IN_COLS])
    w_a = din("w_a", [512, D])
    w_b = din("w_b", [512, D])
    w_out = din("w_out", [D, D])
    w_ffi = din("w_ffi", [D, 2 * DFF])
    w_ffd = din("w_ffd", [DFF, D])
    gpm_d = din("gpm", [128, D])
    gpo_d = din("gpo", [128, D])
    gpf_d = din("gpf", [128, D])
    gpff_d = din("gpff", [128, D])
    gsg_d = din("gsg", [128, 512])
    bsg_d = din("bsg", [128, 512])
    bfb_d = din("bfb", [128, 8])
    gqc_d = din("gqc", [128, 1])
    gkc_d = din("gkc", [128, 1])
    wsp_d = din("wsp", [8, 128, 128])
    bsp_d = din("bsp", [128, 8])
    msk_d = din("msk", [128, NS * 256])
    out_d = nc.dram_tensor("out", [TOWN, D], F32, kind="ExternalOutput").ap()
    dbg = {}
    if debug:
        for nm, shp in [("d_kt", [128, 4 * SEQ]), ("d_qt", [128, 4 * TOWN]), ("d_v", [128, NB * 8 * 65]),
                        ("d_c", [128, 256]), ("d_at", [128, 4 * TOWN]), ("d_x1", [128, 8 * 1024])]:
            dbg[nm] = nc.dram_tensor(nm, shp, F32, kind="ExternalOutput").ap()

    w_in_v = w_in.rearrange("(c p) n -> p c n", p=128)
    w_a_v = w_a.rearrange("(c p) n -> p c n", p=128)
    w_b_v = w_b.rearrange("(c p) n -> p c n", p=128)
    w_out_v = w_out.rearrange("(c p) n -> p c n", p=128)
    w_ffi_v = w_ffi.rearrange("(c p) n -> p c n", p=128)
    w_ffd_v = w_ffd.rearrange("(c p) n -> p c n", p=128)

    TOTAL = 212480
    with ExitStack() as stack:
        sb = stack.enter_context(nc.sbuf_tensor("sb", [128, TOTAL // 2], BF16))
        psF = [stack.enter_context(nc.psum_tensor("psf%d" % i, [128, 512], F32)) for i in range(6)]
        psBf = [stack.enter_context(nc.psum_tensor("psb%d" % i, [128, 512], F32)) for i in range(2)]
        psB = [t[:, :].bitcast(BF16) for t in psBf]
        A = Arena(sb, TOTAL)
        P = Prog(nc)

        def PSK(i):
            return ("ps", i)

        def MM(out, lhsT, rhs, start, stop, reads, writes):
            P.op("pe", lambda e: e.matmul(out, lhsT=lhsT, rhs=rhs, start=start, stop=stop), reads, writes)

        def TR(out, in_, reads, writes):
            P.op("pe", lambda e: e.transpose(out=out, in_=in_, identity=ident), list(reads) + ["ident"], writes)

        def ACT(out, in_, func, reads, writes, **kw):
            P.op("act", lambda e: e.activation(out=out, in_=in_, func=func, **kw), reads, writes)

        def TT(eng, out, in0, in1, op, reads, writes):
            P.op(eng, lambda e: e.tensor_tensor(out=out, in0=in0, in1=in1, op=op), reads, writes)

        def TS(eng, out, in0, s1, s2, op0, op1, reads, writes):
            if s2 is None:
                P.op(eng, lambda e: e.tensor_scalar(out=out, in0=in0, scalar1=s1, scalar2=None, op0=op0), reads, writes)
            else:
                P.op(eng, lambda e: e.tensor_scalar(out=out, in0=in0, scalar1=s1, scalar2=s2, op0=op0, op1=op1), reads, writes)

        def STT(out, in0, scalar, in1, op0, op1, reads, writes):
            P.op("dve", lambda e: e.scalar_tensor_tensor(out=out, in0=in0, scalar=scalar, in1=in1, op0=op0, op1=op1),
                 reads, writes)

        def CP(eng, out, in_, reads, writes):
            P.op(eng, lambda e: e.tensor_copy(out=out, in_=in_), reads, writes)

        def MS(eng, ap, val, reads, writes):
            P.op(eng, lambda e: e.memset(ap, val), reads, writes)

        def DMA(eng, semkey, out, in_, reads, writes):
            P.dma(eng, semkey, lambda e: e.dma_start(out=out, in_=in_), reads, writes)

        def load_w(dst_view, src_view, key):
            DMA("pool", ("w", key), dst_view, src_view, [], [key])

        ident = A.alloc([128], BF16)
        bones = A.alloc([128], BF16)
        tri = A.alloc([128], F32)
        aones = A.alloc([128], F32)
        gqc = A.alloc([1], F32)
        gkc = A.alloc([1], F32)
        bfb = A.alloc([8], F32)
        gpm = A.alloc([D], F32)
        attnT = A.alloc([4, TOWN], BF16)
        NSTG = 4
        stg = [A.alloc([8, 512], BF16) for _ in range(NSTG)]
        xt = [A.alloc([D], F32) for _ in range(2)]
        NXN = 4
        xn = [A.alloc([D], BF16) for _ in range(NXN)]
        stt_ = [A.alloc([8], F32) for _ in range(NXN)]

        MS("pool", ident, 0.0, [], ["ident"])
        P.op("pool", lambda e: e.affine_select(out=ident, in_=ident, pattern=[[-1, 128]], compare_op=ALU.not_equal,
                                               fill=1.0, base=0, channel_multiplier=1),
             reads=["ident"], writes=["ident"])
        MS("pool", bones, 0.0, [], ["bones"])
        MS("pool", bones[0:64, 0:64], 1.0, ["bones"], ["bones"])
        MS("pool", bones[64:128, 64:128], 1.0, ["bones"], ["bones"])
        MS("pool", aones, 1.0, [], ["aones"])
        MS("pool", tri, 1.0, [], ["tri"])
        P.op("pool", lambda e: e.affine_select(out=tri, in_=tri, pattern=[[1, 128]], compare_op=ALU.is_ge,
                                               fill=0.0, base=0, channel_multiplier=-1),
             reads=["tri"], writes=["tri"])
        DMA("sp", "c0", gqc, gqc_d, [], ["gqc"])
        DMA("sp", "c1", gkc, gkc_d, [], ["gkc"])
        DMA("sp", "c2", bfb, bfb_d, [], ["bfb"])
        DMA("sp", "c3", gpm, gpm_d, [], ["gpm"])

        gsg = A.alloc([512], F32)
        bsg = A.alloc([512], F32)
        wsT = A.alloc([8, 128], BF16)
        bsp = A.alloc([8], F32)
        wtmp8, wtmpb8 = [], []

        def prep_mix_consts_a():
            DMA("sp", "c7", gsg, gsg_d, [], ["gsg"])
            DMA("sp", "c8", bsg, bsg_d, [], ["bsg"])
            DMA("sp", "c9", bsp, bsp_d, [], ["bsp"])
            for g in range(8):
                DMA("pool", ("c10", g), wtmpb8[g], wsp_d[g], [], [("wtmpb", g)])
                MS("dve", wtmpb8[g][0:64, 64:128], 0.0, [("wtmpb", g)], [("wtmpb", g)])

        def prep_mix_consts_b():
            for g in range(8):
                w_ = g % 2
                TR(psB[w_][:, 0:128], wtmpb8[g], [("wtmpb", g)], [PSK(6 + w_)])
                CP("dve", wsT[:, g, :], psB[w_][:, 0:128], [PSK(6 + w_)], ["wsT"])

        xctr = [0]
        psb_ctr = [0]

        xnc = [0]

        def nt_a(src_rows, gbc, gkey, eps_scale=1.0, keep=None):
            if keep is None:
                b = xctr[0] % len(xt)
                xctr[0] += 1
                x_t, xk = xt[b], ("xt", b)
                DMA("sp", ("xt", b), x_t, src_rows, [], [xk])
            else:
                x_t, xk = keep
            xi = nt_a1(x_t, xk, eps_scale)
            nt_a2(xi, x_t, xk, gbc, gkey)
            return xi

        def nt_a1(x_t, xk, eps_scale=1.0):
            xi = xnc[0] % NXN
            xnc[0] += 1
            x_n, nk = xn[xi], ("xn", xi)
            s_t, sk = stt_[xi], ("st", xi)
            ACT(x_n, x_t, AF.Square, [xk], [nk, sk], accum_out=s_t[:, 0:1])
            ACT(s_t[:, 1:2], s_t[:, 0:1], AF.Ln, [sk], [sk], scale=1.0 / D, bias=EPS * eps_scale)
            ACT(s_t[:, 2:3], s_t[:, 1:2], AF.Exp, [sk], [sk], scale=-0.5)
            return xi

        def nt_a2(xi, x_t, xk, gbc, gkey):
            x_n, nk = xn[xi], ("xn", xi)
            s_t, sk = stt_[xi], ("st", xi)
            STT(x_n, x_t, s_t[:, 2:3], gbc, ALU.mult, ALU.mult, [xk, sk, gkey], [nk])

        def nt_b(xi, dst_view, dst_key, on_act=False):
            x_n, nk = xn[xi], ("xn", xi)
            pb = psb_ctr[0] % 2
            psb_ctr[0] += 1
            pst = psB[pb]
            for c in range(8):
                TR(pst[:, c * 128:(c + 1) * 128], x_n[:, c * 128:(c + 1) * 128], [nk], [PSK(6 + pb)])
            if on_act:
                ACT(dst_view, pst.rearrange("p (c t) -> p c t", c=8), AF.Copy, [PSK(6 + pb)], [dst_key])
            else:
                CP("dve", dst_view, pst.rearrange("p (c t) -> p c t", c=8), [PSK(6 + pb)], [dst_key])

        m_l1 = A.mark()
        KT = A.alloc([4, SEQ], BF16)
        Vaug = A.alloc([NB, 8, 65], BF16)
        QT = A.alloc([4, TOWN], BF16)
        zf = A.alloc([NB, 8], F32)
        Cc = A.alloc([NB, 8], F32)
        Eb = A.alloc([NB + 1, 8], F32)
        msk = A.alloc([NS, 256], BF16)
        m_l2 = A.mark()
        hTg = [A.alloc([8, 512], BF16) for _ in range(2)]
        sq = [A.alloc([512], BF16) for _ in range(3)]
        rs = [A.alloc([512], F32) for _ in range(3)]
        wv2 = A.alloc([8, 264], BF16)
        xt.append(A.alloc([D], F32))
        xt.append(A.alloc([D], F32))
        for _g in range(8):
            wtmpb8.append(A.alloc([128], BF16))

        MS("pool", Vaug.rearrange("p a b c -> p (a b) c")[:, :, 64:65], 1.0, [], ["vones"])

        load_w(stg[0], w_in_v[:, :, K_OFF:K_OFF + 512], ("stg", 0))
        load_w(stg[1][:, :, 0:256], w_in_v[:, :, V_OFF:V_OFF + 256], ("stg", 1))
        load_w(wv2, w_in_v[:, :, V_OFF + 256:V_OFF + 520], "wv2")
        def late_loads():
            load_w(stg[2], w_in_v[:, :, Q_OFF:Q_OFF + 512], ("stg", 2))
            load_w(msk.rearrange("p a b -> p (a b)"), msk_d, "msk")

        fm_ctr = [0]

        def proj_fm_1(wview, wkey, p, hT, hkey):
            i = fm_ctr[0] % 3
            fm_ctr[0] += 1
            ps = psF[i]
            for c in range(8):
                MM(ps[:, :], wview[:, c, p * 128:(p + 1) * 128], hT[:, c, :], c == 0, c == 7, [wkey, hkey], [PSK(i)])
            ACT(sq[i], ps[:, :], AF.Square, [PSK(i)], [("sq", i)])
            return i

        def proj_fm_2(i, dst, dkey, gcol, gckey):
            ps, ps2 = psF[i], psF[3]
            MM(ps2[:, :], bones, sq[i], True, True, ["bones", ("sq", i)], [PSK(3)])
            ACT(rs[i], ps2[:, :], AF.Ln, [PSK(3)], [("rs", i)], scale=1.0 / DH, bias=EPS)
            ACT(rs[i], rs[i], AF.Exp, [("rs", i)], [("rs", i)], scale=-0.5)
            STT(dst, ps[:, :], gcol[:, 0:1], rs[i], ALU.mult, ALU.mult, [PSK(i), ("rs", i), gckey], [dkey])

        NG = 12
        xis = {}

        def ab_s1a(g):
            src = xseq if g < 8 else xown
            g0 = g if g < 8 else g - 8
            xis[g] = [nt_a(src[(g0 * 4 + bl) * 128:(g0 * 4 + bl + 1) * 128, :], gpm, "gpm") for bl in range(4)]

        def ab_s1b(g):
            hb = g % 2
            for bl in range(4):
                nt_b(xis[g][bl], hTg[hb][:, :, bl * 128:(bl + 1) * 128], ("hTg", hb))

        def ab_s2(g):
            hb = g % 2
            if g < 8:
                grp = g
                for p in range(4):
                    bl = p
                    ii = proj_fm_1(stg[0], ("stg", 0), p, hTg[hb], ("hTg", hb))
                    if p >= 1:
                        proj_fm_2(prev[0], *prev[1])
                    prev = (ii, (KT[:, p, grp * 512:(grp + 1) * 512], ("KT", p, grp), gkc, "gkc"))
                    blk = grp * 4 + bl
                    for c in range(8):
                        MM(psF[4][:, 0:256], hTg[hb][:, c, bl * 128:(bl + 1) * 128], stg[1][:, c, 0:256], c == 0, c == 7,
                           [("hTg", hb), ("stg", 1)], [PSK(4)])
                    for c in range(8):
                        MM(psF[5][:, 0:264], hTg[hb][:, c, bl * 128:(bl + 1) * 128], wv2[:, c, :], c == 0, c == 7,
                           [("hTg", hb), "wv2"], [PSK(5)])
                    ACT(Vaug[:, blk, 0:4, 0:64], psF[4][:, 0:256].rearrange("p (h d) -> p h d", h=4), AF.Copy,
                        [PSK(4)], [("V", blk)])
                    ACT(Vaug[:, blk, 4:8, 0:64], psF[5][:, 0:256].rearrange("p (h d) -> p h d", h=4), AF.Copy,
                        [PSK(5), ("V", blk)], [("V", blk)])
                    TT("dve", zf[:, blk, :], psF[5][:, 256:264], bfb, ALU.add, [PSK(5), "bfb"], ["zf"])
                return prev
            else:
                grp = g - 8
                for p in range(4):
                    ii = proj_fm_1(stg[2], ("stg", 2), p, hTg[hb], ("hTg", hb))
                    if p >= 1:
                        proj_fm_2(prev[0], *prev[1])
                    prev = (ii, (QT[:, p, grp * 512:(grp + 1) * 512], ("QT", p, grp), gqc, "gqc"))
                return prev

        zf2 = zf.rearrange("p a b -> p (a b)")
        Cc2 = Cc.rearrange("p a b -> p (a b)")

        def phase_c1():
            ACT(zf2, zf2, AF.Exp, ["zf"], ["zf"], scale=-1.0)
            ACT(zf2, zf2, AF.Ln, ["zf"], ["zf"], bias=1.0)

        def phase_c2():
            MM(psF[0][:, 0:256], tri, zf2, True, True, ["tri", "zf"], [PSK(0)])
            MM(psF[1][:, 0:256], aones, zf2, True, True, ["aones", "zf"], [PSK(1)])
            MS("dve", Eb[:, 0, :], 0.0, [], ["Eb"])
            CP("dve", Cc2, psF[1][:, 0:256], [PSK(1)], ["Cc"])
            for j in range(1, NB + 1):
                TT("dve", Eb[:, j, :], Eb[:, j - 1, :], Cc[:, j - 1, :], ALU.add, ["Eb", "Cc"], ["Eb"])
            TT("dve", Cc2, psF[0][:, 0:256], Eb[:, 0:NB, :].rearrange("p a b -> p (a b)"), ALU.add, [PSK(0), "Eb"], ["Cc"])

        for t in range(NG + 1):
            if t == 2:
                late_loads()
            if t == 4:
                prep_mix_consts_a()
            if t == 5:
                prep_mix_consts_b()
            if t < NG:
                ab_s1a(t)
            last = ab_s2(t - 1) if t >= 1 else None
            if t < NG:
                ab_s1b(t)
            if last is not None:
                proj_fm_2(last[0], *last[1])
            if t == 8:
                phase_c1()
            if t == 9:
                phase_c2()

        def dump(items):
            P.barrier()
            mk_ = A.mark()
            dv = xn[0].bitcast(F32)
            for nm, src, n in items:
                for o in range(0, n, 256):
                    m = min(256, n - o)
                    CP("dve", dv[:, 0:m], src[:, o:o + m], ["dbgsrc"], ["dv"])
                    DMA("sp", "dbg", dbg[nm][:, o:o + m], dv[:, 0:m], ["dv"], ["dbgout"])
            P.barrier()
            A.release(mk_)

        def finish():
            P.final_wait("sp", [("out", 0), ("out", 1), "dbg"])
            P.emit()

        if debug:
            dump([("d_kt", KT.rearrange("p a b -> p (a b)"), 4 * SEQ), ("d_qt", QT.rearrange("p a b -> p (a b)"), 4 * TOWN),
                  ("d_v", Vaug.rearrange("p a b c -> p (a b c)"), NB * 8 * 65), ("d_c", Cc.rearrange("p a b -> p (a b)"), 256)])
            if upto == 1:
                finish()
                return nc

        hs = [stg[k // 2][:, :, (k % 2) * 256:(k % 2 + 1) * 256] for k in range(8)]

        def e2_b0(ch, wb):
            even = (ch % 2 == 0)
            if wb == 0:
                return 4 if even else 0
            return 0 if even else 4

        def e2_chunk_loads(ch, wb=0):
            b0 = e2_b0(ch, wb)
            c0 = ch * 256
            s0, s1_ = b0 // 2, b0 // 2 + 1
            load_w(hs[b0][:, 0:4, :], w_a_v[:, :, c0:c0 + 256], ("sa", s0))
            load_w(hs[b0][:, 4:8, :], w_b_v[:, :, c0:c0 + 256], ("sb", s0))
            load_w(hs[b0 + 1], w_in_v[:, :, G_OFF + c0:G_OFF + c0 + 256], ("sc", s0))
            load_w(hs[b0 + 2][:, 0:4, :], w_in_v[:, 0:4, G_OFF + 1024 + c0:G_OFF + 1024 + c0 + 256], ("sa", s1_))
            load_w(hs[b0 + 2][:, 4:8, :], w_in_v[:, 4:8, G_OFF + 1024 + c0:G_OFF + 1024 + c0 + 256], ("sb", s1_))

        pre_x = {}

        def e1_prefetch_x(half_):
            for sl_ in range(2):
                gs_ = half_ * 8 + sl_
                DMA("sp", ("xt", sl_), xt[sl_], xown[gs_ * 128:(gs_ + 1) * 128, :], [], [("xt", sl_)])
                pre_x[(half_, sl_)] = sl_

        def e1_loads(wb=0):
            load_w(stg[wb], w_in_v[:, :, U_OFF:U_OFF + 512], ("stg", wb))
            load_w(stg[wb + 1], w_in_v[:, :, U_OFF + 512:U_OFF + 1024], ("stg", wb + 1))

        def f_loads(fq, foff=0):
            nfc = 4 if fq < 5 else 2
            sg_, su_ = (2 * fq + foff) % NSTG, (2 * fq + 1 + foff) % NSTG
            load_w(stg[sg_][:, :, 0:nfc * 128], w_ffi_v[:, :, fq * 512:fq * 512 + nfc * 128], ("stg", sg_))
            load_w(stg[su_][:, :, 0:nfc * 128], w_ffi_v[:, :, DFF + fq * 512:DFF + fq * 512 + nfc * 128], ("stg", su_))

        P.barrier()
        A.release(m_l2)
        del xt[2:]
        pT = [A.alloc([NB * 128], BF16) for _ in range(2)]
        Vp = [A.alloc([NB, 65], BF16) for _ in range(4)]
        wS = [A.alloc([NB, 8], F32) for _ in range(2)]
        wB = [A.alloc([NB, 8], F32) for _ in range(2)]
        atok = [A.alloc([512], BF16) for _ in range(2)]
        rec = [A.alloc([8], F32) for _ in range(2)]

        units = [(s, p) for s in range(NS) for p in range(4)]
        items = []
        for ui, (s, p) in enumerate(units):
            nb_ = J_of(s) + 1
            for c0 in range(0, nb_, 4):
                items.append((ui, c0))
        cc = [0]

        NSET = 3
        LOOK = 6

        def acc_banks(s):
            return (psBf[0], PSK(6)), (psBf[1], PSK(7))

        def att_prep(ui):
            s, p = units[ui]
            J = J_of(s)
            nb = J + 1
            sb_ = s % 2
            if p == 0:
                TT("dve", wB[sb_][:, 0:nb, :], Cc[:, 0:nb, :], Eb[:, J:J + 1, :].to_broadcast([128, nb, 8]), ALU.subtract,
                   ["Cc", "Eb"], [("wB", sb_)])
                ACT(wS[sb_][:, 0:nb, :], wB[sb_][:, 0:nb, :], AF.Exp, [("wB", sb_)], [("wS", sb_)])
            for ab in range(2):
                h = 2 * p + ab
                vi = 2 * (ui % 2) + ab
                TT("dve", Vp[vi][:, 0:nb, :], Vaug[:, 0:nb, h, :], wS[sb_][:, 0:nb, h:h + 1].to_broadcast([128, nb, 65]), ALU.mult,
                   [("V", j) for j in range(nb)] + ["vones", ("wS", sb_)], [("Vp", vi)])

        def att_qk(ui, c0):
            s, p = units[ui]
            J = J_of(s)
            nb = J + 1
            n = min(4, nb - c0)
            set_ = cc[0] % NSET
            cc[0] += 1
            for jj in range(n):
                j = c0 + jj
                for ab in range(2):
                    r0 = ab * 64
                    bk = 2 * set_ + ab
                    MM(psF[bk][:, jj * 128:(jj + 1) * 128], KT[r0:r0 + 64, p, j * 128:(j + 1) * 128],
                       QT[r0:r0 + 64, p, s * 128:(s + 1) * 128], True, True,
                       [("KT", p, j // 4), ("QT", p, s // 4)], [PSK(bk)])
            for ab in range(2):
                bk = 2 * set_ + ab
                ACT(pT[ab][:, c0 * 128:(c0 + n) * 128], psF[bk][:, 0:n * 128], AF.Exp, [PSK(bk)], [("pT", ab, c0 // 4)], scale=0.125)
            if c0 + n == nb:
                for ab in range(2):
                    TT("pool", pT[ab][:, (J - 1) * 128:(J + 1) * 128], pT[ab][:, (J - 1) * 128:(J + 1) * 128], msk[:, s, :], ALU.mult,
                       [("pT", ab, c0 // 4), "msk"], [("pT", ab, c0 // 4)])

        def att_pv(ui, c0):
            s, p = units[ui]
            J = J_of(s)
            nb = J + 1
            n = min(4, nb - c0)
            banks = acc_banks(s)
            for ab in range(2):
                acc, ak = banks[ab]
                vi = 2 * (ui % 2) + ab
                for jj in range(n):
                    j = c0 + jj
                    MM(acc[:, p * 65:(p + 1) * 65], pT[ab][:, j * 128:(j + 1) * 128], Vp[vi][:, j, :], j == 0, j == J,
                       [("pT", ab, c0 // 4), ("Vp", vi)], [ak])
            if c0 + n == nb and p == 3:
                tb = s % 2
                for ab in range(2):
                    acc, ak = banks[ab]
                    a3 = acc[:, 0:260].rearrange("p (h d) -> p h d", h=4)
                    rk = ("rec", ab)
                    P.op("dve", lambda e, o_=rec[ab][:, 0:4], i_=a3[:, :, 64]: e.reciprocal(out=o_, in_=i_), [ak], [rk])
                    TT("dve", atok[tb].rearrange("p (q two d) -> p q two d", two=2, d=64)[:, :, ab, :], a3[:, :, 0:64],
                       rec[ab][:, 0:4].unsqueeze(2).to_broadcast([128, 4, 64]), ALU.mult, [ak, rk], [("atok", tb)])
                deferred.append((s, tb))

        deferred = []

        def att_fin2():
            while deferred:
                s, tb = deferred.pop(0)
                set_ = cc[0] % NSET
                cc[0] += 1
                bk = 2 * set_
                ptr = psF[bk][:, :].bitcast(BF16)
                for c in range(4):
                    TR(ptr[:, c * 128:(c + 1) * 128], atok[tb][:, c * 128:(c + 1) * 128], [("atok", tb)], [PSK(bk)])
                CP("dve", attnT[:, :, s * 128:(s + 1) * 128], ptr[:, 0:512].rearrange("p (c t) -> p c t", c=4),
                   [PSK(bk)], [("attnT", s)])

        pending = []
        last_unit = [-1]
        for k in range(len(items)):
            while pending and (len(pending) > LOOK or any(items[q][1] == items[k][1] for q in pending)):
                had = bool(deferred)
                att_pv(*items[pending.pop(0)])
                if had:
                    att_fin2()
            if items[k][0] != last_unit[0]:
                att_prep(items[k][0])
                last_unit[0] = items[k][0]
            att_qk(*items[k])
            pending.append(k)
            if k == 40:
                e1_loads()
                e2_chunk_loads(0)
            if k == len(items) - 12:
                e1_prefetch_x(0)
        while pending:
            att_pv(*items[pending.pop(0)])
        att_fin2()

        if debug:
            dump([("d_at", attnT.rearrange("p a b -> p (a b)"), 4 * TOWN)])
            if upto == 2:
                finish()
                return nc

        P.barrier()
        A.release(m_l1)
        gpo = A.alloc([D], F32)
        gpf = A.alloc([D], F32)
        gpff = A.alloc([D], F32)
        x1 = A.alloc([8, D], F32)
        h2T = A.alloc([8, 1024], BF16)
        DMA("sp", "c4", gpo, gpo_d, [], ["gpo"])
        DMA("sp", "c5", gpf, gpf_d, [], ["gpf"])
        DMA("sp", "c6", gpff, gpff_d, [], ["gpff"])
        m_l2b = A.mark()

        for half in range(2):
            def hk(nm, half=half):
                return (nm, half)
            wb = 0 if half == 0 else 2
            wo = 2 - wb

            A.release(m_l2b)
            hTo = A.alloc([8, 1024], BF16)
            sguT = A.alloc([4, 1024], BF16)
            mT = A.alloc([8, 1024], BF16)
            lst = [A.alloc([8], F32) for _ in range(3)]
            del xt[2:]
            xt.append(A.alloc([D], F32))
            wkb = [A.alloc([512], F32) for _ in range(12)]

            def WK(i):
                return hk(("wk", i))
            t1 = [[wkb[0], wkb[1]], [wkb[2], wkb[3]]]
            t1k = [[WK(0), WK(1)], [WK(2), WK(3)]]
            gl = [[wkb[4], wkb[5]], [wkb[6], wkb[7]]]
            glk = [[WK(4), WK(5)], [WK(6), WK(7)]]
            vn = [wkb[8].bitcast(BF16)[:, 0:512], wkb[9].bitcast(BF16)[:, 0:512]]
            vnk = [WK(8), WK(9)]
            sgu = [wkb[10].bitcast(BF16)[:, 0:512], wkb[11].bitcast(BF16)[:, 0:512]]
            sguk = [WK(10), WK(11)]
            tg2 = [[wkb[0], wkb[1]], [wkb[2], wkb[3]]]
            m122 = [[wkb[4], wkb[5]], [wkb[6], wkb[7]]]

            if half == 1:
                e2_chunk_loads(0, wb)
            e_x = {}
            SQC = 0.21145921592026346

            def e1_s0a_act(sl):
                gs = half * 8 + sl
                if (half, sl) in pre_x:
                    b = pre_x[(half, sl)]
                    xctr[0] = b + 1
                else:
                    b = xctr[0] % len(xt)
                    xctr[0] += 1
                    DMA("sp", ("xt", b), xt[b], xown[gs * 128:(gs + 1) * 128, :], [], [("xt", b)])
                e_x[sl] = (nt_a1(xt[b], ("xt", b)), b)

            def e1_s0a_dve(sl):
                xi, b = e_x[sl]
                nt_a2(xi, xt[b], ("xt", b), gpm, "gpm")

            def e1_s0b(sl):
                nt_b(e_x[sl][0], hTo[:, :, sl * 128:(sl + 1) * 128], hk(("hTo", sl)), on_act=False)

            def e1_s1_pe(sl):
                par = sl % 2
                for which in range(2):
                    ps, pk = psF[2 * par + which], PSK(2 * par + which)
                    for c in range(8):
                        MM(ps[:, :], hTo[:, c, sl * 128:(sl + 1) * 128], stg[wb + which][:, c, :], c == 0, c == 7,
                           [hk(("hTo", sl)), ("stg", wb + which)], [pk])
                for which in range(2):
                    ps, pk = psF[2 * par + which], PSK(2 * par + which)
                    tt_, tk = t1[par][which], t1k[par][which]
                    ACT(tt_, ps[:, :], AF.Square, [pk], [tk], scale=SQC)

            def e1_s1_inner(sl):
                par = sl % 2
                for which in range(2):
                    ps, pk = psF[2 * par + which], PSK(2 * par + which)
                    tt_, tk = t1[par][which], t1k[par][which]
                    STT(tt_, tt_, 1.0, ps[:, :], ALU.add, ALU.mult, [tk, pk], [tk])

            def e1_s1_exp(sl):
                par = sl % 2
                for which in range(2):
                    tt_, tk = t1[par][which], t1k[par][which]
                    ACT(tt_, tt_, AF.Exp, [tk], [tk], scale=-2.0 * GELU_C)
                for which in range(2):
                    tt_, tk = t1[par][which], t1k[par][which]
                    ACT(tt_, tt_, AF.Ln, [tk], [tk], bias=1.0)
                for which in range(2):
                    tt_, tk = t1[par][which], t1k[par][which]
                    ACT(tt_, tt_, AF.Exp, [tk], [tk], scale=-1.0)

            def e1_s1_gl(sl):
                par = sl % 2
                l_, lk = lst[par], hk(("lst", par))
                for which in range(2):
                    ps, pk = psF[2 * par + which], PSK(2 * par + which)
                    tt_, tk = t1[par][which], t1k[par][which]
                    if which == 0:
                        STT(gl[par][which], tt_, 2.0, ps[:, :], ALU.mult, ALU.mult, [tk, pk], [glk[par][which]])
                    else:
                        P.op("dve", lambda e, o_=gl[par][1], t_=tt_, p_=ps[:, :], a_=l_[:, 0:1]: e.scalar_tensor_tensor(
                            out=o_, in0=t_, scalar=2.0, in1=p_, op0=ALU.mult, op1=ALU.mult, accum_out=a_),
                             [tk, pk], [glk[par][1], lk])

            def e1_s2(sl):
                par = sl % 2
                l_, lk = lst[par], hk(("lst", par))
                v_, vk = gl[par][1], glk[par][1]
                j_, jk = t1[par][1], t1k[par][1]
                ACT(j_, v_, AF.Square, [vk], [jk, lk], accum_out=l_[:, 1:2])
                TS("dve", l_[:, 2:3], l_[:, 0:1], 1.0 / 512, None, ALU.mult, None, [lk], [lk])
                TT("dve", l_[:, 3:4], l_[:, 2:3], l_[:, 2:3], ALU.mult, [lk], [lk])
                STT(l_[:, 4:5], l_[:, 1:2], 1.0 / 512, l_[:, 3:4], ALU.mult, ALU.subtract, [lk], [lk])
                ACT(l_[:, 5:6], l_[:, 4:5], AF.Ln, [lk], [lk], bias=4.0 * EPS)
                ACT(l_[:, 6:7], l_[:, 5:6], AF.Exp, [lk], [lk], scale=-0.5)
                TS("dve", v_, v_, l_[:, 2:3], l_[:, 6:7], ALU.subtract, ALU.mult, [vk, lk], [vk])
                TT("dve", v_, v_, gsg, ALU.mult, [vk, "gsg"], [vk])
                TT("dve", vn[par], v_, bsg, ALU.add, [vk, "bsg"], [vnk[par]])

            def e1_s3(sl):
                par = sl % 2
                pm, pmk = psF[4 + par], PSK(4 + par)
                for g in range(8):
                    MM(pm[:, g * 64:(g + 1) * 64], wsT[:, g, :], vn[par][:, g * 64:(g + 1) * 64], True, True,
                       ["wsT", vnk[par]], [pmk])
                s1_, s1k = wkb[8 + par], vnk[par]
                TT("dve", s1_.rearrange("p (g c) -> p g c", g=8), pm.rearrange("p (g c) -> p g c", g=8),
                   bsp.unsqueeze(2).to_broadcast([128, 8, 64]), ALU.add, [pmk, "bsp"], [s1k])
                TT("dve", sgu[par], s1_, gl[par][0], ALU.mult, [s1k, glk[par][0]], [sguk[par]])

            def e1_s4(sl):
                par = sl % 2
                pb = psb_ctr[0] % 2
                psb_ctr[0] += 1
                for c in range(4):
                    TR(psB[pb][:, c * 128:(c + 1) * 128], sgu[par][:, c * 128:(c + 1) * 128], [sguk[par]], [PSK(6 + pb)])
                ACT(sguT[:, :, sl * 128:(sl + 1) * 128], psB[pb][:, 0:512].rearrange("p (c t) -> p c t", c=4), AF.Copy,
                    [PSK(6 + pb)], [hk(("sguT", sl // 4))])

            for t in range(8 + 4):
                s1ok = 0 <= t - 1 < 8
                if t < 8:
                    e1_s0a_act(t)
                if s1ok:
                    e1_s1_pe(t - 1)
                if t < 8:
                    e1_s0a_dve(t)
                if s1ok:
                    e1_s1_inner(t - 1)
                if t < 8:
                    e1_s0b(t)
                if s1ok:
                    e1_s1_exp(t - 1)
                if 0 <= t - 4 < 8:
                    e1_s4(t - 4)
                if 0 <= t - 3 < 8:
                    e1_s3(t - 3)
                if 0 <= t - 2 < 8:
                    e1_s2(t - 2)
                if s1ok:
                    e1_s1_gl(t - 1)

            for sl in range(8):
                gs = half * 8 + sl
                DMA("sp", ("x1ld", sl), x1[:, sl, :], xown[gs * 128:(gs + 1) * 128, :], [], [("x1", half, sl)])

            units2 = [(ch, tt, o) for ch in range(4) for tt in range(2) for o in range(2)]

            def e2_s1(ui):
                ch, tt, o = units2[ui]
                if tt == 0 and o == 0 and ch + 1 < 4:
                    e2_chunk_loads(ch + 1, wb)
                if ch == 3 and tt == 0 and o == 0:
                    load_w(stg[wo], w_out_v[:, :, 0:512], ("stg", wo))
                    load_w(stg[wo + 1], w_out_v[:, :, 512:1024], ("stg", wo + 1))
                b0 = e2_b0(ch, wb)
                s0, s1_ = b0 // 2, b0 // 2 + 1
                k0a, k0b, k1 = ("sa", s0), ("sb", s0), ("sc", s0)
                k2 = [("sa", s1_), ("sb", s1_)]
                pset = 4 * (ui % 2)
                tcols = slice(tt * 512, (tt + 1) * 512)
                slh = slice(half * 1024 + tt * 512, half * 1024 + (tt + 1) * 512)
                akeys = [("attnT", s_) for s_ in range(half * 8 + tt * 4, half * 8 + tt * 4 + 4)]
                hkeys = [hk(("hTo", s_)) for s_ in range(tt * 4, tt * 4 + 4)]
                ocols = slice(o * 128, (o + 1) * 128)
                bank = lambda i: (psF[pset + i] if pset + i < 6 else psBf[pset + i - 6])
                for kc in range(4):
                    MM(bank(0)[:, :], hs[b0][:, kc, ocols], attnT[:, kc, slh], kc == 0, kc == 3, [k0a] + akeys, [PSK(pset)])
                for c in range(8):
                    MM(bank(1)[:, :], hs[b0 + 1][:, c, ocols], hTo[:, c, tcols], c == 0, c == 7, [k1] + hkeys, [PSK(pset + 1)])
                for kc in range(4):
                    MM(bank(2)[:, :], hs[b0][:, 4 + kc, ocols], sguT[:, kc, tcols], kc == 0, kc == 3,
                       [k0b, hk(("sguT", tt))], [PSK(pset + 2)])
                for c in range(8):
                    MM(bank(3)[:, :], hs[b0 + 2][:, c, ocols], hTo[:, c, tcols], c == 0, c == 7, k2 + hkeys, [PSK(pset + 3)])

            def e2_s2(ui):
                ch, tt, o = units2[ui]
                par = ui % 2
                pset = 4 * par
                oc = ch * 2 + o
                tcols = slice(tt * 512, (tt + 1) * 512)
                bank = lambda i: (psF[pset + i] if pset + i < 6 else psBf[pset + i - 6])
                ta, tak = tg2[par][0], WK(2 * par)
                tb_, tbk = tg2[par][1], WK(2 * par + 1)
                m1, m1k = m122[par][0], WK(4 + 2 * par)
                m2, m2k = m122[par][1], WK(5 + 2 * par)
                ACT(ta, bank(1)[:, :], AF.Exp, [PSK(pset + 1)], [tak], scale=-1.0)
                ACT(tb_, bank(3)[:, :], AF.Exp, [PSK(pset + 3)], [tbk], scale=-1.0)
                ACT(ta, ta, AF.Ln, [tak], [tak], bias=1.0)
                ACT(tb_, tb_, AF.Ln, [tbk], [tbk], bias=1.0)
                ACT(ta, ta, AF.Exp, [tak], [tak], scale=-1.0)
                ACT(tb_, tb_, AF.Exp, [tbk], [tbk], scale=-1.0)
                TT("dve", m1, ta, bank(0)[:, :], ALU.mult, [tak, PSK(pset)], [m1k])
                TT("dve", m2, tb_, bank(2)[:, :], ALU.mult, [tbk, PSK(pset + 2)], [m2k])
                STT(mT[:, oc, tcols], m1, 2.0, m2, ALU.mult, ALU.add, [m1k, m2k], [hk(("mT", tt))])

            for t in range(len(units2) + 1):
                if t >= 1:
                    e2_s2(t - 1)
                if t < len(units2):
                    e2_s1(t)

            e3_xi = {}

            def zt_view(k):
                return (wkb[2 * k], wkb[2 * k + 1]), (WK(2 * k), WK(2 * k + 1))

            def e3_s1(sl):
                k = sl % 3
                tt = sl // 4
                (z0, z1), (zk0, zk1) = zt_view(k)
                l_, lk = lst[k], hk(("lst", k))
                for hf in range(2):
                    for c in range(8):
                        MM(psF[2 * k + hf][:, :], mT[:, c, sl * 128:(sl + 1) * 128], stg[wo + hf][:, c, :], c == 0, c == 7,
                           [hk(("mT", tt)), ("stg", wo + hf)], [PSK(2 * k + hf)])
                ACT(z0, psF[2 * k][:, :], AF.Square, [PSK(2 * k)], [zk0, lk], accum_out=l_[:, 0:1])
                ACT(z1, psF[2 * k + 1][:, :], AF.Square, [PSK(2 * k + 1)], [zk1, lk], accum_out=l_[:, 1:2])

            def e3_s2(sl):
                k = sl % 3
                l_, lk = lst[k], hk(("lst", k))
                TT("dve", l_[:, 2:3], l_[:, 0:1], l_[:, 1:2], ALU.add, [lk], [lk])
                ACT(l_[:, 3:4], l_[:, 2:3], AF.Ln, [lk], [lk], scale=1.0 / D, bias=4.0 * EPS)
                ACT(l_[:, 4:5], l_[:, 3:4], AF.Exp, [lk], [lk], scale=-0.5)

            def e3_s3(sl):
                k = sl % 3
                (z0, z1), (zk0, zk1) = zt_view(k)
                l_, lk = lst[k], hk(("lst", k))
                xk_ = ("x1", half, sl)
                STT(z0, psF[2 * k][:, :], l_[:, 4:5], gpo[:, 0:512], ALU.mult, ALU.mult, [PSK(2 * k), lk, "gpo"], [zk0])
                STT(z1, psF[2 * k + 1][:, :], l_[:, 4:5], gpo[:, 512:1024], ALU.mult, ALU.mult,
                    [PSK(2 * k + 1), lk, "gpo"], [zk1])
                TT("dve", x1[:, sl, 0:512], z0, x1[:, sl, 0:512], ALU.add, [zk0, xk_], [xk_])
                TT("dve", x1[:, sl, 512:1024], z1, x1[:, sl, 512:1024], ALU.add, [zk1, xk_], [xk_])

            def e3_s4(sl):
                e3_xi[sl] = nt_a1(x1[:, sl, :], ("x1", half, sl))

            def e3_s5(sl):
                nt_a2(e3_xi[sl], x1[:, sl, :], ("x1", half, sl), gpf, "gpf")

            def e3_s6(sl):
                nt_b(e3_xi[sl], h2T[:, :, sl * 128:(sl + 1) * 128], ("h2T", half, sl // 4), on_act=True)

            e3_stages = [e3_s1, e3_s2, e3_s3, e3_s4, e3_s5, e3_s6]
            for t in range(8 + 5):
                for si_ in range(5, -1, -1):
                    sl_ = t - si_
                    if 0 <= sl_ < 8:
                        e3_stages[si_](sl_)
                if t == 6:
                    f_loads(0, wb)

            if debug and upto == 4:
                dump([("d_x1", x1.rearrange("p a b -> p (a b)"), 8 * 1024)])
                finish()
                return nc
            P.barrier()
            A.release(m_l2b)
            del xt[2:]
            actT = A.alloc([NFC, 1024], BF16)
            ffA = A.alloc([8, 512], F32)
            fw = A.alloc([4, 512], F32)
            tgf = [fw[:, 0, :], fw[:, 2, :]]
            a1 = [fw[:, 1, :], fw[:, 3, :]]
            ot = [fw[:, 0:2, :].rearrange("p a b -> p (a b)"), fw[:, 2:4, :].rearrange("p a b -> p (a b)")]
            fst = A.alloc([8, 4], F32)
            fctr = 0
            for fq in range(6):
                nfc = 4 if fq < 5 else 2
                sg_, su_ = (2 * fq + wb) % NSTG, (2 * fq + 1 + wb) % NSTG
                if fq >= 1:
                    f_loads(fq, wb)
                for f in range(nfc):
                    fc = fq * 4 + f
                    fcols = slice(f * 128, (f + 1) * 128)
                    for tt in range(2):
                        tcols = slice(tt * 512, (tt + 1) * 512)
                        pi = fctr % 2
                        fctr += 1
                        pg, pu = psF[pi], psF[2 + pi]
                        for c in range(8):
                            MM(pg[:, :], stg[sg_][:, c, fcols], h2T[:, c, tcols], c == 0, c == 7,
                               [("stg", sg_), ("h2T", half, tt)], [PSK(pi)])
                        for c in range(8):
                            MM(pu[:, :], stg[su_][:, c, fcols], h2T[:, c, tcols], c == 0, c == 7,
                               [("stg", su_), ("h2T", half, tt)], [PSK(2 + pi)])
                        ACT(tgf[pi], pg[:, :], AF.Exp, [PSK(pi)], [hk(("fw", 2 * pi))], scale=-1.0)
                        ACT(tgf[pi], tgf[pi], AF.Ln, [hk(("fw", 2 * pi))], [hk(("fw", 2 * pi))], bias=1.0)
                        ACT(tgf[pi], tgf[pi], AF.Exp, [hk(("fw", 2 * pi))], [hk(("fw", 2 * pi))], scale=-1.0)
                        TT("dve", a1[pi], tgf[pi], pg[:, :], ALU.mult, [hk(("fw", 2 * pi)), PSK(pi)], [hk(("fw", 2 * pi + 1))])
                        TT("dve", actT[:, fc, tcols], a1[pi], pu[:, :], ALU.mult, [hk(("fw", 2 * pi + 1)), PSK(2 + pi)], [hk(("actT", tt))])

            if debug and upto == 5:
                finish()
                return nc
            gjunk = A.alloc([512], BF16)

            def bank_of(sl):
                return psF[sl] if sl < 6 else psBf[sl - 6]

            def g_keys(sl):
                ob = sl % 2
                return hk(("fw", 2 * ob)), hk(("fw", 2 * ob + 1)), hk(("fst", sl))

            def g_ev1(r, sl):
                pbank = bank_of(sl)
                ok, ok2, fk = g_keys(sl)
                ACT(gjunk, pbank[:, :], AF.Square, [PSK(sl)], [hk("gjunk"), fk], accum_out=fst[:, sl, r:r + 1])
                if r == 0:
                    TT("dve", ffA[:, sl, :], pbank[:, :], gpff[:, 0:512], ALU.mult, [PSK(sl), "gpff"], [hk(("ffA", sl))])

            def g_ev2(sl):
                ok, ok2, fk = g_keys(sl)
                TT("dve", fst[:, sl, 2:3], fst[:, sl, 0:1], fst[:, sl, 1:2], ALU.add, [fk], [fk])
                ACT(fst[:, sl, 3:4], fst[:, sl, 2:3], AF.Ln, [fk], [fk], scale=1.0 / D, bias=EPS)
                ACT(fst[:, sl, 3:4], fst[:, sl, 3:4], AF.Exp, [fk], [fk], scale=-0.5)

            def g_ev3(sl):
                gs = half * 8 + sl
                pbank = bank_of(sl)
                ob = sl % 2
                ok, ok2, fk = g_keys(sl)
                STT(ot[ob][:, 0:512], ffA[:, sl, :], fst[:, sl, 3:4], x1[:, sl, 0:512], ALU.mult, ALU.add,
                    [hk(("ffA", sl)), fk, ("x1", half, sl)], [ok, ok2])
                TT("dve", ot[ob][:, 512:1024], pbank[:, :], gpff[:, 512:1024], ALU.mult, [PSK(sl), "gpff"], [ok, ok2])
                STT(ot[ob][:, 512:1024], ot[ob][:, 512:1024], fst[:, sl, 3:4], x1[:, sl, 512:1024], ALU.mult, ALU.add,
                    [ok, ok2, fk, ("x1", half, sl)], [ok, ok2])
                DMA("sp", ("out", ob), out_d[gs * 128:(gs + 1) * 128, :], ot[ob], [ok, ok2], [("outd", gs)])

            for r in range(2):
                sis = []
                for k3 in range(3):
                    n8 = 8 if k3 < 2 else 6
                    si = (k3 + r * 3 + wb) % NSTG
                    sis.append((si, n8))
                    load_w(stg[si][:, 0:n8, :], w_ffd_v[:, k3 * 8:k3 * 8 + n8, r * 512:(r + 1) * 512], ("stg", si))
                si, n8 = sis[0]
                for f in range(n8):
                    fc = f
                    for sl in range(8):
                        pbank = psF[sl] if sl < 6 else psBf[sl - 6]
                        MM(pbank[:, :], actT[:, fc, sl * 128:(sl + 1) * 128], stg[si][:, f, :], fc == 0, fc == NFC - 1,
                           [hk(("actT", sl // 4)), ("stg", si)], [PSK(sl)])
                if r == 1 and half == 0:
                    e1_loads(2)
                    e1_prefetch_x(1)
                for sl in range(8):
                    pbank = psF[sl] if sl < 6 else psBf[sl - 6]
                    for k3 in (1, 2):
                        si, n8 = sis[k3]
                        for f in range(n8):
                            fc = k3 * 8 + f
                            MM(pbank[:, :], actT[:, fc, sl * 128:(sl + 1) * 128], stg[si][:, f, :], fc == 0, fc == NFC - 1,
                               [hk(("actT", sl // 4)), ("stg", si)], [PSK(sl)])
                    g_ev1(r, sl)
                    if r == 1:
                        if sl >= 1:
                            g_ev2(sl - 1)
                        if sl >= 2:
                            g_ev3(sl - 2)
                if r == 1:
                    g_ev2(7)
                    g_ev3(6)
                    g_ev3(7)
            P.barrier()
            if debug and upto == 6:
                finish()
                return nc

        finish()
    return nc


_NC_CACHE = {}


def _layout_inputs(inp):
    f = lambda a: np.ascontiguousarray(np.asarray(a, dtype=np.float32))
    x = f(inp["x"])
    rep = lambda v, n=128: np.ascontiguousarray(np.broadcast_to(f(v).reshape(1, -1), (n, f(v).size)))
    common = {
        "w_in": f(inp["w_in"][0]), "w_a": f(inp["w_branch_a"][0]), "w_b": f(inp["w_branch_b"][0]),
        "w_out": f(inp["w_out"][0]), "w_ffi": f(inp["w_ffn_in"][0]), "w_ffd": f(inp["w_ffn_down"][0]),
        "gpm": rep(inp["g_pre_mix"][0]), "gpo": rep(inp["g_post_mix"][0]), "gpf": rep(inp["g_pre_ffn"][0]),
        "gpff": rep(inp["g_post_ffn"][0]), "gsg": rep(inp["g_sgu"][0]), "bsg": rep(inp["b_sgu"][0]),
        "bfb": rep(inp["b_forget"][0]),
        "gqc": np.ascontiguousarray(np.tile(f(inp["g_q"][0]).reshape(64, 1), (2, 1))),
        "gkc": np.ascontiguousarray(np.tile(f(inp["g_k"][0]).reshape(64, 1), (2, 1))),
        "wsp": f(inp["w_spatial"][0]),
        "bsp": np.ascontiguousarray(f(inp["b_spatial"][0]).T),
    }
    ones = np.ones((128, 128), np.float32)
    zeros = np.zeros((128, 128), np.float32)
    tri = np.triu(np.ones((128, 128), np.float32))
    in_maps = []
    for c in range(8):
        b, par = c // 2, c % 2
        blocks = []
        mk = np.zeros((128, NS, 2, 128), np.float32)
        for s in range(NS):
            m = s // 2
            if par == 0:
                i = 4 * m if s % 2 == 0 else 4 * m + 3
            else:
                i = 4 * m + 1 if s % 2 == 0 else 4 * m + 2
            blocks.append(i)
            J = J_of(s)
            for k in range(2):
                j = J - 1 + k
                mk[:, s, k, :] = ones if j < i else (tri if j == i else zeros)
        xo = np.concatenate([x[b, i * 128:(i + 1) * 128] for i in blocks], axis=0)
        d = dict(common)
        d["xseq"] = np.ascontiguousarray(x[b])
        d["xown"] = np.ascontiguousarray(xo)
        d["msk"] = np.ascontiguousarray(mk.reshape(128, NS * 256))
        in_maps.append((d, blocks))
    return in_maps


def kernel(**inputs):
    maps = _layout_inputs(inputs)
    if "nc" not in _NC_CACHE:
        _NC_CACHE["nc"] = build_nc()
    nc = _NC_CACHE["nc"]
    res = run_bass_kernel_spmd(nc, [m[0] for m in maps], core_ids=list(range(8)))
    out = np.empty((4, SEQ, D), np.float32)
    for c in range(8):
        b = c // 2
        o = res.results[c]["out"]
        for s, i in enumerate(maps[c][1]):
            out[b, i * 128:(i + 1) * 128] = o[s * 128:(s + 1) * 128]
    return out
```

```python
import numpy as np
from contextlib import ExitStack
import concourse.bass as bass
import concourse.mybir as mybir
from concourse.bass_utils import run_bass_kernel_spmd

F32 = mybir.dt.float32
BF16 = mybir.dt.bfloat16
AF = mybir.ActivationFunctionType
ALU = mybir.AluOpType
AX = mybir.AxisListType

D = 1024
SEQ = 4096
NB = 32
NS = 16
TOWN = 2048
HEADS = 8
DH = 64
Q_OFF, K_OFF, V_OFF, F_OFF, U_OFF, G_OFF = 0, 512, 1024, 1536, 1544, 2568
IN_COLS = 4616
DFF = 2816
NFC = 22
EPS = 1e-6
GELU_C = 0.7978845608028654


def J_of(slot):
    return 4 * (slot // 2) + (1 if slot % 2 == 0 else 3)


ENGS = ["pe", "act", "dve", "pool", "sp"]


class Prog:
    def __init__(self, nc):
        self.nc = nc
        self.recs = {e: [] for e in ENGS}
        self.lastw = {}
        self.readers = {}
        self.dma_count = {}

    def _deps(self, eng, reads, writes, is_dma):
        deps = set()
        for k in reads:
            t = self.lastw.get(k)
            if t is not None:
                deps.add(t)
            if isinstance(k, tuple) and k[0] == "ps":
                for t2 in self.readers.get(k, {}).values():
                    if not (t2[0] == "e" and t2[1] == eng):
                        deps.add(t2)
        strict = is_dma or eng != "pe"
        for k in writes:
            t = self.lastw.get(k)
            if t is not None and (strict or not (t[0] == "e" and t[1] == eng)):
                deps.add(t)
            for t2 in self.readers.get(k, {}).values():
                if strict or not (t2[0] == "e" and t2[1] == eng):
                    deps.add(t2)
        return deps

    def _register(self, tok, rk, reads, writes):
        for k in reads:
            self.readers.setdefault(k, {})[rk] = tok
        for k in writes:
            self.lastw[k] = tok
            self.readers[k] = {}

    @staticmethod
    def _expand(keys):
        out = []
        for k in keys:
            out.append(k)
            if isinstance(k, tuple) and len(k) == 2 and k[0] == "stg":
                out += [("sa", k[1]), ("sb", k[1]), ("sc", k[1])]
        return out

    def op(self, eng, fn, reads=(), writes=()):
        reads, writes = self._expand(reads), self._expand(writes)
        idx = len(self.recs[eng])
        deps = self._deps(eng, reads, writes, False)
        tok = ("e", eng, idx)
        self.recs[eng].append(dict(fn=fn, deps=deps, dma=None))
        self._register(tok, eng, reads, writes)

    def dma(self, eng, semkey, fn, reads=(), writes=()):
        reads, writes = self._expand(reads), self._expand(writes)
        n = self.dma_count.get(semkey, 0) + 1
        self.dma_count[semkey] = n
        tok = ("d", semkey, n)
        deps = self._deps(eng, reads, writes, True)
        self.recs[eng].append(dict(fn=fn, deps=deps, dma=semkey))
        self._register(tok, ("d", semkey), reads, writes)

    def barrier(self):
        toks = set()
        for e in ENGS:
            if self.recs[e]:
                for i in range(len(self.recs[e]) - 1, -1, -1):
                    r = self.recs[e][i]
                    if r["fn"] is not None and r["dma"] is None:
                        toks.add(("e", e, i))
                        break
        for k, n in self.dma_count.items():
            toks.add(("d", k, n))
        for e in ENGS:
            deps = set(t for t in toks if not (t[0] == "e" and t[1] == e))
            self.recs[e].append(dict(fn=None, deps=deps, dma=None))

    def final_wait(self, eng, semkeys):
        deps = set(("d", k, self.dma_count[k]) for k in semkeys if k in self.dma_count)
        self.recs[eng].append(dict(fn=None, deps=deps, dma=None))

    def emit(self):
        nc = self.nc
        sig = {e: [False] * len(self.recs[e]) for e in ENGS}
        for e in ENGS:
            for r in self.recs[e]:
                for t in r["deps"]:
                    if t[0] == "e":
                        sig[t[1]][t[2]] = True
        rank = {}
        for e in ENGS:
            c = 0
            rk = []
            for i in range(len(self.recs[e])):
                if sig[e][i]:
                    c += 1
                rk.append(c)
            rank[e] = rk
        with ExitStack() as st:
            esem = {e: st.enter_context(nc.semaphore("s_" + e)) for e in ENGS}
            dsem = {}
            for i, k in enumerate(sorted(self.dma_count.keys(), key=str)):
                dsem[k] = st.enter_context(nc.semaphore("d%d" % i))
            block = st.enter_context(nc.Block())
            bname = {"pe": "tensor", "act": "scalar", "dve": "vector", "pool": "gpsimd", "sp": "sync"}
            for e in ENGS:
                def body(engine, e=e):
                    waited = {}
                    for i, r in enumerate(self.recs[e]):
                        for t in sorted(r["deps"], key=str):
                            if t[0] == "e":
                                key = ("e", t[1]); val = rank[t[1]][t[2]]; sem = esem[t[1]]
                            else:
                                key = ("d", t[1]); val = 16 * t[2]; sem = dsem[t[1]]
                            if waited.get(key, 0) >= val:
                                continue
                            engine.wait_ge(sem, val)
                            waited[key] = val
                        if r["fn"] is None:
                            continue
                        ins = r["fn"](engine)
                        if r["dma"] is not None:
                            ins.then_inc(dsem[r["dma"]], 16)
                        elif sig[e][i]:
                            ins.then_inc(esem[e], 1)
                getattr(block, bname[e])(body)


class Arena:
    def __init__(self, sb, total):
        self.sb = sb
        self.total = total
        self.off = 0
        self.peak = 0

    def alloc(self, free_shape, dt):
        n = 1
        for s in free_shape:
            n *= s
        esz = 4 if dt == F32 else 2
        nbytes = (n * esz + 63) // 64 * 64
        o = self.off
        self.off += nbytes
        self.peak = max(self.peak, self.off)
        assert self.off <= self.total, ("SBUF arena overflow", self.off, self.total)
        v = self.sb[:, o // 2:o // 2 + n * esz // 2]
        if dt == F32:
            v = v.bitcast(F32)
        if len(free_shape) == 2:
            v = v.rearrange("p (a b) -> p a b", a=free_shape[0])
        elif len(free_shape) == 3:
            v = v.rearrange("p (a b c) -> p a b c", a=free_shape[0], b=free_shape[1])
        return v

    def mark(self):
        return self.off

    def release(self, m):
        self.off = m


def build_nc(debug=False, upto=3):
    nc = bass.Bass("TRN2", target_bir_lowering=False)

    def din(name, shape):
        return nc.dram_tensor(name, list(shape), F32, kind="ExternalInput").ap()

    xseq = din("xseq", [SEQ, D])
    xown = din("xown", [TOWN, D])
    w_in = din("w_in", [D, IN_COLS])
    w_a = din("w_a", [512, D])
    w_b = din("w_b", [512, D])
    w_out = din("w_out", [D, D])
    w_ffi = din("w_ffi", [D, 2 * DFF])
    w_ffd = din("w_ffd", [DFF, D])
    gpm_d = din("gpm", [128, D])
    gpo_d = din("gpo", [128, D])
    gpf_d = din("gpf", [128, D])
    gpff_d = din("gpff", [128, D])
    gsg_d = din("gsg", [128, 512])
    bsg_d = din("bsg", [128, 512])
    bfb_d = din("bfb", [128, 8])
    gqc_d = din("gqc", [128, 1])
    gkc_d = din("gkc", [128, 1])
    wsp_d = din("wsp", [8, 128, 128])
    bsp_d = din("bsp", [128, 8])
    msk_d = din("msk", [128, NS * 256])
    out_d = nc.dram_tensor("out", [TOWN, D], F32, kind="ExternalOutput").ap()
    dbg = {}
    if debug:
        for nm, shp in [("d_kt", [128, 4 * SEQ]), ("d_qt", [128, 4 * TOWN]), ("d_v", [128, NB * 8 * 65]),
                        ("d_c", [128, 256]), ("d_at", [128, 4 * TOWN]), ("d_x1", [128, 8 * 1024])]:
            dbg[nm] = nc.dram_tensor(nm, shp, F32, kind="ExternalOutput").ap()

    w_in_v = w_in.rearrange("(c p) n -> p c n", p=128)
    w_a_v = w_a.rearrange("(c p) n -> p c n", p=128)
    w_b_v = w_b.rearrange("(c p) n -> p c n", p=128)
    w_out_v = w_out.rearrange("(c p) n -> p c n", p=128)
    w_ffi_v = w_ffi.rearrange("(c p) n -> p c n", p=128)
    w_ffd_v = w_ffd.rearrange("(c p) n -> p c n", p=128)

    TOTAL = 212480
    with ExitStack() as stack:
        sb = stack.enter_context(nc.sbuf_tensor("sb", [128, TOTAL // 2], BF16))
        psF = [stack.enter_context(nc.psum_tensor("psf%d" % i, [128, 512], F32)) for i in range(6)]
        psBf = [stack.enter_context(nc.psum_tensor("psb%d" % i, [128, 512], F32)) for i in range(2)]
        psB = [t[:, :].bitcast(BF16) for t in psBf]
        A = Arena(sb, TOTAL)
        P = Prog(nc)

        def PSK(i):
            return ("ps", i)

        def MM(out, lhsT, rhs, start, stop, reads, writes):
            P.op("pe", lambda e: e.matmul(out, lhsT=lhsT, rhs=rhs, start=start, stop=stop), reads, writes)

        def TR(out, in_, reads, writes):
            P.op("pe", lambda e: e.transpose(out=out, in_=in_, identity=ident), list(reads) + ["ident"], writes)

        def ACT(out, in_, func, reads, writes, **kw):
            P.op("act", lambda e: e.activation(out=out, in_=in_, func=func, **kw), reads, writes)

        def TT(eng, out, in0, in1, op, reads, writes):
            P.op(eng, lambda e: e.tensor_tensor(out=out, in0=in0, in1=in1, op=op), reads, writes)

        def TS(eng, out, in0, s1, s2, op0, op1, reads, writes):
            if s2 is None:
                P.op(eng, lambda e: e.tensor_scalar(out=out, in0=in0, scalar1=s1, scalar2=None, op0=op0), reads, writes)
            else:
                P.op(eng, lambda e: e.tensor_scalar(out=out, in0=in0, scalar1=s1, scalar2=s2, op0=op0, op1=op1), reads, writes)

        def STT(out, in0, scalar, in1, op0, op1, reads, writes):
            P.op("dve", lambda e: e.scalar_tensor_tensor(out=out, in0=in0, scalar=scalar, in1=in1, op0=op0, op1=op1),
                 reads, writes)

        def CP(eng, out, in_, reads, writes):
            P.op(eng, lambda e: e.tensor_copy(out=out, in_=in_), reads, writes)

        def MS(eng, ap, val, reads, writes):
            P.op(eng, lambda e: e.memset(ap, val), reads, writes)

        def DMA(eng, semkey, out, in_, reads, writes):
            P.dma(eng, semkey, lambda e: e.dma_start(out=out, in_=in_), reads, writes)

        def load_w(dst_view, src_view, key):
            DMA("pool", ("w", key), dst_view, src_view, [], [key])

        ident = A.alloc([128], BF16)
        bones = A.alloc([128], BF16)
        tri = A.alloc([128], F32)
        aones = A.alloc([128], F32)
        gqc = A.alloc([1], F32)
        gkc = A.alloc([1], F32)
        bfb = A.alloc([8], F32)
        gpm = A.alloc([D], F32)
        attnT = A.alloc([4, TOWN], BF16)
        NSTG = 4
        stg = [A.alloc([8, 512], BF16) for _ in range(NSTG)]
        xt = [A.alloc([D], F32) for _ in range(2)]
        NXN = 4
        xn = [A.alloc([D], BF16) for _ in range(NXN)]
        stt_ = [A.alloc([8], F32) for _ in range(NXN)]

        MS("pool", ident, 0.0, [], ["ident"])
        P.op("pool", lambda e: e.affine_select(out=ident, in_=ident, pattern=[[-1, 128]], compare_op=ALU.not_equal,
                                               fill=1.0, base=0, channel_multiplier=1),
             reads=["ident"], writes=["ident"])
        MS("pool", bones, 0.0, [], ["bones"])
        MS("pool", bones[0:64, 0:64], 1.0, ["bones"], ["bones"])
        MS("pool", bones[64:128, 64:128], 1.0, ["bones"], ["bones"])
        MS("pool", aones, 1.0, [], ["aones"])
        MS("pool", tri, 1.0, [], ["tri"])
        P.op("pool", lambda e: e.affine_select(out=tri, in_=tri, pattern=[[1, 128]], compare_op=ALU.is_ge,
                                               fill=0.0, base=0, channel_multiplier=-1),
             reads=["tri"], writes=["tri"])
        DMA("sp", "c0", gqc, gqc_d, [], ["gqc"])
        DMA("sp", "c1", gkc, gkc_d, [], ["gkc"])
        DMA("sp", "c2", bfb, bfb_d, [], ["bfb"])
        DMA("sp", "c3", gpm, gpm_d, [], ["gpm"])

        gsg = A.alloc([512], F32)
        bsg = A.alloc([512], F32)
        wsT = A.alloc([8, 128], BF16)
        bsp = A.alloc([8], F32)
        wtmp8, wtmpb8 = [], []

        def prep_mix_consts_a():
            DMA("sp", "c7", gsg, gsg_d, [], ["gsg"])
            DMA("sp", "c8", bsg, bsg_d, [], ["bsg"])
            DMA("sp", "c9", bsp, bsp_d, [], ["bsp"])
            for g in range(8):
                DMA("pool", ("c10", g), wtmpb8[g], wsp_d[g], [], [("wtmpb", g)])
                MS("dve", wtmpb8[g][0:64, 64:128], 0.0, [("wtmpb", g)], [("wtmpb", g)])

        def prep_mix_consts_b():
            for g in range(8):
                w_ = g % 2
                TR(psB[w_][:, 0:128], wtmpb8[g], [("wtmpb", g)], [PSK(6 + w_)])
                CP("dve", wsT[:, g, :], psB[w_][:, 0:128], [PSK(6 + w_)], ["wsT"])

        xctr = [0]
        psb_ctr = [0]

        xnc = [0]

        def nt_a(src_rows, gbc, gkey, eps_scale=1.0, keep=None):
            if keep is None:
                b = xctr[0] % len(xt)
                xctr[0] += 1
                x_t, xk = xt[b], ("xt", b)
                DMA("sp", ("xt", b), x_t, src_rows, [], [xk])
            else:
                x_t, xk = keep
            xi = nt_a1(x_t, xk, eps_scale)
            nt_a2(xi, x_t, xk, gbc, gkey)
            return xi

        def nt_a1(x_t, xk, eps_scale=1.0):
            xi = xnc[0] % NXN
            xnc[0] += 1
            x_n, nk = xn[xi], ("xn", xi)
            s_t, sk = stt_[xi], ("st", xi)
            ACT(x_n, x_t, AF.Square, [xk], [nk, sk], accum_out=s_t[:, 0:1])
            ACT(s_t[:, 1:2], s_t[:, 0:1], AF.Ln, [sk], [sk], scale=1.0 / D, bias=EPS * eps_scale)
            ACT(s_t[:, 2:3], s_t[:, 1:2], AF.Exp, [sk], [sk], scale=-0.5)
            return xi

        def nt_a2(xi, x_t, xk, gbc, gkey):
            x_n, nk = xn[xi], ("xn", xi)
            s_t, sk = stt_[xi], ("st", xi)
            STT(x_n, x_t, s_t[:, 2:3], gbc, ALU.mult, ALU.mult, [xk, sk, gkey], [nk])

        def nt_b(xi, dst_view, dst_key, on_act=False):
            x_n, nk = xn[xi], ("xn", xi)
            pb = psb_ctr[0] % 2
            psb_ctr[0] += 1
            pst = psB[pb]
            for c in range(8):
                TR(pst[:, c * 128:(c + 1) * 128], x_n[:, c * 128:(c + 1) * 128], [nk], [PSK(6 + pb)])
            if on_act:
                ACT(dst_view, pst.rearrange("p (c t) -> p c t", c=8), AF.Copy, [PSK(6 + pb)], [dst_key])
            else:
                CP("dve", dst_view, pst.rearrange("p (c t) -> p c t", c=8), [PSK(6 + pb)], [dst_key])

        m_l1 = A.mark()
        KT = A.alloc([4, SEQ], BF16)
        Vaug = A.alloc([NB, 8, 65], BF16)
        QT = A.alloc([4, TOWN], BF16)
        zf = A.alloc([NB, 8], F32)
        Cc = A.alloc([NB, 8], F32)
        Eb = A.alloc([NB + 1, 8], F32)
        msk = A.alloc([NS, 256], BF16)
        m_l2 = A.mark()
        hTg = [A.alloc([8, 512], BF16) for _ in range(2)]
        sq = [A.alloc([512], BF16) for _ in range(3)]
        rs = [A.alloc([512], F32) for _ in range(3)]
        wv2 = A.alloc([8, 264], BF16)
        xt.append(A.alloc([D], F32))
        xt.append(A.alloc([D], F32))
        for _g in range(8):
            wtmpb8.append(A.alloc([128], BF16))

        MS("pool", Vaug.rearrange("p a b c -> p (a b) c")[:, :, 64:65], 1.0, [], ["vones"])

        load_w(stg[0], w_in_v[:, :, K_OFF:K_OFF + 512], ("stg", 0))
        load_w(stg[1][:, :, 0:256], w_in_v[:, :, V_OFF:V_OFF + 256], ("stg", 1))
        load_w(wv2, w_in_v[:, :, V_OFF + 256:V_OFF + 520], "wv2")
        def late_loads():
            load_w(stg[2], w_in_v[:, :, Q_OFF:Q_OFF + 512], ("stg", 2))
            load_w(msk.rearrange("p a b -> p (a b)"), msk_d, "msk")

        fm_ctr = [0]

        def proj_fm_1(wview, wkey, p, hT, hkey):
            i = fm_ctr[0] % 3
            fm_ctr[0] += 1
            ps = psF[i]
            for c in range(8):
                MM(ps[:, :], wview[:, c, p * 128:(p + 1) * 128], hT[:, c, :], c == 0, c == 7, [wkey, hkey], [PSK(i)])
            ACT(sq[i], ps[:, :], AF.Square, [PSK(i)], [("sq", i)])
            return i

        def proj_fm_2(i, dst, dkey, gcol, gckey):
            ps, ps2 = psF[i], psF[3]
            MM(ps2[:, :], bones, sq[i], True, True, ["bones", ("sq", i)], [PSK(3)])
            ACT(rs[i], ps2[:, :], AF.Ln, [PSK(3)], [("rs", i)], scale=1.0 / DH, bias=EPS)
            ACT(rs[i], rs[i], AF.Exp, [("rs", i)], [("rs", i)], scale=-0.5)
            STT(dst, ps[:, :], gcol[:, 0:1], rs[i], ALU.mult, ALU.mult, [PSK(i), ("rs", i), gckey], [dkey])

        NG = 12
        xis = {}

        def ab_s1a(g):
            src = xseq if g < 8 else xown
            g0 = g if g < 8 else g - 8
            xis[g] = [nt_a(src[(g0 * 4 + bl) * 128:(g0 * 4 + bl + 1) * 128, :], gpm, "gpm") for bl in range(4)]

        def ab_s1b(g):
            hb = g % 2
            for bl in range(4):
                nt_b(xis[g][bl], hTg[hb][:, :, bl * 128:(bl + 1) * 128], ("hTg", hb))

        def ab_s2(g):
            hb = g % 2
            if g < 8:
                grp = g
                for p in range(4):
                    bl = p
                    ii = proj_fm_1(stg[0], ("stg", 0), p, hTg[hb], ("hTg", hb))
                    if p >= 1:
                        proj_fm_2(prev[0], *prev[1])
                    prev = (ii, (KT[:, p, grp * 512:(grp + 1) * 512], ("KT", p, grp), gkc, "gkc"))
                    blk = grp * 4 + bl
                    for c in range(8):
                        MM(psF[4][:, 0:256], hTg[hb][:, c, bl * 128:(bl + 1) * 128], stg[1][:, c, 0:256], c == 0, c == 7,
                           [("hTg", hb), ("stg", 1)], [PSK(4)])
                    for c in range(8):
                        MM(psF[5][:, 0:264], hTg[hb][:, c, bl * 128:(bl + 1) * 128], wv2[:, c, :], c == 0, c == 7,
                           [("hTg", hb), "wv2"], [PSK(5)])
                    ACT(Vaug[:, blk, 0:4, 0:64], psF[4][:, 0:256].rearrange("p (h d) -> p h d", h=4), AF.Copy,
                        [PSK(4)], [("V", blk)])
                    ACT(Vaug[:, blk, 4:8, 0:64], psF[5][:, 0:256].rearrange("p (h d) -> p h d", h=4), AF.Copy,
                        [PSK(5), ("V", blk)], [("V", blk)])
                    TT("dve", zf[:, blk, :], psF[5][:, 256:264], bfb, ALU.add, [PSK(5), "bfb"], ["zf"])
                return prev
            else:
                grp = g - 8
                for p in range(4):
                    ii = proj_fm_1(stg[2], ("stg", 2), p, hTg[hb], ("hTg", hb))
                    if p >= 1:
                        proj_fm_2(prev[0], *prev[1])
                    prev = (ii, (QT[:, p, grp * 512:(grp + 1) * 512], ("QT", p, grp), gqc, "gqc"))
                return prev

        zf2 = zf.rearrange("p a b -> p (a b)")
        Cc2 = Cc.rearrange("p a b -> p (a b)")

        def phase_c1():
            ACT(zf2, zf2, AF.Exp, ["zf"], ["zf"], scale=-1.0)
            ACT(zf2, zf2, AF.Ln, ["zf"], ["zf"], bias=1.0)

        def phase_c2():
            MM(psF[0][:, 0:256], tri, zf2, True, True, ["tri", "zf"], [PSK(0)])
            MM(psF[1][:, 0:256], aones, zf2, True, True, ["aones", "zf"], [PSK(1)])
            MS("dve", Eb[:, 0, :], 0.0, [], ["Eb"])
            CP("dve", Cc2, psF[1][:, 0:256], [PSK(1)], ["Cc"])
            for j in range(1, NB + 1):
                TT("dve", Eb[:, j, :], Eb[:, j - 1, :], Cc[:, j - 1, :], ALU.add, ["Eb", "Cc"], ["Eb"])
            TT("dve", Cc2, psF[0][:, 0:256], Eb[:, 0:NB, :].rearrange("p a b -> p (a b)"), ALU.add, [PSK(0), "Eb"], ["Cc"])

        for t in range(NG + 1):
            if t == 2:
                late_loads()
            if t == 4:
                prep_mix_consts_a()
            if t == 5:
                prep_mix_consts_b()
            if t < NG:
                ab_s1a(t)
            last = ab_s2(t - 1) if t >= 1 else None
            if t < NG:
                ab_s1b(t)
            if last is not None:
                proj_fm_2(last[0], *last[1])
            if t == 8:
                phase_c1()
            if t == 9:
                phase_c2()

        def dump(items):
            P.barrier()
            mk_ = A.mark()
            dv = xn[0].bitcast(F32)
            for nm, src, n in items:
                for o in range(0, n, 256):
                    m = min(256, n - o)
                    CP("dve", dv[:, 0:m], src[:, o:o + m], ["dbgsrc"], ["dv"])
                    DMA("sp", "dbg", dbg[nm][:, o:o + m], dv[:, 0:m], ["dv"], ["dbgout"])
            P.barrier()
            A.release(mk_)

        def finish():
            P.final_wait("sp", [("out", 0), ("out", 1), "dbg"])
            P.emit()

        if debug:
            dump([("d_kt", KT.rearrange("p a b -> p (a b)"), 4 * SEQ), ("d_qt", QT.rearrange("p a b -> p (a b)"), 4 * TOWN),
                  ("d_v", Vaug.rearrange("p a b c -> p (a b c)"), NB * 8 * 65), ("d_c", Cc.rearrange("p a b -> p (a b)"), 256)])
            if upto == 1:
                finish()
                return nc

        hs = [stg[k // 2][:, :, (k % 2) * 256:(k % 2 + 1) * 256] for k in range(8)]

        def e2_b0(ch, wb):
            even = (ch % 2 == 0)
            if wb == 0:
                return 4 if even else 0
            return 0 if even else 4

        def e2_chunk_loads(ch, wb=0):
            b0 = e2_b0(ch, wb)
            c0 = ch * 256
            s0, s1_ = b0 // 2, b0 // 2 + 1
            load_w(hs[b0][:, 0:4, :], w_a_v[:, :, c0:c0 + 256], ("sa", s0))
            load_w(hs[b0][:, 4:8, :], w_b_v[:, :, c0:c0 + 256], ("sb", s0))
            load_w(hs[b0 + 1], w_in_v[:, :, G_OFF + c0:G_OFF + c0 + 256], ("sc", s0))
            load_w(hs[b0 + 2][:, 0:4, :], w_in_v[:, 0:4, G_OFF + 1024 + c0:G_OFF + 1024 + c0 + 256], ("sa", s1_))
            load_w(hs[b0 + 2][:, 4:8, :], w_in_v[:, 4:8, G_OFF + 1024 + c0:G_OFF + 1024 + c0 + 256], ("sb", s1_))

        pre_x = {}

        def e1_prefetch_x(half_):
            for sl_ in range(2):
                gs_ = half_ * 8 + sl_
                DMA("sp", ("xt", sl_), xt[sl_], xown[gs_ * 128:(gs_ + 1) * 128, :], [], [("xt", sl_)])
                pre_x[(half_, sl_)] = sl_

        def e1_loads(wb=0):
            load_w(stg[wb], w_in_v[:, :, U_OFF:U_OFF + 512], ("stg", wb))
            load_w(stg[wb + 1], w_in_v[:, :, U_OFF + 512:U_OFF + 1024], ("stg", wb + 1))

        def f_loads(fq, foff=0):
            nfc = 4 if fq < 5 else 2
            sg_, su_ = (2 * fq + foff) % NSTG, (2 * fq + 1 + foff) % NSTG
            load_w(stg[sg_][:, :, 0:nfc * 128], w_ffi_v[:, :, fq * 512:fq * 512 + nfc * 128], ("stg", sg_))
            load_w(stg[su_][:, :, 0:nfc * 128], w_ffi_v[:, :, DFF + fq * 512:DFF + fq * 512 + nfc * 128], ("stg", su_))

        P.barrier()
        A.release(m_l2)
        del xt[2:]
        pT = [A.alloc([NB * 128], BF16) for _ in range(2)]
        Vp = [A.alloc([NB, 65], BF16) for _ in range(4)]
        wS = [A.alloc([NB, 8], F32) for _ in range(2)]
        wB = [A.alloc([NB, 8], F32) for _ in range(2)]
        atok = [A.alloc([512], BF16) for _ in range(2)]
        rec = [A.alloc([8], F32) for _ in range(2)]

        units = [(s, p) for s in range(NS) for p in range(4)]
        items = []
        for ui, (s, p) in enumerate(units):
            nb_ = J_of(s) + 1
            for c0 in range(0, nb_, 4):
                items.append((ui, c0))
        cc = [0]

        NSET = 3
        LOOK = 6

        def acc_banks(s):
            return (psBf[0], PSK(6)), (psBf[1], PSK(7))

        def att_prep(ui):
            s, p = units[ui]
            J = J_of(s)
            nb = J + 1
            sb_ = s % 2
            if p == 0:
                TT("dve", wB[sb_][:, 0:nb, :], Cc[:, 0:nb, :], Eb[:, J:J + 1, :].to_broadcast([128, nb, 8]), ALU.subtract,
                   ["Cc", "Eb"], [("wB", sb_)])
                ACT(wS[sb_][:, 0:nb, :], wB[sb_][:, 0:nb, :], AF.Exp, [("wB", sb_)], [("wS", sb_)])
            for ab in range(2):
                h = 2 * p + ab
                vi = 2 * (ui % 2) + ab
                TT("dve", Vp[vi][:, 0:nb, :], Vaug[:, 0:nb, h, :], wS[sb_][:, 0:nb, h:h + 1].to_broadcast([128, nb, 65]), ALU.mult,
                   [("V", j) for j in range(nb)] + ["vones", ("wS", sb_)], [("Vp", vi)])

        def att_qk(ui, c0):
            s, p = units[ui]
            J = J_of(s)
            nb = J + 1
            n = min(4, nb - c0)
            set_ = cc[0] % NSET
            cc[0] += 1
            for jj in range(n):
                j = c0 + jj
                for ab in range(2):
                    r0 = ab * 64
                    bk = 2 * set_ + ab
                    MM(psF[bk][:, jj * 128:(jj + 1) * 128], KT[r0:r0 + 64, p, j * 128:(j + 1) * 128],
                       QT[r0:r0 + 64, p, s * 128:(s + 1) * 128], True, True,
                       [("KT", p, j // 4), ("QT", p, s // 4)], [PSK(bk)])
            for ab in range(2):
                bk = 2 * set_ + ab
                ACT(pT[ab][:, c0 * 128:(c0 + n) * 128], psF[bk][:, 0:n * 128], AF.Exp, [PSK(bk)], [("pT", ab, c0 // 4)], scale=0.125)
            if c0 + n == nb:
                for ab in range(2):
                    TT("pool", pT[ab][:, (J - 1) * 128:(J + 1) * 128], pT[ab][:, (J - 1) * 128:(J + 1) * 128], msk[:, s, :], ALU.mult,
                       [("pT", ab, c0 // 4), "msk"], [("pT", ab, c0 // 4)])

        def att_pv(ui, c0):
            s, p = units[ui]
            J = J_of(s)
            nb = J + 1
            n = min(4, nb - c0)
            banks = acc_banks(s)
            for ab in range(2):
                acc, ak = banks[ab]
                vi = 2 * (ui % 2) + ab
                for jj in range(n):
                    j = c0 + jj
                    MM(acc[:, p * 65:(p + 1) * 65], pT[ab][:, j * 128:(j + 1) * 128], Vp[vi][:, j, :], j == 0, j == J,
                       [("pT", ab, c0 // 4), ("Vp", vi)], [ak])
            if c0 + n == nb and p == 3:
                tb = s % 2
                for ab in range(2):
                    acc, ak = banks[ab]
                    a3 = acc[:, 0:260].rearrange("p (h d) -> p h d", h=4)
                    rk = ("rec", ab)
                    P.op("dve", lambda e, o_=rec[ab][:, 0:4], i_=a3[:, :, 64]: e.reciprocal(out=o_, in_=i_), [ak], [rk])
                    TT("dve", atok[tb].rearrange("p (q two d) -> p q two d", two=2, d=64)[:, :, ab, :], a3[:, :, 0:64],
                       rec[ab][:, 0:4].unsqueeze(2).to_broadcast([128, 4, 64]), ALU.mult, [ak, rk], [("atok", tb)])
                deferred.append((s, tb))

        deferred = []

        def att_fin2():
            while deferred:
                s, tb = deferred.pop(0)
                set_ = cc[0] % NSET
                cc[0] += 1
                bk = 2 * set_
                ptr = psF[bk][:, :].bitcast(BF16)
                for c in range(4):
                    TR(ptr[:, c * 128:(c + 1) * 128], atok[tb][:, c * 128:(c + 1) * 128], [("atok", tb)], [PSK(bk)])
                CP("dve", attnT[:, :, s * 128:(s + 1) * 128], ptr[:, 0:512].rearrange("p (c t) -> p c t", c=4),
                   [PSK(bk)], [("attnT", s)])

        pending = []
        last_unit = [-1]
        for k in range(len(items)):
            while pending and (len(pending) > LOOK or any(items[q][1] == items[k][1] for q in pending)):
                had = bool(deferred)
                att_pv(*items[pending.pop(0)])
                if had:
                    att_fin2()
            if items[k][0] != last_unit[0]:
                att_prep(items[k][0])
                last_unit[0] = items[k][0]
            att_qk(*items[k])
            pending.append(k)
            if k == 40:
                e1_loads()
                e2_chunk_loads(0)
            if k == len(items) - 12:
                e1_prefetch_x(0)
        while pending:
            att_pv(*items[pending.pop(0)])
        att_fin2()

        if debug:
            dump([("d_at", attnT.rearrange("p a b -> p (a b)"), 4 * TOWN)])
            if upto == 2:
                finish()
                return nc

        P.barrier()
        A.release(m_l1)
        gpo = A.alloc([D], F32)
        gpf = A.alloc([D], F32)
        gpff = A.alloc([D], F32)
        x1 = A.alloc([8, D], F32)
        h2T = A.alloc([8, 1024], BF16)
        DMA("sp", "c4", gpo, gpo_d, [], ["gpo"])
        DMA("sp", "c5", gpf, gpf_d, [], ["gpf"])
        DMA("sp", "c6", gpff, gpff_d, [], ["gpff"])
        m_l2b = A.mark()

        for half in range(2):
            def hk(nm, half=half):
                return (nm, half)
            wb = 0 if half == 0 else 2
            wo = 2 - wb

            A.release(m_l2b)
            hTo = A.alloc([8, 1024], BF16)
            sguT = A.alloc([4, 1024], BF16)
            mT = A.alloc([8, 1024], BF16)
            lst = [A.alloc([8], F32) for _ in range(3)]
            del xt[2:]
            xt.append(A.alloc([D], F32))
            wkb = [A.alloc([512], F32) for _ in range(12)]

            def WK(i):
                return hk(("wk", i))
            t1 = [[wkb[0], wkb[1]], [wkb[2], wkb[3]]]
            t1k = [[WK(0), WK(1)], [WK(2), WK(3)]]
            gl = [[wkb[4], wkb[5]], [wkb[6], wkb[7]]]
            glk = [[WK(4), WK(5)], [WK(6), WK(7)]]
            vn = [wkb[8].bitcast(BF16)[:, 0:512], wkb[9].bitcast(BF16)[:, 0:512]]
            vnk = [WK(8), WK(9)]
            sgu = [wkb[10].bitcast(BF16)[:, 0:512], wkb[11].bitcast(BF16)[:, 0:512]]
            sguk = [WK(10), WK(11)]
            tg2 = [[wkb[0], wkb[1]], [wkb[2], wkb[3]]]
            m122 = [[wkb[4], wkb[5]], [wkb[6], wkb[7]]]

            if half == 1:
                e2_chunk_loads(0, wb)
            e_x = {}
            SQC = 0.21145921592026346

            def e1_s0a_act(sl):
                gs = half * 8 + sl
                if (half, sl) in pre_x:
                    b = pre_x[(half, sl)]
                    xctr[0] = b + 1
                else:
                    b = xctr[0] % len(xt)
                    xctr[0] += 1
                    DMA("sp", ("xt", b), xt[b], xown[gs * 128:(gs + 1) * 128, :], [], [("xt", b)])
                e_x[sl] = (nt_a1(xt[b], ("xt", b)), b)

            def e1_s0a_dve(sl):
                xi, b = e_x[sl]
                nt_a2(xi, xt[b], ("xt", b), gpm, "gpm")

            def e1_s0b(sl):
                nt_b(e_x[sl][0], hTo[:, :, sl * 128:(sl + 1) * 128], hk(("hTo", sl)), on_act=False)

            def e1_s1_pe(sl):
                par = sl % 2
                for which in range(2):
                    ps, pk = psF[2 * par + which], PSK(2 * par + which)
                    for c in range(8):
                        MM(ps[:, :], hTo[:, c, sl * 128:(sl + 1) * 128], stg[wb + which][:, c, :], c == 0, c == 7,
                           [hk(("hTo", sl)), ("stg", wb + which)], [pk])
                for which in range(2):
                    ps, pk = psF[2 * par + which], PSK(2 * par + which)
                    tt_, tk = t1[par][which], t1k[par][which]
                    ACT(tt_, ps[:, :], AF.Square, [pk], [tk], scale=SQC)

            def e1_s1_inner(sl):
                par = sl % 2
                for which in range(2):
                    ps, pk = psF[2 * par + which], PSK(2 * par + which)
                    tt_, tk = t1[par][which], t1k[par][which]
                    STT(tt_, tt_, 1.0, ps[:, :], ALU.add, ALU.mult, [tk, pk], [tk])

            def e1_s1_exp(sl):
                par = sl % 2
                for which in range(2):
                    tt_, tk = t1[par][which], t1k[par][which]
                    ACT(tt_, tt_, AF.Exp, [tk], [tk], scale=-2.0 * GELU_C)
                for which in range(2):
                    tt_, tk = t1[par][which], t1k[par][which]
                    ACT(tt_, tt_, AF.Ln, [tk], [tk], bias=1.0)
                for which in range(2):
                    tt_, tk = t1[par][which], t1k[par][which]
                    ACT(tt_, tt_, AF.Exp, [tk], [tk], scale=-1.0)

            def e1_s1_gl(sl):
                par = sl % 2
                l_, lk = lst[par], hk(("lst", par))
                for which in range(2):
                    ps, pk = psF[2 * par + which], PSK(2 * par + which)
                    tt_, tk = t1[par][which], t1k[par][which]
                    if which == 0:
                        STT(gl[par][which], tt_, 2.0, ps[:, :], ALU.mult, ALU.mult, [tk, pk], [glk[par][which]])
                    else:
                        P.op("dve", lambda e, o_=gl[par][1], t_=tt_, p_=ps[:, :], a_=l_[:, 0:1]: e.scalar_tensor_tensor(
                            out=o_, in0=t_, scalar=2.0, in1=p_, op0=ALU.mult, op1=ALU.mult, accum_out=a_),
                             [tk, pk], [glk[par][1], lk])

            def e1_s2(sl):
                par = sl % 2
                l_, lk = lst[par], hk(("lst", par))
                v_, vk = gl[par][1], glk[par][1]
                j_, jk = t1[par][1], t1k[par][1]
                ACT(j_, v_, AF.Square, [vk], [jk, lk], accum_out=l_[:, 1:2])
                TS("dve", l_[:, 2:3], l_[:, 0:1], 1.0 / 512, None, ALU.mult, None, [lk], [lk])
                TT("dve", l_[:, 3:4], l_[:, 2:3], l_[:, 2:3], ALU.mult, [lk], [lk])
                STT(l_[:, 4:5], l_[:, 1:2], 1.0 / 512, l_[:, 3:4], ALU.mult, ALU.subtract, [lk], [lk])
                ACT(l_[:, 5:6], l_[:, 4:5], AF.Ln, [lk], [lk], bias=4.0 * EPS)
                ACT(l_[:, 6:7], l_[:, 5:6], AF.Exp, [lk], [lk], scale=-0.5)
                TS("dve", v_, v_, l_[:, 2:3], l_[:, 6:7], ALU.subtract, ALU.mult, [vk, lk], [vk])
                TT("dve", v_, v_, gsg, ALU.mult, [vk, "gsg"], [vk])
                TT("dve", vn[par], v_, bsg, ALU.add, [vk, "bsg"], [vnk[par]])

            def e1_s3(sl):
                par = sl % 2
                pm, pmk = psF[4 + par], PSK(4 + par)
                for g in range(8):
                    MM(pm[:, g * 64:(g + 1) * 64], wsT[:, g, :], vn[par][:, g * 64:(g + 1) * 64], True, True,
                       ["wsT", vnk[par]], [pmk])
                s1_, s1k = wkb[8 + par], vnk[par]
                TT("dve", s1_.rearrange("p (g c) -> p g c", g=8), pm.rearrange("p (g c) -> p g c", g=8),
                   bsp.unsqueeze(2).to_broadcast([128, 8, 64]), ALU.add, [pmk, "bsp"], [s1k])
                TT("dve", sgu[par], s1_, gl[par][0], ALU.mult, [s1k, glk[par][0]], [sguk[par]])

            def e1_s4(sl):
                par = sl % 2
                pb = psb_ctr[0] % 2
                psb_ctr[0] += 1
                for c in range(4):
                    TR(psB[pb][:, c * 128:(c + 1) * 128], sgu[par][:, c * 128:(c + 1) * 128], [sguk[par]], [PSK(6 + pb)])
                CP("dve", sguT[:, :, sl * 128:(sl + 1) * 128], psB[pb][:, 0:512].rearrange("p (c t) -> p c t", c=4),
                   [PSK(6 + pb)], [hk(("sguT", sl // 4))])

            for t in range(8 + 4):
                s1ok = 0 <= t - 1 < 8
                if t < 8:
                    e1_s0a_act(t)
                if s1ok:
                    e1_s1_pe(t - 1)
                if t < 8:
                    e1_s0a_dve(t)
                if s1ok:
                    e1_s1_inner(t - 1)
                if t < 8:
                    e1_s0b(t)
                if s1ok:
                    e1_s1_exp(t - 1)
                if 0 <= t - 4 < 8:
                    e1_s4(t - 4)
                if 0 <= t - 3 < 8:
                    e1_s3(t - 3)
                if 0 <= t - 2 < 8:
                    e1_s2(t - 2)
                if s1ok:
                    e1_s1_gl(t - 1)

            for sl in range(8):
                gs = half * 8 + sl
                DMA("sp", ("x1ld", sl), x1[:, sl, :], xown[gs * 128:(gs + 1) * 128, :], [], [("x1", half, sl)])

            units2 = [(ch, tt, o) for ch in range(4) for tt in range(2) for o in range(2)]

            def e2_s1(ui):
                ch, tt, o = units2[ui]
                if tt == 0 and o == 0 and ch + 1 < 4:
                    e2_chunk_loads(ch + 1, wb)
                if ch == 3 and tt == 0 and o == 0:
                    load_w(stg[wo], w_out_v[:, :, 0:512], ("stg", wo))
                    load_w(stg[wo + 1], w_out_v[:, :, 512:1024], ("stg", wo + 1))
                b0 = e2_b0(ch, wb)
                s0, s1_ = b0 // 2, b0 // 2 + 1
                k0a, k0b, k1 = ("sa", s0), ("sb", s0), ("sc", s0)
                k2 = [("sa", s1_), ("sb", s1_)]
                pset = 4 * (ui % 2)
                tcols = slice(tt * 512, (tt + 1) * 512)
                slh = slice(half * 1024 + tt * 512, half * 1024 + (tt + 1) * 512)
                akeys = [("attnT", s_) for s_ in range(half * 8 + tt * 4, half * 8 + tt * 4 + 4)]
                hkeys = [hk(("hTo", s_)) for s_ in range(tt * 4, tt * 4 + 4)]
                ocols = slice(o * 128, (o + 1) * 128)
                bank = lambda i: (psF[pset + i] if pset + i < 6 else psBf[pset + i - 6])
                for kc in range(4):
                    MM(bank(0)[:, :], hs[b0][:, kc, ocols], attnT[:, kc, slh], kc == 0, kc == 3, [k0a] + akeys, [PSK(pset)])
                for c in range(8):
                    MM(bank(1)[:, :], hs[b0 + 1][:, c, ocols], hTo[:, c, tcols], c == 0, c == 7, [k1] + hkeys, [PSK(pset + 1)])
                for kc in range(4):
                    MM(bank(2)[:, :], hs[b0][:, 4 + kc, ocols], sguT[:, kc, tcols], kc == 0, kc == 3,
                       [k0b, hk(("sguT", tt))], [PSK(pset + 2)])
                for c in range(8):
                    MM(bank(3)[:, :], hs[b0 + 2][:, c, ocols], hTo[:, c, tcols], c == 0, c == 7, k2 + hkeys, [PSK(pset + 3)])

            def e2_s2(ui):
                ch, tt, o = units2[ui]
                par = ui % 2
                pset = 4 * par
                oc = ch * 2 + o
                tcols = slice(tt * 512, (tt + 1) * 512)
                bank = lambda i: (psF[pset + i] if pset + i < 6 else psBf[pset + i - 6])
                ta, tak = tg2[par][0], WK(2 * par)
                tb_, tbk = tg2[par][1], WK(2 * par + 1)
                m1, m1k = m122[par][0], WK(4 + 2 * par)
                m2, m2k = m122[par][1], WK(5 + 2 * par)
                ACT(ta, bank(1)[:, :], AF.Exp, [PSK(pset + 1)], [tak], scale=-1.0)
                ACT(tb_, bank(3)[:, :], AF.Exp, [PSK(pset + 3)], [tbk], scale=-1.0)
                ACT(ta, ta, AF.Ln, [tak], [tak], bias=1.0)
                ACT(tb_, tb_, AF.Ln, [tbk], [tbk], bias=1.0)
                ACT(ta, ta, AF.Exp, [tak], [tak], scale=-1.0)
                ACT(tb_, tb_, AF.Exp, [tbk], [tbk], scale=-1.0)
                TT("dve", m1, ta, bank(0)[:, :], ALU.mult, [tak, PSK(pset)], [m1k])
                TT("dve", m2, tb_, bank(2)[:, :], ALU.mult, [tbk, PSK(pset + 2)], [m2k])
                STT(mT[:, oc, tcols], m1, 2.0, m2, ALU.mult, ALU.add, [m1k, m2k], [hk(("mT", tt))])

            for t in range(len(units2) + 1):
                if t >= 1:
                    e2_s2(t - 1)
                if t < len(units2):
                    e2_s1(t)

            e3_xi = {}

            def zt_view(k):
                return (wkb[2 * k], wkb[2 * k + 1]), (WK(2 * k), WK(2 * k + 1))

            def e3_s1(sl):
                k = sl % 3
                tt = sl // 4
                (z0, z1), (zk0, zk1) = zt_view(k)
                l_, lk = lst[k], hk(("lst", k))
                for hf in range(2):
                    for c in range(8):
                        MM(psF[2 * k + hf][:, :], mT[:, c, sl * 128:(sl + 1) * 128], stg[wo + hf][:, c, :], c == 0, c == 7,
                           [hk(("mT", tt)), ("stg", wo + hf)], [PSK(2 * k + hf)])
                ACT(z0, psF[2 * k][:, :], AF.Square, [PSK(2 * k)], [zk0, lk], accum_out=l_[:, 0:1])
                ACT(z1, psF[2 * k + 1][:, :], AF.Square, [PSK(2 * k + 1)], [zk1, lk], accum_out=l_[:, 1:2])

            def e3_s2(sl):
                k = sl % 3
                l_, lk = lst[k], hk(("lst", k))
                TT("dve", l_[:, 2:3], l_[:, 0:1], l_[:, 1:2], ALU.add, [lk], [lk])
                ACT(l_[:, 3:4], l_[:, 2:3], AF.Ln, [lk], [lk], scale=1.0 / D, bias=4.0 * EPS)
                ACT(l_[:, 4:5], l_[:, 3:4], AF.Exp, [lk], [lk], scale=-0.5)

            def e3_s3(sl):
                k = sl % 3
                (z0, z1), (zk0, zk1) = zt_view(k)
                l_, lk = lst[k], hk(("lst", k))
                xk_ = ("x1", half, sl)
                STT(z0, psF[2 * k][:, :], l_[:, 4:5], gpo[:, 0:512], ALU.mult, ALU.mult, [PSK(2 * k), lk, "gpo"], [zk0])
                STT(z1, psF[2 * k + 1][:, :], l_[:, 4:5], gpo[:, 512:1024], ALU.mult, ALU.mult,
                    [PSK(2 * k + 1), lk, "gpo"], [zk1])
                TT("dve", x1[:, sl, 0:512], z0, x1[:, sl, 0:512], ALU.add, [zk0, xk_], [xk_])
                TT("dve", x1[:, sl, 512:1024], z1, x1[:, sl, 512:1024], ALU.add, [zk1, xk_], [xk_])

            def e3_s4(sl):
                e3_xi[sl] = nt_a1(x1[:, sl, :], ("x1", half, sl))

            def e3_s5(sl):
                nt_a2(e3_xi[sl], x1[:, sl, :], ("x1", half, sl), gpf, "gpf")

            def e3_s6(sl):
                nt_b(e3_xi[sl], h2T[:, :, sl * 128:(sl + 1) * 128], ("h2T", half, sl // 4), on_act=True)

            e3_stages = [e3_s1, e3_s2, e3_s3, e3_s4, e3_s5, e3_s6]
            for t in range(8 + 5):
                for si_ in range(5, -1, -1):
                    sl_ = t - si_
                    if 0 <= sl_ < 8:
                        e3_stages[si_](sl_)
                if t == 6:
                    f_loads(0, wb)

            if debug and upto == 4:
                dump([("d_x1", x1.rearrange("p a b -> p (a b)"), 8 * 1024)])
                finish()
                return nc
            P.barrier()
            A.release(m_l2b)
            del xt[2:]
            actT = A.alloc([NFC, 1024], BF16)
            ffA = A.alloc([8, 512], F32)
            fw = A.alloc([4, 512], F32)
            tgf = [fw[:, 0, :], fw[:, 2, :]]
            a1 = [fw[:, 1, :], fw[:, 3, :]]
            ot = [fw[:, 0:2, :].rearrange("p a b -> p (a b)"), fw[:, 2:4, :].rearrange("p a b -> p (a b)")]
            fst = A.alloc([8, 4], F32)
            fctr = 0
            for fq in range(6):
                nfc = 4 if fq < 5 else 2
                sg_, su_ = (2 * fq + wb) % NSTG, (2 * fq + 1 + wb) % NSTG
                if fq >= 1:
                    f_loads(fq, wb)
                for f in range(nfc):
                    fc = fq * 4 + f
                    fcols = slice(f * 128, (f + 1) * 128)
                    for tt in range(2):
                        tcols = slice(tt * 512, (tt + 1) * 512)
                        pi = fctr % 2
                        fctr += 1
                        pg, pu = psF[pi], psF[2 + pi]
                        for c in range(8):
                            MM(pg[:, :], stg[sg_][:, c, fcols], h2T[:, c, tcols], c == 0, c == 7,
                               [("stg", sg_), ("h2T", half, tt)], [PSK(pi)])
                        for c in range(8):
                            MM(pu[:, :], stg[su_][:, c, fcols], h2T[:, c, tcols], c == 0, c == 7,
                               [("stg", su_), ("h2T", half, tt)], [PSK(2 + pi)])
                        ACT(tgf[pi], pg[:, :], AF.Exp, [PSK(pi)], [hk(("fw", 2 * pi))], scale=-1.0)
                        ACT(tgf[pi], tgf[pi], AF.Ln, [hk(("fw", 2 * pi))], [hk(("fw", 2 * pi))], bias=1.0)
                        ACT(tgf[pi], tgf[pi], AF.Exp, [hk(("fw", 2 * pi))], [hk(("fw", 2 * pi))], scale=-1.0)
                        TT("dve", a1[pi], tgf[pi], pg[:, :], ALU.mult, [hk(("fw", 2 * pi)), PSK(pi)], [hk(("fw", 2 * pi + 1))])
                        TT("dve", actT[:, fc, tcols], a1[pi], pu[:, :], ALU.mult, [hk(("fw", 2 * pi + 1)), PSK(2 + pi)], [hk(("actT", tt))])

            if debug and upto == 5:
                finish()
                return nc
            gjunk = A.alloc([512], BF16)

            def bank_of(sl):
                return psF[sl] if sl < 6 else psBf[sl - 6]

            def g_keys(sl):
                ob = sl % 2
                return hk(("fw", 2 * ob)), hk(("fw", 2 * ob + 1)), hk(("fst", sl))

            def g_ev1(r, sl):
                pbank = bank_of(sl)
                ok, ok2, fk = g_keys(sl)
                ACT(gjunk, pbank[:, :], AF.Square, [PSK(sl)], [hk("gjunk"), fk], accum_out=fst[:, sl, r:r + 1])
                if r == 0:
                    TT("dve", ffA[:, sl, :], pbank[:, :], gpff[:, 0:512], ALU.mult, [PSK(sl), "gpff"], [hk(("ffA", sl))])

            def g_ev2(sl):
                ok, ok2, fk = g_keys(sl)
                TT("dve", fst[:, sl, 2:3], fst[:, sl, 0:1], fst[:, sl, 1:2], ALU.add, [fk], [fk])
                ACT(fst[:, sl, 3:4], fst[:, sl, 2:3], AF.Ln, [fk], [fk], scale=1.0 / D, bias=EPS)
                ACT(fst[:, sl, 3:4], fst[:, sl, 3:4], AF.Exp, [fk], [fk], scale=-0.5)

            def g_ev3(sl):
                gs = half * 8 + sl
                pbank = bank_of(sl)
                ob = sl % 2
                ok, ok2, fk = g_keys(sl)
                STT(ot[ob][:, 0:512], ffA[:, sl, :], fst[:, sl, 3:4], x1[:, sl, 0:512], ALU.mult, ALU.add,
                    [hk(("ffA", sl)), fk, ("x1", half, sl)], [ok, ok2])
                TT("dve", ot[ob][:, 512:1024], pbank[:, :], gpff[:, 512:1024], ALU.mult, [PSK(sl), "gpff"], [ok, ok2])
                STT(ot[ob][:, 512:1024], ot[ob][:, 512:1024], fst[:, sl, 3:4], x1[:, sl, 512:1024], ALU.mult, ALU.add,
                    [ok, ok2, fk, ("x1", half, sl)], [ok, ok2])
                DMA("sp", ("out", ob), out_d[gs * 128:(gs + 1) * 128, :], ot[ob], [ok, ok2], [("outd", gs)])

            for r in range(2):
                sis = []
                for k3 in range(3):
                    n8 = 8 if k3 < 2 else 6
                    si = (k3 + r * 3 + wb) % NSTG
                    sis.append((si, n8))
                    load_w(stg[si][:, 0:n8, :], w_ffd_v[:, k3 * 8:k3 * 8 + n8, r * 512:(r + 1) * 512], ("stg", si))
                si, n8 = sis[0]
                for f in range(n8):
                    fc = f
                    for sl in range(8):
                        pbank = psF[sl] if sl < 6 else psBf[sl - 6]
                        MM(pbank[:, :], actT[:, fc, sl * 128:(sl + 1) * 128], stg[si][:, f, :], fc == 0, fc == NFC - 1,
                           [hk(("actT", sl // 4)), ("stg", si)], [PSK(sl)])
                if r == 1 and half == 0:
                    e1_loads(2)
                    e1_prefetch_x(1)
                for sl in range(8):
                    pbank = psF[sl] if sl < 6 else psBf[sl - 6]
                    for k3 in (1, 2):
                        si, n8 = sis[k3]
                        for f in range(n8):
                            fc = k3 * 8 + f
                            MM(pbank[:, :], actT[:, fc, sl * 128:(sl + 1) * 128], stg[si][:, f, :], fc == 0, fc == NFC - 1,
                               [hk(("actT", sl // 4)), ("stg", si)], [PSK(sl)])
                    g_ev1(r, sl)
                    if r == 1:
                        if sl >= 1:
                            g_ev2(sl - 1)
                        if sl >= 2:
                            g_ev3(sl - 2)
                if r == 1:
                    g_ev2(7)
                    g_ev3(6)
                    g_ev3(7)
            P.barrier()
            if debug and upto == 6:
                finish()
                return nc

        finish()
    return nc


_NC_CACHE = {}


def _layout_inputs(inp):
    f = lambda a: np.ascontiguousarray(np.asarray(a, dtype=np.float32))
    x = f(inp["x"])
    rep = lambda v, n=128: np.ascontiguousarray(np.broadcast_to(f(v).reshape(1, -1), (n, f(v).size)))
    common = {
        "w_in": f(inp["w_in"][0]), "w_a": f(inp["w_branch_a"][0]), "w_b": f(inp["w_branch_b"][0]),
        "w_out": f(inp["w_out"][0]), "w_ffi": f(inp["w_ffn_in"][0]), "w_ffd": f(inp["w_ffn_down"][0]),
        "gpm": rep(inp["g_pre_mix"][0]), "gpo": rep(inp["g_post_mix"][0]), "gpf": rep(inp["g_pre_ffn"][0]),
        "gpff": rep(inp["g_post_ffn"][0]), "gsg": rep(inp["g_sgu"][0]), "bsg": rep(inp["b_sgu"][0]),
        "bfb": rep(inp["b_forget"][0]),
        "gqc": np.ascontiguousarray(np.tile(f(inp["g_q"][0]).reshape(64, 1), (2, 1))),
        "gkc": np.ascontiguousarray(np.tile(f(inp["g_k"][0]).reshape(64, 1), (2, 1))),
        "wsp": f(inp["w_spatial"][0]),
        "bsp": np.ascontiguousarray(f(inp["b_spatial"][0]).T),
    }
    ones = np.ones((128, 128), np.float32)
    zeros = np.zeros((128, 128), np.float32)
    tri = np.triu(np.ones((128, 128), np.float32))
    in_maps = []
    for c in range(8):
        b, par = c // 2, c % 2
        blocks = []
        mk = np.zeros((128, NS, 2, 128), np.float32)
        for s in range(NS):
            m = s // 2
            if par == 0:
                i = 4 * m if s % 2 == 0 else 4 * m + 3
            else:
                i = 4 * m + 1 if s % 2 == 0 else 4 * m + 2
            blocks.append(i)
            J = J_of(s)
            for k in range(2):
                j = J - 1 + k
                mk[:, s, k, :] = ones if j < i else (tri if j == i else zeros)
        xo = np.concatenate([x[b, i * 128:(i + 1) * 128] for i in blocks], axis=0)
        d = dict(common)
        d["xseq"] = np.ascontiguousarray(x[b])
        d["xown"] = np.ascontiguousarray(xo)
        d["msk"] = np.ascontiguousarray(mk.reshape(128, NS * 256))
        in_maps.append((d, blocks))
    return in_maps


def kernel(**inputs):
    maps = _layout_inputs(inputs)
    if "nc" not in _NC_CACHE:
        _NC_CACHE["nc"] = build_nc()
    nc = _NC_CACHE["nc"]
    res = run_bass_kernel_spmd(nc, [m[0] for m in maps], core_ids=list(range(8)))
    out = np.empty((4, SEQ, D), np.float32)
    for c in range(8):
        b = c // 2
        o = res.results[c]["out"]
        for s, i in enumerate(maps[c][1]):
            out[b, i * 128:(i + 1) * 128] = o[s * 128:(s + 1) * 128]
    return out
```

```python
import numpy as np
from contextlib import ExitStack
import concourse.bass as bass
import concourse.mybir as mybir
from concourse.bass_utils import run_bass_kernel_spmd

F32 = mybir.dt.float32
BF16 = mybir.dt.bfloat16
AF = mybir.ActivationFunctionType
ALU = mybir.AluOpType
AX = mybir.AxisListType

D = 1024
SEQ = 4096
NB = 32
NS = 16
TOWN = 2048
HEADS = 8
DH = 64
Q_OFF, K_OFF, V_OFF, F_OFF, U_OFF, G_OFF = 0, 512, 1024, 1536, 1544, 2568
IN_COLS = 4616
DFF = 2816
NFC = 22
EPS = 1e-6
GELU_C = 0.7978845608028654


def J_of(slot):
    return 4 * (slot // 2) + (1 if slot % 2 == 0 else 3)


ENGS = ["pe", "act", "dve", "pool", "sp"]


class Prog:
    def __init__(self, nc):
        self.nc = nc
        self.recs = {e: [] for e in ENGS}
        self.lastw = {}
        self.readers = {}
        self.dma_count = {}

    def _deps(self, eng, reads, writes, is_dma):
        deps = set()
        for k in reads:
            t = self.lastw.get(k)
            if t is not None:
                deps.add(t)
            if isinstance(k, tuple) and k[0] == "ps":
                for t2 in self.readers.get(k, {}).values():
                    if not (t2[0] == "e" and t2[1] == eng):
                        deps.add(t2)
        strict = is_dma or eng != "pe"
        for k in writes:
            t = self.lastw.get(k)
            if t is not None and (strict or not (t[0] == "e" and t[1] == eng)):
                deps.add(t)
            for t2 in self.readers.get(k, {}).values():
                if strict or not (t2[0] == "e" and t2[1] == eng):
                    deps.add(t2)
        return deps

    def _register(self, tok, rk, reads, writes):
        for k in reads:
            self.readers.setdefault(k, {})[rk] = tok
        for k in writes:
            self.lastw[k] = tok
            self.readers[k] = {}

    @staticmethod
    def _expand(keys):
        out = []
        for k in keys:
            out.append(k)
            if isinstance(k, tuple) and len(k) == 2 and k[0] == "stg":
                out += [("sa", k[1]), ("sb", k[1]), ("sc", k[1])]
        return out

    def op(self, eng, fn, reads=(), writes=()):
        reads, writes = self._expand(reads), self._expand(writes)
        idx = len(self.recs[eng])
        deps = self._deps(eng, reads, writes, False)
        tok = ("e", eng, idx)
        self.recs[eng].append(dict(fn=fn, deps=deps, dma=None))
        self._register(tok, eng, reads, writes)

    def dma(self, eng, semkey, fn, reads=(), writes=()):
        reads, writes = self._expand(reads), self._expand(writes)
        n = self.dma_count.get(semkey, 0) + 1
        self.dma_count[semkey] = n
        tok = ("d", semkey, n)
        deps = self._deps(eng, reads, writes, True)
        self.recs[eng].append(dict(fn=fn, deps=deps, dma=semkey))
        self._register(tok, ("d", semkey), reads, writes)

    def barrier(self):
        toks = set()
        for e in ENGS:
            if self.recs[e]:
                for i in range(len(self.recs[e]) - 1, -1, -1):
                    r = self.recs[e][i]
                    if r["fn"] is not None and r["dma"] is None:
                        toks.add(("e", e, i))
                        break
        for k, n in self.dma_count.items():
            toks.add(("d", k, n))
        for e in ENGS:
            deps = set(t for t in toks if not (t[0] == "e" and t[1] == e))
            self.recs[e].append(dict(fn=None, deps=deps, dma=None))

    def final_wait(self, eng, semkeys):
        deps = set(("d", k, self.dma_count[k]) for k in semkeys if k in self.dma_count)
        self.recs[eng].append(dict(fn=None, deps=deps, dma=None))

    def emit(self):
        nc = self.nc
        sig = {e: [False] * len(self.recs[e]) for e in ENGS}
        for e in ENGS:
            for r in self.recs[e]:
                for t in r["deps"]:
                    if t[0] == "e":
                        sig[t[1]][t[2]] = True
        rank = {}
        for e in ENGS:
            c = 0
            rk = []
            for i in range(len(self.recs[e])):
                if sig[e][i]:
                    c += 1
                rk.append(c)
            rank[e] = rk
        with ExitStack() as st:
            esem = {e: st.enter_context(nc.semaphore("s_" + e)) for e in ENGS}
            dsem = {}
            for i, k in enumerate(sorted(self.dma_count.keys(), key=str)):
                dsem[k] = st.enter_context(nc.semaphore("d%d" % i))
            block = st.enter_context(nc.Block())
            bname = {"pe": "tensor", "act": "scalar", "dve": "vector", "pool": "gpsimd", "sp": "sync"}
            for e in ENGS:
                def body(engine, e=e):
                    waited = {}
                    for i, r in enumerate(self.recs[e]):
                        for t in sorted(r["deps"], key=str):
                            if t[0] == "e":
                                key = ("e", t[1]); val = rank[t[1]][t[2]]; sem = esem[t[1]]
                            else:
                                key = ("d", t[1]); val = 16 * t[2]; sem = dsem[t[1]]
                            if waited.get(key, 0) >= val:
                                continue
                            engine.wait_ge(sem, val)
                            waited[key] = val
                        if r["fn"] is None:
                            continue
                        ins = r["fn"](engine)
                        if r["dma"] is not None:
                            ins.then_inc(dsem[r["dma"]], 16)
                        elif sig[e][i]:
                            ins.then_inc(esem[e], 1)
                getattr(block, bname[e])(body)


class Arena:
    def __init__(self, sb, total):
        self.sb = sb
        self.total = total
        self.off = 0
        self.peak = 0

    def alloc(self, free_shape, dt):
        n = 1
        for s in free_shape:
            n *= s
        esz = 4 if dt == F32 else 2
        nbytes = (n * esz + 63) // 64 * 64
        o = self.off
        self.off += nbytes
        self.peak = max(self.peak, self.off)
        assert self.off <= self.total, ("SBUF arena overflow", self.off, self.total)
        v = self.sb[:, o // 2:o // 2 + n * esz // 2]
        if dt == F32:
            v = v.bitcast(F32)
        if len(free_shape) == 2:
            v = v.rearrange("p (a b) -> p a b", a=free_shape[0])
        elif len(free_shape) == 3:
            v = v.rearrange("p (a b c) -> p a b c", a=free_shape[0], b=free_shape[1])
        return v

    def mark(self):
        return self.off

    def release(self, m):
        self.off = m


def build_nc(debug=False, upto=3):
    nc = bass.Bass("TRN2", target_bir_lowering=False)

    def din(name, shape):
        return nc.dram_tensor(name, list(shape), F32, kind="ExternalInput").ap()

    xseq = din("xseq", [SEQ, D])
    xown = din("xown", [TOWN, D])
    w_in = din("w_in", [D, IN_COLS])
    w_a = din("w_a", [512, D])
    w_b = din("w_b", [512, D])
    w_out = din("w_out", [D, D])
    w_ffi = din("w_ffi", [D, 2 * DFF])
    w_ffd = din("w_ffd", [DFF, D])
    gpm_d = din("gpm", [128, D])
    gpo_d = din("gpo", [128, D])
    gpf_d = din("gpf", [128, D])
    gpff_d = din("gpff", [128, D])
    gsg_d = din("gsg", [128, 512])
    bsg_d = din("bsg", [128, 512])
    bfb_d = din("bfb", [128, 8])
    gqc_d = din("gqc", [128, 1])
    gkc_d = din("gkc", [128, 1])
    wsp_d = din("wsp", [8, 128, 128])
    bsp_d = din("bsp", [128, 8])
    msk_d = din("msk", [128, NS * 256])
    out_d = nc.dram_tensor("out", [TOWN, D], F32, kind="ExternalOutput").ap()
    dbg = {}
    if debug:
        for nm, shp in [("d_kt", [128, 4 * SEQ]), ("d_qt", [128, 4 * TOWN]), ("d_v", [128, NB * 8 * 65]),
                        ("d_c", [128, 256]), ("d_at", [128, 4 * TOWN]), ("d_x1", [128, 8 * 1024])]:
            dbg[nm] = nc.dram_tensor(nm, shp, F32, kind="ExternalOutput").ap()

    w_in_v = w_in.rearrange("(c p) n -> p c n", p=128)
    w_a_v = w_a.rearrange("(c p) n -> p c n", p=128)
    w_b_v = w_b.rearrange("(c p) n -> p c n", p=128)
    w_out_v = w_out.rearrange("(c p) n -> p c n", p=128)
    w_ffi_v = w_ffi.rearrange("(c p) n -> p c n", p=128)
    w_ffd_v = w_ffd.rearrange("(c p) n -> p c n", p=128)

    TOTAL = 212480
    with ExitStack() as stack:
        sb = stack.enter_context(nc.sbuf_tensor("sb", [128, TOTAL // 2], BF16))
        psF = [stack.enter_context(nc.psum_tensor("psf%d" % i, [128, 512], F32)) for i in range(6)]
        psBf = [stack.enter_context(nc.psum_tensor("psb%d" % i, [128, 512], F32)) for i in range(2)]
        psB = [t[:, :].bitcast(BF16) for t in psBf]
        A = Arena(sb, TOTAL)
        P = Prog(nc)

        def PSK(i):
            return ("ps", i)

        def MM(out, lhsT, rhs, start, stop, reads, writes):
            P.op("pe", lambda e: e.matmul(out, lhsT=lhsT, rhs=rhs, start=start, stop=stop), reads, writes)

        def TR(out, in_, reads, writes):
            P.op("pe", lambda e: e.transpose(out=out, in_=in_, identity=ident), list(reads) + ["ident"], writes)

        def ACT(out, in_, func, reads, writes, **kw):
            P.op("act", lambda e: e.activation(out=out, in_=in_, func=func, **kw), reads, writes)

        def TT(eng, out, in0, in1, op, reads, writes):
            P.op(eng, lambda e: e.tensor_tensor(out=out, in0=in0, in1=in1, op=op), reads, writes)

        def TS(eng, out, in0, s1, s2, op0, op1, reads, writes):
            if s2 is None:
                P.op(eng, lambda e: e.tensor_scalar(out=out, in0=in0, scalar1=s1, scalar2=None, op0=op0), reads, writes)
            else:
                P.op(eng, lambda e: e.tensor_scalar(out=out, in0=in0, scalar1=s1, scalar2=s2, op0=op0, op1=op1), reads, writes)

        def STT(out, in0, scalar, in1, op0, op1, reads, writes):
            P.op("dve", lambda e: e.scalar_tensor_tensor(out=out, in0=in0, scalar=scalar, in1=in1, op0=op0, op1=op1),
                 reads, writes)

        def CP(eng, out, in_, reads, writes):
            P.op(eng, lambda e: e.tensor_copy(out=out, in_=in_), reads, writes)

        def MS(eng, ap, val, reads, writes):
            P.op(eng, lambda e: e.memset(ap, val), reads, writes)

        def DMA(eng, semkey, out, in_, reads, writes):
            P.dma(eng, semkey, lambda e: e.dma_start(out=out, in_=in_), reads, writes)

        def load_w(dst_view, src_view, key):
            DMA("pool", ("w", key), dst_view, src_view, [], [key])

        ident = A.alloc([128], BF16)
        bones = A.alloc([128], BF16)
        tri = A.alloc([128], F32)
        aones = A.alloc([128], F32)
        gqc = A.alloc([1], F32)
        gkc = A.alloc([1], F32)
        bfb = A.alloc([8], F32)
        gpm = A.alloc([D], F32)
        attnT = A.alloc([4, TOWN], BF16)
        NSTG = 4
        stg = [A.alloc([8, 512], BF16) for _ in range(NSTG)]
        xt = [A.alloc([D], F32) for _ in range(2)]
        NXN = 4
        xn = [A.alloc([D], BF16) for _ in range(NXN)]
        stt_ = [A.alloc([8], F32) for _ in range(NXN)]

        MS("pool", ident, 0.0, [], ["ident"])
        P.op("pool", lambda e: e.affine_select(out=ident, in_=ident, pattern=[[-1, 128]], compare_op=ALU.not_equal,
                                               fill=1.0, base=0, channel_multiplier=1),
             reads=["ident"], writes=["ident"])
        MS("pool", bones, 0.0, [], ["bones"])
        MS("pool", bones[0:64, 0:64], 1.0, ["bones"], ["bones"])
        MS("pool", bones[64:128, 64:128], 1.0, ["bones"], ["bones"])
        MS("pool", aones, 1.0, [], ["aones"])
        MS("pool", tri, 1.0, [], ["tri"])
        P.op("pool", lambda e: e.affine_select(out=tri, in_=tri, pattern=[[1, 128]], compare_op=ALU.is_ge,
                                               fill=0.0, base=0, channel_multiplier=-1),
             reads=["tri"], writes=["tri"])
        DMA("sp", "c0", gqc, gqc_d, [], ["gqc"])
        DMA("sp", "c1", gkc, gkc_d, [], ["gkc"])
        DMA("sp", "c2", bfb, bfb_d, [], ["bfb"])
        DMA("sp", "c3", gpm, gpm_d, [], ["gpm"])

        gsg = A.alloc([512], F32)
        bsg = A.alloc([512], F32)
        wsT = A.alloc([8, 128], BF16)
        bsp = A.alloc([8], F32)
        wtmp8, wtmpb8 = [], []

        def prep_mix_consts_a():
            DMA("sp", "c7", gsg, gsg_d, [], ["gsg"])
            DMA("sp", "c8", bsg, bsg_d, [], ["bsg"])
            DMA("sp", "c9", bsp, bsp_d, [], ["bsp"])
            for g in range(8):
                DMA("pool", ("c10", g), wtmpb8[g], wsp_d[g], [], [("wtmpb", g)])
                MS("dve", wtmpb8[g][0:64, 64:128], 0.0, [("wtmpb", g)], [("wtmpb", g)])

        def prep_mix_consts_b():
            for g in range(8):
                w_ = g % 2
                TR(psB[w_][:, 0:128], wtmpb8[g], [("wtmpb", g)], [PSK(6 + w_)])
                CP("dve", wsT[:, g, :], psB[w_][:, 0:128], [PSK(6 + w_)], ["wsT"])

        xctr = [0]
        psb_ctr = [0]

        xnc = [0]

        def nt_a(src_rows, gbc, gkey, eps_scale=1.0, keep=None):
            if keep is None:
                b = xctr[0] % len(xt)
                xctr[0] += 1
                x_t, xk = xt[b], ("xt", b)
                DMA("sp", ("xt", b), x_t, src_rows, [], [xk])
            else:
                x_t, xk = keep
            xi = nt_a1(x_t, xk, eps_scale)
            nt_a2(xi, x_t, xk, gbc, gkey)
            return xi

        def nt_a1(x_t, xk, eps_scale=1.0):
            xi = xnc[0] % NXN
            xnc[0] += 1
            x_n, nk = xn[xi], ("xn", xi)
            s_t, sk = stt_[xi], ("st", xi)
            ACT(x_n, x_t, AF.Square, [xk], [nk, sk], accum_out=s_t[:, 0:1])
            ACT(s_t[:, 1:2], s_t[:, 0:1], AF.Ln, [sk], [sk], scale=1.0 / D, bias=EPS * eps_scale)
            ACT(s_t[:, 2:3], s_t[:, 1:2], AF.Exp, [sk], [sk], scale=-0.5)
            return xi

        def nt_a2(xi, x_t, xk, gbc, gkey):
            x_n, nk = xn[xi], ("xn", xi)
            s_t, sk = stt_[xi], ("st", xi)
            STT(x_n, x_t, s_t[:, 2:3], gbc, ALU.mult, ALU.mult, [xk, sk, gkey], [nk])

        def nt_b(xi, dst_view, dst_key, on_act=False):
            x_n, nk = xn[xi], ("xn", xi)
            pb = psb_ctr[0] % 2
            psb_ctr[0] += 1
            pst = psB[pb]
            for c in range(8):
                TR(pst[:, c * 128:(c + 1) * 128], x_n[:, c * 128:(c + 1) * 128], [nk], [PSK(6 + pb)])
            if on_act:
                ACT(dst_view, pst.rearrange("p (c t) -> p c t", c=8), AF.Copy, [PSK(6 + pb)], [dst_key])
            else:
                CP("dve", dst_view, pst.rearrange("p (c t) -> p c t", c=8), [PSK(6 + pb)], [dst_key])

        m_l1 = A.mark()
        KT = A.alloc([4, SEQ], BF16)
        Vaug = A.alloc([NB, 8, 65], BF16)
        QT = A.alloc([4, TOWN], BF16)
        zf = A.alloc([NB, 8], F32)
        Cc = A.alloc([NB, 8], F32)
        Eb = A.alloc([NB + 1, 8], F32)
        msk = A.alloc([NS, 256], BF16)
        m_l2 = A.mark()
        hTg = [A.alloc([8, 512], BF16) for _ in range(2)]
        sq = [A.alloc([512], BF16) for _ in range(3)]
        rs = [A.alloc([512], F32) for _ in range(3)]
        wv2 = A.alloc([8, 264], BF16)
        xt.append(A.alloc([D], F32))
        xt.append(A.alloc([D], F32))
        for _g in range(8):
            wtmpb8.append(A.alloc([128], BF16))

        MS("pool", Vaug.rearrange("p a b c -> p (a b) c")[:, :, 64:65], 1.0, [], ["vones"])

        load_w(stg[0], w_in_v[:, :, K_OFF:K_OFF + 512], ("stg", 0))
        load_w(stg[1][:, :, 0:256], w_in_v[:, :, V_OFF:V_OFF + 256], ("stg", 1))
        load_w(wv2, w_in_v[:, :, V_OFF + 256:V_OFF + 520], "wv2")
        def late_loads():
            load_w(stg[2], w_in_v[:, :, Q_OFF:Q_OFF + 512], ("stg", 2))
            load_w(msk.rearrange("p a b -> p (a b)"), msk_d, "msk")

        fm_ctr = [0]

        def proj_fm_1(wview, wkey, p, hT, hkey):
            i = fm_ctr[0] % 3
            fm_ctr[0] += 1
            ps = psF[i]
            for c in range(8):
                MM(ps[:, :], wview[:, c, p * 128:(p + 1) * 128], hT[:, c, :], c == 0, c == 7, [wkey, hkey], [PSK(i)])
            ACT(sq[i], ps[:, :], AF.Square, [PSK(i)], [("sq", i)])
            return i

        def proj_fm_2(i, dst, dkey, gcol, gckey):
            ps, ps2 = psF[i], psF[3]
            MM(ps2[:, :], bones, sq[i], True, True, ["bones", ("sq", i)], [PSK(3)])
            ACT(rs[i], ps2[:, :], AF.Ln, [PSK(3)], [("rs", i)], scale=1.0 / DH, bias=EPS)
            ACT(rs[i], rs[i], AF.Exp, [("rs", i)], [("rs", i)], scale=-0.5)
            STT(dst, ps[:, :], gcol[:, 0:1], rs[i], ALU.mult, ALU.mult, [PSK(i), ("rs", i), gckey], [dkey])

        NG = 12
        xis = {}

        def ab_s1a(g):
            src = xseq if g < 8 else xown
            g0 = g if g < 8 else g - 8
            xis[g] = [nt_a(src[(g0 * 4 + bl) * 128:(g0 * 4 + bl + 1) * 128, :], gpm, "gpm") for bl in range(4)]

        def ab_s1b(g):
            hb = g % 2
            for bl in range(4):
                nt_b(xis[g][bl], hTg[hb][:, :, bl * 128:(bl + 1) * 128], ("hTg", hb))

        def ab_s2(g):
            hb = g % 2
            if g < 8:
                grp = g
                for p in range(4):
                    bl = p
                    ii = proj_fm_1(stg[0], ("stg", 0), p, hTg[hb], ("hTg", hb))
                    if p >= 1:
                        proj_fm_2(prev[0], *prev[1])
                    prev = (ii, (KT[:, p, grp * 512:(grp + 1) * 512], ("KT", p, grp), gkc, "gkc"))
                    blk = grp * 4 + bl
                    for c in range(8):
                        MM(psF[4][:, 0:256], hTg[hb][:, c, bl * 128:(bl + 1) * 128], stg[1][:, c, 0:256], c == 0, c == 7,
                           [("hTg", hb), ("stg", 1)], [PSK(4)])
                    for c in range(8):
                        MM(psF[5][:, 0:264], hTg[hb][:, c, bl * 128:(bl + 1) * 128], wv2[:, c, :], c == 0, c == 7,
                           [("hTg", hb), "wv2"], [PSK(5)])
                    ACT(Vaug[:, blk, 0:4, 0:64], psF[4][:, 0:256].rearrange("p (h d) -> p h d", h=4), AF.Copy,
                        [PSK(4)], [("V", blk)])
                    ACT(Vaug[:, blk, 4:8, 0:64], psF[5][:, 0:256].rearrange("p (h d) -> p h d", h=4), AF.Copy,
                        [PSK(5), ("V", blk)], [("V", blk)])
                    TT("dve", zf[:, blk, :], psF[5][:, 256:264], bfb, ALU.add, [PSK(5), "bfb"], ["zf"])
                return prev
            else:
                grp = g - 8
                for p in range(4):
                    ii = proj_fm_1(stg[2], ("stg", 2), p, hTg[hb], ("hTg", hb))
                    if p >= 1:
                        proj_fm_2(prev[0], *prev[1])
                    prev = (ii, (QT[:, p, grp * 512:(grp + 1) * 512], ("QT", p, grp), gqc, "gqc"))
                return prev

        zf2 = zf.rearrange("p a b -> p (a b)")
        Cc2 = Cc.rearrange("p a b -> p (a b)")

        def phase_c1():
            ACT(zf2, zf2, AF.Exp, ["zf"], ["zf"], scale=-1.0)
            ACT(zf2, zf2, AF.Ln, ["zf"], ["zf"], bias=1.0)

        def phase_c2():
            MM(psF[0][:, 0:256], tri, zf2, True, True, ["tri", "zf"], [PSK(0)])
            MM(psF[1][:, 0:256], aones, zf2, True, True, ["aones", "zf"], [PSK(1)])
            MS("dve", Eb[:, 0, :], 0.0, [], ["Eb"])
            CP("dve", Cc2, psF[1][:, 0:256], [PSK(1)], ["Cc"])
            for j in range(1, NB + 1):
                TT("dve", Eb[:, j, :], Eb[:, j - 1, :], Cc[:, j - 1, :], ALU.add, ["Eb", "Cc"], ["Eb"])
            TT("dve", Cc2, psF[0][:, 0:256], Eb[:, 0:NB, :].rearrange("p a b -> p (a b)"), ALU.add, [PSK(0), "Eb"], ["Cc"])

        for t in range(NG + 1):
            if t == 2:
                late_loads()
            if t == 4:
                prep_mix_consts_a()
            if t == 5:
                prep_mix_consts_b()
            if t < NG:
                ab_s1a(t)
            last = ab_s2(t - 1) if t >= 1 else None
            if t < NG:
                ab_s1b(t)
            if last is not None:
                proj_fm_2(last[0], *last[1])
            if t == 8:
                phase_c1()
            if t == 9:
                phase_c2()

        def dump(items):
            P.barrier()
            mk_ = A.mark()
            dv = xn[0].bitcast(F32)
            for nm, src, n in items:
                for o in range(0, n, 256):
                    m = min(256, n - o)
                    CP("dve", dv[:, 0:m], src[:, o:o + m], ["dbgsrc"], ["dv"])
                    DMA("sp", "dbg", dbg[nm][:, o:o + m], dv[:, 0:m], ["dv"], ["dbgout"])
            P.barrier()
            A.release(mk_)

        def finish():
            P.final_wait("sp", [("out", 0), ("out", 1), "dbg"])
            P.emit()

        if debug:
            dump([("d_kt", KT.rearrange("p a b -> p (a b)"), 4 * SEQ), ("d_qt", QT.rearrange("p a b -> p (a b)"), 4 * TOWN),
                  ("d_v", Vaug.rearrange("p a b c -> p (a b c)"), NB * 8 * 65), ("d_c", Cc.rearrange("p a b -> p (a b)"), 256)])
            if upto == 1:
                finish()
                return nc

        hs = [stg[k // 2][:, :, (k % 2) * 256:(k % 2 + 1) * 256] for k in range(8)]

        def e2_b0(ch, wb):
            even = (ch % 2 == 0)
            if wb == 0:
                return 4 if even else 0
            return 0 if even else 4

        def e2_chunk_loads(ch, wb=0):
            b0 = e2_b0(ch, wb)
            c0 = ch * 256
            s0, s1_ = b0 // 2, b0 // 2 + 1
            load_w(hs[b0][:, 0:4, :], w_a_v[:, :, c0:c0 + 256], ("sa", s0))
            load_w(hs[b0][:, 4:8, :], w_b_v[:, :, c0:c0 + 256], ("sb", s0))
            load_w(hs[b0 + 1], w_in_v[:, :, G_OFF + c0:G_OFF + c0 + 256], ("sc", s0))
            load_w(hs[b0 + 2][:, 0:4, :], w_in_v[:, 0:4, G_OFF + 1024 + c0:G_OFF + 1024 + c0 + 256], ("sa", s1_))
            load_w(hs[b0 + 2][:, 4:8, :], w_in_v[:, 4:8, G_OFF + 1024 + c0:G_OFF + 1024 + c0 + 256], ("sb", s1_))

        pre_x = {}

        def e1_prefetch_x(half_):
            for sl_ in range(2):
                gs_ = half_ * 8 + sl_
                DMA("sp", ("xt", sl_), xt[sl_], xown[gs_ * 128:(gs_ + 1) * 128, :], [], [("xt", sl_)])
                pre_x[(half_, sl_)] = sl_

        def e1_loads(wb=0):
            load_w(stg[wb], w_in_v[:, :, U_OFF:U_OFF + 512], ("stg", wb))
            load_w(stg[wb + 1], w_in_v[:, :, U_OFF + 512:U_OFF + 1024], ("stg", wb + 1))

        def f_loads(fq, foff=0):
            nfc = 4 if fq < 5 else 2
            sg_, su_ = (2 * fq + foff) % NSTG, (2 * fq + 1 + foff) % NSTG
            load_w(stg[sg_][:, :, 0:nfc * 128], w_ffi_v[:, :, fq * 512:fq * 512 + nfc * 128], ("stg", sg_))
            load_w(stg[su_][:, :, 0:nfc * 128], w_ffi_v[:, :, DFF + fq * 512:DFF + fq * 512 + nfc * 128], ("stg", su_))

        P.barrier()
        A.release(m_l2)
        del xt[2:]
        pT = [A.alloc([NB * 128], BF16) for _ in range(2)]
        Vp = [A.alloc([NB, 65], BF16) for _ in range(4)]
        wS = [A.alloc([NB, 8], F32) for _ in range(2)]
        wB = [A.alloc([NB, 8], F32) for _ in range(2)]
        atok = [A.alloc([512], BF16) for _ in range(2)]
        rec = [A.alloc([8], F32) for _ in range(2)]

        units = [(s, p) for s in range(NS) for p in range(4)]
        items = []
        for ui, (s, p) in enumerate(units):
            nb_ = J_of(s) + 1
            for c0 in range(0, nb_, 4):
                items.append((ui, c0))
        cc = [0]

        NSET = 3
        LOOK = 6

        def acc_banks(s):
            return (psBf[0], PSK(6)), (psBf[1], PSK(7))

        def att_prep(ui):
            s, p = units[ui]
            J = J_of(s)
            nb = J + 1
            sb_ = s % 2
            if p == 0:
                TT("dve", wB[sb_][:, 0:nb, :], Cc[:, 0:nb, :], Eb[:, J:J + 1, :].to_broadcast([128, nb, 8]), ALU.subtract,
                   ["Cc", "Eb"], [("wB", sb_)])
                ACT(wS[sb_][:, 0:nb, :], wB[sb_][:, 0:nb, :], AF.Exp, [("wB", sb_)], [("wS", sb_)])
            for ab in range(2):
                h = 2 * p + ab
                vi = 2 * (ui % 2) + ab
                TT("dve", Vp[vi][:, 0:nb, :], Vaug[:, 0:nb, h, :], wS[sb_][:, 0:nb, h:h + 1].to_broadcast([128, nb, 65]), ALU.mult,
                   [("V", j) for j in range(nb)] + ["vones", ("wS", sb_)], [("Vp", vi)])

        def att_qk(ui, c0):
            s, p = units[ui]
            J = J_of(s)
            nb = J + 1
            n = min(4, nb - c0)
            set_ = cc[0] % NSET
            cc[0] += 1
            for jj in range(n):
                j = c0 + jj
                for ab in range(2):
                    r0 = ab * 64
                    bk = 2 * set_ + ab
                    MM(psF[bk][:, jj * 128:(jj + 1) * 128], KT[r0:r0 + 64, p, j * 128:(j + 1) * 128],
                       QT[r0:r0 + 64, p, s * 128:(s + 1) * 128], True, True,
                       [("KT", p, j // 4), ("QT", p, s // 4)], [PSK(bk)])
            for ab in range(2):
                bk = 2 * set_ + ab
                ACT(pT[ab][:, c0 * 128:(c0 + n) * 128], psF[bk][:, 0:n * 128], AF.Exp, [PSK(bk)], [("pT", ab, c0 // 4)], scale=0.125)
            if c0 + n == nb:
                for ab in range(2):
                    TT("pool", pT[ab][:, (J - 1) * 128:(J + 1) * 128], pT[ab][:, (J - 1) * 128:(J + 1) * 128], msk[:, s, :], ALU.mult,
                       [("pT", ab, c0 // 4), "msk"], [("pT", ab, c0 // 4)])

        def att_pv(ui, c0):
            s, p = units[ui]
            J = J_of(s)
            nb = J + 1
            n = min(4, nb - c0)
            banks = acc_banks(s)
            for ab in range(2):
                acc, ak = banks[ab]
                vi = 2 * (ui % 2) + ab
                for jj in range(n):
                    j = c0 + jj
                    MM(acc[:, p * 65:(p + 1) * 65], pT[ab][:, j * 128:(j + 1) * 128], Vp[vi][:, j, :], j == 0, j == J,
                       [("pT", ab, c0 // 4), ("Vp", vi)], [ak])
            if c0 + n == nb and p == 3:
                tb = s % 2
                for ab in range(2):
                    acc, ak = banks[ab]
                    a3 = acc[:, 0:260].rearrange("p (h d) -> p h d", h=4)
                    rk = ("rec", ab)
                    P.op("dve", lambda e, o_=rec[ab][:, 0:4], i_=a3[:, :, 64]: e.reciprocal(out=o_, in_=i_), [ak], [rk])
                    TT("dve", atok[tb].rearrange("p (q two d) -> p q two d", two=2, d=64)[:, :, ab, :], a3[:, :, 0:64],
                       rec[ab][:, 0:4].unsqueeze(2).to_broadcast([128, 4, 64]), ALU.mult, [ak, rk], [("atok", tb)])
                deferred.append((s, tb))

        deferred = []

        def att_fin2():
            while deferred:
                s, tb = deferred.pop(0)
                set_ = cc[0] % NSET
                cc[0] += 1
                bk = 2 * set_
                ptr = psF[bk][:, :].bitcast(BF16)
                for c in range(4):
                    TR(ptr[:, c * 128:(c + 1) * 128], atok[tb][:, c * 128:(c + 1) * 128], [("atok", tb)], [PSK(bk)])
                CP("dve", attnT[:, :, s * 128:(s + 1) * 128], ptr[:, 0:512].rearrange("p (c t) -> p c t", c=4),
                   [PSK(bk)], [("attnT", s)])

        pending = []
        last_unit = [-1]
        for k in range(len(items)):
            while pending and (len(pending) > LOOK or any(items[q][1] == items[k][1] for q in pending)):
                had = bool(deferred)
                att_pv(*items[pending.pop(0)])
                if had:
                    att_fin2()
            if items[k][0] != last_unit[0]:
                att_prep(items[k][0])
                last_unit[0] = items[k][0]
            att_qk(*items[k])
            pending.append(k)
            if k == 40:
                e1_loads()
                e2_chunk_loads(0)
            if k == len(items) - 12:
                e1_prefetch_x(0)
        while pending:
            att_pv(*items[pending.pop(0)])
        att_fin2()

        if debug:
            dump([("d_at", attnT.rearrange("p a b -> p (a b)"), 4 * TOWN)])
            if upto == 2:
                finish()
                return nc

        P.barrier()
        A.release(m_l1)
        gpo = A.alloc([D], F32)
        gpf = A.alloc([D], F32)
        gpff = A.alloc([D], F32)
        x1 = A.alloc([8, D], F32)
        h2T = A.alloc([8, 1024], BF16)
        DMA("sp", "c4", gpo, gpo_d, [], ["gpo"])
        DMA("sp", "c5", gpf, gpf_d, [], ["gpf"])
        DMA("sp", "c6", gpff, gpff_d, [], ["gpff"])
        m_l2b = A.mark()

        for half in range(2):
            def hk(nm, half=half):
                return (nm, half)
            wb = 0 if half == 0 else 2
            wo = 2 - wb

            A.release(m_l2b)
            hTo = A.alloc([8, 1024], BF16)
            sguT = A.alloc([4, 1024], BF16)
            mT = A.alloc([8, 1024], BF16)
            lst = [A.alloc([8], F32) for _ in range(3)]
            del xt[2:]
            xt.append(A.alloc([D], F32))
            wkb = [A.alloc([512], F32) for _ in range(12)]

            def WK(i):
                return hk(("wk", i))
            t1 = [[wkb[0], wkb[1]], [wkb[2], wkb[3]]]
            t1k = [[WK(0), WK(1)], [WK(2), WK(3)]]
            gl = [[wkb[4], wkb[5]], [wkb[6], wkb[7]]]
            glk = [[WK(4), WK(5)], [WK(6), WK(7)]]
            vn = [wkb[8].bitcast(BF16)[:, 0:512], wkb[9].bitcast(BF16)[:, 0:512]]
            vnk = [WK(8), WK(9)]
            sgu = [wkb[10].bitcast(BF16)[:, 0:512], wkb[11].bitcast(BF16)[:, 0:512]]
            sguk = [WK(10), WK(11)]
            tg2 = [[wkb[0], wkb[1]], [wkb[2], wkb[3]]]
            m122 = [[wkb[4], wkb[5]], [wkb[6], wkb[7]]]

            if half == 1:
                e2_chunk_loads(0, wb)
            e_x = {}
            SQC = 0.21145921592026346

            def e1_s0a_act(sl):
                gs = half * 8 + sl
                if (half, sl) in pre_x:
                    b = pre_x[(half, sl)]
                    xctr[0] = b + 1
                else:
                    b = xctr[0] % len(xt)
                    xctr[0] += 1
                    DMA("sp", ("xt", b), xt[b], xown[gs * 128:(gs + 1) * 128, :], [], [("xt", b)])
                e_x[sl] = (nt_a1(xt[b], ("xt", b)), b)

            def e1_s0a_dve(sl):
                xi, b = e_x[sl]
                nt_a2(xi, xt[b], ("xt", b), gpm, "gpm")

            def e1_s0b(sl):
                nt_b(e_x[sl][0], hTo[:, :, sl * 128:(sl + 1) * 128], hk(("hTo", sl)), on_act=False)

            def e1_s1_pe(sl):
                par = sl % 2
                for which in range(2):
                    ps, pk = psF[2 * par + which], PSK(2 * par + which)
                    for c in range(8):
                        MM(ps[:, :], hTo[:, c, sl * 128:(sl + 1) * 128], stg[wb + which][:, c, :], c == 0, c == 7,
                           [hk(("hTo", sl)), ("stg", wb + which)], [pk])
                for which in range(2):
                    ps, pk = psF[2 * par + which], PSK(2 * par + which)
                    tt_, tk = t1[par][which], t1k[par][which]
                    ACT(tt_, ps[:, :], AF.Square, [pk], [tk], scale=SQC)

            def e1_s1_inner(sl):
                par = sl % 2
                for which in range(2):
                    ps, pk = psF[2 * par + which], PSK(2 * par + which)
                    tt_, tk = t1[par][which], t1k[par][which]
                    STT(tt_, tt_, 1.0, ps[:, :], ALU.add, ALU.mult, [tk, pk], [tk])

            def e1_s1_exp(sl):
                par = sl % 2
                for which in range(2):
                    tt_, tk = t1[par][which], t1k[par][which]
                    ACT(tt_, tt_, AF.Exp, [tk], [tk], scale=-2.0 * GELU_C)
                for which in range(2):
                    tt_, tk = t1[par][which], t1k[par][which]
                    ACT(tt_, tt_, AF.Ln, [tk], [tk], bias=1.0)
                for which in range(2):
                    tt_, tk = t1[par][which], t1k[par][which]
                    ACT(tt_, tt_, AF.Exp, [tk], [tk], scale=-1.0)

            def e1_s1_gl(sl):
                par = sl % 2
                l_, lk = lst[par], hk(("lst", par))
                for which in range(2):
                    ps, pk = psF[2 * par + which], PSK(2 * par + which)
                    tt_, tk = t1[par][which], t1k[par][which]
                    if which == 0:
                        STT(gl[par][which], tt_, 2.0, ps[:, :], ALU.mult, ALU.mult, [tk, pk], [glk[par][which]])
                    else:
                        P.op("dve", lambda e, o_=gl[par][1], t_=tt_, p_=ps[:, :], a_=l_[:, 0:1]: e.scalar_tensor_tensor(
                            out=o_, in0=t_, scalar=2.0, in1=p_, op0=ALU.mult, op1=ALU.mult, accum_out=a_),
                             [tk, pk], [glk[par][1], lk])

            def e1_s2(sl):
                par = sl % 2
                l_, lk = lst[par], hk(("lst", par))
                v_, vk = gl[par][1], glk[par][1]
                j_, jk = t1[par][1], t1k[par][1]
                ACT(j_, v_, AF.Square, [vk], [jk, lk], accum_out=l_[:, 1:2])
                TS("dve", l_[:, 2:3], l_[:, 0:1], 1.0 / 512, None, ALU.mult, None, [lk], [lk])
                TT("dve", l_[:, 3:4], l_[:, 2:3], l_[:, 2:3], ALU.mult, [lk], [lk])
                STT(l_[:, 4:5], l_[:, 1:2], 1.0 / 512, l_[:, 3:4], ALU.mult, ALU.subtract, [lk], [lk])
                ACT(l_[:, 5:6], l_[:, 4:5], AF.Ln, [lk], [lk], bias=4.0 * EPS)
                ACT(l_[:, 6:7], l_[:, 5:6], AF.Exp, [lk], [lk], scale=-0.5)
                TS("dve", v_, v_, l_[:, 2:3], l_[:, 6:7], ALU.subtract, ALU.mult, [vk, lk], [vk])
                TT("dve", v_, v_, gsg, ALU.mult, [vk, "gsg"], [vk])
                TT("dve", vn[par], v_, bsg, ALU.add, [vk, "bsg"], [vnk[par]])

            def e1_s3(sl):
                par = sl % 2
                pm, pmk = psF[4 + par], PSK(4 + par)
                for g in range(8):
                    MM(pm[:, g * 64:(g + 1) * 64], wsT[:, g, :], vn[par][:, g * 64:(g + 1) * 64], True, True,
                       ["wsT", vnk[par]], [pmk])
                s1_, s1k = wkb[8 + par], vnk[par]
                TT("dve", s1_.rearrange("p (g c) -> p g c", g=8), pm.rearrange("p (g c) -> p g c", g=8),
                   bsp.unsqueeze(2).to_broadcast([128, 8, 64]), ALU.add, [pmk, "bsp"], [s1k])
                TT("dve", sgu[par], s1_, gl[par][0], ALU.mult, [s1k, glk[par][0]], [sguk[par]])

            def e1_s4(sl):
                par = sl % 2
                pb = psb_ctr[0] % 2
                psb_ctr[0] += 1
                for c in range(4):
                    TR(psB[pb][:, c * 128:(c + 1) * 128], sgu[par][:, c * 128:(c + 1) * 128], [sguk[par]], [PSK(6 + pb)])
                CP("dve", sguT[:, :, sl * 128:(sl + 1) * 128], psB[pb][:, 0:512].rearrange("p (c t) -> p c t", c=4),
                   [PSK(6 + pb)], [hk(("sguT", sl // 4))])

            for t in range(8 + 4):
                s1ok = 0 <= t - 1 < 8
                if t < 8:
                    e1_s0a_act(t)
                if s1ok:
                    e1_s1_pe(t - 1)
                if t < 8:
                    e1_s0a_dve(t)
                if s1ok:
                    e1_s1_inner(t - 1)
                if t < 8:
                    e1_s0b(t)
                if s1ok:
                    e1_s1_exp(t - 1)
                if 0 <= t - 4 < 8:
                    e1_s4(t - 4)
                if 0 <= t - 3 < 8:
                    e1_s3(t - 3)
                if 0 <= t - 2 < 8:
                    e1_s2(t - 2)
                if s1ok:
                    e1_s1_gl(t - 1)

            for sl in range(8):
                gs = half * 8 + sl
                DMA("sp", ("x1ld", sl), x1[:, sl, :], xown[gs * 128:(gs + 1) * 128, :], [], [("x1", half, sl)])

            units2 = [(ch, tt, o) for ch in range(4) for tt in range(2) for o in range(2)]

            def e2_s1(ui):
                ch, tt, o = units2[ui]
                if tt == 0 and o == 0 and ch + 1 < 4:
                    e2_chunk_loads(ch + 1, wb)
                if ch == 3 and tt == 0 and o == 0:
                    load_w(stg[wo], w_out_v[:, :, 0:512], ("stg", wo))
                    load_w(stg[wo + 1], w_out_v[:, :, 512:1024], ("stg", wo + 1))
                b0 = e2_b0(ch, wb)
                s0, s1_ = b0 // 2, b0 // 2 + 1
                k0a, k0b, k1 = ("sa", s0), ("sb", s0), ("sc", s0)
                k2 = [("sa", s1_), ("sb", s1_)]
                pset = 4 * (ui % 2)
                tcols = slice(tt * 512, (tt + 1) * 512)
                slh = slice(half * 1024 + tt * 512, half * 1024 + (tt + 1) * 512)
                akeys = [("attnT", s_) for s_ in range(half * 8 + tt * 4, half * 8 + tt * 4 + 4)]
                hkeys = [hk(("hTo", s_)) for s_ in range(tt * 4, tt * 4 + 4)]
                ocols = slice(o * 128, (o + 1) * 128)
                bank = lambda i: (psF[pset + i] if pset + i < 6 else psBf[pset + i - 6])
                for kc in range(4):
                    MM(bank(0)[:, :], hs[b0][:, kc, ocols], attnT[:, kc, slh], kc == 0, kc == 3, [k0a] + akeys, [PSK(pset)])
                for c in range(8):
                    MM(bank(1)[:, :], hs[b0 + 1][:, c, ocols], hTo[:, c, tcols], c == 0, c == 7, [k1] + hkeys, [PSK(pset + 1)])
                for kc in range(4):
                    MM(bank(2)[:, :], hs[b0][:, 4 + kc, ocols], sguT[:, kc, tcols], kc == 0, kc == 3,
                       [k0b, hk(("sguT", tt))], [PSK(pset + 2)])
                for c in range(8):
                    MM(bank(3)[:, :], hs[b0 + 2][:, c, ocols], hTo[:, c, tcols], c == 0, c == 7, k2 + hkeys, [PSK(pset + 3)])

            def e2_s2(ui):
                ch, tt, o = units2[ui]
                par = ui % 2
                pset = 4 * par
                oc = ch * 2 + o
                tcols = slice(tt * 512, (tt + 1) * 512)
                bank = lambda i: (psF[pset + i] if pset + i < 6 else psBf[pset + i - 6])
                ta, tak = tg2[par][0], WK(2 * par)
                tb_, tbk = tg2[par][1], WK(2 * par + 1)
                m1, m1k = m122[par][0], WK(4 + 2 * par)
                m2, m2k = m122[par][1], WK(5 + 2 * par)
                ACT(ta, bank(1)[:, :], AF.Exp, [PSK(pset + 1)], [tak], scale=-1.0)
                ACT(tb_, bank(3)[:, :], AF.Exp, [PSK(pset + 3)], [tbk], scale=-1.0)
                ACT(ta, ta, AF.Ln, [tak], [tak], bias=1.0)
                ACT(tb_, tb_, AF.Ln, [tbk], [tbk], bias=1.0)
                ACT(ta, ta, AF.Exp, [tak], [tak], scale=-1.0)
                ACT(tb_, tb_, AF.Exp, [tbk], [tbk], scale=-1.0)
                TT("dve", m1, ta, bank(0)[:, :], ALU.mult, [tak, PSK(pset)], [m1k])
                TT("dve", m2, tb_, bank(2)[:, :], ALU.mult, [tbk, PSK(pset + 2)], [m2k])
                STT(mT[:, oc, tcols], m1, 2.0, m2, ALU.mult, ALU.add, [m1k, m2k], [hk(("mT", tt))])

            for t in range(len(units2) + 1):
                if t >= 1:
                    e2_s2(t - 1)
                if t < len(units2):
                    e2_s1(t)

            e3_xi = {}

            def zt_view(k):
                return (wkb[2 * k], wkb[2 * k + 1]), (WK(2 * k), WK(2 * k + 1))

            def e3_s1(sl):
                k = sl % 3
                tt = sl // 4
                (z0, z1), (zk0, zk1) = zt_view(k)
                l_, lk = lst[k], hk(("lst", k))
                for hf in range(2):
                    for c in range(8):
                        MM(psF[2 * k + hf][:, :], mT[:, c, sl * 128:(sl + 1) * 128], stg[wo + hf][:, c, :], c == 0, c == 7,
                           [hk(("mT", tt)), ("stg", wo + hf)], [PSK(2 * k + hf)])
                ACT(z0, psF[2 * k][:, :], AF.Square, [PSK(2 * k)], [zk0, lk], accum_out=l_[:, 0:1])
                ACT(z1, psF[2 * k + 1][:, :], AF.Square, [PSK(2 * k + 1)], [zk1, lk], accum_out=l_[:, 1:2])

            def e3_s2(sl):
                k = sl % 3
                l_, lk = lst[k], hk(("lst", k))
                TT("dve", l_[:, 2:3], l_[:, 0:1], l_[:, 1:2], ALU.add, [lk], [lk])
                ACT(l_[:, 3:4], l_[:, 2:3], AF.Ln, [lk], [lk], scale=1.0 / D, bias=4.0 * EPS)
                ACT(l_[:, 4:5], l_[:, 3:4], AF.Exp, [lk], [lk], scale=-0.5)

            def e3_s3(sl):
                k = sl % 3
                (z0, z1), (zk0, zk1) = zt_view(k)
                l_, lk = lst[k], hk(("lst", k))
                xk_ = ("x1", half, sl)
                STT(z0, psF[2 * k][:, :], l_[:, 4:5], gpo[:, 0:512], ALU.mult, ALU.mult, [PSK(2 * k), lk, "gpo"], [zk0])
                STT(z1, psF[2 * k + 1][:, :], l_[:, 4:5], gpo[:, 512:1024], ALU.mult, ALU.mult,
                    [PSK(2 * k + 1), lk, "gpo"], [zk1])
                TT("dve", x1[:, sl, 0:512], z0, x1[:, sl, 0:512], ALU.add, [zk0, xk_], [xk_])
                TT("dve", x1[:, sl, 512:1024], z1, x1[:, sl, 512:1024], ALU.add, [zk1, xk_], [xk_])

            def e3_s4(sl):
                e3_xi[sl] = nt_a1(x1[:, sl, :], ("x1", half, sl))

            def e3_s5(sl):
                nt_a2(e3_xi[sl], x1[:, sl, :], ("x1", half, sl), gpf, "gpf")

            def e3_s6(sl):
                nt_b(e3_xi[sl], h2T[:, :, sl * 128:(sl + 1) * 128], ("h2T", half, sl // 4), on_act=False)

            e3_stages = [e3_s1, e3_s2, e3_s3, e3_s4, e3_s5, e3_s6]
            for t in range(8 + 5):
                for si_ in range(5, -1, -1):
                    sl_ = t - si_
                    if 0 <= sl_ < 8:
                        e3_stages[si_](sl_)
                if t == 6:
                    f_loads(0, wb)

            if debug and upto == 4:
                dump([("d_x1", x1.rearrange("p a b -> p (a b)"), 8 * 1024)])
                finish()
                return nc
            P.barrier()
            A.release(m_l2b)
            del xt[2:]
            actT = A.alloc([NFC, 1024], BF16)
            ffA = A.alloc([8, 512], F32)
            fw = A.alloc([4, 512], F32)
            tgf = [fw[:, 0, :], fw[:, 2, :]]
            a1 = [fw[:, 1, :], fw[:, 3, :]]
            ot = [fw[:, 0:2, :].rearrange("p a b -> p (a b)"), fw[:, 2:4, :].rearrange("p a b -> p (a b)")]
            fst = A.alloc([8, 4], F32)
            fctr = 0
            for fq in range(6):
                nfc = 4 if fq < 5 else 2
                sg_, su_ = (2 * fq + wb) % NSTG, (2 * fq + 1 + wb) % NSTG
                if fq >= 1:
                    f_loads(fq, wb)
                for f in range(nfc):
                    fc = fq * 4 + f
                    fcols = slice(f * 128, (f + 1) * 128)
                    for tt in range(2):
                        tcols = slice(tt * 512, (tt + 1) * 512)
                        pi = fctr % 2
                        fctr += 1
                        pg, pu = psF[pi], psF[2 + pi]
                        for c in range(8):
                            MM(pg[:, :], stg[sg_][:, c, fcols], h2T[:, c, tcols], c == 0, c == 7,
                               [("stg", sg_), ("h2T", half, tt)], [PSK(pi)])
                        for c in range(8):
                            MM(pu[:, :], stg[su_][:, c, fcols], h2T[:, c, tcols], c == 0, c == 7,
                               [("stg", su_), ("h2T", half, tt)], [PSK(2 + pi)])
                        ACT(tgf[pi], pg[:, :], AF.Exp, [PSK(pi)], [hk(("fw", 2 * pi))], scale=-1.0)
                        ACT(tgf[pi], tgf[pi], AF.Ln, [hk(("fw", 2 * pi))], [hk(("fw", 2 * pi))], bias=1.0)
                        ACT(tgf[pi], tgf[pi], AF.Exp, [hk(("fw", 2 * pi))], [hk(("fw", 2 * pi))], scale=-1.0)
                        TT("dve", a1[pi], tgf[pi], pg[:, :], ALU.mult, [hk(("fw", 2 * pi)), PSK(pi)], [hk(("fw", 2 * pi + 1))])
                        TT("dve", actT[:, fc, tcols], a1[pi], pu[:, :], ALU.mult, [hk(("fw", 2 * pi + 1)), PSK(2 + pi)], [hk(("actT", tt))])

            if debug and upto == 5:
                finish()
                return nc
            gjunk = A.alloc([512], BF16)

            def bank_of(sl):
                return psF[sl] if sl < 6 else psBf[sl - 6]

            def g_keys(sl):
                ob = sl % 2
                return hk(("fw", 2 * ob)), hk(("fw", 2 * ob + 1)), hk(("fst", sl))

            def g_ev1(r, sl):
                pbank = bank_of(sl)
                ok, ok2, fk = g_keys(sl)
                ACT(gjunk, pbank[:, :], AF.Square, [PSK(sl)], [hk("gjunk"), fk], accum_out=fst[:, sl, r:r + 1])
                if r == 0:
                    TT("dve", ffA[:, sl, :], pbank[:, :], gpff[:, 0:512], ALU.mult, [PSK(sl), "gpff"], [hk(("ffA", sl))])

            def g_ev2(sl):
                ok, ok2, fk = g_keys(sl)
                TT("dve", fst[:, sl, 2:3], fst[:, sl, 0:1], fst[:, sl, 1:2], ALU.add, [fk], [fk])
                ACT(fst[:, sl, 3:4], fst[:, sl, 2:3], AF.Ln, [fk], [fk], scale=1.0 / D, bias=EPS)
                ACT(fst[:, sl, 3:4], fst[:, sl, 3:4], AF.Exp, [fk], [fk], scale=-0.5)

            def g_ev3(sl):
                gs = half * 8 + sl
                pbank = bank_of(sl)
                ob = sl % 2
                ok, ok2, fk = g_keys(sl)
                STT(ot[ob][:, 0:512], ffA[:, sl, :], fst[:, sl, 3:4], x1[:, sl, 0:512], ALU.mult, ALU.add,
                    [hk(("ffA", sl)), fk, ("x1", half, sl)], [ok, ok2])
                TT("dve", ot[ob][:, 512:1024], pbank[:, :], gpff[:, 512:1024], ALU.mult, [PSK(sl), "gpff"], [ok, ok2])
                STT(ot[ob][:, 512:1024], ot[ob][:, 512:1024], fst[:, sl, 3:4], x1[:, sl, 512:1024], ALU.mult, ALU.add,
                    [ok, ok2, fk, ("x1", half, sl)], [ok, ok2])
                DMA("sp", ("out", ob), out_d[gs * 128:(gs + 1) * 128, :], ot[ob], [ok, ok2], [("outd", gs)])

            for r in range(2):
                sis = []
                for k3 in range(3):
                    n8 = 8 if k3 < 2 else 6
                    si = (k3 + r * 3 + wb) % NSTG
                    sis.append((si, n8))
                    load_w(stg[si][:, 0:n8, :], w_ffd_v[:, k3 * 8:k3 * 8 + n8, r * 512:(r + 1) * 512], ("stg", si))
                si, n8 = sis[0]
                for f in range(n8):
                    fc = f
                    for sl in range(8):
                        pbank = psF[sl] if sl < 6 else psBf[sl - 6]
                        MM(pbank[:, :], actT[:, fc, sl * 128:(sl + 1) * 128], stg[si][:, f, :], fc == 0, fc == NFC - 1,
                           [hk(("actT", sl // 4)), ("stg", si)], [PSK(sl)])
                if r == 1 and half == 0:
                    e1_loads(2)
                    e1_prefetch_x(1)
                for sl in range(8):
                    pbank = psF[sl] if sl < 6 else psBf[sl - 6]
                    for k3 in (1, 2):
                        si, n8 = sis[k3]
                        for f in range(n8):
                            fc = k3 * 8 + f
                            MM(pbank[:, :], actT[:, fc, sl * 128:(sl + 1) * 128], stg[si][:, f, :], fc == 0, fc == NFC - 1,
                               [hk(("actT", sl // 4)), ("stg", si)], [PSK(sl)])
                    g_ev1(r, sl)
                    if r == 1:
                        if sl >= 1:
                            g_ev2(sl - 1)
                        if sl >= 2:
                            g_ev3(sl - 2)
                if r == 1:
                    g_ev2(7)
                    g_ev3(6)
                    g_ev3(7)
            P.barrier()
            if debug and upto == 6:
                finish()
                return nc

        finish()
    return nc


_NC_CACHE = {}


def _layout_inputs(inp):
    f = lambda a: np.ascontiguousarray(np.asarray(a, dtype=np.float32))
    x = f(inp["x"])
    rep = lambda v, n=128: np.ascontiguousarray(np.broadcast_to(f(v).reshape(1, -1), (n, f(v).size)))
    common = {
        "w_in": f(inp["w_in"][0]), "w_a": f(inp["w_branch_a"][0]), "w_b": f(inp["w_branch_b"][0]),
        "w_out": f(inp["w_out"][0]), "w_ffi": f(inp["w_ffn_in"][0]), "w_ffd": f(inp["w_ffn_down"][0]),
        "gpm": rep(inp["g_pre_mix"][0]), "gpo": rep(inp["g_post_mix"][0]), "gpf": rep(inp["g_pre_ffn"][0]),
        "gpff": rep(inp["g_post_ffn"][0]), "gsg": rep(inp["g_sgu"][0]), "bsg": rep(inp["b_sgu"][0]),
        "bfb": rep(inp["b_forget"][0]),
        "gqc": np.ascontiguousarray(np.tile(f(inp["g_q"][0]).reshape(64, 1), (2, 1))),
        "gkc": np.ascontiguousarray(np.tile(f(inp["g_k"][0]).reshape(64, 1), (2, 1))),
        "wsp": f(inp["w_spatial"][0]),
        "bsp": np.ascontiguousarray(f(inp["b_spatial"][0]).T),
    }
    ones = np.ones((128, 128), np.float32)
    zeros = np.zeros((128, 128), np.float32)
    tri = np.triu(np.ones((128, 128), np.float32))
    in_maps = []
    for c in range(8):
        b, par = c // 2, c % 2
        blocks = []
        mk = np.zeros((128, NS, 2, 128), np.float32)
        for s in range(NS):
            m = s // 2
            if par == 0:
                i = 4 * m if s % 2 == 0 else 4 * m + 3
            else:
                i = 4 * m + 1 if s % 2 == 0 else 4 * m + 2
            blocks.append(i)
            J = J_of(s)
            for k in range(2):
                j = J - 1 + k
                mk[:, s, k, :] = ones if j < i else (tri if j == i else zeros)
        xo = np.concatenate([x[b, i * 128:(i + 1) * 128] for i in blocks], axis=0)
        d = dict(common)
        d["xseq"] = np.ascontiguousarray(x[b])
        d["xown"] = np.ascontiguousarray(xo)
        d["msk"] = np.ascontiguousarray(mk.reshape(128, NS * 256))
        in_maps.append((d, blocks))
    return in_maps


def kernel(**inputs):
    maps = _layout_inputs(inputs)
    if "nc" not in _NC_CACHE:
        _NC_CACHE["nc"] = build_nc()
    nc = _NC_CACHE["nc"]
    res = run_bass_kernel_spmd(nc, [m[0] for m in maps], core_ids=list(range(8)))
    out = np.empty((4, SEQ, D), np.float32)
    for c in range(8):
        b = c // 2
        o = res.results[c]["out"]
        for s, i in enumerate(maps[c][1]):
            out[b, i * 128:(i + 1) * 128] = o[s * 128:(s + 1) * 128]
    return out
```

```python
import numpy as np
from contextlib import ExitStack
import concourse.bass as bass
import concourse.mybir as mybir
from concourse.bass_utils import run_bass_kernel_spmd

F32 = mybir.dt.float32
BF16 = mybir.dt.bfloat16
AF = mybir.ActivationFunctionType
ALU = mybir.AluOpType
AX = mybir.AxisListType

D = 1024
SEQ = 4096
NB = 32
NS = 16
TOWN = 2048
HEADS = 8
DH = 64
Q_OFF, K_OFF, V_OFF, F_OFF, U_OFF, G_OFF = 0, 512, 1024, 1536, 1544, 2568
IN_COLS = 4616
DFF = 2816
NFC = 22
EPS = 1e-6
GELU_C = 0.7978845608028654


def J_of(slot):
    return 4 * (slot // 2) + (1 if slot % 2 == 0 else 3)


ENGS = ["pe", "act", "dve", "pool", "sp"]


class Prog:
    def __init__(self, nc):
        self.nc = nc
        self.recs = {e: [] for e in ENGS}
        self.lastw = {}
        self.readers = {}
        self.dma_count = {}

    def _deps(self, eng, reads, writes, is_dma):
        deps = set()
        for k in reads:
            t = self.lastw.get(k)
            if t is not None:
                deps.add(t)
            if isinstance(k, tuple) and k[0] == "ps":
                for t2 in self.readers.get(k, {}).values():
                    if not (t2[0] == "e" and t2[1] == eng):
                        deps.add(t2)
        strict = is_dma or eng != "pe"
        for k in writes:
            t = self.lastw.get(k)
            if t is not None and (strict or not (t[0] == "e" and t[1] == eng)):
                deps.add(t)
            for t2 in self.readers.get(k, {}).values():
                if strict or not (t2[0] == "e" and t2[1] == eng):
                    deps.add(t2)
        return deps

    def _register(self, tok, rk, reads, writes):
        for k in reads:
            self.readers.setdefault(k, {})[rk] = tok
        for k in writes:
            self.lastw[k] = tok
            self.readers[k] = {}

    @staticmethod
    def _expand(keys):
        out = []
        for k in keys:
            out.append(k)
            if isinstance(k, tuple) and len(k) == 2 and k[0] == "stg":
                out += [("sa", k[1]), ("sb", k[1]), ("sc", k[1])]
        return out

    def op(self, eng, fn, reads=(), writes=()):
        reads, writes = self._expand(reads), self._expand(writes)
        idx = len(self.recs[eng])
        deps = self._deps(eng, reads, writes, False)
        tok = ("e", eng, idx)
        self.recs[eng].append(dict(fn=fn, deps=deps, dma=None))
        self._register(tok, eng, reads, writes)

    def dma(self, eng, semkey, fn, reads=(), writes=()):
        reads, writes = self._expand(reads), self._expand(writes)
        n = self.dma_count.get(semkey, 0) + 1
        self.dma_count[semkey] = n
        tok = ("d", semkey, n)
        deps = self._deps(eng, reads, writes, True)
        self.recs[eng].append(dict(fn=fn, deps=deps, dma=semkey))
        self._register(tok, ("d", semkey), reads, writes)

    def barrier(self):
        toks = set()
        for e in ENGS:
            if self.recs[e]:
                for i in range(len(self.recs[e]) - 1, -1, -1):
                    r = self.recs[e][i]
                    if r["fn"] is not None and r["dma"] is None:
                        toks.add(("e", e, i))
                        break
        for k, n in self.dma_count.items():
            toks.add(("d", k, n))
        for e in ENGS:
            deps = set(t for t in toks if not (t[0] == "e" and t[1] == e))
            self.recs[e].append(dict(fn=None, deps=deps, dma=None))

    def final_wait(self, eng, semkeys):
        deps = set(("d", k, self.dma_count[k]) for k in semkeys if k in self.dma_count)
        self.recs[eng].append(dict(fn=None, deps=deps, dma=None))

    def emit(self):
        nc = self.nc
        sig = {e: [False] * len(self.recs[e]) for e in ENGS}
        for e in ENGS:
            for r in self.recs[e]:
                for t in r["deps"]:
                    if t[0] == "e":
                        sig[t[1]][t[2]] = True
        rank = {}
        for e in ENGS:
            c = 0
            rk = []
            for i in range(len(self.recs[e])):
                if sig[e][i]:
                    c += 1
                rk.append(c)
            rank[e] = rk
        with ExitStack() as st:
            esem = {e: st.enter_context(nc.semaphore("s_" + e)) for e in ENGS}
            dsem = {}
            for i, k in enumerate(sorted(self.dma_count.keys(), key=str)):
                dsem[k] = st.enter_context(nc.semaphore("d%d" % i))
            block = st.enter_context(nc.Block())
            bname = {"pe": "tensor", "act": "scalar", "dve": "vector", "pool": "gpsimd", "sp": "sync"}
            for e in ENGS:
                def body(engine, e=e):
                    waited = {}
                    for i, r in enumerate(self.recs[e]):
                        for t in sorted(r["deps"], key=str):
                            if t[0] == "e":
                                key = ("e", t[1]); val = rank[t[1]][t[2]]; sem = esem[t[1]]
                            else:
                                key = ("d", t[1]); val = 16 * t[2]; sem = dsem[t[1]]
                            if waited.get(key, 0) >= val:
                                continue
                            engine.wait_ge(sem, val)
                            waited[key] = val
                        if r["fn"] is None:
                            continue
                        ins = r["fn"](engine)
                        if r["dma"] is not None:
                            ins.then_inc(dsem[r["dma"]], 16)
                        elif sig[e][i]:
                            ins.then_inc(esem[e], 1)
                getattr(block, bname[e])(body)


class Arena:
    def __init__(self, sb, total):
        self.sb = sb
        self.total = total
        self.off = 0
        self.peak = 0

    def alloc(self, free_shape, dt):
        n = 1
        for s in free_shape:
            n *= s
        esz = 4 if dt == F32 else 2
        nbytes = (n * esz + 63) // 64 * 64
        o = self.off
        self.off += nbytes
        self.peak = max(self.peak, self.off)
        assert self.off <= self.total, ("SBUF arena overflow", self.off, self.total)
        v = self.sb[:, o // 2:o // 2 + n * esz // 2]
        if dt == F32:
            v = v.bitcast(F32)
        if len(free_shape) == 2:
            v = v.rearrange("p (a b) -> p a b", a=free_shape[0])
        elif len(free_shape) == 3:
            v = v.rearrange("p (a b c) -> p a b c", a=free_shape[0], b=free_shape[1])
        return v

    def mark(self):
        return self.off

    def release(self, m):
        self.off = m


def build_nc(debug=False, upto=3):
    nc = bass.Bass("TRN2", target_bir_lowering=False)

    def din(name, shape):
        return nc.dram_tensor(name, list(shape), F32, kind="ExternalInput").ap()

    xseq = din("xseq", [SEQ, D])
    xown = din("xown", [TOWN, D])
    w_in = din("w_in", [D, IN_COLS])
    w_a = din("w_a", [512, D])
    w_b = din("w_b", [512, D])
    w_out = din("w_out", [D, D])
    w_ffi = din("w_ffi", [D, 2 * DFF])
    w_ffd = din("w_ffd", [DFF, D])
    gpm_d = din("gpm", [128, D])
    gpo_d = din("gpo", [128, D])
    gpf_d = din("gpf", [128, D])
    gpff_d = din("gpff", [128, D])
    gsg_d = din("gsg", [128, 512])
    bsg_d = din("bsg", [128, 512])
    bfb_d = din("bfb", [128, 8])
    gqc_d = din("gqc", [128, 1])
    gkc_d = din("gkc", [128, 1])
    wsp_d = din("wsp", [8, 128, 128])
    bsp_d = din("bsp", [128, 8])
    msk_d = din("msk", [128, NS * 256])
    out_d = nc.dram_tensor("out", [TOWN, D], F32, kind="ExternalOutput").ap()
    dbg = {}
    if debug:
        for nm, shp in [("d_kt", [128, 4 * SEQ]), ("d_qt", [128, 4 * TOWN]), ("d_v", [128, NB * 8 * 65]),
                        ("d_c", [128, 256]), ("d_at", [128, 4 * TOWN]), ("d_x1", [128, 8 * 1024])]:
            dbg[nm] = nc.dram_tensor(nm, shp, F32, kind="ExternalOutput").ap()

    w_in_v = w_in.rearrange("(c p) n -> p c n", p=128)
    w_a_v = w_a.rearrange("(c p) n -> p c n", p=128)
    w_b_v = w_b.rearrange("(c p) n -> p c n", p=128)
    w_out_v = w_out.rearrange("(c p) n -> p c n", p=128)
    w_ffi_v = w_ffi.rearrange("(c p) n -> p c n", p=128)
    w_ffd_v = w_ffd.rearrange("(c p) n -> p c n", p=128)

    TOTAL = 212480
    with ExitStack() as stack:
        sb = stack.enter_context(nc.sbuf_tensor("sb", [128, TOTAL // 2], BF16))
        psF = [stack.enter_context(nc.psum_tensor("psf%d" % i, [128, 512], F32)) for i in range(6)]
        psBf = [stack.enter_context(nc.psum_tensor("psb%d" % i, [128, 512], F32)) for i in range(2)]
        psB = [t[:, :].bitcast(BF16) for t in psBf]
        A = Arena(sb, TOTAL)
        P = Prog(nc)

        def PSK(i):
            return ("ps", i)

        def MM(out, lhsT, rhs, start, stop, reads, writes):
            P.op("pe", lambda e: e.matmul(out, lhsT=lhsT, rhs=rhs, start=start, stop=stop), reads, writes)

        def TR(out, in_, reads, writes):
            P.op("pe", lambda e: e.transpose(out=out, in_=in_, identity=ident), list(reads) + ["ident"], writes)

        def ACT(out, in_, func, reads, writes, **kw):
            P.op("act", lambda e: e.activation(out=out, in_=in_, func=func, **kw), reads, writes)

        def TT(eng, out, in0, in1, op, reads, writes):
            P.op(eng, lambda e: e.tensor_tensor(out=out, in0=in0, in1=in1, op=op), reads, writes)

        def TS(eng, out, in0, s1, s2, op0, op1, reads, writes):
            if s2 is None:
                P.op(eng, lambda e: e.tensor_scalar(out=out, in0=in0, scalar1=s1, scalar2=None, op0=op0), reads, writes)
            else:
                P.op(eng, lambda e: e.tensor_scalar(out=out, in0=in0, scalar1=s1, scalar2=s2, op0=op0, op1=op1), reads, writes)

        def STT(out, in0, scalar, in1, op0, op1, reads, writes):
            P.op("dve", lambda e: e.scalar_tensor_tensor(out=out, in0=in0, scalar=scalar, in1=in1, op0=op0, op1=op1),
                 reads, writes)

        def CP(eng, out, in_, reads, writes):
            P.op(eng, lambda e: e.tensor_copy(out=out, in_=in_), reads, writes)

        def MS(eng, ap, val, reads, writes):
            P.op(eng, lambda e: e.memset(ap, val), reads, writes)

        def DMA(eng, semkey, out, in_, reads, writes):
            P.dma(eng, semkey, lambda e: e.dma_start(out=out, in_=in_), reads, writes)

        def load_w(dst_view, src_view, key):
            DMA("pool", ("w", key), dst_view, src_view, [], [key])

        ident = A.alloc([128], BF16)
        bones = A.alloc([128], BF16)
        tri = A.alloc([128], F32)
        aones = A.alloc([128], F32)
        gqc = A.alloc([1], F32)
        gkc = A.alloc([1], F32)
        bfb = A.alloc([8], F32)
        gpm = A.alloc([D], F32)
        attnT = A.alloc([4, TOWN], BF16)
        NSTG = 4
        stg = [A.alloc([8, 512], BF16) for _ in range(NSTG)]
        xt = [A.alloc([D], F32) for _ in range(2)]
        NXN = 4
        xn = [A.alloc([D], BF16) for _ in range(NXN)]
        stt_ = [A.alloc([8], F32) for _ in range(NXN)]

        MS("pool", ident, 0.0, [], ["ident"])
        P.op("pool", lambda e: e.affine_select(out=ident, in_=ident, pattern=[[-1, 128]], compare_op=ALU.not_equal,
                                               fill=1.0, base=0, channel_multiplier=1),
             reads=["ident"], writes=["ident"])
        MS("pool", bones, 0.0, [], ["bones"])
        MS("pool", bones[0:64, 0:64], 1.0, ["bones"], ["bones"])
        MS("pool", bones[64:128, 64:128], 1.0, ["bones"], ["bones"])
        MS("pool", aones, 1.0, [], ["aones"])
        MS("pool", tri, 1.0, [], ["tri"])
        P.op("pool", lambda e: e.affine_select(out=tri, in_=tri, pattern=[[1, 128]], compare_op=ALU.is_ge,
                                               fill=0.0, base=0, channel_multiplier=-1),
             reads=["tri"], writes=["tri"])
        DMA("sp", "c3", gpm, gpm_d, [], ["gpm"])

        def small_const_loads():
            DMA("sp", "c0", gqc, gqc_d, [], ["gqc"])
            DMA("sp", "c1", gkc, gkc_d, [], ["gkc"])
            DMA("sp", "c2", bfb, bfb_d, [], ["bfb"])

        gsg = A.alloc([512], F32)
        bsg = A.alloc([512], F32)
        wsT = A.alloc([8, 128], BF16)
        bsp = A.alloc([8], F32)
        wtmp8, wtmpb8 = [], []

        def prep_mix_consts_a():
            DMA("sp", "c7", gsg, gsg_d, [], ["gsg"])
            DMA("sp", "c8", bsg, bsg_d, [], ["bsg"])
            DMA("sp", "c9", bsp, bsp_d, [], ["bsp"])
            for g in range(8):
                DMA("pool", ("c10", g), wtmpb8[g], wsp_d[g], [], [("wtmpb", g)])
                MS("dve", wtmpb8[g][0:64, 64:128], 0.0, [("wtmpb", g)], [("wtmpb", g)])

        def prep_mix_consts_b():
            for g in range(8):
                w_ = g % 2
                TR(psB[w_][:, 0:128], wtmpb8[g], [("wtmpb", g)], [PSK(6 + w_)])
                CP("dve", wsT[:, g, :], psB[w_][:, 0:128], [PSK(6 + w_)], ["wsT"])

        xctr = [0]
        psb_ctr = [0]

        xnc = [0]

        def nt_a(src_rows, gbc, gkey, eps_scale=1.0, keep=None):
            if keep is None:
                b = xctr[0] % len(xt)
                xctr[0] += 1
                x_t, xk = xt[b], ("xt", b)
                DMA("sp", ("xt", b), x_t, src_rows, [], [xk])
            else:
                x_t, xk = keep
            xi = nt_a1(x_t, xk, eps_scale)
            nt_a2(xi, x_t, xk, gbc, gkey)
            return xi

        def nt_a1(x_t, xk, eps_scale=1.0):
            xi = xnc[0] % NXN
            xnc[0] += 1
            x_n, nk = xn[xi], ("xn", xi)
            s_t, sk = stt_[xi], ("st", xi)
            ACT(x_n, x_t, AF.Square, [xk], [nk, sk], accum_out=s_t[:, 0:1])
            ACT(s_t[:, 1:2], s_t[:, 0:1], AF.Ln, [sk], [sk], scale=1.0 / D, bias=EPS * eps_scale)
            ACT(s_t[:, 2:3], s_t[:, 1:2], AF.Exp, [sk], [sk], scale=-0.5)
            return xi

        def nt_a2(xi, x_t, xk, gbc, gkey):
            x_n, nk = xn[xi], ("xn", xi)
            s_t, sk = stt_[xi], ("st", xi)
            STT(x_n, x_t, s_t[:, 2:3], gbc, ALU.mult, ALU.mult, [xk, sk, gkey], [nk])

        def nt_b(xi, dst_view, dst_key, on_act=False):
            x_n, nk = xn[xi], ("xn", xi)
            pb = psb_ctr[0] % 2
            psb_ctr[0] += 1
            pst = psB[pb]
            for c in range(8):
                TR(pst[:, c * 128:(c + 1) * 128], x_n[:, c * 128:(c + 1) * 128], [nk], [PSK(6 + pb)])
            if on_act:
                ACT(dst_view, pst.rearrange("p (c t) -> p c t", c=8), AF.Copy, [PSK(6 + pb)], [dst_key])
            else:
                CP("dve", dst_view, pst.rearrange("p (c t) -> p c t", c=8), [PSK(6 + pb)], [dst_key])

        m_l1 = A.mark()
        KT = A.alloc([4, SEQ], BF16)
        Vaug = A.alloc([NB, 8, 65], BF16)
        QT = A.alloc([4, TOWN], BF16)
        zf = A.alloc([NB, 8], F32)
        Cc = A.alloc([NB, 8], F32)
        Eb = A.alloc([NB + 1, 8], F32)
        msk = A.alloc([NS, 256], BF16)
        m_l2 = A.mark()
        hTg = [A.alloc([8, 512], BF16) for _ in range(2)]
        sq = [A.alloc([512], BF16) for _ in range(3)]
        rs = [A.alloc([512], F32) for _ in range(3)]
        wv2 = A.alloc([8, 264], BF16)
        xt.append(A.alloc([D], F32))
        xt.append(A.alloc([D], F32))
        for _g in range(8):
            wtmpb8.append(A.alloc([128], BF16))

        MS("pool", Vaug.rearrange("p a b c -> p (a b) c")[:, :, 64:65], 1.0, [], ["vones"])

        load_w(stg[0], w_in_v[:, :, K_OFF:K_OFF + 512], ("stg", 0))
        load_w(stg[1][:, :, 0:256], w_in_v[:, :, V_OFF:V_OFF + 256], ("stg", 1))
        load_w(wv2, w_in_v[:, :, V_OFF + 256:V_OFF + 520], "wv2")
        def late_loads():
            load_w(stg[2], w_in_v[:, :, Q_OFF:Q_OFF + 512], ("stg", 2))
            load_w(msk.rearrange("p a b -> p (a b)"), msk_d, "msk")

        fm_ctr = [0]

        def proj_fm_1(wview, wkey, p, hT, hkey):
            i = fm_ctr[0] % 3
            fm_ctr[0] += 1
            ps = psF[i]
            for c in range(8):
                MM(ps[:, :], wview[:, c, p * 128:(p + 1) * 128], hT[:, c, :], c == 0, c == 7, [wkey, hkey], [PSK(i)])
            ACT(sq[i], ps[:, :], AF.Square, [PSK(i)], [("sq", i)])
            return i

        def proj_fm_2(i, dst, dkey, gcol, gckey):
            ps, ps2 = psF[i], psF[3]
            MM(ps2[:, :], bones, sq[i], True, True, ["bones", ("sq", i)], [PSK(3)])
            ACT(rs[i], ps2[:, :], AF.Ln, [PSK(3)], [("rs", i)], scale=1.0 / DH, bias=EPS)
            ACT(rs[i], rs[i], AF.Exp, [("rs", i)], [("rs", i)], scale=-0.5)
            STT(dst, ps[:, :], gcol[:, 0:1], rs[i], ALU.mult, ALU.mult, [PSK(i), ("rs", i), gckey], [dkey])

        NG = 12
        xis = {}

        def ab_s1a(g):
            src = xseq if g < 8 else xown
            g0 = g if g < 8 else g - 8
            xis[g] = [nt_a(src[(g0 * 4 + bl) * 128:(g0 * 4 + bl + 1) * 128, :], gpm, "gpm") for bl in range(4)]

        def ab_s1b(g):
            hb = g % 2
            for bl in range(4):
                nt_b(xis[g][bl], hTg[hb][:, :, bl * 128:(bl + 1) * 128], ("hTg", hb))

        def ab_s2(g):
            hb = g % 2
            if g < 8:
                grp = g
                for p in range(4):
                    bl = p
                    ii = proj_fm_1(stg[0], ("stg", 0), p, hTg[hb], ("hTg", hb))
                    if p >= 1:
                        proj_fm_2(prev[0], *prev[1])
                    prev = (ii, (KT[:, p, grp * 512:(grp + 1) * 512], ("KT", p, grp), gkc, "gkc"))
                    blk = grp * 4 + bl
                    for c in range(8):
                        MM(psF[4][:, 0:256], hTg[hb][:, c, bl * 128:(bl + 1) * 128], stg[1][:, c, 0:256], c == 0, c == 7,
                           [("hTg", hb), ("stg", 1)], [PSK(4)])
                    for c in range(8):
                        MM(psF[5][:, 0:264], hTg[hb][:, c, bl * 128:(bl + 1) * 128], wv2[:, c, :], c == 0, c == 7,
                           [("hTg", hb), "wv2"], [PSK(5)])
                    ACT(Vaug[:, blk, 0:4, 0:64], psF[4][:, 0:256].rearrange("p (h d) -> p h d", h=4), AF.Copy,
                        [PSK(4)], [("V", blk)])
                    ACT(Vaug[:, blk, 4:8, 0:64], psF[5][:, 0:256].rearrange("p (h d) -> p h d", h=4), AF.Copy,
                        [PSK(5), ("V", blk)], [("V", blk)])
                    TT("dve", zf[:, blk, :], psF[5][:, 256:264], bfb, ALU.add, [PSK(5), "bfb"], ["zf"])
                return prev
            else:
                grp = g - 8
                for p in range(4):
                    ii = proj_fm_1(stg[2], ("stg", 2), p, hTg[hb], ("hTg", hb))
                    if p >= 1:
                        proj_fm_2(prev[0], *prev[1])
                    prev = (ii, (QT[:, p, grp * 512:(grp + 1) * 512], ("QT", p, grp), gqc, "gqc"))
                return prev

        zf2 = zf.rearrange("p a b -> p (a b)")
        Cc2 = Cc.rearrange("p a b -> p (a b)")

        def phase_c1():
            ACT(zf2, zf2, AF.Exp, ["zf"], ["zf"], scale=-1.0)
            ACT(zf2, zf2, AF.Ln, ["zf"], ["zf"], bias=1.0)

        def phase_c2():
            MM(psF[0][:, 0:256], tri, zf2, True, True, ["tri", "zf"], [PSK(0)])
            MM(psF[1][:, 0:256], aones, zf2, True, True, ["aones", "zf"], [PSK(1)])
            MS("dve", Eb[:, 0, :], 0.0, [], ["Eb"])
            CP("dve", Cc2, psF[1][:, 0:256], [PSK(1)], ["Cc"])
            for j in range(1, NB + 1):
                TT("dve", Eb[:, j, :], Eb[:, j - 1, :], Cc[:, j - 1, :], ALU.add, ["Eb", "Cc"], ["Eb"])
            TT("dve", Cc2, psF[0][:, 0:256], Eb[:, 0:NB, :].rearrange("p a b -> p (a b)"), ALU.add, [PSK(0), "Eb"], ["Cc"])

        for t in range(NG + 1):
            if t == 2:
                late_loads()
            if t == 4:
                prep_mix_consts_a()
            if t == 5:
                prep_mix_consts_b()
            if t < NG:
                ab_s1a(t)
            if t == 0:
                small_const_loads()
            last = ab_s2(t - 1) if t >= 1 else None
            if t < NG:
                ab_s1b(t)
            if last is not None:
                proj_fm_2(last[0], *last[1])
            if t == 8:
                phase_c1()
            if t == 9:
                phase_c2()

        def dump(items):
            P.barrier()
            mk_ = A.mark()
            dv = xn[0].bitcast(F32)
            for nm, src, n in items:
                for o in range(0, n, 256):
                    m = min(256, n - o)
                    CP("dve", dv[:, 0:m], src[:, o:o + m], ["dbgsrc"], ["dv"])
                    DMA("sp", "dbg", dbg[nm][:, o:o + m], dv[:, 0:m], ["dv"], ["dbgout"])
            P.barrier()
            A.release(mk_)

        def finish():
            P.final_wait("sp", [("out", 0), ("out", 1), "dbg"])
            P.emit()

        if debug:
            dump([("d_kt", KT.rearrange("p a b -> p (a b)"), 4 * SEQ), ("d_qt", QT.rearrange("p a b -> p (a b)"), 4 * TOWN),
                  ("d_v", Vaug.rearrange("p a b c -> p (a b c)"), NB * 8 * 65), ("d_c", Cc.rearrange("p a b -> p (a b)"), 256)])
            if upto == 1:
                finish()
                return nc

        hs = [stg[k // 2][:, :, (k % 2) * 256:(k % 2 + 1) * 256] for k in range(8)]

        def e2_b0(ch, wb):
            even = (ch % 2 == 0)
            if wb == 0:
                return 4 if even else 0
            return 0 if even else 4

        def e2_chunk_loads(ch, wb=0):
            b0 = e2_b0(ch, wb)
            c0 = ch * 256
            s0, s1_ = b0 // 2, b0 // 2 + 1
            load_w(hs[b0][:, 0:4, :], w_a_v[:, :, c0:c0 + 256], ("sa", s0))
            load_w(hs[b0][:, 4:8, :], w_b_v[:, :, c0:c0 + 256], ("sb", s0))
            load_w(hs[b0 + 1], w_in_v[:, :, G_OFF + c0:G_OFF + c0 + 256], ("sc", s0))
            load_w(hs[b0 + 2][:, 0:4, :], w_in_v[:, 0:4, G_OFF + 1024 + c0:G_OFF + 1024 + c0 + 256], ("sa", s1_))
            load_w(hs[b0 + 2][:, 4:8, :], w_in_v[:, 4:8, G_OFF + 1024 + c0:G_OFF + 1024 + c0 + 256], ("sb", s1_))

        pre_x = {}

        def e1_prefetch_x(half_):
            for sl_ in range(2):
                gs_ = half_ * 8 + sl_
                DMA("sp", ("xt", sl_), xt[sl_], xown[gs_ * 128:(gs_ + 1) * 128, :], [], [("xt", sl_)])
                pre_x[(half_, sl_)] = sl_

        def e1_loads(wb=0):
            load_w(stg[wb], w_in_v[:, :, U_OFF:U_OFF + 512], ("stg", wb))
            load_w(stg[wb + 1], w_in_v[:, :, U_OFF + 512:U_OFF + 1024], ("stg", wb + 1))

        def f_loads(fq, foff=0):
            nfc = 4 if fq < 5 else 2
            sg_, su_ = (2 * fq + foff) % NSTG, (2 * fq + 1 + foff) % NSTG
            load_w(stg[sg_][:, :, 0:nfc * 128], w_ffi_v[:, :, fq * 512:fq * 512 + nfc * 128], ("stg", sg_))
            load_w(stg[su_][:, :, 0:nfc * 128], w_ffi_v[:, :, DFF + fq * 512:DFF + fq * 512 + nfc * 128], ("stg", su_))

        P.barrier()
        A.release(m_l2)
        del xt[2:]
        pT = [A.alloc([NB * 128], BF16) for _ in range(2)]
        Vp = [A.alloc([NB, 65], BF16) for _ in range(4)]
        wS = [A.alloc([NB, 8], F32) for _ in range(2)]
        wB = [A.alloc([NB, 8], F32) for _ in range(2)]
        atok = [A.alloc([512], BF16) for _ in range(2)]
        rec = [A.alloc([8], F32) for _ in range(2)]

        units = [(s, p) for s in range(NS) for p in range(4)]
        items = []
        for ui, (s, p) in enumerate(units):
            nb_ = J_of(s) + 1
            for c0 in range(0, nb_, 4):
                items.append((ui, c0))
        cc = [0]

        NSET = 3
        LOOK = 6

        def acc_banks(s):
            return (psBf[0], PSK(6)), (psBf[1], PSK(7))

        def att_prep(ui):
            s, p = units[ui]
            J = J_of(s)
            nb = J + 1
            sb_ = s % 2
            if p == 0:
                TT("dve", wB[sb_][:, 0:nb, :], Cc[:, 0:nb, :], Eb[:, J:J + 1, :].to_broadcast([128, nb, 8]), ALU.subtract,
                   ["Cc", "Eb"], [("wB", sb_)])
                ACT(wS[sb_][:, 0:nb, :], wB[sb_][:, 0:nb, :], AF.Exp, [("wB", sb_)], [("wS", sb_)])
            for ab in range(2):
                h = 2 * p + ab
                vi = 2 * (ui % 2) + ab
                TT("dve", Vp[vi][:, 0:nb, :], Vaug[:, 0:nb, h, :], wS[sb_][:, 0:nb, h:h + 1].to_broadcast([128, nb, 65]), ALU.mult,
                   [("V", j) for j in range(nb)] + ["vones", ("wS", sb_)], [("Vp", vi)])

        def att_qk(ui, c0):
            s, p = units[ui]
            J = J_of(s)
            nb = J + 1
            n = min(4, nb - c0)
            set_ = cc[0] % NSET
            cc[0] += 1
            for jj in range(n):
                j = c0 + jj
                for ab in range(2):
                    r0 = ab * 64
                    bk = 2 * set_ + ab
                    MM(psF[bk][:, jj * 128:(jj + 1) * 128], KT[r0:r0 + 64, p, j * 128:(j + 1) * 128],
                       QT[r0:r0 + 64, p, s * 128:(s + 1) * 128], True, True,
                       [("KT", p, j // 4), ("QT", p, s // 4)], [PSK(bk)])
            for ab in range(2):
                bk = 2 * set_ + ab
                ACT(pT[ab][:, c0 * 128:(c0 + n) * 128], psF[bk][:, 0:n * 128], AF.Exp, [PSK(bk)], [("pT", ab, c0 // 4)], scale=0.125)
            if c0 + n == nb:
                for ab in range(2):
                    TT("pool", pT[ab][:, (J - 1) * 128:(J + 1) * 128], pT[ab][:, (J - 1) * 128:(J + 1) * 128], msk[:, s, :], ALU.mult,
                       [("pT", ab, c0 // 4), "msk"], [("pT", ab, c0 // 4)])

        def att_pv(ui, c0):
            s, p = units[ui]
            J = J_of(s)
            nb = J + 1
            n = min(4, nb - c0)
            banks = acc_banks(s)
            for ab in range(2):
                acc, ak = banks[ab]
                vi = 2 * (ui % 2) + ab
                for jj in range(n):
                    j = c0 + jj
                    MM(acc[:, p * 65:(p + 1) * 65], pT[ab][:, j * 128:(j + 1) * 128], Vp[vi][:, j, :], j == 0, j == J,
                       [("pT", ab, c0 // 4), ("Vp", vi)], [ak])
            if c0 + n == nb and p == 3:
                tb = s % 2
                for ab in range(2):
                    acc, ak = banks[ab]
                    a3 = acc[:, 0:260].rearrange("p (h d) -> p h d", h=4)
                    rk = ("rec", ab)
                    P.op("dve", lambda e, o_=rec[ab][:, 0:4], i_=a3[:, :, 64]: e.reciprocal(out=o_, in_=i_), [ak], [rk])
                    TT("dve", atok[tb].rearrange("p (q two d) -> p q two d", two=2, d=64)[:, :, ab, :], a3[:, :, 0:64],
                       rec[ab][:, 0:4].unsqueeze(2).to_broadcast([128, 4, 64]), ALU.mult, [ak, rk], [("atok", tb)])
                deferred.append((s, tb))

        deferred = []

        def att_fin2():
            while deferred:
                s, tb = deferred.pop(0)
                set_ = cc[0] % NSET
                cc[0] += 1
                bk = 2 * set_
                ptr = psF[bk][:, :].bitcast(BF16)
                for c in range(4):
                    TR(ptr[:, c * 128:(c + 1) * 128], atok[tb][:, c * 128:(c + 1) * 128], [("atok", tb)], [PSK(bk)])
                CP("dve", attnT[:, :, s * 128:(s + 1) * 128], ptr[:, 0:512].rearrange("p (c t) -> p c t", c=4),
                   [PSK(bk)], [("attnT", s)])

        pending = []
        last_unit = [-1]
        for k in range(len(items)):
            while pending and (len(pending) > LOOK or any(items[q][1] == items[k][1] for q in pending)):
                had = bool(deferred)
                att_pv(*items[pending.pop(0)])
                if had:
                    att_fin2()
            if items[k][0] != last_unit[0]:
                att_prep(items[k][0])
                last_unit[0] = items[k][0]
            att_qk(*items[k])
            pending.append(k)
            if k == 40:
                e1_loads()
                e2_chunk_loads(0)
            if k == len(items) - 12:
                e1_prefetch_x(0)
        while pending:
            att_pv(*items[pending.pop(0)])
        att_fin2()

        if debug:
            dump([("d_at", attnT.rearrange("p a b -> p (a b)"), 4 * TOWN)])
            if upto == 2:
                finish()
                return nc

        P.barrier()
        A.release(m_l1)
        gpo = A.alloc([D], F32)
        gpf = A.alloc([D], F32)
        gpff = A.alloc([D], F32)
        x1 = A.alloc([8, D], F32)
        h2T = A.alloc([8, 1024], BF16)
        DMA("sp", "c4", gpo, gpo_d, [], ["gpo"])
        DMA("sp", "c5", gpf, gpf_d, [], ["gpf"])
        DMA("sp", "c6", gpff, gpff_d, [], ["gpff"])
        m_l2b = A.mark()

        for half in range(2):
            def hk(nm, half=half):
                return (nm, half)
            wb = 0 if half == 0 else 2
            wo = 2 - wb

            A.release(m_l2b)
            hTo = A.alloc([8, 1024], BF16)
            sguT = A.alloc([4, 1024], BF16)
            mT = A.alloc([8, 1024], BF16)
            lst = [A.alloc([8], F32) for _ in range(3)]
            del xt[2:]
            xt.append(A.alloc([D], F32))
            wkb = [A.alloc([512], F32) for _ in range(12)]

            def WK(i):
                return hk(("wk", i))
            t1 = [[wkb[0], wkb[1]], [wkb[2], wkb[3]]]
            t1k = [[WK(0), WK(1)], [WK(2), WK(3)]]
            gl = [[wkb[4], wkb[5]], [wkb[6], wkb[7]]]
            glk = [[WK(4), WK(5)], [WK(6), WK(7)]]
            vn = [wkb[8].bitcast(BF16)[:, 0:512], wkb[9].bitcast(BF16)[:, 0:512]]
            vnk = [WK(8), WK(9)]
            sgu = [wkb[10].bitcast(BF16)[:, 0:512], wkb[11].bitcast(BF16)[:, 0:512]]
            sguk = [WK(10), WK(11)]
            tg2 = [[wkb[0], wkb[1]], [wkb[2], wkb[3]]]
            m122 = [[wkb[4], wkb[5]], [wkb[6], wkb[7]]]

            if half == 1:
                e2_chunk_loads(0, wb)
            e_x = {}
            SQC = 0.21145921592026346

            def e1_s0a_act(sl):
                gs = half * 8 + sl
                if (half, sl) in pre_x:
                    b = pre_x[(half, sl)]
                    xctr[0] = b + 1
                else:
                    b = xctr[0] % len(xt)
                    xctr[0] += 1
                    DMA("sp", ("xt", b), xt[b], xown[gs * 128:(gs + 1) * 128, :], [], [("xt", b)])
                e_x[sl] = (nt_a1(xt[b], ("xt", b)), b)

            def e1_s0a_dve(sl):
                xi, b = e_x[sl]
                nt_a2(xi, xt[b], ("xt", b), gpm, "gpm")

            def e1_s0b(sl):
                nt_b(e_x[sl][0], hTo[:, :, sl * 128:(sl + 1) * 128], hk(("hTo", sl)), on_act=False)

            def e1_s1_pe(sl):
                par = sl % 2
                for which in range(2):
                    ps, pk = psF[2 * par + which], PSK(2 * par + which)
                    for c in range(8):
                        MM(ps[:, :], hTo[:, c, sl * 128:(sl + 1) * 128], stg[wb + which][:, c, :], c == 0, c == 7,
                           [hk(("hTo", sl)), ("stg", wb + which)], [pk])
                for which in range(2):
                    ps, pk = psF[2 * par + which], PSK(2 * par + which)
                    tt_, tk = t1[par][which], t1k[par][which]
                    ACT(tt_, ps[:, :], AF.Square, [pk], [tk], scale=SQC)

            def e1_s1_inner(sl):
                par = sl % 2
                for which in range(2):
                    ps, pk = psF[2 * par + which], PSK(2 * par + which)
                    tt_, tk = t1[par][which], t1k[par][which]
                    STT(tt_, tt_, 1.0, ps[:, :], ALU.add, ALU.mult, [tk, pk], [tk])

            def e1_s1_exp(sl):
                par = sl % 2
                for which in range(2):
                    tt_, tk = t1[par][which], t1k[par][which]
                    ACT(tt_, tt_, AF.Exp, [tk], [tk], scale=-2.0 * GELU_C)
                for which in range(2):
                    tt_, tk = t1[par][which], t1k[par][which]
                    ACT(tt_, tt_, AF.Ln, [tk], [tk], bias=1.0)
                for which in range(2):
                    tt_, tk = t1[par][which], t1k[par][which]
                    ACT(tt_, tt_, AF.Exp, [tk], [tk], scale=-1.0)

            def e1_s1_gl(sl):
                par = sl % 2
                l_, lk = lst[par], hk(("lst", par))
                for which in range(2):
                    ps, pk = psF[2 * par + which], PSK(2 * par + which)
                    tt_, tk = t1[par][which], t1k[par][which]
                    if which == 0:
                        STT(gl[par][which], tt_, 2.0, ps[:, :], ALU.mult, ALU.mult, [tk, pk], [glk[par][which]])
                    else:
                        P.op("dve", lambda e, o_=gl[par][1], t_=tt_, p_=ps[:, :], a_=l_[:, 0:1]: e.scalar_tensor_tensor(
                            out=o_, in0=t_, scalar=2.0, in1=p_, op0=ALU.mult, op1=ALU.mult, accum_out=a_),
                             [tk, pk], [glk[par][1], lk])

            def e1_s2(sl):
                par = sl % 2
                l_, lk = lst[par], hk(("lst", par))
                v_, vk = gl[par][1], glk[par][1]
                j_, jk = t1[par][1], t1k[par][1]
                ACT(j_, v_, AF.Square, [vk], [jk, lk], accum_out=l_[:, 1:2])
                TS("dve", l_[:, 2:3], l_[:, 0:1], 1.0 / 512, None, ALU.mult, None, [lk], [lk])
                TT("dve", l_[:, 3:4], l_[:, 2:3], l_[:, 2:3], ALU.mult, [lk], [lk])
                STT(l_[:, 4:5], l_[:, 1:2], 1.0 / 512, l_[:, 3:4], ALU.mult, ALU.subtract, [lk], [lk])
                ACT(l_[:, 5:6], l_[:, 4:5], AF.Ln, [lk], [lk], bias=4.0 * EPS)
                ACT(l_[:, 6:7], l_[:, 5:6], AF.Exp, [lk], [lk], scale=-0.5)
                TS("dve", v_, v_, l_[:, 2:3], l_[:, 6:7], ALU.subtract, ALU.mult, [vk, lk], [vk])
                TT("dve", v_, v_, gsg, ALU.mult, [vk, "gsg"], [vk])
                TT("dve", vn[par], v_, bsg, ALU.add, [vk, "bsg"], [vnk[par]])

            def e1_s3(sl):
                par = sl % 2
                pm, pmk = psF[4 + par], PSK(4 + par)
                for g in range(8):
                    MM(pm[:, g * 64:(g + 1) * 64], wsT[:, g, :], vn[par][:, g * 64:(g + 1) * 64], True, True,
                       ["wsT", vnk[par]], [pmk])
                s1_, s1k = wkb[8 + par], vnk[par]
                TT("dve", s1_.rearrange("p (g c) -> p g c", g=8), pm.rearrange("p (g c) -> p g c", g=8),
                   bsp.unsqueeze(2).to_broadcast([128, 8, 64]), ALU.add, [pmk, "bsp"], [s1k])
                TT("dve", sgu[par], s1_, gl[par][0], ALU.mult, [s1k, glk[par][0]], [sguk[par]])

            def e1_s4(sl):
                par = sl % 2
                pb = psb_ctr[0] % 2
                psb_ctr[0] += 1
                for c in range(4):
                    TR(psB[pb][:, c * 128:(c + 1) * 128], sgu[par][:, c * 128:(c + 1) * 128], [sguk[par]], [PSK(6 + pb)])
                CP("dve", sguT[:, :, sl * 128:(sl + 1) * 128], psB[pb][:, 0:512].rearrange("p (c t) -> p c t", c=4),
                   [PSK(6 + pb)], [hk(("sguT", sl // 4))])

            for t in range(8 + 4):
                s1ok = 0 <= t - 1 < 8
                if t < 8:
                    e1_s0a_act(t)
                if s1ok:
                    e1_s1_pe(t - 1)
                if t < 8:
                    e1_s0a_dve(t)
                if s1ok:
                    e1_s1_inner(t - 1)
                if t < 8:
                    e1_s0b(t)
                if s1ok:
                    e1_s1_exp(t - 1)
                if 0 <= t - 4 < 8:
                    e1_s4(t - 4)
                if 0 <= t - 3 < 8:
                    e1_s3(t - 3)
                if 0 <= t - 2 < 8:
                    e1_s2(t - 2)
                if s1ok:
                    e1_s1_gl(t - 1)

            for sl in range(8):
                gs = half * 8 + sl
                DMA("sp", ("x1ld", sl), x1[:, sl, :], xown[gs * 128:(gs + 1) * 128, :], [], [("x1", half, sl)])

            units2 = [(ch, tt, o) for ch in range(4) for tt in range(2) for o in range(2)]

            def e2_s1(ui):
                ch, tt, o = units2[ui]
                if tt == 0 and o == 0 and ch + 1 < 4:
                    e2_chunk_loads(ch + 1, wb)
                if ch == 3 and tt == 0 and o == 0:
                    load_w(stg[wo], w_out_v[:, :, 0:512], ("stg", wo))
                    load_w(stg[wo + 1], w_out_v[:, :, 512:1024], ("stg", wo + 1))
                b0 = e2_b0(ch, wb)
                s0, s1_ = b0 // 2, b0 // 2 + 1
                k0a, k0b, k1 = ("sa", s0), ("sb", s0), ("sc", s0)
                k2 = [("sa", s1_), ("sb", s1_)]
                pset = 4 * (ui % 2)
                tcols = slice(tt * 512, (tt + 1) * 512)
                slh = slice(half * 1024 + tt * 512, half * 1024 + (tt + 1) * 512)
                akeys = [("attnT", s_) for s_ in range(half * 8 + tt * 4, half * 8 + tt * 4 + 4)]
                hkeys = [hk(("hTo", s_)) for s_ in range(tt * 4, tt * 4 + 4)]
                ocols = slice(o * 128, (o + 1) * 128)
                bank = lambda i: (psF[pset + i] if pset + i < 6 else psBf[pset + i - 6])
                for kc in range(4):
                    MM(bank(0)[:, :], hs[b0][:, kc, ocols], attnT[:, kc, slh], kc == 0, kc == 3, [k0a] + akeys, [PSK(pset)])
                for c in range(8):
                    MM(bank(1)[:, :], hs[b0 + 1][:, c, ocols], hTo[:, c, tcols], c == 0, c == 7, [k1] + hkeys, [PSK(pset + 1)])
                for kc in range(4):
                    MM(bank(2)[:, :], hs[b0][:, 4 + kc, ocols], sguT[:, kc, tcols], kc == 0, kc == 3,
                       [k0b, hk(("sguT", tt))], [PSK(pset + 2)])
                for c in range(8):
                    MM(bank(3)[:, :], hs[b0 + 2][:, c, ocols], hTo[:, c, tcols], c == 0, c == 7, k2 + hkeys, [PSK(pset + 3)])

            def e2_s2(ui):
                ch, tt, o = units2[ui]
                par = ui % 2
                pset = 4 * par
                oc = ch * 2 + o
                tcols = slice(tt * 512, (tt + 1) * 512)
                bank = lambda i: (psF[pset + i] if pset + i < 6 else psBf[pset + i - 6])
                ta, tak = tg2[par][0], WK(2 * par)
                tb_, tbk = tg2[par][1], WK(2 * par + 1)
                m1, m1k = m122[par][0], WK(4 + 2 * par)
                m2, m2k = m122[par][1], WK(5 + 2 * par)
                ACT(ta, bank(1)[:, :], AF.Exp, [PSK(pset + 1)], [tak], scale=-1.0)
                ACT(tb_, bank(3)[:, :], AF.Exp, [PSK(pset + 3)], [tbk], scale=-1.0)
                ACT(ta, ta, AF.Ln, [tak], [tak], bias=1.0)
                ACT(tb_, tb_, AF.Ln, [tbk], [tbk], bias=1.0)
                ACT(ta, ta, AF.Exp, [tak], [tak], scale=-1.0)
                ACT(tb_, tb_, AF.Exp, [tbk], [tbk], scale=-1.0)
                TT("dve", m1, ta, bank(0)[:, :], ALU.mult, [tak, PSK(pset)], [m1k])
                TT("dve", m2, tb_, bank(2)[:, :], ALU.mult, [tbk, PSK(pset + 2)], [m2k])
                STT(mT[:, oc, tcols], m1, 2.0, m2, ALU.mult, ALU.add, [m1k, m2k], [hk(("mT", tt))])

            for t in range(len(units2) + 1):
                if t >= 1:
                    e2_s2(t - 1)
                if t < len(units2):
                    e2_s1(t)

            e3_xi = {}

            def zt_view(k):
                return (wkb[2 * k], wkb[2 * k + 1]), (WK(2 * k), WK(2 * k + 1))

            def e3_s1(sl):
                k = sl % 3
                tt = sl // 4
                (z0, z1), (zk0, zk1) = zt_view(k)
                l_, lk = lst[k], hk(("lst", k))
                for hf in range(2):
                    for c in range(8):
                        MM(psF[2 * k + hf][:, :], mT[:, c, sl * 128:(sl + 1) * 128], stg[wo + hf][:, c, :], c == 0, c == 7,
                           [hk(("mT", tt)), ("stg", wo + hf)], [PSK(2 * k + hf)])
                ACT(z0, psF[2 * k][:, :], AF.Square, [PSK(2 * k)], [zk0, lk], accum_out=l_[:, 0:1])
                ACT(z1, psF[2 * k + 1][:, :], AF.Square, [PSK(2 * k + 1)], [zk1, lk], accum_out=l_[:, 1:2])

            def e3_s2(sl):
                k = sl % 3
                l_, lk = lst[k], hk(("lst", k))
                TT("dve", l_[:, 2:3], l_[:, 0:1], l_[:, 1:2], ALU.add, [lk], [lk])
                ACT(l_[:, 3:4], l_[:, 2:3], AF.Ln, [lk], [lk], scale=1.0 / D, bias=4.0 * EPS)
                ACT(l_[:, 4:5], l_[:, 3:4], AF.Exp, [lk], [lk], scale=-0.5)

            def e3_s3(sl):
                k = sl % 3
                (z0, z1), (zk0, zk1) = zt_view(k)
                l_, lk = lst[k], hk(("lst", k))
                xk_ = ("x1", half, sl)
                STT(z0, psF[2 * k][:, :], l_[:, 4:5], gpo[:, 0:512], ALU.mult, ALU.mult, [PSK(2 * k), lk, "gpo"], [zk0])
                STT(z1, psF[2 * k + 1][:, :], l_[:, 4:5], gpo[:, 512:1024], ALU.mult, ALU.mult,
                    [PSK(2 * k + 1), lk, "gpo"], [zk1])
                TT("dve", x1[:, sl, 0:512], z0, x1[:, sl, 0:512], ALU.add, [zk0, xk_], [xk_])
                TT("dve", x1[:, sl, 512:1024], z1, x1[:, sl, 512:1024], ALU.add, [zk1, xk_], [xk_])

            def e3_s4(sl):
                e3_xi[sl] = nt_a1(x1[:, sl, :], ("x1", half, sl))

            def e3_s5(sl):
                nt_a2(e3_xi[sl], x1[:, sl, :], ("x1", half, sl), gpf, "gpf")

            def e3_s6(sl):
                nt_b(e3_xi[sl], h2T[:, :, sl * 128:(sl + 1) * 128], ("h2T", half, sl // 4), on_act=True)

            e3_stages = [e3_s1, e3_s2, e3_s3, e3_s4, e3_s5, e3_s6]
            for t in range(8 + 5):
                for si_ in range(5, -1, -1):
                    sl_ = t - si_
                    if 0 <= sl_ < 8:
                        e3_stages[si_](sl_)
                if t == 6:
                    f_loads(0, wb)

            if debug and upto == 4:
                dump([("d_x1", x1.rearrange("p a b -> p (a b)"), 8 * 1024)])
                finish()
                return nc
            P.barrier()
            A.release(m_l2b)
            del xt[2:]
            actT = A.alloc([NFC, 1024], BF16)
            ffA = A.alloc([8, 512], F32)
            fw = A.alloc([4, 512], F32)
            tgf = [fw[:, 0, :], fw[:, 2, :]]
            a1 = [fw[:, 1, :], fw[:, 3, :]]
            ot = [fw[:, 0:2, :].rearrange("p a b -> p (a b)"), fw[:, 2:4, :].rearrange("p a b -> p (a b)")]
            fst = A.alloc([8, 4], F32)
            fctr = 0
            for fq in range(6):
                nfc = 4 if fq < 5 else 2
                sg_, su_ = (2 * fq + wb) % NSTG, (2 * fq + 1 + wb) % NSTG
                if fq >= 1:
                    f_loads(fq, wb)
                for f in range(nfc):
                    fc = fq * 4 + f
                    fcols = slice(f * 128, (f + 1) * 128)
                    for tt in range(2):
                        tcols = slice(tt * 512, (tt + 1) * 512)
                        pi = fctr % 2
                        fctr += 1
                        pg, pu = psF[pi], psF[2 + pi]
                        for c in range(8):
                            MM(pg[:, :], stg[sg_][:, c, fcols], h2T[:, c, tcols], c == 0, c == 7,
                               [("stg", sg_), ("h2T", half, tt)], [PSK(pi)])
                        for c in range(8):
                            MM(pu[:, :], stg[su_][:, c, fcols], h2T[:, c, tcols], c == 0, c == 7,
                               [("stg", su_), ("h2T", half, tt)], [PSK(2 + pi)])
                        ACT(tgf[pi], pg[:, :], AF.Exp, [PSK(pi)], [hk(("fw", 2 * pi))], scale=-1.0)
                        ACT(tgf[pi], tgf[pi], AF.Ln, [hk(("fw", 2 * pi))], [hk(("fw", 2 * pi))], bias=1.0)
                        ACT(tgf[pi], tgf[pi], AF.Exp, [hk(("fw", 2 * pi))], [hk(("fw", 2 * pi))], scale=-1.0)
                        TT("dve", a1[pi], tgf[pi], pg[:, :], ALU.mult, [hk(("fw", 2 * pi)), PSK(pi)], [hk(("fw", 2 * pi + 1))])
                        TT("dve", actT[:, fc, tcols], a1[pi], pu[:, :], ALU.mult, [hk(("fw", 2 * pi + 1)), PSK(2 + pi)], [hk(("actT", tt))])

            if debug and upto == 5:
                finish()
                return nc
            gjunk = A.alloc([512], BF16)

            def bank_of(sl):
                return psF[sl] if sl < 6 else psBf[sl - 6]

            def g_keys(sl):
                ob = sl % 2
                return hk(("fw", 2 * ob)), hk(("fw", 2 * ob + 1)), hk(("fst", sl))

            def g_ev1(r, sl):
                pbank = bank_of(sl)
                ok, ok2, fk = g_keys(sl)
                ACT(gjunk, pbank[:, :], AF.Square, [PSK(sl)], [hk("gjunk"), fk], accum_out=fst[:, sl, r:r + 1])
                if r == 0:
                    TT("dve", ffA[:, sl, :], pbank[:, :], gpff[:, 0:512], ALU.mult, [PSK(sl), "gpff"], [hk(("ffA", sl))])

            def g_ev2(sl):
                ok, ok2, fk = g_keys(sl)
                TT("dve", fst[:, sl, 2:3], fst[:, sl, 0:1], fst[:, sl, 1:2], ALU.add, [fk], [fk])
                ACT(fst[:, sl, 3:4], fst[:, sl, 2:3], AF.Ln, [fk], [fk], scale=1.0 / D, bias=EPS)
                ACT(fst[:, sl, 3:4], fst[:, sl, 3:4], AF.Exp, [fk], [fk], scale=-0.5)

            def g_ev3(sl):
                gs = half * 8 + sl
                pbank = bank_of(sl)
                ob = sl % 2
                ok, ok2, fk = g_keys(sl)
                STT(ot[ob][:, 0:512], ffA[:, sl, :], fst[:, sl, 3:4], x1[:, sl, 0:512], ALU.mult, ALU.add,
                    [hk(("ffA", sl)), fk, ("x1", half, sl)], [ok, ok2])
                TT("dve", ot[ob][:, 512:1024], pbank[:, :], gpff[:, 512:1024], ALU.mult, [PSK(sl), "gpff"], [ok, ok2])
                STT(ot[ob][:, 512:1024], ot[ob][:, 512:1024], fst[:, sl, 3:4], x1[:, sl, 512:1024], ALU.mult, ALU.add,
                    [ok, ok2, fk, ("x1", half, sl)], [ok, ok2])
                DMA("sp", ("out", ob), out_d[gs * 128:(gs + 1) * 128, :], ot[ob], [ok, ok2], [("outd", gs)])

            for r in range(2):
                sis = []
                for k3 in range(3):
                    n8 = 8 if k3 < 2 else 6
                    si = (k3 + r * 3 + wb) % NSTG
                    sis.append((si, n8))
                    load_w(stg[si][:, 0:n8, :], w_ffd_v[:, k3 * 8:k3 * 8 + n8, r * 512:(r + 1) * 512], ("stg", si))
                si, n8 = sis[0]
                for f in range(n8):
                    fc = f
                    for sl in range(8):
                        pbank = psF[sl] if sl < 6 else psBf[sl - 6]
                        MM(pbank[:, :], actT[:, fc, sl * 128:(sl + 1) * 128], stg[si][:, f, :], fc == 0, fc == NFC - 1,
                           [hk(("actT", sl // 4)), ("stg", si)], [PSK(sl)])
                if r == 1 and half == 0:
                    e1_loads(2)
                    e1_prefetch_x(1)
                for sl in range(8):
                    pbank = psF[sl] if sl < 6 else psBf[sl - 6]
                    for k3 in (1, 2):
                        si, n8 = sis[k3]
                        for f in range(n8):
                            fc = k3 * 8 + f
                            MM(pbank[:, :], actT[:, fc, sl * 128:(sl + 1) * 128], stg[si][:, f, :], fc == 0, fc == NFC - 1,
                               [hk(("actT", sl // 4)), ("stg", si)], [PSK(sl)])
                    g_ev1(r, sl)
                    if r == 1:
                        if sl >= 1:
                            g_ev2(sl - 1)
                        if sl >= 2:
                            g_ev3(sl - 2)
                if r == 1:
                    g_ev2(7)
                    g_ev3(6)
                    g_ev3(7)
            P.barrier()
            if debug and upto == 6:
                finish()
                return nc

        finish()
    return nc


_NC_CACHE = {}


def _layout_inputs(inp):
    f = lambda a: np.ascontiguousarray(np.asarray(a, dtype=np.float32))
    x = f(inp["x"])
    rep = lambda v, n=128: np.ascontiguousarray(np.broadcast_to(f(v).reshape(1, -1), (n, f(v).size)))
    common = {
        "w_in": f(inp["w_in"][0]), "w_a": f(inp["w_branch_a"][0]), "w_b": f(inp["w_branch_b"][0]),
        "w_out": f(inp["w_out"][0]), "w_ffi": f(inp["w_ffn_in"][0]), "w_ffd": f(inp["w_ffn_down"][0]),
        "gpm": rep(inp["g_pre_mix"][0]), "gpo": rep(inp["g_post_mix"][0]), "gpf": rep(inp["g_pre_ffn"][0]),
        "gpff": rep(inp["g_post_ffn"][0]), "gsg": rep(inp["g_sgu"][0]), "bsg": rep(inp["b_sgu"][0]),
        "bfb": rep(inp["b_forget"][0]),
        "gqc": np.ascontiguousarray(np.tile(f(inp["g_q"][0]).reshape(64, 1), (2, 1))),
        "gkc": np.ascontiguousarray(np.tile(f(inp["g_k"][0]).reshape(64, 1), (2, 1))),
        "wsp": f(inp["w_spatial"][0]),
        "bsp": np.ascontiguousarray(f(inp["b_spatial"][0]).T),
    }
    ones = np.ones((128, 128), np.float32)
    zeros = np.zeros((128, 128), np.float32)
    tri = np.triu(np.ones((128, 128), np.float32))
    in_maps = []
    for c in range(8):
        b, par = c // 2, c % 2
        blocks = []
        mk = np.zeros((128, NS, 2, 128), np.float32)
        for s in range(NS):
            m = s // 2
            if par == 0:
                i = 4 * m if s % 2 == 0 else 4 * m + 3
            else:
                i = 4 * m + 1 if s % 2 == 0 else 4 * m + 2
            blocks.append(i)
            J = J_of(s)
            for k in range(2):
                j = J - 1 + k
                mk[:, s, k, :] = ones if j < i else (tri if j == i else zeros)
        xo = np.concatenate([x[b, i * 128:(i + 1) * 128] for i in blocks], axis=0)
        d = dict(common)
        d["xseq"] = np.ascontiguousarray(x[b])
        d["xown"] = np.ascontiguousarray(xo)
        d["msk"] = np.ascontiguousarray(mk.reshape(128, NS * 256))
        in_maps.append((d, blocks))
    return in_maps


def kernel(**inputs):
    maps = _layout_inputs(inputs)
    if "nc" not in _NC_CACHE:
        _NC_CACHE["nc"] = build_nc()
    nc = _NC_CACHE["nc"]
    res = run_bass_kernel_spmd(nc, [m[0] for m in maps], core_ids=list(range(8)))
    out = np.empty((4, SEQ, D), np.float32)
    for c in range(8):
        b = c // 2
        o = res.results[c]["out"]
        for s, i in enumerate(maps[c][1]):
            out[b, i * 128:(i + 1) * 128] = o[s * 128:(s + 1) * 128]
    return out
```
